# Optimizing a Trainium2 kernel written in Bass

```python
import math
import jax, jax.numpy as jnp
from jax import lax
import numpy as np

D_MODEL = 1024
BATCH = 4
SEQ = 4096
DEPTH = 2

N_EVEN = (DEPTH + 1) // 2
N_ODD = DEPTH // 2

RMS_EPS = 1e-6

S5_WIDTH = D_MODEL // 2
S5_GROUP = 16
S5_GROUPS = S5_WIDTH // S5_GROUP
S5_STATE = 64
DT_MIN = 1e-3
DT_MAX = 1e-1

DIFF_WIDTH = D_MODEL - S5_WIDTH
DIFF_HEAD_DIM = 64
DIFF_VALUE_DIM = 2 * DIFF_HEAD_DIM
DIFF_HEADS = DIFF_WIDTH // DIFF_VALUE_DIM
ROT_DIM = DIFF_HEAD_DIM // 4
ROPE_THETA = 500000.0
Q_BLOCK = 128
EVEN_IN_COLS = S5_WIDTH + 3 * DIFF_WIDTH

RWKV_HEAD = 64
RWKV_HEADS = D_MODEL // RWKV_HEAD
DECAY_LORA = 64
AAA_LORA = 64
GATE_LORA = 128
GN_EPS = 64e-5

FFN_HIDDEN = ((8 * D_MODEL // 3 + 255) // 256) * 256

kernel_name = "hybrid_s5_diffattn_rwkv7_adaln_block"


def rms_norm(x, g, eps=RMS_EPS):
    xf = x.astype(jnp.float32)
    y = xf * lax.rsqrt(jnp.mean(xf * xf, axis=-1, keepdims=True) + eps)
    return (y * g.astype(jnp.float32)).astype(x.dtype)


def apply_partial_rope(x, cos, sin):
    half = ROT_DIM // 2
    xf = x[..., :ROT_DIM].astype(jnp.float32)
    x1, x2 = xf[..., :half], xf[..., half:]
    rot = jnp.concatenate([x1 * cos - x2 * sin, x2 * cos + x1 * sin], axis=-1).astype(x.dtype)
    return jnp.concatenate([rot, x[..., ROT_DIM:]], axis=-1)


def s5_ssm(u, lam_re, lam_im, log_dt, b_re, b_im, c_re, c_im, d_skip):
    uf = u.astype(jnp.float32)
    lr = lam_re.astype(jnp.float32)
    li = lam_im.astype(jnp.float32)
    dt = jnp.exp(log_dt.astype(jnp.float32))[:, None]
    mag = jnp.exp(lr * dt)
    abar_r = mag * jnp.cos(li * dt)
    abar_i = mag * jnp.sin(li * dt)
    den = lr * lr + li * li
    num_r = abar_r - 1.0
    q_r = (num_r * lr + abar_i * li) / den
    q_i = (abar_i * lr - num_r * li) / den
    br = b_re.astype(jnp.float32)
    bi = b_im.astype(jnp.float32)
    bb_r = q_r[..., None] * br - q_i[..., None] * bi
    bb_i = q_r[..., None] * bi + q_i[..., None] * br
    bu_r = jnp.einsum('bsgh,gph->bsgp', uf, bb_r)
    bu_i = jnp.einsum('bsgh,gph->bsgp', uf, bb_i)
    a_r = jnp.broadcast_to(abar_r, bu_r.shape)
    a_i = jnp.broadcast_to(abar_i, bu_i.shape)

    def combine(e1, e2):
        a1r, a1i, b1r, b1i = e1
        a2r, a2i, b2r, b2i = e2
        return (a2r * a1r - a2i * a1i,
                a2r * a1i + a2i * a1r,
                a2r * b1r - a2i * b1i + b2r,
                a2r * b1i + a2i * b1r + b2i)

    _, _, x_r, x_i = lax.associative_scan(combine, (a_r, a_i, bu_r, bu_i), axis=1)
    y = (jnp.einsum('bsgp,ghp->bsgh', x_r, c_re.astype(jnp.float32))
         - jnp.einsum('bsgp,ghp->bsgh', x_i, c_im.astype(jnp.float32))
         + d_skip.astype(jnp.float32) * uf)
    return y.astype(u.dtype)


def diff_attention(q, k, v, lam, lam_init, subln_g):
    seq = q.shape[1]
    scale = DIFF_HEAD_DIM ** -0.5
    qf = q.astype(jnp.float32)
    kf = k.astype(jnp.float32)
    vf = v.astype(jnp.float32)
    outs = []
    for i in range(seq // Q_BLOCK):
        kv_len = (i + 1) * Q_BLOCK
        q_blk = qf[:, i * Q_BLOCK:kv_len]
        s = jnp.einsum('bqhcd,bkhcd->bhcqk', q_blk, kf[:, :kv_len]) * scale
        q_pos = i * Q_BLOCK + jnp.arange(Q_BLOCK)
        k_pos = jnp.arange(kv_len)
        mask = k_pos[None, :] <= q_pos[:, None]
        p = jax.nn.softmax(jnp.where(mask, s, -jnp.inf), axis=-1)
        w = p[:, :, 0] - lam * p[:, :, 1]
        outs.append(jnp.einsum('bhqk,bkhe->bqhe', w, vf[:, :kv_len]))
    o = jnp.concatenate(outs, axis=1)
    o = rms_norm(o, subln_g) * (1.0 - lam_init)
    return o.astype(q.dtype)


def even_mixer(h, cos, sin, w_in, lam_re, lam_im, log_dt, b_re, b_im, c_re, c_im,
               d_skip, w_glu, q_norm, k_norm, lq1, lk1, lq2, lk2, subln, w_out, lam_init):
    bsz, seq, _ = h.shape
    proj = h @ w_in
    u, q, k, v = jnp.split(proj, [S5_WIDTH, S5_WIDTH + DIFF_WIDTH, S5_WIDTH + 2 * DIFF_WIDTH], axis=-1)
    y = s5_ssm(u.reshape(bsz, seq, S5_GROUPS, S5_GROUP), lam_re, lam_im, log_dt,
               b_re, b_im, c_re, c_im, d_skip).reshape(bsz, seq, S5_WIDTH)
    y = jax.nn.gelu(y)
    y = y * jax.nn.sigmoid(y @ w_glu)
    q = q.reshape(bsz, seq, DIFF_HEADS, 2, DIFF_HEAD_DIM)
    k = k.reshape(bsz, seq, DIFF_HEADS, 2, DIFF_HEAD_DIM)
    v = v.reshape(bsz, seq, DIFF_HEADS, DIFF_VALUE_DIM)
    q = apply_partial_rope(rms_norm(q, q_norm), cos, sin)
    k = apply_partial_rope(rms_norm(k, k_norm), cos, sin)
    lam = (jnp.exp(jnp.sum(lq1.astype(jnp.float32) * lk1.astype(jnp.float32)))
           - jnp.exp(jnp.sum(lq2.astype(jnp.float32) * lk2.astype(jnp.float32))) + lam_init)
    o = diff_attention(q, k, v, lam, lam_init, subln).reshape(bsz, seq, DIFF_WIDTH)
    return jnp.concatenate([y, o], axis=-1) @ w_out


def rwkv7_recurrence(r, decay, k, v, kk, a):
    bsz, _, nh, n = r.shape

    def step(state, inp):
        r_t, w_t, k_t, v_t, kk_t, b_t = inp
        sa = jnp.einsum('bhvk,bhk->bhv', state, -kk_t)
        state = (state * w_t[:, :, None, :] + sa[..., None] * b_t[:, :, None, :]
                 + v_t[..., None] * k_t[:, :, None, :])
        return state, jnp.einsum('bhvk,bhk->bhv', state, r_t)

    xs = tuple(jnp.moveaxis(t.astype(jnp.float32), 1, 0) for t in (r, decay, k, v, kk, kk * a))
    s0 = jnp.zeros((bsz, nh, n, n), jnp.float32)
    _, ys = lax.scan(step, s0, xs)
    return jnp.moveaxis(ys, 0, 1)


def rwkv7_mixer(h, mu, w_r, w_k, w_v, w_o, w0, w1, w2, a0, a1, a2, g1, g2,
                k_k, k_a, r_k, ln_g, ln_b):
    bsz, seq, dm = h.shape
    h_prev = jnp.pad(h, ((0, 0), (1, 0), (0, 0)))[:, :-1]
    dx = h_prev - h
    xr = h + dx * mu[0]
    xw = h + dx * mu[1]
    xk = h + dx * mu[2]
    xv = h + dx * mu[3]
    xa = h + dx * mu[4]
    xg = h + dx * mu[5]
    r = xr @ w_r
    w = -jax.nn.softplus(-(w0 + jnp.tanh(xw @ w1) @ w2)) - 0.5
    decay = jnp.exp(-jnp.exp(w.astype(jnp.float32)))
    k = xk @ w_k
    v = xv @ w_v
    a = jax.nn.sigmoid(a0 + (xa @ a1) @ a2)
    g = jax.nn.sigmoid(xg @ g1) @ g2
    hs = (bsz, seq, RWKV_HEADS, RWKV_HEAD)
    kk = (k * k_k).reshape(hs).astype(jnp.float32)
    kk = kk / jnp.maximum(jnp.sqrt(jnp.sum(kk * kk, axis=-1, keepdims=True)), 1e-12)
    k = k * (1.0 + (a - 1.0) * k_a)
    rh = r.reshape(hs).astype(jnp.float32)
    kh = k.reshape(hs).astype(jnp.float32)
    vh = v.reshape(hs).astype(jnp.float32)
    y = rwkv7_recurrence(rh, decay.reshape(hs), kh, vh, kk, a.reshape(hs))
    mean = jnp.mean(y, axis=-1, keepdims=True)
    var = jnp.mean((y - mean) ** 2, axis=-1, keepdims=True)
    y = ((y - mean) * lax.rsqrt(var + GN_EPS)).reshape(bsz, seq, dm)
    y = y * ln_g.astype(jnp.float32) + ln_b.astype(jnp.float32)
    bonus = jnp.sum(rh * kh * r_k.astype(jnp.float32), axis=-1, keepdims=True) * vh
    y = (y + bonus.reshape(bsz, seq, dm)).astype(h.dtype)
    return (y * g) @ w_o


def swiglu(h, w_gate, w_up, w_down):
    return (jax.nn.silu(h @ w_gate) * (h @ w_up)) @ w_down


def setup_inputs(seed: int = 0) -> dict:
    key = jax.random.key(seed)
    ks = iter(jax.random.split(key, 64))
    f32 = jnp.float32

    def nrm(shape, s):
        return jax.random.normal(next(ks), shape, f32) * s

    def unif(shape, lo, hi):
        return jax.random.uniform(next(ks), shape, f32, lo, hi)

    D = D_MODEL
    G, P, H16 = S5_GROUPS, S5_STATE, S5_GROUP
    d = DIFF_HEAD_DIM
    inp = {}
    inp["x"] = nrm((BATCH, SEQ, D), 1.0)
    inp["c"] = nrm((BATCH, D), 1.0)
    inp["positions"] = jnp.broadcast_to(jnp.arange(SEQ, dtype=jnp.int32), (BATCH, SEQ))
    inp["w_ada"] = nrm((DEPTH, D, 6 * D), 0.5 * D ** -0.5)
    inp["b_ada"] = nrm((DEPTH, 6 * D), 0.02)
    inp["norm_mix"] = 1.0 + nrm((DEPTH, D), 0.02)
    inp["norm_ffn"] = 1.0 + nrm((DEPTH, D), 0.02)
    inp["ffn_w_gate"] = nrm((DEPTH, D, FFN_HIDDEN), D ** -0.5)
    inp["ffn_w_up"] = nrm((DEPTH, D, FFN_HIDDEN), D ** -0.5)
    inp["ffn_w_down"] = nrm((DEPTH, FFN_HIDDEN, D), FFN_HIDDEN ** -0.5)
    inp["ev_w_in"] = nrm((N_EVEN, D, EVEN_IN_COLS), D ** -0.5)
    inp["ev_s5_lam_re"] = -0.5 + nrm((N_EVEN, G, P), 0.01)
    inp["ev_s5_lam_im"] = math.pi * jnp.arange(P, dtype=f32) + nrm((N_EVEN, G, P), 0.01)
    inp["ev_s5_log_dt"] = unif((N_EVEN, G), math.log(DT_MIN), math.log(DT_MAX))
    inp["ev_s5_b_re"] = nrm((N_EVEN, G, P, H16), (2 * H16) ** -0.5)
    inp["ev_s5_b_im"] = nrm((N_EVEN, G, P, H16), (2 * H16) ** -0.5)
    inp["ev_s5_c_re"] = nrm((N_EVEN, G, H16, P), (2 * P) ** -0.5)
    inp["ev_s5_c_im"] = nrm((N_EVEN, G, H16, P), (2 * P) ** -0.5)
    inp["ev_s5_d"] = nrm((N_EVEN, G, H16), 1.0)
    inp["ev_s5_w_glu"] = nrm((N_EVEN, S5_WIDTH, S5_WIDTH), S5_WIDTH ** -0.5)
    inp["ev_q_norm"] = 1.0 + nrm((N_EVEN, d), 0.02)
    inp["ev_k_norm"] = 1.0 + nrm((N_EVEN, d), 0.02)
    inp["ev_lambda_q1"] = nrm((N_EVEN, d), 0.1)
    inp["ev_lambda_k1"] = nrm((N_EVEN, d), 0.1)
    inp["ev_lambda_q2"] = nrm((N_EVEN, d), 0.1)
    inp["ev_lambda_k2"] = nrm((N_EVEN, d), 0.1)
    inp["ev_subln"] = 1.0 + nrm((N_EVEN, DIFF_VALUE_DIM), 0.02)
    inp["ev_w_out"] = nrm((N_EVEN, D, D), D ** -0.5)
    inp["od_mu"] = unif((N_ODD, 6, D), 0.0, 1.0)
    inp["od_w_r"] = nrm((N_ODD, D, D), D ** -0.5)
    inp["od_w_k"] = nrm((N_ODD, D, D), D ** -0.5)
    inp["od_w_v"] = nrm((N_ODD, D, D), D ** -0.5)
    inp["od_w_o"] = nrm((N_ODD, D, D), D ** -0.5)
    inp["od_w0"] = unif((N_ODD, D), -6.0, 0.0)
    inp["od_w1"] = nrm((N_ODD, D, DECAY_LORA), D ** -0.5)
    inp["od_w2"] = nrm((N_ODD, DECAY_LORA, D), 0.5 * DECAY_LORA ** -0.5)
    inp["od_a0"] = nrm((N_ODD, D), 0.1)
    inp["od_a1"] = nrm((N_ODD, D, AAA_LORA), D ** -0.5)
    inp["od_a2"] = nrm((N_ODD, AAA_LORA, D), 0.5 * AAA_LORA ** -0.5)
    inp["od_g1"] = nrm((N_ODD, D, GATE_LORA), D ** -0.5)
    inp["od_g2"] = nrm((N_ODD, GATE_LORA, D), GATE_LORA ** -0.5)
    inp["od_k_k"] = 0.85 + nrm((N_ODD, D), 0.02)
    inp["od_k_a"] = 1.0 + nrm((N_ODD, D), 0.02)
    inp["od_r_k"] = nrm((N_ODD, RWKV_HEADS, RWKV_HEAD), 0.1)
    inp["od_ln_g"] = 1.0 + nrm((N_ODD, D), 0.02)
    inp["od_ln_b"] = nrm((N_ODD, D), 0.02)
    return inp


def reference(x, c, positions, w_ada, b_ada, norm_mix, norm_ffn, ffn_w_gate, ffn_w_up,
              ffn_w_down, ev_w_in, ev_s5_lam_re, ev_s5_lam_im, ev_s5_log_dt, ev_s5_b_re,
              ev_s5_b_im, ev_s5_c_re, ev_s5_c_im, ev_s5_d, ev_s5_w_glu, ev_q_norm, ev_k_norm,
              ev_lambda_q1, ev_lambda_k1, ev_lambda_q2, ev_lambda_k2, ev_subln, ev_w_out,
              od_mu, od_w_r, od_w_k, od_w_v, od_w_o, od_w0, od_w1, od_w2, od_a0, od_a1,
              od_a2, od_g1, od_g2, od_k_k, od_k_a, od_r_k, od_ln_g, od_ln_b):
    inv_freq = ROPE_THETA ** (-jnp.arange(0, ROT_DIM, 2, dtype=jnp.float32) / ROT_DIM)
    ang = positions.astype(jnp.float32)[..., None] * inv_freq
    cos = jnp.cos(ang)[:, :, None, None, :]
    sin = jnp.sin(ang)[:, :, None, None, :]
    c_act = jax.nn.silu(c)
    for l in range(DEPTH):
        mod = (c_act @ w_ada[l] + b_ada[l])[:, None, :]
        shift_m, scale_m, gate_m, shift_f, scale_f, gate_f = jnp.split(mod, 6, axis=-1)
        h = rms_norm(x, norm_mix[l]) * (1.0 + scale_m) + shift_m
        if l % 2 == 0:
            e = l // 2
            lam_init = 0.8 - 0.6 * math.exp(-0.3 * l)
            mix = even_mixer(h, cos, sin, ev_w_in[e], ev_s5_lam_re[e], ev_s5_lam_im[e],
                             ev_s5_log_dt[e], ev_s5_b_re[e], ev_s5_b_im[e], ev_s5_c_re[e],
                             ev_s5_c_im[e], ev_s5_d[e], ev_s5_w_glu[e], ev_q_norm[e],
                             ev_k_norm[e], ev_lambda_q1[e], ev_lambda_k1[e], ev_lambda_q2[e],
                             ev_lambda_k2[e], ev_subln[e], ev_w_out[e], lam_init)
        else:
            o = l // 2
            mix = rwkv7_mixer(h, od_mu[o], od_w_r[o], od_w_k[o], od_w_v[o], od_w_o[o],
                              od_w0[o], od_w1[o], od_w2[o], od_a0[o], od_a1[o], od_a2[o],
                              od_g1[o], od_g2[o], od_k_k[o], od_k_a[o], od_r_k[o],
                              od_ln_g[o], od_ln_b[o])
        x = x + gate_m * mix
        h = rms_norm(x, norm_ffn[l]) * (1.0 + scale_f) + shift_f
        x = x + gate_f * swiglu(h, ffn_w_gate[l], ffn_w_up[l], ffn_w_down[l])
    return x
```

```python
import contextlib
import math
import numpy as np
import concourse.bass as bass
import concourse.mybir as mybir
from concourse.bass_utils import run_bass_kernel_spmd

F32 = mybir.dt.float32
BF16 = mybir.dt.bfloat16
I32 = mybir.dt.int32
AF = mybir.ActivationFunctionType
ALU = mybir.AluOpType

D = 1024
KC = 8
FFN = 2816
HC = 22
EPS = 1e-6


class _State:
    __slots__ = ("w", "r")

    def __init__(self):
        self.w = None
        self.r = {}


class Tile:
    def __init__(self, h, name):
        self.h = h
        self.name = name
        self.st = {None: _State()}

    def __getitem__(self, idx):
        return Ref(self, self.h[idx], None)

    def sub(self, key, ap):
        return Ref(self, ap, key)

    def r(self, ap):
        return Ref(self, ap, None)

    def states(self, key):
        if key is None:
            return list(self.st.values())
        if key not in self.st:
            self.st[key] = _State()
        return [self.st[key], self.st[None]]

    def wstate(self, key):
        if key not in self.st:
            self.st[key] = _State()
        return self.st[key]


class Ref:
    __slots__ = ("t", "ap", "key")

    def __init__(self, t, ap, key):
        self.t, self.ap, self.key = t, ap, key


class Sched:
    NDMA = 48

    def __init__(self, nc):
        self.nc = nc
        self.es = contextlib.ExitStack()
        self.eng = {"pe": nc.tensor, "act": nc.scalar, "dve": nc.vector,
                    "pool": nc.gpsimd, "sp": nc.sync}
        self.root_es = self.es
        self.semmap = {}
        self.ekey = {}
        self.gen = 0
        self.cnt = {}
        self.known = {}
        for e in self.eng:
            self.known[e] = {}
        self.new_engine_sems()
        self.lastclk = {e: {} for e in self.eng}
        self.dsem = [self.es.enter_context(nc.semaphore("d%d" % i)) for i in range(self.NDMA)]
        self.dtarget = [0] * self.NDMA
        self.dclock = [None] * self.NDMA
        self.dnext = 0
        self.ntile = 0
        self.nwait = 0
        self.ninstr = 0

    def new_engine_sems(self):
        self.gen += 1
        for e in self.eng:
            key = "%s#%d" % (e, self.gen)
            self.ekey[e] = key
            self.semmap[key] = self.root_es.enter_context(self.nc.semaphore("s_%s_%d" % (e, self.gen)))
            self.cnt[e] = 0

    def sb(self, shape, dt, name=None):
        self.ntile += 1
        name = "%s_%d" % (name or "t", self.ntile)
        h = self.es.enter_context(self.nc.sbuf_tensor(name, list(shape), dt))
        return Tile(h, name)

    def ps(self, shape, dt=F32, name=None):
        self.ntile += 1
        name = "%s_%d" % (name or "p", self.ntile)
        h = self.es.enter_context(self.nc.psum_tensor(name, list(shape), dt))
        return Tile(h, name)

    def dram(self, shape, dt, name, kind="Internal"):
        h = self.nc.dram_tensor(name, list(shape), dt, kind=kind).ap()
        return Tile(h, name)

    def _semobj(self, key):
        return self.semmap[key] if isinstance(key, str) else self.dsem[key]

    def _wait(self, e, ev):
        if ev is None:
            return
        key, val, clock = ev
        kn = self.known[e]
        if kn.get(key, 0) >= val:
            return
        self.eng[e].wait_ge(self._semobj(key), val)
        self.nwait += 1
        new = dict(kn)
        if clock:
            for k2, v2 in clock.items():
                if new.get(k2, 0) < v2:
                    new[k2] = v2
        new[key] = val
        self.known[e] = new

    def _deps(self, e, reads, writes, is_dma):
        for rf in reads:
            for st in rf.t.states(rf.key):
                if st.w is not None:
                    self._wait(e, st.w)
        for rf in writes:
            for st in rf.t.states(rf.key):
                if st.w is not None and (is_dma or st.w[0] != self.ekey[e]):
                    self._wait(e, st.w)
                for k, (v, c) in st.r.items():
                    if is_dma or k != self.ekey[e]:
                        self._wait(e, (k, v, c))

    def _record(self, ev, reads, writes):
        key, val, clock = ev
        for rf in reads:
            st = rf.t.wstate(rf.key)
            st.r[key] = (val, clock)
        for rf in writes:
            if rf.key is None:
                for k in list(rf.t.st.keys()):
                    if k is not None:
                        del rf.t.st[k]
            st = rf.t.wstate(rf.key)
            st.w = ev
            st.r = {}

    def op(self, e, fn, reads, writes):
        reads = [r for r in reads if isinstance(r, Ref)]
        self._deps(e, reads, writes, False)
        ins = fn(self.eng[e])
        self.cnt[e] += 1
        ins.then_inc(self.semmap[self.ekey[e]], 1)
        ev = (self.ekey[e], self.cnt[e], self.known[e])
        self.lastclk[e] = self.known[e]
        self._record(ev, reads, writes)
        self.ninstr += 1
        return ev

    def dma(self, q, out, in_, **kw):
        i = self.dnext
        self.dnext = (self.dnext + 1) % self.NDMA
        if self.dtarget[i] > 0:
            self._wait(q, (i, self.dtarget[i], self.dclock[i]))
        self._deps(q, [in_], [out], True)
        ins = self.eng[q].dma_start(out=out.ap, in_=in_.ap, **kw)
        ins.then_inc(self.dsem[i], 16)
        self.dtarget[i] += 16
        self.dclock[i] = self.known[q]
        ev = (i, self.dtarget[i], self.known[q])
        self._record(ev, [in_], [out])
        self.ninstr += 1
        return ev

    def wait_all(self, e, refs):
        for rf in refs:
            for st in rf.t.states(rf.key):
                self._wait(e, st.w)

    def close(self):
        self.es.close()
        if self.root_es is not self.es:
            self.root_es.close()

    def mm(self, out, lhsT, rhs, start=True, stop=True):
        return self.op("pe", lambda g: g.matmul(out.ap, lhsT.ap, rhs.ap, start=start, stop=stop),
                       [lhsT, rhs], [out])

    def transpose(self, out, in_, ident):
        return self.op("pe", lambda g: g.transpose(out.ap, in_.ap, ident.ap), [in_, ident], [out])

    def act(self, out, in_, func, bias=None, scale=None, e="act"):
        kw = {}
        if bias is not None:
            kw["bias"] = bias.ap if isinstance(bias, Ref) else bias
        if scale is not None:
            kw["scale"] = scale.ap if isinstance(scale, Ref) else scale
        return self.op(e, lambda g: g.activation(out.ap, in_.ap, func, **kw),
                       [in_, bias, scale], [out])

    def tt(self, e, out, in0, in1, op):
        return self.op(e, lambda g: g.tensor_tensor(out.ap, in0.ap, in1.ap, op), [in0, in1], [out])

    def ts(self, e, out, in0, s1, s2, op0, op1=None):
        a1 = s1.ap if isinstance(s1, Ref) else s1
        a2 = s2.ap if isinstance(s2, Ref) else s2
        if op1 is None:
            return self.op(e, lambda g: g.tensor_scalar(out.ap, in0.ap, a1, None, op0), [in0, s1], [out])
        return self.op(e, lambda g: g.tensor_scalar(out.ap, in0.ap, a1, a2, op0, op1), [in0, s1, s2], [out])

    def stt(self, e, out, in0, sc, in1, op0, op1):
        a = sc.ap if isinstance(sc, Ref) else sc
        return self.op(e, lambda g: g.scalar_tensor_tensor(out.ap, in0.ap, a, in1.ap, op0, op1),
                       [in0, sc, in1], [out])

    def scan(self, e, out, d0, d1, init, op0=ALU.mult, op1=ALU.add):
        a = init.ap if isinstance(init, Ref) else init
        return self.op(e, lambda g: g.tensor_tensor_scan(out.ap, d0.ap, d1.ap, a, op0, op1),
                       [d0, d1, init], [out])

    def copy(self, e, out, in_):
        if e == "act":
            return self.op(e, lambda g: g.copy(out.ap, in_.ap), [in_], [out])
        return self.op(e, lambda g: g.tensor_copy(out.ap, in_.ap), [in_], [out])

    def memset(self, e, out, val):
        return self.op(e, lambda g: g.memset(out.ap, val), [], [out])


def _barrier(S):
    for e in S.eng:
        for f in S.eng:
            if f != e and S.cnt[f] > 0:
                S._wait(e, (S.ekey[f], S.cnt[f], S.lastclk[f]))
        for i in range(S.NDMA):
            if S.dtarget[i] > 0:
                S._wait(e, (i, S.dtarget[i], S.dclock[i]))


class Phase:
    def __init__(self, S):
        self.S = S

    def __enter__(self):
        self.saved = self.S.es
        self.S.es = contextlib.ExitStack()
        return self

    def __exit__(self, *a):
        _barrier(self.S)
        self.S.es.close()
        self.S.es = self.saved
        self.S.new_engine_sems()
        return False


def pk(v):
    v = np.asarray(v, dtype=np.float32)
    lead = v.shape[:-1]
    n = v.shape[-1] // 128
    v = v.reshape(lead + (n, 128))
    v = np.moveaxis(v, -1, 0)
    return np.ascontiguousarray(v.reshape(128, -1))


class Ctx:
    pass


def load_cast(S, C, dram_ref, shape, q="sp", ceng="pool", name="w"):
    st = S.sb(shape, F32, name + "_st")
    S.dma(q, st[:], dram_ref)
    wt = S.sb(shape, BF16, name + "_bf")
    S.copy(ceng, wt[:], st[:])
    return wt


def setup_consts(S, C):
    C.ones_bf = S.sb([128, 128], BF16, "ones")
    S.memset("pool", C.ones_bf[:], 1.0)
    C.ones_f = S.sb([128, 128], F32, "onesf")
    S.memset("pool", C.ones_f[:], 1.0)
    C.blk_f = S.sb([128, 128], F32, "blkf")
    S.memset("pool", C.blk_f[:], 0.0)
    S.memset("pool", C.blk_f[0:64, 0:64], 1.0)
    S.memset("pool", C.blk_f[64:128, 64:128], 1.0)
    C.blk_bf = S.sb([128, 128], BF16, "blkbf")
    S.copy("pool", C.blk_bf[:], C.blk_f[:])


def setup_adaln(S, C, I):
    C.mod = S.sb([128, 96], F32, "mod")
    C.gsm = S.sb([128, 16], F32, "gsm")
    C.gsf = S.sb([128, 16], F32, "gsf")
    with Phase(S):
        cact = S.sb([128, 8], F32, "cact")
        S.dma("sp", cact[:], I["c_pk"][:])
        S.act(cact[:], cact[:], AF.Silu)
        bada = S.sb([128, 96], F32, "bada")
        S.dma("sp", bada[:], I["b_ada_pk"][:])
        nm = S.sb([128, 16], F32, "nm")
        S.dma("sp", nm[:], I["norm_mix_pk"][:])
        nf = S.sb([128, 16], F32, "nf")
        S.dma("sp", nf[:], I["norm_ffn_pk"][:])
        pm = S.ps([128, 96], F32, "pmod")
        wt = [S.sb([128, 8, 512], F32, "wada%d" % i) for i in range(2)]
        it = 0
        for l in range(2):
            wl = I["w_ada"].h[l].rearrange("(kc p) n -> p kc n", p=128)
            for ng in range(12):
                w = wt[it % 2]
                it += 1
                S.dma("sp" if it % 2 else "act", w[:], I["w_ada"].r(wl[:, :, ng * 512:(ng + 1) * 512]))
                for j in range(4):
                    col = l * 48 + ng * 4 + j
                    for kc in range(8):
                        S.mm(pm[:, col:col + 1], w[:, kc, j * 128:(j + 1) * 128], cact[:, kc:kc + 1],
                             start=(kc == 0), stop=(kc == 7))
        S.tt("dve", C.mod[:], pm[:], bada[:], ALU.add)
        for l in range(2):
            S.stt("dve", C.gsm[:, l * 8:(l + 1) * 8], C.mod[:, l * 48 + 8:l * 48 + 16], 1.0,
                  nm[:, l * 8:(l + 1) * 8], ALU.add, ALU.mult)
            S.stt("dve", C.gsf[:, l * 8:(l + 1) * 8], C.mod[:, l * 48 + 32:l * 48 + 40], 1.0,
                  nf[:, l * 8:(l + 1) * 8], ALU.add, ALU.mult)


def rsqrt_to(S, out, in_, scale, eps):
    S.ts("dve", out, in_, scale, eps, ALU.mult, ALU.add)
    S.act(out, out, AF.Sqrt)
    S.op("dve", lambda g: g.reciprocal(out.ap, out.ap), [out], [out])


def modcol(C, l, part, kc):
    c = l * 48 + part * 8 + kc
    return C.mod[:, c:c + 1]


def rmsnorm_mod(S, C, x, h, gs, l, shift_part, W, tmp, sq, rstd):
    for kc in range(8):
        S.act(sq[:, kc, :], x[:, kc, :], AF.Square)
    for s0 in range(0, W, 512):
        ss = C.ps_ss
        for kc in range(8):
            S.mm(ss[:, :], C.ones_bf[:], sq[:, kc, s0:s0 + 512], start=(kc == 0), stop=(kc == 7))
        rsqrt_to(S, rstd[:, s0:s0 + 512], ss[:, :], 1.0 / D, EPS)
    for kc in range(8):
        e = "dve" if kc % 2 == 0 else "pool"
        S.tt(e, tmp[kc % 2][:, :], x[:, kc, :], rstd[:, :], ALU.mult)
        S.act(h[:, kc, :], tmp[kc % 2][:, :], AF.Identity, bias=modcol(C, l, shift_part, kc),
              scale=gs[:, l * 8 + kc:l * 8 + kc + 1])


def phase_F(S, C, I, l, T, mo, xin, xout, wo, TBS=1024):
    wg = I["ffn_w_gate"].h[l].rearrange("(kc p) n -> p kc n", p=128)
    wu = I["ffn_w_up"].h[l].rearrange("(kc p) n -> p kc n", p=128)
    wd = I["ffn_w_down"].h[l].rearrange("(j p) n -> p j n", p=128)
    wov = wo.rearrange("(kc p) n -> p kc n", p=128)
    NS = TBS // 512
    with Phase(S):
        C.ps_ss = S.ps([128, 512], F32, "ss")
        psA = [S.ps([128, 512], F32, "psA%d" % i) for i in range(3)]
        psB = [S.ps([128, 512], F32, "psB%d" % i) for i in range(3)]
        x = S.sb([128, 8, TBS], F32, "x")
        mot = S.sb([128, 8, TBS], BF16, "mo")
        sq = S.sb([128, 8, TBS], BF16, "sq")
        h = S.sb([128, 8, TBS], BF16, "h")
        a = S.sb([128, HC, TBS], BF16, "a")
        rstd = S.sb([128, TBS], F32, "rstd")
        tmp = [S.sb([128, TBS], F32, "tmp%d" % i) for i in range(2)]
        sg = [S.sb([128, 512], F32, "sg%d" % i) for i in range(2)]
        wst = [S.sb([128, 8, 128], F32, "wst%d" % i) for i in range(4)]
        wbf = [S.sb([128, 8, 128], BF16, "wbf%d" % i) for i in range(4)]
        wdst = [S.sb([128, HC, 128], F32, "wdst%d" % i) for i in range(2)]
        wdbf = [S.sb([128, HC, 128], BF16, "wdbf%d" % i) for i in range(2)]
        wi = 0
        pi = 0
        for t0 in range(0, T, TBS):
            S.dma("sp", x[:], xin.r(xin.h[:, t0:t0 + TBS].rearrange("(kc p) t -> p kc t", p=128)))
            S.dma("act", mot[:], mo.r(mo.h[:, t0:t0 + TBS].rearrange("(kc p) t -> p kc t", p=128)))
            for n in range(8):
                k = wi % 4
                wi += 1
                S.dma("sp", wst[k][:], Ref(I["_w"], wov[:, :, n * 128:(n + 1) * 128], None))
                S.copy("pool", wbf[k][:], wst[k][:])
                for s in range(NS):
                    p = psA[pi % 3]
                    pi += 1
                    for kc in range(8):
                        S.mm(p[:, :], wbf[k][:, kc, :], mot[:, kc, s * 512:(s + 1) * 512],
                             start=(kc == 0), stop=(kc == 7))
                    S.stt("dve", x[:, n, s * 512:(s + 1) * 512], p[:, :], modcol(C, l, 2, n),
                          x[:, n, s * 512:(s + 1) * 512], ALU.mult, ALU.add)
            rmsnorm_mod(S, C, x, h, C.gsf, l, 3, TBS, tmp, sq, rstd)
            for j in range(HC):
                k0 = wi % 4
                k1 = (wi + 1) % 4
                wi += 2
                S.dma("sp", wst[k0][:], Ref(I["_w"], wg[:, :, j * 128:(j + 1) * 128], None))
                S.dma("act", wst[k1][:], Ref(I["_w"], wu[:, :, j * 128:(j + 1) * 128], None))
                S.copy("pool", wbf[k0][:], wst[k0][:])
                S.copy("pool", wbf[k1][:], wst[k1][:])
                for s in range(NS):
                    pg = psA[pi % 3]
                    pu = psB[pi % 3]
                    pi += 1
                    for kc in range(8):
                        S.mm(pg[:, :], wbf[k0][:, kc, :], h[:, kc, s * 512:(s + 1) * 512],
                             start=(kc == 0), stop=(kc == 7))
                    for kc in range(8):
                        S.mm(pu[:, :], wbf[k1][:, kc, :], h[:, kc, s * 512:(s + 1) * 512],
                             start=(kc == 0), stop=(kc == 7))
                    g = sg[pi % 2]
                    S.act(g[:, :], pg[:, :], AF.Silu)
                    S.tt("dve", a[:, j, s * 512:(s + 1) * 512], g[:, :], pu[:, :], ALU.mult)
            for n in range(8):
                k = n % 2
                S.dma("sp" if n % 2 == 0 else "act", wdst[k][:],
                      Ref(I["_w"], wd[:, :, n * 128:(n + 1) * 128], None))
                S.copy("pool", wdbf[k][:], wdst[k][:])
                for s in range(NS):
                    p = psA[pi % 3]
                    pi += 1
                    for j in range(HC):
                        S.mm(p[:, :], wdbf[k][:, j, :], a[:, j, s * 512:(s + 1) * 512],
                             start=(j == 0), stop=(j == HC - 1))
                    S.stt("dve", x[:, n, s * 512:(s + 1) * 512], p[:, :], modcol(C, l, 5, n),
                          x[:, n, s * 512:(s + 1) * 512], ALU.mult, ALU.add)
            S.dma("sp", xout.r(xout.h[:, t0:t0 + TBS].rearrange("(kc p) t -> p kc t", p=128)), x[:])


def declare_inputs(nc, specs):
    I = {}
    for name, (shape, dt) in specs.items():
        ap = nc.dram_tensor(name, list(shape), dt, kind="ExternalInput").ap()
        I[name] = Tile(ap, name)
    I["_w"] = Tile(None, "_w")
    return I


def input_specs(T):
    G, P = 32, 64
    sp = {
        "xT": ([D, T], F32), "c_pk": ([128, 8], F32),
        "w_ada": ([2, D, 6 * D], F32), "b_ada_pk": ([128, 96], F32),
        "norm_mix_pk": ([128, 16], F32), "norm_ffn_pk": ([128, 16], F32),
        "ffn_w_gate": ([2, D, FFN], F32), "ffn_w_up": ([2, D, FFN], F32),
        "ffn_w_down": ([2, FFN, D], F32),
        "ev_w_out": ([1, D, D], F32), "od_w_o": ([1, D, D], F32),
        "ev_w_in": ([1, D, 2048], F32), "ev_s5_w_glu": ([1, 512, 512], F32),
        "pos": ([T], I32), "rotm": ([128, 128], F32), "invf_col": ([128, 1], F32),
        "qk_gain_col": ([128, 2], F32), "masks": ([128, 4, 512], BF16),
        "lam_vecs": ([4, 64], F32), "subln_col": ([128, 1], F32),
        "swapm": ([128, 128], F32), "rowmask": ([128, 8], F32), "s5_d_pk": ([128, 4], F32),
        "lamre_row": ([128, 4, 64], F32), "lamim_row": ([128, 4, 64], F32), "logdt_row": ([128, 4], F32),
        "bT_re": ([128, 4, 64], F32), "bT_im": ([128, 4, 64], F32),
        "lamre_col": ([128, 32], F32), "lamim_col": ([128, 32], F32), "logdt_col": ([128, 32], F32),
        "cT_re_col": ([128, 32, 16], F32), "cT_im_col": ([128, 32, 16], F32),
        "od_w_r": ([1, D, D], F32), "od_w_k": ([1, D, D], F32), "od_w_v": ([1, D, D], F32),
        "od_w1": ([1, D, 64], F32), "od_a1": ([1, D, 64], F32), "od_g1": ([1, D, 128], F32),
        "od_w2": ([1, 64, D], F32), "od_a2": ([1, 64, D], F32), "od_g2": ([1, 128, D], F32),
        "od_vecs_pk": ([128, 5, 8], F32), "od_mu_pk": ([128, 48], F32), "ident": ([128, 128], F32),
        "rw_masks": ([64, 4, 8, 64], F32), "od_ln_pk64": ([64, 2, 16], F32),
    }
    return sp


def build(T=4096, mode="full"):
    nc = bass.Bass("TRN2", target_bir_lowering=False)
    specs = input_specs(T)
    if mode == "F":
        specs["moT"] = ([D, T], BF16)
    I = declare_inputs(nc, specs)
    out = Tile(nc.dram_tensor("yT", [D, T], F32, kind="ExternalOutput").ap(), "yT")
    S = Sched(nc)
    C = Ctx()
    setup_consts(S, C)
    setup_adaln(S, C, I)
    dbg = mode != "full"
    kind = "ExternalOutput" if dbg else "Internal"
    if mode == "F":
        phase_F(S, C, I, 0, T, I["moT"], I["xT"], out, I["ev_w_out"].h[0])
    if mode in ("L0", "full"):
        uT = S.dram([512, T], BF16, "uT", kind)
        qT = S.dram([512, T], BF16, "qT", kind)
        kT = S.dram([512, T], BF16, "kT", kind)
        vtm = S.dram([T, 512], BF16, "vtm", kind)
        ygT = S.dram([512, T], BF16, "ygT", kind)
        mo0 = S.dram([D, T], BF16, "mo0", kind)
        phase_A0(S, C, I, T, I["xT"], uT, qT, kT, vtm)
        phase_S5(S, C, I, T, uT, ygT)
        phase_glu(S, C, I, T, ygT, mo0)
        phase_attn(S, C, I, T, qT, kT, vtm, mo0, 0.8 - 0.6 * math.exp(-0.3 * 0))
        x1 = out if mode == "L0" else S.dram([D, T], F32, "x1T", kind)
        phase_F(S, C, I, 0, T, mo0, I["xT"], x1, I["ev_w_out"].h[0])
    if mode in ("L1", "full"):
        x1 = I["xT"] if mode == "L1" else x1
        R = {k: S.dram([D, T], F32, k, kind) for k in ("AR", "RR", "KB", "BB", "BON", "GG")}
        for k in ("KT", "BT", "VT"):
            R[k] = S.dram([T, D], F32, k, kind)
        R["GL"] = S.dram([D, T // 64], F32, "GL", kind)
        mo1 = S.dram([D, T], BF16, "mo1", kind)
        phase_B0(S, C, I, T, x1, R)
        phase_B1(S, C, I, T, R, mo1)
        phase_F(S, C, I, 1, T, mo1, x1, out, I["od_w_o"].h[0])
    S.wait_all("sp", [out[:]])
    _barrier(S)
    print("instrs", S.ninstr, "waits", S.nwait, {e: S.cnt[e] for e in S.cnt})
    S.close()
    return nc


def host_inputs(inp, b, T):
    f = lambda a: np.ascontiguousarray(np.asarray(a, dtype=np.float32))
    m = {
        "xT": np.ascontiguousarray(f(inp["x"][b, :T]).T),
        "c_pk": pk(f(inp["c"])[b]),
        "w_ada": f(inp["w_ada"]),
        "b_ada_pk": pk(f(inp["b_ada"]).reshape(-1)),
        "norm_mix_pk": pk(f(inp["norm_mix"]).reshape(-1)),
        "norm_ffn_pk": pk(f(inp["norm_ffn"]).reshape(-1)),
        "ffn_w_gate": f(inp["ffn_w_gate"]), "ffn_w_up": f(inp["ffn_w_up"]),
        "ffn_w_down": f(inp["ffn_w_down"]),
        "ev_w_out": f(inp["ev_w_out"]), "od_w_o": f(inp["od_w_o"]),
        "ev_w_in": f(inp["ev_w_in"]), "ev_s5_w_glu": f(inp["ev_s5_w_glu"]),
        "pos": np.ascontiguousarray(np.asarray(inp["positions"])[b, :T].astype(np.int32)),
    }
    m.update(const_inputs())
    qg = f(inp["ev_q_norm"])[0]; kg = f(inp["ev_k_norm"])[0]
    m["qk_gain_col"] = np.ascontiguousarray(np.stack([np.tile(qg, 2), np.tile(kg, 2)], axis=1))
    m["lam_vecs"] = np.ascontiguousarray(np.stack([f(inp["ev_lambda_q1"])[0], f(inp["ev_lambda_k1"])[0],
                                                   f(inp["ev_lambda_q2"])[0], f(inp["ev_lambda_k2"])[0]]))
    m["subln_col"] = np.ascontiguousarray(f(inp["ev_subln"])[0].reshape(128, 1))
    m["s5_d_pk"] = pk(f(inp["ev_s5_d"])[0].reshape(-1))
    lre = f(inp["ev_s5_lam_re"])[0]; lim = f(inp["ev_s5_lam_im"])[0]; ldt = f(inp["ev_s5_log_dt"])[0]
    rowg = (np.arange(128) // 16)[:, None] + 8 * np.arange(4)[None, :]
    m["lamre_row"] = np.ascontiguousarray(lre[rowg]); m["lamim_row"] = np.ascontiguousarray(lim[rowg])
    m["logdt_row"] = np.ascontiguousarray(ldt[rowg])
    hh = (np.arange(128) % 16)
    bre = f(inp["ev_s5_b_re"])[0]; bim = f(inp["ev_s5_b_im"])[0]
    m["bT_re"] = np.ascontiguousarray(bre[rowg, :, hh[:, None]]); m["bT_im"] = np.ascontiguousarray(bim[rowg, :, hh[:, None]])
    m["lamre_col"] = np.ascontiguousarray(np.concatenate([lre.T, lre.T], 0)); m["lamim_col"] = np.ascontiguousarray(np.concatenate([lim.T, lim.T], 0))
    m["logdt_col"] = np.ascontiguousarray(np.broadcast_to(ldt[None, :], (128, 32)))
    for nm in ("od_w_r", "od_w_k", "od_w_v", "od_w1", "od_a1", "od_g1", "od_w2", "od_a2", "od_g2"):
        m[nm] = f(inp[nm])
    vecs = np.stack([f(inp["od_w0"])[0], f(inp["od_a0"])[0], f(inp["od_k_k"])[0], f(inp["od_k_a"])[0],
                     f(inp["od_r_k"])[0].reshape(-1)])
    m["od_vecs_pk"] = np.ascontiguousarray(pk(vecs).reshape(128, 5, 8))
    m["od_mu_pk"] = pk(f(inp["od_mu"])[0])
    ln = np.stack([f(inp["od_ln_g"])[0], f(inp["od_ln_b"])[0]])
    m["od_ln_pk64"] = np.ascontiguousarray(np.transpose(ln.reshape(2, 16, 64), (2, 0, 1)))
    cre = np.transpose(f(inp["ev_s5_c_re"])[0], (2, 0, 1)); cim = np.transpose(f(inp["ev_s5_c_im"])[0], (2, 0, 1))
    m["cT_re_col"] = np.ascontiguousarray(np.concatenate([cre, cre], 0)); m["cT_im_col"] = np.ascontiguousarray(np.concatenate([cim, cim], 0))
    return m


def const_inputs():
    import ml_dtypes
    c = {}
    rotm = np.zeros((128, 128), np.float32)
    for base in (0, 64):
        for i in range(8):
            rotm[base + i + 8, base + i] = -1.0
            rotm[base + i, base + i + 8] = 1.0
    c["rotm"] = rotm
    invf = np.zeros((128, 1), np.float32)
    fr = (500000.0 ** (-np.arange(0, 16, 2, dtype=np.float32) / 16)).astype(np.float32)
    for base in (0, 64):
        invf[base:base + 8, 0] = fr
        invf[base + 8:base + 16, 0] = fr
    c["invf_col"] = invf
    kk = np.arange(128)[:, None]; qq = np.arange(512)[None, :]
    c["masks"] = np.stack([(qq >= 128 * j + kk) for j in range(4)], axis=1).astype(np.float32).astype(ml_dtypes.bfloat16)
    sw = np.zeros((128, 128), np.float32)
    for p in range(64):
        sw[64 + p, p] = 1.0
        sw[p, 64 + p] = -1.0
    c["swapm"] = sw
    c["ident"] = np.eye(128, dtype=np.float32)
    ss = np.arange(64)[:, None]; tt = np.arange(64)[None, :]
    m4 = np.stack([(ss < tt), (ss <= tt), (tt < ss), (ss == tt)]).astype(np.float32)
    c["rw_masks"] = np.ascontiguousarray(np.broadcast_to(np.transpose(m4, (1, 0, 2))[:, :, None, :], (64, 4, 8, 64)))
    c["rowmask"] = (np.arange(128)[:, None] // 16 == np.arange(8)[None, :]).astype(np.float32)
    return c


TWO_PI = 2.0 * math.pi


def sincos(S, sin_out, cos_out, ang, tmps):
    y, kf, ki = tmps
    for out, off in ((sin_out, 0.5), (cos_out, 0.75)):
        if out is None:
            continue
        S.ts("dve", y, ang, 1.0 / TWO_PI, off, ALU.mult, ALU.add)
        S.copy("dve", ki, y)
        S.copy("dve", kf, ki)
        S.tt("dve", y, y, kf, ALU.subtract)
        S.ts("dve", kf, y, 0.0, None, ALU.is_lt)
        S.tt("dve", y, y, kf, ALU.add)
        S.ts("dve", y, y, TWO_PI, -math.pi, ALU.mult, ALU.add)
        S.act(out, y, AF.Sin)


def phase_A0(S, C, I, T, xin, uT, qT, kT, vtm):
    win_v = I["ev_w_in"].h[0].rearrange("(kc p) n -> p kc n", p=128)
    with Phase(S):
        C.ps_ss = S.ps([128, 512], F32, "ss")
        psA = [S.ps([128, 512], F32, "psA%d" % i) for i in range(3)]
        psR = S.ps([128, 512], F32, "psR")
        win = S.sb([128, 8, 2048], BF16, "win")
        wst = [S.sb([128, 8, 256], F32, "wst%d" % i) for i in range(2)]
        for i in range(8):
            S.dma("sp" if i % 2 == 0 else "act", wst[i % 2][:], Ref(I["_w"], win_v[:, :, i * 256:(i + 1) * 256], None))
            S.copy("pool", win[:, :, i * 256:(i + 1) * 256], wst[i % 2][:])
        rotm = S.sb([128, 128], F32, "rotm")
        S.dma("sp", rotm[:], I["rotm"][:])
        invf = S.sb([128, 1], F32, "invf")
        S.dma("sp", invf[:], I["invf_col"][:])
        gq = S.sb([128, 2], F32, "gq")
        S.dma("sp", gq[:], I["qk_gain_col"][:])
        S.ts("dve", gq[:, 0:1], gq[:, 0:1], 0.125, None, ALU.mult)
        x = S.sb([128, 8, 512], F32, "x")
        sq = S.sb([128, 8, 512], BF16, "sq")
        h = S.sb([128, 8, 512], BF16, "h")
        rstd = S.sb([128, 512], F32, "rstd")
        tmp = [S.sb([128, 512], F32, "tmp%d" % i) for i in range(2)]
        posi = S.sb([128, 512], I32, "posi")
        ang = S.sb([128, 512], F32, "ang")
        cosT = S.sb([128, 512], F32, "cosT")
        sinT = S.sb([128, 512], F32, "sinT")
        sq2 = S.sb([128, 512], F32, "sq2")
        qn = S.sb([128, 512], F32, "qn")
        r2 = S.sb([128, 512], F32, "r2")
        ob = [S.sb([128, 512], BF16, "ob%d" % i) for i in range(3)]
        oi = 0
        pi = 0
        for t0 in range(0, T, 512):
            S.dma("sp", x[:], xin.r(xin.h[:, t0:t0 + 512].rearrange("(kc p) t -> p kc t", p=128)))
            S.dma("act", posi[:], I["pos"].r(I["pos"].h[t0:t0 + 512].partition_broadcast(128)))
            S.copy("dve", ang[:], posi[:])
            S.ts("dve", ang[:], ang[:], invf[:, 0:1], None, ALU.mult)
            sincos(S, sinT[:], cosT[:], ang[:], (tmp[0][:], tmp[1][:], posi[:]))
            rmsnorm_mod(S, C, x, h, C.gsm, 0, 0, 512, tmp, sq, rstd)
            for n in range(12):
                p = psA[pi % 3]
                pi += 1
                for kc in range(8):
                    S.mm(p[:, :], win[:, kc, n * 128:(n + 1) * 128], h[:, kc, :], start=(kc == 0), stop=(kc == 7))
                o = ob[oi % 3]
                oi += 1
                if n < 4:
                    S.copy("act", o[:], p[:, :])
                    S.dma("pool", uT.r(uT.h[n * 128:(n + 1) * 128, t0:t0 + 512]), o[:])
                    continue
                isq = n < 8
                S.act(sq2[:], p[:, :], AF.Square)
                S.mm(C.ps_ss[:, :], C.blk_f[:], sq2[:])
                rsqrt_to(S, rstd[:], C.ps_ss[:, :], 1.0 / 64, EPS)
                S.tt("dve", qn[:], p[:, :], rstd[:], ALU.mult)
                S.ts("dve", qn[:], qn[:], gq[:, 0:1] if isq else gq[:, 1:2], None, ALU.mult)
                S.mm(psR[:, :], rotm[:], qn[:])
                S.tt("dve", r2[:], psR[:, :], sinT[:], ALU.mult)
                S.tt("pool", qn[:], qn[:], cosT[:], ALU.add if False else ALU.mult)
                S.tt("pool", o[:], qn[:], r2[:], ALU.add)
                dst = qT if isq else kT
                hd = (n - 4) % 4
                S.dma("pool", dst.r(dst.h[hd * 128:(hd + 1) * 128, t0:t0 + 512]), o[:])
            for tt_ in range(4):
                p = psA[pi % 3]
                pi += 1
                for kc in range(8):
                    S.mm(p[:, :], h[:, kc, tt_ * 128:(tt_ + 1) * 128], win[:, kc, 1536:2048],
                         start=(kc == 0), stop=(kc == 7))
                o = ob[oi % 3]
                oi += 1
                S.copy("act", o[:], p[:, :])
                S.dma("pool", vtm.r(vtm.h[t0 + tt_ * 128:t0 + (tt_ + 1) * 128, :]), o[:])


def phase_attn(S, C, I, T, qT, kT, vtm, moT, lam_init):
    NQ = T // 512
    NK = T // 128
    with Phase(S):
        C.ps_ss = S.ps([128, 512], F32, "ss")
        psS = [S.ps([128, 512], F32, "psS%d" % i) for i in range(2)]
        psO = [S.ps([128, 512], F32, "psO%d" % i) for i in range(2)]
        psZ = [S.ps([128, 512], F32, "psZ%d" % i) for i in range(2)]
        masks = S.sb([128, 4, 512], BF16, "masks")
        S.dma("sp", masks[:], I["masks"][:])
        lv = S.sb([128, 4, 64], F32, "lv")
        S.dma("sp", lv[:], I["lam_vecs"].r(I["lam_vecs"].h[:, :].partition_broadcast(128)))
        lp = S.sb([128, 2, 64], F32, "lp")
        S.tt("dve", lp[:, 0, :], lv[:, 0, :], lv[:, 1, :], ALU.mult)
        S.tt("dve", lp[:, 1, :], lv[:, 2, :], lv[:, 3, :], ALU.mult)
        ls = S.sb([128, 2], F32, "ls")
        S.op("dve", lambda g: g.reduce_sum(ls.h[:, :], lp.h[:, :, :], axis=mybir.AxisListType.X), [lp[:]], [ls[:]])
        S.act(ls[:], ls[:], AF.Exp)
        nlam = S.sb([128, 1], F32, "nlam")
        S.tt("dve", nlam[:], ls[:, 1:2], ls[:, 0:1], ALU.subtract)
        S.ts("dve", nlam[:], nlam[:], -lam_init, None, ALU.add)
        sg = S.sb([128, 1], F32, "sg")
        S.dma("sp", sg[:], I["subln_col"][:])
        S.ts("dve", sg[:], sg[:], 1.0 - lam_init, None, ALU.mult)
        q = S.sb([128, T], BF16, "q")
        k = S.sb([128, T], BF16, "k")
        v = S.sb([128, NK, 128], BF16, "v")
        pt = [S.sb([128, 512], BF16, "pt%d" % i) for i in range(4)]
        rs = [S.sb([128, 512], F32, "rs%d" % i) for i in range(2)]
        o0 = S.sb([128, 512], F32, "o0")
        o1 = S.sb([128, 512], F32, "o1")
        sq2 = S.sb([128, 512], F32, "sq2")
        rstd = S.sb([128, 512], F32, "rstd")
        ob = [S.sb([128, 512], BF16, "ob%d" % i) for i in range(2)]
        it = 0
        for hd in range(4):
            S.dma("sp", q[:], qT.r(qT.h[hd * 128:(hd + 1) * 128, :]))
            S.dma("act", k[:], kT.r(kT.h[hd * 128:(hd + 1) * 128, :]))
            S.dma("sp", v[:], vtm.r(vtm.h[:, hd * 128:(hd + 1) * 128].rearrange("(kb p) e -> p kb e", p=128)))
            for qb in range(NQ):
                nkb = 4 * (qb + 1)
                for kb in range(nkb):
                    for c in range(2):
                        ps = psS[it % 2]
                        p = pt[it % 4]
                        it += 1
                        S.mm(ps[:, :], k[c * 64:(c + 1) * 64, kb * 128:(kb + 1) * 128],
                             q[c * 64:(c + 1) * 64, qb * 512:(qb + 1) * 512])
                        S.act(p[:], ps[:, :], AF.Exp)
                        j = kb - 4 * qb
                        if j >= 0:
                            S.tt("pool", p[:], p[:], masks[:, j, :], ALU.mult)
                        S.mm(psO[c][:, :], v[:, kb, :], p[:], start=(kb == 0), stop=(kb == nkb - 1))
                        S.mm(psZ[c][:, :], C.ones_bf[:], p[:], start=(kb == 0), stop=(kb == nkb - 1))
                for c in range(2):
                    S.op("dve", lambda g, c=c: g.reciprocal(rs[c].h[:, :], psZ[c].h[:, :]), [psZ[c][:]], [rs[c][:]])
                S.tt("dve", o0[:], psO[0][:, :], rs[0][:], ALU.mult)
                S.tt("dve", o1[:], psO[1][:, :], rs[1][:], ALU.mult)
                S.stt("dve", o0[:], o1[:], nlam[:, 0:1], o0[:], ALU.mult, ALU.add)
                S.act(sq2[:], o0[:], AF.Square)
                S.mm(C.ps_ss[:, :], C.ones_f[:], sq2[:])
                rsqrt_to(S, rstd[:], C.ps_ss[:, :], 1.0 / 128, EPS)
                S.tt("pool", o0[:], o0[:], rstd[:], ALU.mult)
                o = ob[qb % 2]
                S.ts("dve", o[:], o0[:], sg[:, 0:1], None, ALU.mult)
                S.dma("pool", moT.r(moT.h[512 + hd * 128:512 + (hd + 1) * 128, qb * 512:(qb + 1) * 512]), o[:])


def phase_S5(S, C, I, T, uT, ygT):
    NB = T // 512
    with Phase(S):
        psA = [S.ps([128, 512], F32, "psA%d" % i) for i in range(2)]
        psB = [S.ps([128, 512], F32, "psB%d" % i) for i in range(2)]
        psY = S.ps([128, 512], F32, "psY")
        psW = S.ps([128, 8], F32, "psW")
        swap = S.sb([128, 128], F32, "swap")
        S.dma("sp", swap[:], I["swapm"][:])
        rowmask = S.sb([128, 8], F32, "rowmask")
        S.dma("sp", rowmask[:], I["rowmask"][:])
        dcol = S.sb([128, 4], F32, "dcol")
        S.dma("sp", dcol[:], I["s5_d_pk"][:])
        lr = S.sb([128, 4, 64], F32, "lr")
        li = S.sb([128, 4, 64], F32, "li")
        ldt = S.sb([128, 4], F32, "ldt")
        S.dma("sp", lr[:], I["lamre_row"][:])
        S.dma("act", li[:], I["lamim_row"][:])
        S.dma("sp", ldt[:], I["logdt_row"][:])
        S.act(ldt[:], ldt[:], AF.Exp)
        dtb = ldt.h[:, :].unsqueeze(2).to_broadcast([128, 4, 64])
        th = S.sb([128, 4, 64], F32, "th")
        mag = S.sb([128, 4, 64], F32, "mag")
        S.tt("dve", th[:], li[:], ldt.r(dtb), ALU.mult)
        S.tt("dve", mag[:], lr[:], ldt.r(dtb), ALU.mult)
        S.act(mag[:], mag[:], AF.Exp)
        cs = S.sb([128, 4, 64], F32, "cs")
        sn = S.sb([128, 4, 64], F32, "sn")
        t4 = S.sb([128, 4, 64], F32, "t4")
        t4b = S.sb([128, 4, 64], F32, "t4b")
        t4i = S.sb([128, 4, 64], I32, "t4i")
        sincos(S, sn[:], cs[:], th[:], (t4[:], t4b[:], t4i[:]))
        ar = S.sb([128, 4, 64], F32, "ar")
        ai = S.sb([128, 4, 64], F32, "ai")
        S.tt("dve", ar[:], mag[:], cs[:], ALU.mult)
        S.tt("dve", ai[:], mag[:], sn[:], ALU.mult)
        S.ts("dve", ar[:], ar[:], -1.0, None, ALU.add)
        den = S.sb([128, 4, 64], F32, "den")
        S.tt("dve", den[:], lr[:], lr[:], ALU.mult)
        S.tt("dve", t4[:], li[:], li[:], ALU.mult)
        S.tt("dve", den[:], den[:], t4[:], ALU.add)
        S.op("dve", lambda g: g.reciprocal(den.h[:], den.h[:]), [den[:]], [den[:]])
        qr = S.sb([128, 4, 64], F32, "qr")
        qi = S.sb([128, 4, 64], F32, "qi")
        S.tt("dve", qr[:], ar[:], lr[:], ALU.mult)
        S.tt("dve", t4[:], ai[:], li[:], ALU.mult)
        S.tt("dve", qr[:], qr[:], t4[:], ALU.add)
        S.tt("dve", qr[:], qr[:], den[:], ALU.mult)
        S.tt("dve", qi[:], ai[:], lr[:], ALU.mult)
        S.tt("dve", t4[:], ar[:], li[:], ALU.mult)
        S.tt("dve", qi[:], qi[:], t4[:], ALU.subtract)
        S.tt("dve", qi[:], qi[:], den[:], ALU.mult)
        btr = S.sb([128, 4, 64], F32, "btr")
        bti = S.sb([128, 4, 64], F32, "bti")
        S.dma("sp", btr[:], I["bT_re"][:])
        S.dma("act", bti[:], I["bT_im"][:])
        bbr = S.sb([128, 4, 64], F32, "bbr")
        bbi = S.sb([128, 4, 64], F32, "bbi")
        S.tt("dve", bbr[:], qr[:], btr[:], ALU.mult)
        S.tt("dve", t4[:], qi[:], bti[:], ALU.mult)
        S.tt("dve", bbr[:], bbr[:], t4[:], ALU.subtract)
        S.tt("dve", bbi[:], qr[:], bti[:], ALU.mult)
        S.tt("dve", t4[:], qi[:], btr[:], ALU.mult)
        S.tt("dve", bbi[:], bbi[:], t4[:], ALU.add)
        nbbr = S.sb([128, 4, 64], F32, "nbbr")
        S.ts("dve", nbbr[:], bbr[:], -1.0, None, ALU.mult)
        lrc = S.sb([128, 32], F32, "lrc")
        lic = S.sb([128, 32], F32, "lic")
        dtc = S.sb([128, 32], F32, "dtc")
        S.dma("sp", lrc[:], I["lamre_col"][:])
        S.dma("act", lic[:], I["lamim_col"][:])
        S.dma("sp", dtc[:], I["logdt_col"][:])
        S.act(dtc[:], dtc[:], AF.Exp)
        thc = S.sb([128, 32], F32, "thc")
        rc = S.sb([128, 32], F32, "rc")
        S.tt("dve", thc[:], lic[:], dtc[:], ALU.mult)
        S.tt("dve", rc[:], lrc[:], dtc[:], ALU.mult)
        S.act(rc[:], rc[:], AF.Exp)
        c1 = S.sb([128, 32], F32, "c1")
        s1 = S.sb([128, 32], F32, "s1")
        t32 = S.sb([128, 32], F32, "t32")
        t32b = S.sb([128, 32], F32, "t32b")
        t32i = S.sb([128, 32], I32, "t32i")
        sincos(S, s1[:], c1[:], thc[:], (t32[:], t32b[:], t32i[:]))
        ctc = S.sb([128, 32, 16], F32, "ctc")
        cti = S.sb([128, 32, 16], F32, "cti")
        S.dma("sp", ctc[:], I["cT_re_col"][:])
        S.dma("act", cti[:], I["cT_im_col"][:])
        S.ts("dve", cti[:], cti[:], -1.0, None, ALU.mult)
        Ct = S.sb([128, 8, 512], F32, "Ct")
        St = S.sb([128, 8, 512], F32, "St")
        Cm = S.sb([128, 8, 512], F32, "Cm")
        Sm = S.sb([128, 8, 512], F32, "Sm")
        Rt = S.sb([128, 8, 512], F32, "Rt")
        tA = S.sb([128, 8, 256], F32, "tA")
        LA = S.sb([128, 8, 128], BF16, "LA")
        LB = S.sb([128, 8, 128], BF16, "LB")
        LC1 = S.sb([128, 8, 128], BF16, "LC1")
        LC2 = S.sb([128, 8, 128], BF16, "LC2")
        W = S.sb([128, 8, 512], F32, "W")
        w0 = S.sb([128, 8], F32, "w0")
        wl = S.sb([128, 8], F32, "wl")
        tw = S.sb([128, 8], F32, "tw")
        u = [S.sb([128, 512], BF16, "u%d" % i) for i in range(2)]
        t1 = [S.sb([128, 512], F32, "t1_%d" % i) for i in range(2)]
        t2 = [S.sb([128, 512], F32, "t2_%d" % i) for i in range(2)]
        R1 = [S.sb([128, 512], BF16, "R1_%d" % i) for i in range(2)]
        R2 = [S.sb([128, 512], BF16, "R2_%d" % i) for i in range(2)]
        yv = S.sb([128, 512], F32, "yv")
        y2 = S.sb([128, 512], F32, "y2")
        yo = [S.sb([128, 512], BF16, "yo%d" % i) for i in range(2)]
        it = 0
        for cc in range(4):
            g0 = cc * 8
            S.copy("dve", Ct[:, :, 0:1], c1.r(c1.h[:, g0:g0 + 8].unsqueeze(2)))
            S.copy("dve", St[:, :, 0:1], s1.r(s1.h[:, g0:g0 + 8].unsqueeze(2)))
            n = 1
            while n < 512:
                cn = Ct.r(Ct.h[:, :, n - 1:n].to_broadcast([128, 8, n]))
                sn_ = St.r(St.h[:, :, n - 1:n].to_broadcast([128, 8, n]))
                ta = tA.r(tA.h[:, :, 0:n])
                S.tt("dve", ta, St[:, :, 0:n], sn_, ALU.mult)
                S.tt("dve", Ct[:, :, n:2 * n], Ct[:, :, 0:n], cn, ALU.mult)
                S.tt("dve", Ct[:, :, n:2 * n], Ct[:, :, n:2 * n], ta, ALU.subtract)
                S.tt("dve", ta, St[:, :, 0:n], cn, ALU.mult)
                S.tt("dve", St[:, :, n:2 * n], Ct[:, :, 0:n], sn_, ALU.mult)
                S.tt("dve", St[:, :, n:2 * n], St[:, :, n:2 * n], ta, ALU.add)
                n *= 2
            S.copy("pool", Cm[0:64, :, :], Ct[0:64, :, :])
            S.ts("pool", Cm[64:128, :, :], St[64:128, :, :], -1.0, None, ALU.mult)
            S.copy("pool", Sm[0:64, :, :], St[0:64, :, :])
            S.copy("pool", Sm[64:128, :, :], Ct[64:128, :, :])
            S.memset("pool", Rt[:], 1.0)
            S.tt("pool", Rt[:], Rt[:], rc.r(rc.h[:, g0:g0 + 8].unsqueeze(2).to_broadcast([128, 8, 512])), ALU.mult)
            S.memset("pool", LC1[:], 0.0)
            S.memset("pool", LC2[:], 0.0)
            for gi in range(8):
                S.ts("dve", LA[:, gi, 0:64], bbr[:, cc, :], rowmask[:, gi:gi + 1], None, ALU.mult)
                S.ts("dve", LA[:, gi, 64:128], bbi[:, cc, :], rowmask[:, gi:gi + 1], None, ALU.mult)
                S.ts("dve", LB[:, gi, 0:64], bbi[:, cc, :], rowmask[:, gi:gi + 1], None, ALU.mult)
                S.ts("dve", LB[:, gi, 64:128], nbbr[:, cc, :], rowmask[:, gi:gi + 1], None, ALU.mult)
                S.copy("pool", LC1[:, gi, gi * 16:(gi + 1) * 16], ctc[:, g0 + gi, :])
                S.copy("pool", LC2[:, gi, gi * 16:(gi + 1) * 16], cti[:, g0 + gi, :])
            S.memset("pool", w0[:], 0.0)
            for tb in range(NB):
                ut = u[tb % 2]
                S.dma("sp", ut[:], uT.r(uT.h[cc * 128:(cc + 1) * 128, tb * 512:(tb + 1) * 512]))
                for gi in range(8):
                    k = it % 2
                    it += 1
                    S.mm(psA[k][:, :], LA[:, gi, :], ut[:])
                    S.mm(psB[k][:, :], LB[:, gi, :], ut[:])
                    S.tt("dve", t1[k][:], psA[k][:, :], Ct[:, gi, :], ALU.mult)
                    S.tt("dve", t2[k][:], psB[k][:, :], St[:, gi, :], ALU.mult)
                    S.tt("pool", t1[k][:], t1[k][:], t2[k][:], ALU.add)
                    S.scan("dve", W[:, gi, :], Rt[:, gi, :], t1[k][:], w0[:, gi:gi + 1])
                    S.tt("pool", R1[k][:], W[:, gi, :], Cm[:, gi, :], ALU.mult)
                    S.tt("pool", R2[k][:], W[:, gi, :], Sm[:, gi, :], ALU.mult)
                    S.mm(psY[:, :], LC1[:, gi, :], R1[k][:], start=(gi == 0), stop=False)
                    S.mm(psY[:, :], LC2[:, gi, :], R2[k][:], start=False, stop=(gi == 7))
                S.copy("dve", wl[:], W.r(W.h[:, :, 511]))
                S.mm(psW[:, :], swap[:], wl[:])
                S.tt("dve", tw[:], psW[:, :], St.r(St.h[:, :, 511]), ALU.mult)
                S.tt("dve", w0[:], wl[:], Ct.r(Ct.h[:, :, 511]), ALU.mult)
                S.tt("dve", w0[:], w0[:], tw[:], ALU.subtract)
                S.stt("dve", yv[:], ut[:], dcol[:, cc:cc + 1], psY[:, :], ALU.mult, ALU.add)
                S.act(y2[:], yv[:], AF.Square)
                S.ts("dve", y2[:], y2[:], 0.044715, 1.0, ALU.mult, ALU.add)
                S.tt("pool", y2[:], y2[:], yv[:], ALU.mult)
                S.act(y2[:], y2[:], AF.Sigmoid, scale=1.5957691216057308)
                o = yo[tb % 2]
                S.tt("dve", o[:], yv[:], y2[:], ALU.mult)
                S.dma("pool", ygT.r(ygT.h[cc * 128:(cc + 1) * 128, tb * 512:(tb + 1) * 512]), o[:])


def phase_glu(S, C, I, T, ygT, moT):
    wv = I["ev_s5_w_glu"].h[0].rearrange("(kc p) n -> p kc n", p=128)
    with Phase(S):
        ps = [S.ps([128, 512], F32, "ps%d" % i) for i in range(2)]
        wst = S.sb([128, 4, 512], F32, "wst")
        S.dma("sp", wst[:], Ref(I["_w"], wv, None))
        w = S.sb([128, 4, 512], BF16, "w")
        S.copy("pool", w[:], wst[:])
        yg = [S.sb([128, 4, 512], BF16, "yg%d" % i) for i in range(2)]
        sg = [S.sb([128, 512], F32, "sg%d" % i) for i in range(2)]
        ob = [S.sb([128, 512], BF16, "ob%d" % i) for i in range(2)]
        it = 0
        for tb in range(T // 512):
            y = yg[tb % 2]
            S.dma("sp", y[:], ygT.r(ygT.h[:, tb * 512:(tb + 1) * 512].rearrange("(kc p) t -> p kc t", p=128)))
            for n in range(4):
                p = ps[it % 2]
                for kc in range(4):
                    S.mm(p[:, :], w[:, kc, n * 128:(n + 1) * 128], y[:, kc, :], start=(kc == 0), stop=(kc == 3))
                S.act(sg[it % 2][:], p[:, :], AF.Sigmoid)
                S.tt("dve", ob[it % 2][:], y[:, n, :], sg[it % 2][:], ALU.mult)
                S.dma("pool", moT.r(moT.h[n * 128:(n + 1) * 128, tb * 512:(tb + 1) * 512]), ob[it % 2][:])
                it += 1


def phase_B0(S, C, I, T, xin, R):
    TB = 256
    NCH = TB // 64
    wv_ = {n: I[n].h[0].rearrange("(kc p) n -> p kc n", p=128) for n in ("od_w_r", "od_w_k", "od_w_v", "od_w1", "od_a1", "od_g1")}
    with Phase(S):
        C.ps_ss = S.ps([128, 512], F32, "ss")
        psA_ = [S.ps([128, 512], F32, "psA%d" % i) for i in range(4)]
        psT_ = [S.ps([128, 512], F32, "psT%d" % i) for i in range(2)]

        class _V:
            def __init__(self, t, w):
                self.t, self.w = t, w

            def __getitem__(self, idx):
                return Ref(self.t, self.t.h[:, 0:self.w][idx], None)
        psA = [_V(t, TB) for t in psA_]
        psT = [_V(t, 128) for t in psT_]
        wst = [S.sb([128, 8, 128], F32, "wst%d" % i) for i in range(2)]
        W3 = [S.sb([128, 8, 1024], BF16, "W%d" % i) for i in range(3)]
        wi = 0
        for m, nm in enumerate(("od_w_r", "od_w_k", "od_w_v")):
            for n in range(8):
                S.dma("sp" if wi % 2 == 0 else "act", wst[wi % 2][:], Ref(I["_w"], wv_[nm][:, :, n * 128:(n + 1) * 128], None))
                S.copy("pool", W3[m][:, :, n * 128:(n + 1) * 128], wst[wi % 2][:])
                wi += 1
        L1 = S.sb([128, 8, 256], BF16, "L1")
        for nm, lo, wd in (("od_w1", 0, 64), ("od_a1", 64, 64), ("od_g1", 128, 128)):
            S.dma("sp", wst[wi % 2][:, :, 0:wd], Ref(I["_w"], wv_[nm], None))
            S.copy("pool", L1[:, :, lo:lo + wd], wst[wi % 2][:, :, 0:wd])
            wi += 1
        L2st = S.sb([128, 3, 1024], F32, "L2st")
        S.memset("pool", L2st[:], 0.0)
        S.dma("sp", L2st[0:64, 0, :], I["od_w2"].r(I["od_w2"].h[0]))
        S.dma("sp", L2st[0:64, 1, :], I["od_a2"].r(I["od_a2"].h[0]))
        S.dma("sp", L2st[:, 2, :], I["od_g2"].r(I["od_g2"].h[0]))
        L2 = S.sb([128, 3, 1024], BF16, "L2")
        S.copy("pool", L2[:], L2st[:])
        pv = S.sb([128, 7, 8], F32, "pv")
        S.dma("sp", pv[:, 0:5, :], I["od_vecs_pk"][:])
        S.ts("dve", pv[:, 5, :], pv[:, 3, :], -1.0, 1.0, ALU.mult, ALU.add)
        S.ts("dve", pv[:, 6, :], pv[:, 0, :], -1.0, None, ALU.mult)
        mu = S.sb([128, 48], F32, "mu")
        S.dma("sp", mu[:], I["od_mu_pk"][:])
        ident = S.sb([128, 128], F32, "ident")
        S.dma("sp", ident[:], I["ident"][:])
        rmask = S.sb([128, TB], F32, "rmask")
        S.memset("pool", rmask[:], 1.0)
        for c in range(NCH):
            S.memset("pool", rmask[:, c * 64:c * 64 + 1], 0.0)
        x = S.sb([128, 8, TB], F32, "x")
        sq = S.sb([128, 8, TB], BF16, "sq")
        hs = S.sb([128, 8, TB + 1], F32, "hs")
        S.memset("pool", hs[:], 0.0)
        hh = S.sb([128, 8, TB], F32, "hh")
        dx = S.sb([128, 8, TB], F32, "dx")
        xm = [S.sb([128, 8, TB], BF16, "xm%d" % i) for i in range(6)]
        rstd = S.sb([128, TB], F32, "rstd")
        tmp = [S.sb([128, TB], F32, "tmp%d" % i) for i in range(2)]
        lo1 = S.sb([128, 3, TB], BF16, "lo1")
        S.memset("pool", lo1[:], 0.0)
        E = {k: S.sb([128, TB], F32, k) for k in ("r", "k", "v", "lw", "cum", "ep", "em", "ex", "a", "kk", "t0", "t1", "kmod", "b", "o0", "o1", "o2")}
        tq = [S.sb([128, 128], F32, "tq%d" % i) for i in range(2)]
        gl = S.sb([128, NCH], F32, "gl")
        pi = 0
        ti = 0
        for t0 in range(0, T, TB):
            S.dma("sp", x[:], xin.r(xin.h[:, t0:t0 + TB].rearrange("(kc p) t -> p kc t", p=128)))
            for kc in range(8):
                S.act(sq[:, kc, :], x[:, kc, :], AF.Square)
            for kc in range(8):
                S.mm(C.ps_ss[:, 0:TB], C.ones_bf[:], sq[:, kc, :], start=(kc == 0), stop=(kc == 7))
            rsqrt_to(S, rstd[:], C.ps_ss[:, 0:TB], 1.0 / D, EPS)
            for kc in range(8):
                S.tt("dve", tmp[kc % 2][:], x[:, kc, :], rstd[:], ALU.mult)
                S.act(hh[:, kc, :], tmp[kc % 2][:], AF.Identity, bias=modcol(C, 1, 0, kc), scale=C.gsm[:, 8 + kc:9 + kc])
            S.copy("pool", hs[:, :, 1:TB + 1], hh[:])
            S.tt("dve", dx[:], hs[:, :, 0:TB], hh[:], ALU.subtract)
            S.copy("pool", hs[:, :, 0:1], hh[:, :, TB - 1:TB])
            for m in range(6):
                for kc in range(8):
                    S.stt("dve", xm[m][:, kc, :], dx[:, kc, :],
                          mu[:, m * 8 + kc:m * 8 + kc + 1], hh[:, kc, :], ALU.mult, ALU.add)
            for j, (mx, lo, wd, fn) in enumerate(((1, 0, 64, AF.Tanh), (4, 64, 64, AF.Identity), (5, 128, 128, AF.Sigmoid))):
                p = psA[pi % 4]
                pi += 1
                for kc in range(8):
                    S.mm(p[0:wd, :], L1[:, kc, lo:lo + wd], xm[mx][:, kc, :], start=(kc == 0), stop=(kc == 7))
                S.act(lo1[0:wd, j, :], p[0:wd, :], fn)
            for n in range(8):
                f0 = n * 128
                pr, pk_, pv_, pw = [psA[(pi + i) % 4] for i in range(4)]
                for (p, m, mx) in ((pr, 0, 0), (pk_, 1, 2), (pv_, 2, 3)):
                    for kc in range(8):
                        S.mm(p[:, :], W3[m][:, kc, f0:f0 + 128], xm[mx][:, kc, :], start=(kc == 0), stop=(kc == 7))
                S.copy("act", E["r"][:], pr[:, :])
                S.copy("act", E["k"][:], pk_[:, :])
                S.copy("act", E["v"][:], pv_[:, :])
                S.mm(pw[:, :], L2[:, 0, f0:f0 + 128], lo1[:, 0, :])
                S.act(E["t0"][:], pw[:, :], AF.Exp, bias=pv[:, 6, n:n + 1], scale=-1.0)
                S.ts("dve", E["t0"][:], E["t0"][:], 1.0, None, ALU.add)
                S.act(E["t0"][:], E["t0"][:], AF.Ln)
                S.ts("dve", E["t0"][:], E["t0"][:], -1.0, -0.5, ALU.mult, ALU.add)
                S.act(E["t0"][:], E["t0"][:], AF.Exp)
                S.ts("dve", E["lw"][:], E["t0"][:], -1.0, None, ALU.mult)
                S.scan("dve", E["cum"][:], rmask[:], E["lw"][:], 0.0)
                S.act(E["ep"][:], E["cum"][:], AF.Exp)
                S.act(E["em"][:], E["cum"][:], AF.Exp, scale=-1.0)
                S.tt("pool", E["t1"][:], E["cum"][:], E["lw"][:], ALU.subtract)
                S.act(E["ex"][:], E["t1"][:], AF.Exp)
                S.mm(pw[:, :], L2[:, 1, f0:f0 + 128], lo1[:, 1, :])
                S.act(E["a"][:], pw[:, :], AF.Sigmoid, bias=pv[:, 1, n:n + 1])
                S.ts("dve", E["kk"][:], E["k"][:], pv[:, 2, n:n + 1], None, ALU.mult)
                S.tt("pool", E["t0"][:], E["kk"][:], E["kk"][:], ALU.mult)
                S.mm(C.ps_ss[:, 0:TB], C.blk_f[:], E["t0"][:])
                S.act(E["t0"][:], C.ps_ss[:, 0:TB], AF.Sqrt)
                S.ts("dve", E["t0"][:], E["t0"][:], 1e-12, None, ALU.max)
                S.op("dve", lambda g: g.reciprocal(E["t0"].h[:], E["t0"].h[:]), [E["t0"][:]], [E["t0"][:]])
                S.tt("dve", E["kk"][:], E["kk"][:], E["t0"][:], ALU.mult)
                S.ts("dve", E["t1"][:], E["a"][:], pv[:, 3, n:n + 1], pv[:, 5, n:n + 1], ALU.mult, ALU.add)
                S.tt("dve", E["kmod"][:], E["k"][:], E["t1"][:], ALU.mult)
                S.tt("pool", E["b"][:], E["kk"][:], E["a"][:], ALU.mult)
                S.mm(pw[:, :], L2[:, 2, f0:f0 + 128], lo1[:, 2, :])
                S.copy("act", E["o2"][:], pw[:, :])
                S.dma("pool", R["GG"].r(R["GG"].h[f0:f0 + 128, t0:t0 + TB]), E["o2"][:])
                pi += 4
                S.tt("dve", E["t0"][:], E["r"][:], E["kmod"][:], ALU.mult)
                S.ts("dve", E["t0"][:], E["t0"][:], pv[:, 4, n:n + 1], None, ALU.mult)
                S.mm(C.ps_ss[:, 0:TB], C.blk_f[:], E["t0"][:])
                S.tt("dve", E["o0"][:], C.ps_ss[:, 0:TB], E["v"][:], ALU.mult)
                S.dma("pool", R["BON"].r(R["BON"].h[f0:f0 + 128, t0:t0 + TB]), E["o0"][:])
                S.tt("dve", E["o1"][:], E["r"][:], E["ep"][:], ALU.mult)
                S.dma("pool", R["RR"].r(R["RR"].h[f0:f0 + 128, t0:t0 + TB]), E["o1"][:])
                S.stt("dve", E["o0"][:], E["kk"][:], -1.0, E["ex"][:], ALU.mult, ALU.mult)
                S.dma("pool", R["AR"].r(R["AR"].h[f0:f0 + 128, t0:t0 + TB]), E["o0"][:])
                S.tt("dve", E["kmod"][:], E["kmod"][:], E["em"][:], ALU.mult)
                S.dma("pool", R["KB"].r(R["KB"].h[f0:f0 + 128, t0:t0 + TB]), E["kmod"][:])
                S.tt("dve", E["b"][:], E["b"][:], E["em"][:], ALU.mult)
                S.dma("pool", R["BB"].r(R["BB"].h[f0:f0 + 128, t0:t0 + TB]), E["b"][:])
                S.copy("dve", gl[:], E["ep"].r(E["ep"].h[:, 63:TB:64]))
                S.dma("pool", R["GL"].r(R["GL"].h[f0:f0 + 128, t0 // 64:t0 // 64 + NCH]), gl[:])
                elb = gl.r(gl.h[:, :].unsqueeze(2).to_broadcast([128, NCH, 64]))
                S.tt("dve", E["kmod"].r(E["kmod"].h[:, :].rearrange("p (c t) -> p c t", t=64)),
                     E["kmod"].r(E["kmod"].h[:, :].rearrange("p (c t) -> p c t", t=64)), elb, ALU.mult)
                S.tt("dve", E["b"].r(E["b"].h[:, :].rearrange("p (c t) -> p c t", t=64)),
                     E["b"].r(E["b"].h[:, :].rearrange("p (c t) -> p c t", t=64)), elb, ALU.mult)
                for (src, dst) in ((E["kmod"], R["KT"]), (E["b"], R["BT"]), (E["v"], R["VT"])):
                    for s in range(TB // 128):
                        pt_ = psT[ti % 2]
                        q_ = tq[ti % 2]
                        ti += 1
                        S.transpose(pt_[:, :], src[:, s * 128:(s + 1) * 128], ident[:])
                        S.copy("act", q_[:], pt_[:, :])
                        S.dma("sp", dst.r(dst.h[t0 + s * 128:t0 + (s + 1) * 128, f0:f0 + 128]), q_[:])


def phase_B1(S, C, I, T, R, moT):
    SC = 256
    NCS = SC // 64
    NCH = T // 64
    GN_EPS = 64e-5
    with Phase(S):
        banks = [S.ps([64, 8, 64], F32, "bk%d" % i) for i in range(7)]
        psE = S.ps([64, 512], F32, "psE")
        bstate = [0]

        def bank():
            b = banks[bstate[0] % 7]
            bstate[0] += 1
            return b
        cm = S.sb([64, 4, 8, 64], F32, "cmask")
        S.dma("sp", cm[:], I["rw_masks"][:])
        MS, MI, MST, I8 = [cm.r(cm.h[:, i]) for i in range(4)]
        lng = S.sb([64, 2, 16], F32, "lng")
        S.dma("sp", lng[:], I["od_ln_pk64"][:])
        ones64 = C.ones_f[0:64, 0:64]
        fm = {k: [S.sb([64, 8, SC], F32, "%s%d" % (k, i)) for i in range(2)] for k in ("AR", "RR", "KB", "BB")}
        tm = {k: [S.sb([64, NCS, 512], F32, "%s%d" % (k, i)) for i in range(2)] for k in ("KT", "BT", "VT")}
        ep = {k: S.sb([64, 8, SC], F32, k) for k in ("BON", "GG")}
        GLt = S.sb([64, 8, NCH], F32, "GLt")
        ST = S.sb([64, 8, 64], F32, "ST")
        Y = S.sb([64, 8, SC], F32, "Y")
        Yc = S.sb([64, 8, SC], F32, "Yc")
        Ysq = S.sb([64, 8, SC], F32, "Ysq")
        rstd = S.sb([64, 512], F32, "rstd")
        ob = S.sb([64, 8, SC], BF16, "ob")
        A = {k: S.sb([64, 8, 64], F32, k) for k in ("N", "NT", "ak", "kr", "br", "Tm", "X", "UT")}
        Mp = [S.sb([64, 8, 64], F32, "M%d" % i) for i in range(2)]
        MTp = [S.sb([64, 8, 64], F32, "MT%d" % i) for i in range(2)]
        for g in range(2):
            rows = slice(g * 512, (g + 1) * 512)
            S.dma("sp", GLt[:], R["GL"].r(R["GL"].h[rows, :].rearrange("(h i) c -> i h c", i=64)))
            S.memset("pool", ST[:], 0.0)
            for si, t0 in enumerate(range(0, T, SC)):
                F = {}
                for k in fm:
                    F[k] = fm[k][si % 2]
                    S.dma("sp" if k in ("AR", "KB") else "act", F[k][:],
                          R[k].r(R[k].h[rows, t0:t0 + SC].rearrange("(h i) t -> i h t", i=64)))
                for k in tm:
                    F[k] = tm[k][si % 2]
                    S.dma("sp", F[k][:], R[k].r(R[k].h[t0:t0 + SC, rows].rearrange("(c s) f -> s c f", s=64)))
                for k in ep:
                    S.dma("act", ep[k][:], R[k].r(R[k].h[rows, t0:t0 + SC].rearrange("(h i) t -> i h t", i=64)))
                for c in range(NCS):
                    cs = slice(c * 64, (c + 1) * 64)
                    cg = t0 // 64 + c

                    def hv(name, h):
                        return F[name][:, c, h * 64:(h + 1) * 64]
                    for (dst, l, r_, msk) in ((A["N"], "BB", "AR", MS), (A["NT"], "AR", "BB", MST),
                                              (A["ak"], "KB", "AR", MS), (A["kr"], "KB", "RR", MI),
                                              (A["br"], "BB", "RR", MI)):
                        p = bank()
                        for h in range(8):
                            S.mm(p[:, h, :], F[l][:, h, cs], F[r_][:, h, cs])
                        S.tt("dve", dst[:], p[:], msk, ALU.mult)
                    S.tt("pool", A["Tm"][:], A["N"][:], I8, ALU.add)
                    M, MT = A["N"], A["NT"]
                    for lv in range(5):
                        p1, p2 = bank(), bank()
                        for h in range(8):
                            S.mm(p1[:, h, :], MT[:, h, :], M[:, h, :])
                            S.mm(p2[:, h, :], M[:, h, :], MT[:, h, :])
                        M2, MT2 = Mp[lv % 2], MTp[lv % 2]
                        S.copy("act", M2[:], p1[:])
                        S.copy("dve", MT2[:], p2[:])
                        p3 = bank()
                        for h in range(8):
                            S.mm(p3[:, h, :], MT2[:, h, :], A["Tm"][:, h, :])
                        S.tt("dve", A["Tm"][:], A["Tm"][:], p3[:], ALU.add)
                        M, MT = M2, MT2
                    px = bank()
                    for h in range(8):
                        S.mm(px[:, h, :], F["AR"][:, h, cs], ST[:, h, :], start=True, stop=False)
                        S.mm(px[:, h, :], A["ak"][:, h, :], hv("VT", h), start=False, stop=True)
                    S.copy("act", A["X"][:], px[:])
                    pu = bank()
                    for h in range(8):
                        S.mm(pu[:, h, :], A["Tm"][:, h, :], A["X"][:, h, :])
                    S.copy("act", A["UT"][:], pu[:])
                    py = bank()
                    for h in range(8):
                        S.mm(py[:, h, :], ST[:, h, :], F["RR"][:, h, cs], start=True, stop=False)
                        S.mm(py[:, h, :], hv("VT", h), A["kr"][:, h, :], start=False, stop=False)
                        S.mm(py[:, h, :], A["UT"][:, h, :], A["br"][:, h, :], start=False, stop=True)
                    S.copy("act", Y[:, :, cs], py[:])
                    pst = bank()
                    for h in range(8):
                        S.mm(pst[:, h, :], hv("KT", h), hv("VT", h), start=True, stop=False)
                        S.mm(pst[:, h, :], hv("BT", h), A["UT"][:, h, :], start=False, stop=True)
                    S.tt("dve", ST[:], ST[:], GLt.r(GLt.h[:, :, cg:cg + 1].to_broadcast([64, 8, 64])), ALU.mult)
                    S.tt("dve", ST[:], ST[:], pst[:], ALU.add)
                for q4 in range(8 * SC // 512):
                    hs_ = slice(q4 * (512 // SC), (q4 + 1) * (512 // SC))
                    S.mm(psE[:, :], ones64, Y[:, hs_, :])
                    S.stt("dve", Yc[:, hs_, :], psE[:, :], -1.0 / 64, Y[:, hs_, :], ALU.mult, ALU.add)
                    S.act(Ysq[:, hs_, :], Yc[:, hs_, :], AF.Square)
                    S.mm(psE[:, :], ones64, Ysq[:, hs_, :])
                    rsqrt_to(S, rstd[:], psE[:, :], 1.0 / 64, GN_EPS)
                    S.tt("dve", Yc[:, hs_, :], Yc[:, hs_, :], rstd[:], ALU.mult)
                for h in range(8):
                    hg = g * 8 + h
                    S.act(Yc[:, h, :], Yc[:, h, :], AF.Identity, bias=lng[:, 1, hg:hg + 1], scale=lng[:, 0, hg:hg + 1])
                S.tt("pool", Yc[:], Yc[:], ep["BON"][:], ALU.add)
                S.tt("dve", ob[:], Yc[:], ep["GG"][:], ALU.mult)
                S.dma("pool", moT.r(moT.h[rows, t0:t0 + SC].rearrange("(h i) t -> i h t", i=64)), ob[:])


_NC_CACHE = {}


def kernel(**inputs):
    T = 4096
    if "nc" not in _NC_CACHE:
        _NC_CACHE["nc"] = build(T, "full")
    nc = _NC_CACHE["nc"]
    maps = [host_inputs(inputs, c % 4, T) for c in range(8)]
    res = run_bass_kernel_spmd(nc, maps, core_ids=list(range(8)))
    out = np.stack([np.ascontiguousarray(np.asarray(res.results[b]["yT"]).T) for b in range(4)], axis=0)
    return out.astype(np.float32)
```

```python
import contextlib
import math
import numpy as np
import concourse.bass as bass
import concourse.mybir as mybir
from concourse.bass_utils import run_bass_kernel_spmd

F32 = mybir.dt.float32
BF16 = mybir.dt.bfloat16
I32 = mybir.dt.int32
AF = mybir.ActivationFunctionType
ALU = mybir.AluOpType

D = 1024
KC = 8
FFN = 2816
HC = 22
EPS = 1e-6


class _State:
    __slots__ = ("w", "r")

    def __init__(self):
        self.w = None
        self.r = {}


class Tile:
    def __init__(self, h, name):
        self.h = h
        self.name = name
        self.st = {None: _State()}

    def __getitem__(self, idx):
        return Ref(self, self.h[idx], None)

    def sub(self, key, ap):
        return Ref(self, ap, key)

    def r(self, ap):
        return Ref(self, ap, None)

    def states(self, key):
        if key is None:
            return list(self.st.values())
        if key not in self.st:
            self.st[key] = _State()
        return [self.st[key], self.st[None]]

    def wstate(self, key):
        if key not in self.st:
            self.st[key] = _State()
        return self.st[key]


class Ref:
    __slots__ = ("t", "ap", "key")

    def __init__(self, t, ap, key):
        self.t, self.ap, self.key = t, ap, key


class Sched:
    NDMA = 32

    def __init__(self, nc):
        self.nc = nc
        self.es = contextlib.ExitStack()
        self.eng = {"pe": nc.tensor, "act": nc.scalar, "dve": nc.vector,
                    "pool": nc.gpsimd, "sp": nc.sync}
        self.root_es = self.es
        self.semmap = {}
        self.ekey = {}
        self.gen = 0
        self.cnt = {}
        self.known = {}
        for e in self.eng:
            self.known[e] = {}
        self.new_engine_sems()
        self.lastclk = {e: {} for e in self.eng}
        self.dsem = [self.es.enter_context(nc.semaphore("d%d" % i)) for i in range(self.NDMA)]
        self.dtarget = [0] * self.NDMA
        self.dclock = [None] * self.NDMA
        self.dnext = 0
        self.ntile = 0
        self.nwait = 0
        self.ninstr = 0

    def new_engine_sems(self):
        self.gen += 1
        for e in self.eng:
            key = "%s#%d" % (e, self.gen)
            self.ekey[e] = key
            self.semmap[key] = self.root_es.enter_context(self.nc.semaphore("s_%s_%d" % (e, self.gen)))
            self.cnt[e] = 0

    def sb(self, shape, dt, name=None):
        self.ntile += 1
        name = "%s_%d" % (name or "t", self.ntile)
        h = self.es.enter_context(self.nc.sbuf_tensor(name, list(shape), dt))
        return Tile(h, name)

    def ps(self, shape, dt=F32, name=None):
        self.ntile += 1
        name = "%s_%d" % (name or "p", self.ntile)
        h = self.es.enter_context(self.nc.psum_tensor(name, list(shape), dt))
        return Tile(h, name)

    def dram(self, shape, dt, name, kind="Internal"):
        h = self.nc.dram_tensor(name, list(shape), dt, kind=kind).ap()
        return Tile(h, name)

    def _semobj(self, key):
        return self.semmap[key] if isinstance(key, str) else self.dsem[key]

    def _wait(self, e, ev):
        if ev is None:
            return
        key, val, clock = ev
        kn = self.known[e]
        if kn.get(key, 0) >= val:
            return
        self.eng[e].wait_ge(self._semobj(key), val)
        self.nwait += 1
        new = dict(kn)
        if clock:
            for k2, v2 in clock.items():
                if new.get(k2, 0) < v2:
                    new[k2] = v2
        new[key] = val
        self.known[e] = new

    def _deps(self, e, reads, writes, is_dma):
        for rf in reads:
            for st in rf.t.states(rf.key):
                if st.w is not None:
                    self._wait(e, st.w)
        for rf in writes:
            for st in rf.t.states(rf.key):
                if st.w is not None and (is_dma or st.w[0] != self.ekey[e]):
                    self._wait(e, st.w)
                for k, (v, c) in st.r.items():
                    if is_dma or k != self.ekey[e]:
                        self._wait(e, (k, v, c))

    def _record(self, ev, reads, writes):
        key, val, clock = ev
        for rf in reads:
            st = rf.t.wstate(rf.key)
            st.r[key] = (val, clock)
        for rf in writes:
            if rf.key is None:
                for k in list(rf.t.st.keys()):
                    if k is not None:
                        del rf.t.st[k]
            st = rf.t.wstate(rf.key)
            st.w = ev
            st.r = {}

    def op(self, e, fn, reads, writes):
        reads = [r for r in reads if isinstance(r, Ref)]
        self._deps(e, reads, writes, False)
        ins = fn(self.eng[e])
        self.cnt[e] += 1
        ins.then_inc(self.semmap[self.ekey[e]], 1)
        ev = (self.ekey[e], self.cnt[e], self.known[e])
        self.lastclk[e] = self.known[e]
        self._record(ev, reads, writes)
        self.ninstr += 1
        return ev

    def dma(self, q, out, in_, **kw):
        i = self.dnext
        self.dnext = (self.dnext + 1) % self.NDMA
        if self.dtarget[i] > 0:
            self._wait(q, (i, self.dtarget[i], self.dclock[i]))
        self._deps(q, [in_], [out], True)
        ins = self.eng[q].dma_start(out=out.ap, in_=in_.ap, **kw)
        ins.then_inc(self.dsem[i], 16)
        self.dtarget[i] += 16
        self.dclock[i] = self.known[q]
        ev = (i, self.dtarget[i], self.known[q])
        self._record(ev, [in_], [out])
        self.ninstr += 1
        return ev

    def collective(self, kind, alu, groups, in_, out):
        sem = self.root_es.enter_context(self.nc.semaphore("cc%d" % len(self.dsem)))
        self.dsem.append(sem)
        self.dtarget.append(0)
        self.dclock.append(None)
        i = len(self.dsem) - 1
        self._deps("pool", [in_], [out], True)
        ins = self.eng["pool"].collective_compute(kind, alu, replica_groups=groups,
                                                  ins=[in_.ap.opt()], outs=[out.ap.opt()])
        ins.then_inc(sem, 1)
        self.dtarget[i] = 1
        self.dclock[i] = self.known["pool"]
        ev = (i, 1, self.known["pool"])
        self._record(ev, [in_], [out])
        self.ninstr += 1
        return ev

    def wait_all(self, e, refs):
        for rf in refs:
            for st in rf.t.states(rf.key):
                self._wait(e, st.w)

    def close(self):
        self.es.close()
        if self.root_es is not self.es:
            self.root_es.close()

    def mm(self, out, lhsT, rhs, start=True, stop=True):
        return self.op("pe", lambda g: g.matmul(out.ap, lhsT.ap, rhs.ap, start=start, stop=stop),
                       [lhsT, rhs], [out])

    def transpose(self, out, in_, ident):
        return self.op("pe", lambda g: g.transpose(out.ap, in_.ap, ident.ap), [in_, ident], [out])

    def act(self, out, in_, func, bias=None, scale=None, e="act"):
        kw = {}
        if bias is not None:
            kw["bias"] = bias.ap if isinstance(bias, Ref) else bias
        if scale is not None:
            kw["scale"] = scale.ap if isinstance(scale, Ref) else scale
        return self.op(e, lambda g: g.activation(out.ap, in_.ap, func, **kw),
                       [in_, bias, scale], [out])

    def tt(self, e, out, in0, in1, op):
        return self.op(e, lambda g: g.tensor_tensor(out.ap, in0.ap, in1.ap, op), [in0, in1], [out])

    def ts(self, e, out, in0, s1, s2, op0, op1=None):
        a1 = s1.ap if isinstance(s1, Ref) else s1
        a2 = s2.ap if isinstance(s2, Ref) else s2
        if op1 is None:
            return self.op(e, lambda g: g.tensor_scalar(out.ap, in0.ap, a1, None, op0), [in0, s1], [out])
        return self.op(e, lambda g: g.tensor_scalar(out.ap, in0.ap, a1, a2, op0, op1), [in0, s1, s2], [out])

    def stt(self, e, out, in0, sc, in1, op0, op1):
        a = sc.ap if isinstance(sc, Ref) else sc
        return self.op(e, lambda g: g.scalar_tensor_tensor(out.ap, in0.ap, a, in1.ap, op0, op1),
                       [in0, sc, in1], [out])

    def scan(self, e, out, d0, d1, init, op0=ALU.mult, op1=ALU.add):
        a = init.ap if isinstance(init, Ref) else init
        return self.op(e, lambda g: g.tensor_tensor_scan(out.ap, d0.ap, d1.ap, a, op0, op1),
                       [d0, d1, init], [out])

    def copy(self, e, out, in_):
        if e == "act":
            return self.op(e, lambda g: g.copy(out.ap, in_.ap), [in_], [out])
        return self.op(e, lambda g: g.tensor_copy(out.ap, in_.ap), [in_], [out])

    def memset(self, e, out, val):
        return self.op(e, lambda g: g.memset(out.ap, val), [], [out])


def _barrier(S):
    for e in S.eng:
        for f in S.eng:
            if f != e and S.cnt[f] > 0:
                S._wait(e, (S.ekey[f], S.cnt[f], S.lastclk[f]))
        for i in range(len(S.dsem)):
            if S.dtarget[i] > 0:
                S._wait(e, (i, S.dtarget[i], S.dclock[i]))


class Phase:
    def __init__(self, S):
        self.S = S

    def __enter__(self):
        self.saved = self.S.es
        self.S.es = contextlib.ExitStack()
        return self

    def __exit__(self, *a):
        _barrier(self.S)
        self.S.es.close()
        self.S.es = self.saved
        if max(self.S.cnt.values()) > 16000:
            self.S.new_engine_sems()
        return False


def pk(v):
    v = np.asarray(v, dtype=np.float32)
    lead = v.shape[:-1]
    n = v.shape[-1] // 128
    v = v.reshape(lead + (n, 128))
    v = np.moveaxis(v, -1, 0)
    return np.ascontiguousarray(v.reshape(128, -1))


class Ctx:
    pass


def load_cast(S, C, dram_ref, shape, q="sp", ceng="pool", name="w"):
    st = S.sb(shape, F32, name + "_st")
    S.dma(q, st[:], dram_ref)
    wt = S.sb(shape, BF16, name + "_bf")
    S.copy(ceng, wt[:], st[:])
    return wt


def setup_consts(S, C):
    C.ones_bf = S.sb([128, 128], BF16, "ones")
    S.memset("pool", C.ones_bf[:], 1.0)
    C.ones_f = S.sb([128, 128], F32, "onesf")
    S.memset("pool", C.ones_f[:], 1.0)
    C.blk_f = S.sb([128, 128], F32, "blkf")
    S.memset("pool", C.blk_f[:], 0.0)
    S.memset("pool", C.blk_f[0:64, 0:64], 1.0)
    S.memset("pool", C.blk_f[64:128, 64:128], 1.0)
    C.blk_bf = S.sb([128, 128], BF16, "blkbf")
    S.copy("pool", C.blk_bf[:], C.blk_f[:])


def setup_adaln(S, C, I):
    C.mod = S.sb([128, 96], F32, "mod")
    C.gsm = S.sb([128, 16], F32, "gsm")
    C.gsf = S.sb([128, 16], F32, "gsf")
    with Phase(S):
        cact = S.sb([128, 8], F32, "cact")
        S.dma("sp", cact[:], I["c_pk"][:])
        S.act(cact[:], cact[:], AF.Silu)
        bada = S.sb([128, 96], F32, "bada")
        S.dma("sp", bada[:], I["b_ada_pk"][:])
        nm = S.sb([128, 16], F32, "nm")
        S.dma("sp", nm[:], I["norm_mix_pk"][:])
        nf = S.sb([128, 16], F32, "nf")
        S.dma("sp", nf[:], I["norm_ffn_pk"][:])
        pm = S.ps([128, 96], F32, "pmod")
        wt = [S.sb([128, 8, 512], F32, "wada%d" % i) for i in range(2)]
        it = 0
        for l in range(2):
            wl = I["w_ada"].h[l].rearrange("(kc p) n -> p kc n", p=128)
            for ng in range(12):
                w = wt[it % 2]
                it += 1
                S.dma("sp" if it % 2 else "act", w[:], I["w_ada"].r(wl[:, :, ng * 512:(ng + 1) * 512]))
                for j in range(4):
                    col = l * 48 + ng * 4 + j
                    for kc in range(8):
                        S.mm(pm[:, col:col + 1], w[:, kc, j * 128:(j + 1) * 128], cact[:, kc:kc + 1],
                             start=(kc == 0), stop=(kc == 7))
        S.tt("dve", C.mod[:], pm[:], bada[:], ALU.add)
        for l in range(2):
            S.stt("dve", C.gsm[:, l * 8:(l + 1) * 8], C.mod[:, l * 48 + 8:l * 48 + 16], 1.0,
                  nm[:, l * 8:(l + 1) * 8], ALU.add, ALU.mult)
            S.stt("dve", C.gsf[:, l * 8:(l + 1) * 8], C.mod[:, l * 48 + 32:l * 48 + 40], 1.0,
                  nf[:, l * 8:(l + 1) * 8], ALU.add, ALU.mult)


def rsqrt_to(S, out, in_, scale, eps):
    S.ts("dve", out, in_, scale, eps, ALU.mult, ALU.add)
    S.act(out, out, AF.Sqrt)
    S.op("dve", lambda g: g.reciprocal(out.ap, out.ap), [out], [out])


def modcol(C, l, part, kc):
    c = l * 48 + part * 8 + kc
    return C.mod[:, c:c + 1]


def rmsnorm_mod(S, C, x, h, gs, l, shift_part, W, tmp, sq, rstd):
    for kc in range(8):
        S.act(sq[:, kc, :], x[:, kc, :], AF.Square)
    for s0 in range(0, W, 512):
        ss = C.ps_ss
        for kc in range(8):
            S.mm(ss[:, :], C.ones_bf[:], sq[:, kc, s0:s0 + 512], start=(kc == 0), stop=(kc == 7))
        rsqrt_to(S, rstd[:, s0:s0 + 512], ss[:, :], 1.0 / D, EPS)
    for kc in range(8):
        e = "dve" if kc % 2 == 0 else "pool"
        S.tt(e, tmp[kc % 2][:, :], x[:, kc, :], rstd[:, :], ALU.mult)
        S.act(h[:, kc, :], tmp[kc % 2][:, :], AF.Identity, bias=modcol(C, l, shift_part, kc),
              scale=gs[:, l * 8 + kc:l * 8 + kc + 1])


def phase_F(S, C, I, l, T, mo, xin, xmid, xout, wo, PD, PDS, groups, TBS=1024):
    HCL = HC // 2
    wg = I["ffn_w_gate_h"].h[l].rearrange("(kc p) n -> p kc n", p=128)
    wu = I["ffn_w_up_h"].h[l].rearrange("(kc p) n -> p kc n", p=128)
    wd = I["ffn_w_down_h"].h[l].rearrange("(j p) n -> p j n", p=128)
    wov = wo.rearrange("(kc p) n -> p kc n", p=128)
    NS = TBS // 512
    with Phase(S):
        C.ps_ss = S.ps([128, 512], F32, "ss")
        psA = [S.ps([128, 512], F32, "psA%d" % i) for i in range(3)]
        psB = [S.ps([128, 512], F32, "psB%d" % i) for i in range(3)]
        x = S.sb([128, 8, TBS], F32, "x")
        mot = S.sb([128, 8, TBS], BF16, "mo")
        sq = S.sb([128, 8, TBS], BF16, "sq")
        h = S.sb([128, 8, TBS], BF16, "h")
        a = S.sb([128, HCL, TBS], BF16, "a")
        rstd = S.sb([128, TBS], F32, "rstd")
        tmp = [S.sb([128, TBS], F32, "tmp%d" % i) for i in range(2)]
        sg = [S.sb([128, 512], F32, "sg%d" % i) for i in range(2)]
        pdt = [S.sb([128, 512], F32, "pdt%d" % i) for i in range(3)]
        wst = [S.sb([128, 8, 128], F32, "wst%d" % i) for i in range(4)]
        wbf = [S.sb([128, 8, 128], BF16, "wbf%d" % i) for i in range(4)]
        wdst = [S.sb([128, HCL, 128], F32, "wdst%d" % i) for i in range(2)]
        wdbf = [S.sb([128, HCL, 128], BF16, "wdbf%d" % i) for i in range(2)]
        wi = 0
        pi = 0
        di = 0
        for t0 in range(0, T, TBS):
            S.dma("sp", x[:], xin.r(xin.h[:, t0:t0 + TBS].rearrange("(kc p) t -> p kc t", p=128)))
            S.dma("act", mot[:], mo.r(mo.h[:, t0:t0 + TBS].rearrange("(kc p) t -> p kc t", p=128)))
            for n in range(8):
                k = wi % 4
                wi += 1
                S.dma("sp", wst[k][:], Ref(I["_w"], wov[:, :, n * 128:(n + 1) * 128], None))
                S.copy("pool", wbf[k][:], wst[k][:])
                for s in range(NS):
                    p = psA[pi % 3]
                    pi += 1
                    for kc in range(8):
                        S.mm(p[:, :], wbf[k][:, kc, :], mot[:, kc, s * 512:(s + 1) * 512],
                             start=(kc == 0), stop=(kc == 7))
                    S.stt("dve", x[:, n, s * 512:(s + 1) * 512], p[:, :], modcol(C, l, 2, n),
                          x[:, n, s * 512:(s + 1) * 512], ALU.mult, ALU.add)
            S.dma("act", xmid.r(xmid.h[:, t0:t0 + TBS].rearrange("(kc p) t -> p kc t", p=128)), x[:])
            rmsnorm_mod(S, C, x, h, C.gsf, l, 3, TBS, tmp, sq, rstd)
            for j in range(HCL):
                k0 = wi % 4
                k1 = (wi + 1) % 4
                wi += 2
                S.dma("sp", wst[k0][:], Ref(I["_w"], wg[:, :, j * 128:(j + 1) * 128], None))
                S.dma("act", wst[k1][:], Ref(I["_w"], wu[:, :, j * 128:(j + 1) * 128], None))
                S.copy("pool", wbf[k0][:], wst[k0][:])
                S.copy("pool", wbf[k1][:], wst[k1][:])
                for s in range(NS):
                    pg = psA[pi % 3]
                    pu = psB[pi % 3]
                    pi += 1
                    for kc in range(8):
                        S.mm(pg[:, :], wbf[k0][:, kc, :], h[:, kc, s * 512:(s + 1) * 512],
                             start=(kc == 0), stop=(kc == 7))
                    for kc in range(8):
                        S.mm(pu[:, :], wbf[k1][:, kc, :], h[:, kc, s * 512:(s + 1) * 512],
                             start=(kc == 0), stop=(kc == 7))
                    g = sg[pi % 2]
                    S.act(g[:, :], pg[:, :], AF.Silu)
                    S.tt("dve", a[:, j, s * 512:(s + 1) * 512], g[:, :], pu[:, :], ALU.mult)
            for n in range(8):
                k = n % 2
                S.dma("sp" if n % 2 == 0 else "act", wdst[k][:],
                      Ref(I["_w"], wd[:, :, n * 128:(n + 1) * 128], None))
                S.copy("pool", wdbf[k][:], wdst[k][:])
                for s in range(NS):
                    p = psA[pi % 3]
                    pi += 1
                    for j in range(HCL):
                        S.mm(p[:, :], wdbf[k][:, j, :], a[:, j, s * 512:(s + 1) * 512],
                             start=(j == 0), stop=(j == HCL - 1))
                    o = pdt[di % 3]
                    di += 1
                    S.copy("act", o[:], p[:, :])
                    S.dma("pool", PD.r(PD.h[n * 128:(n + 1) * 128, t0 + s * 512:t0 + (s + 1) * 512]), o[:])
    for n in range(8):
        S.collective("AllReduce", ALU.add, groups, PD.r(PD.h[n * 128:(n + 1) * 128, :]),
                     PDS.r(PDS.h[n * 128:(n + 1) * 128, :]))
    with Phase(S):
        xs = [S.sb([128, 8, 512], F32, "xs%d" % i) for i in range(2)]
        ps_ = [S.sb([128, 8, 512], F32, "pds%d" % i) for i in range(2)]
        for bi, t0 in enumerate(range(0, T, 512)):
            xx, pp = xs[bi % 2], ps_[bi % 2]
            S.dma("sp", xx[:], xmid.r(xmid.h[:, t0:t0 + 512].rearrange("(kc p) t -> p kc t", p=128)))
            S.dma("act", pp[:], PDS.r(PDS.h[:, t0:t0 + 512].rearrange("(kc p) t -> p kc t", p=128)))
            for n in range(8):
                S.stt("dve", xx[:, n, :], pp[:, n, :], modcol(C, l, 5, n), xx[:, n, :], ALU.mult, ALU.add)
            S.dma("pool", xout.r(xout.h[:, t0:t0 + 512].rearrange("(kc p) t -> p kc t", p=128)), xx[:])


def declare_inputs(nc, specs):
    I = {}
    for name, (shape, dt) in specs.items():
        ap = nc.dram_tensor(name, list(shape), dt, kind="ExternalInput").ap()
        I[name] = Tile(ap, name)
    I["_w"] = Tile(None, "_w")
    return I


def input_specs(T):
    HH = FFN // 2
    sp = {
        "xT": ([D, T], F32), "c_pk": ([128, 8], F32),
        "w_ada": ([2, D, 6 * D], F32), "b_ada_pk": ([128, 96], F32),
        "norm_mix_pk": ([128, 16], F32), "norm_ffn_pk": ([128, 16], F32),
        "ffn_w_gate_h": ([2, D, HH], F32), "ffn_w_up_h": ([2, D, HH], F32),
        "ffn_w_down_h": ([2, HH, D], F32),
        "ev_w_out_p": ([D, D], F32), "od_w_o": ([D, D], F32),
        "ev_w_in_h": ([D, 1024], F32), "ev_s5_w_glu_h": ([512, 256], F32),
        "pos": ([T], I32), "rotm": ([128, 128], F32), "invf_col": ([128, 1], F32),
        "qk_gain_col": ([128, 2], F32), "masks": ([128, 4, 512], BF16),
        "lam_vecs": ([4, 64], F32), "subln_col": ([128, 1], F32),
        "swapm": ([128, 128], F32), "rowmask": ([128, 8], F32), "s5_d_pk": ([128, 2], F32),
        "lamre_row": ([128, 2, 64], F32), "lamim_row": ([128, 2, 64], F32), "logdt_row": ([128, 2], F32),
        "bT_re": ([128, 2, 64], F32), "bT_im": ([128, 2, 64], F32),
        "lamre_col": ([128, 16], F32), "lamim_col": ([128, 16], F32), "logdt_col": ([128, 16], F32),
        "cT_re_col": ([128, 16, 16], F32), "cT_im_col": ([128, 16, 16], F32),
        "od_w_r_h": ([D, 512], F32), "od_w_k_h": ([D, 512], F32), "od_w_v_h": ([D, 512], F32),
        "od_w1": ([1, D, 64], F32), "od_a1": ([1, D, 64], F32), "od_g1": ([1, D, 128], F32),
        "od_w2_h": ([64, 512], F32), "od_a2_h": ([64, 512], F32), "od_g2_h": ([128, 512], F32),
        "od_vecs_pk": ([128, 5, 4], F32), "od_mu_pk": ([128, 48], F32), "ident": ([128, 128], F32),
        "rw_masks": ([64, 4, 8, 64], F32), "od_ln_pk64": ([64, 2, 8], F32),
    }
    return sp


GROUPS = [[0, 4], [1, 5], [2, 6], [3, 7]]


def build(T=4096, mode="full"):
    nc = bass.Bass("TRN2", target_bir_lowering=False)
    I = declare_inputs(nc, input_specs(T))
    out = Tile(nc.dram_tensor("yT", [D, T], F32, kind="ExternalOutput").ap(), "yT")
    S = Sched(nc)
    C = Ctx()
    setup_consts(S, C)
    setup_adaln(S, C, I)
    dr = lambda shape, dt, name: S.dram(shape, dt, name, "Internal")
    uT = dr([256, T], BF16, "uT")
    qT = dr([256, T], BF16, "qT")
    kT = dr([256, T], BF16, "kT")
    vtm = dr([T, 256], BF16, "vtm")
    ygT = dr([256, T], BF16, "ygT")
    ygF = dr([512, T], BF16, "ygF")
    moL = dr([512, T], BF16, "moL")
    mo0 = dr([D, T], BF16, "mo0")
    phase_A0(S, C, I, T, I["xT"], uT, qT, kT, vtm)
    phase_S5(S, C, I, T, uT, ygT)
    S.collective("AllGather", ALU.bypass, GROUPS, ygT[:], ygF[:])
    phase_glu(S, C, I, T, ygF, ygT, moL)
    phase_attn(S, C, I, T, qT, kT, vtm, moL, 0.8 - 0.6 * math.exp(-0.3 * 0))
    for i in range(2):
        S.collective("AllGather", ALU.bypass, GROUPS, moL.r(moL.h[i * 256:(i + 1) * 256, :]),
                     mo0.r(mo0.h[i * 512:(i + 1) * 512, :]))
    xm0 = dr([D, T], F32, "xm0")
    x1 = dr([D, T], F32, "x1T")
    PD0 = dr([D, T], F32, "PD0")
    PS0 = dr([D, T], F32, "PS0")
    phase_F(S, C, I, 0, T, mo0, I["xT"], xm0, x1, I["ev_w_out_p"].h, PD0, PS0, GROUPS)
    R = {k: dr([512, T], F32, k) for k in ("AR", "RR", "KB", "BB", "BON", "GG")}
    for k in ("KT", "BT", "VT"):
        R[k] = dr([T, 512], F32, k)
    R["GL"] = dr([512, T // 64], F32, "GL")
    mo1L = dr([512, T], BF16, "mo1L")
    mo1 = dr([D, T], BF16, "mo1")
    phase_B0(S, C, I, T, x1, R)
    phase_B1(S, C, I, T, R, mo1L)
    for i in range(2):
        S.collective("AllGather", ALU.bypass, GROUPS, mo1L.r(mo1L.h[i * 256:(i + 1) * 256, :]),
                     mo1.r(mo1.h[i * 512:(i + 1) * 512, :]))
    xm1 = dr([D, T], F32, "xm1")
    PD1 = dr([D, T], F32, "PD1")
    PS1 = dr([D, T], F32, "PS1")
    phase_F(S, C, I, 1, T, mo1, x1, xm1, out, I["od_w_o"].h, PD1, PS1, GROUPS)
    S.wait_all("sp", [out[:]])
    _barrier(S)
    print("instrs", S.ninstr, "waits", S.nwait)
    S.close()
    return nc


def host_inputs(inp, b, hf, T):
    f = lambda a: np.ascontiguousarray(np.asarray(a, dtype=np.float32))
    HH = FFN // 2
    hs = slice(hf * HH, (hf + 1) * HH)
    q4 = slice(hf * 256, (hf + 1) * 256)
    win = f(inp["ev_w_in"])[0]
    wout = f(inp["ev_w_out"])[0]
    perm = np.concatenate([np.arange(0, 256), np.arange(512, 768), np.arange(256, 512), np.arange(768, 1024)])
    m = {
        "xT": np.ascontiguousarray(f(inp["x"][b, :T]).T),
        "c_pk": pk(f(inp["c"])[b]),
        "w_ada": f(inp["w_ada"]),
        "b_ada_pk": pk(f(inp["b_ada"]).reshape(-1)),
        "norm_mix_pk": pk(f(inp["norm_mix"]).reshape(-1)),
        "norm_ffn_pk": pk(f(inp["norm_ffn"]).reshape(-1)),
        "ffn_w_gate_h": f(f(inp["ffn_w_gate"])[:, :, hs]), "ffn_w_up_h": f(f(inp["ffn_w_up"])[:, :, hs]),
        "ffn_w_down_h": f(f(inp["ffn_w_down"])[:, hs, :]),
        "ev_w_out_p": f(wout), "od_w_o": f(f(inp["od_w_o"])[0][perm, :]),
        "ev_w_in_h": f(np.concatenate([win[:, 0 + hf * 256:0 + (hf + 1) * 256], win[:, 512 + hf * 256:512 + (hf + 1) * 256],
                                       win[:, 1024 + hf * 256:1024 + (hf + 1) * 256], win[:, 1536 + hf * 256:1536 + (hf + 1) * 256]], axis=1)),
        "ev_s5_w_glu_h": f(f(inp["ev_s5_w_glu"])[0][:, q4]),
        "pos": np.ascontiguousarray(np.asarray(inp["positions"])[b, :T].astype(np.int32)),
    }
    m.update(const_inputs())
    qg = f(inp["ev_q_norm"])[0]; kg = f(inp["ev_k_norm"])[0]
    m["qk_gain_col"] = np.ascontiguousarray(np.stack([np.tile(qg, 2), np.tile(kg, 2)], axis=1))
    m["lam_vecs"] = np.ascontiguousarray(np.stack([f(inp["ev_lambda_q1"])[0], f(inp["ev_lambda_k1"])[0],
                                                   f(inp["ev_lambda_q2"])[0], f(inp["ev_lambda_k2"])[0]]))
    m["subln_col"] = np.ascontiguousarray(f(inp["ev_subln"])[0].reshape(128, 1))
    gs = slice(hf * 16, (hf + 1) * 16)
    m["s5_d_pk"] = pk(f(inp["ev_s5_d"])[0][gs].reshape(-1))
    lre = f(inp["ev_s5_lam_re"])[0][gs]; lim = f(inp["ev_s5_lam_im"])[0][gs]; ldt = f(inp["ev_s5_log_dt"])[0][gs]
    rowg = (np.arange(128) // 16)[:, None] + 8 * np.arange(2)[None, :]
    m["lamre_row"] = np.ascontiguousarray(lre[rowg]); m["lamim_row"] = np.ascontiguousarray(lim[rowg])
    m["logdt_row"] = np.ascontiguousarray(ldt[rowg])
    hh = (np.arange(128) % 16)
    bre = f(inp["ev_s5_b_re"])[0][gs]; bim = f(inp["ev_s5_b_im"])[0][gs]
    m["bT_re"] = np.ascontiguousarray(bre[rowg, :, hh[:, None]]); m["bT_im"] = np.ascontiguousarray(bim[rowg, :, hh[:, None]])
    m["lamre_col"] = np.ascontiguousarray(np.concatenate([lre.T, lre.T], 0)); m["lamim_col"] = np.ascontiguousarray(np.concatenate([lim.T, lim.T], 0))
    m["logdt_col"] = np.ascontiguousarray(np.broadcast_to(ldt[None, :], (128, 16)))
    cre = np.transpose(f(inp["ev_s5_c_re"])[0][gs], (2, 0, 1)); cim = np.transpose(f(inp["ev_s5_c_im"])[0][gs], (2, 0, 1))
    m["cT_re_col"] = np.ascontiguousarray(np.concatenate([cre, cre], 0)); m["cT_im_col"] = np.ascontiguousarray(np.concatenate([cim, cim], 0))
    fs = slice(hf * 512, (hf + 1) * 512)
    for nm in ("od_w_r", "od_w_k", "od_w_v"):
        m[nm + "_h"] = f(f(inp[nm])[0][:, fs])
    for nm in ("od_w1", "od_a1", "od_g1"):
        m[nm] = f(inp[nm])
    for nm in ("od_w2", "od_a2", "od_g2"):
        m[nm + "_h"] = f(f(inp[nm])[0][:, fs])
    vecs = np.stack([f(inp["od_w0"])[0], f(inp["od_a0"])[0], f(inp["od_k_k"])[0], f(inp["od_k_a"])[0],
                     f(inp["od_r_k"])[0].reshape(-1)])[:, fs]
    m["od_vecs_pk"] = np.ascontiguousarray(pk(vecs).reshape(128, 5, 4))
    m["od_mu_pk"] = pk(f(inp["od_mu"])[0])
    ln = np.stack([f(inp["od_ln_g"])[0], f(inp["od_ln_b"])[0]])[:, fs]
    m["od_ln_pk64"] = np.ascontiguousarray(np.transpose(ln.reshape(2, 8, 64), (2, 0, 1)))
    return m


def const_inputs():
    import ml_dtypes
    c = {}
    rotm = np.zeros((128, 128), np.float32)
    for base in (0, 64):
        for i in range(8):
            rotm[base + i + 8, base + i] = -1.0
            rotm[base + i, base + i + 8] = 1.0
    c["rotm"] = rotm
    invf = np.zeros((128, 1), np.float32)
    fr = (500000.0 ** (-np.arange(0, 16, 2, dtype=np.float32) / 16)).astype(np.float32)
    for base in (0, 64):
        invf[base:base + 8, 0] = fr
        invf[base + 8:base + 16, 0] = fr
    c["invf_col"] = invf
    kk = np.arange(128)[:, None]; qq = np.arange(512)[None, :]
    c["masks"] = np.stack([(qq >= 128 * j + kk) for j in range(4)], axis=1).astype(np.float32).astype(ml_dtypes.bfloat16)
    sw = np.zeros((128, 128), np.float32)
    for p in range(64):
        sw[64 + p, p] = 1.0
        sw[p, 64 + p] = -1.0
    c["swapm"] = sw
    c["ident"] = np.eye(128, dtype=np.float32)
    ss = np.arange(64)[:, None]; tt = np.arange(64)[None, :]
    m4 = np.stack([(ss < tt), (ss <= tt), (tt < ss), (ss == tt)]).astype(np.float32)
    c["rw_masks"] = np.ascontiguousarray(np.broadcast_to(np.transpose(m4, (1, 0, 2))[:, :, None, :], (64, 4, 8, 64)))
    c["rowmask"] = (np.arange(128)[:, None] // 16 == np.arange(8)[None, :]).astype(np.float32)
    return c


TWO_PI = 2.0 * math.pi


def sincos(S, sin_out, cos_out, ang, tmps):
    y, kf, ki = tmps
    for out, off in ((sin_out, 0.5), (cos_out, 0.75)):
        if out is None:
            continue
        S.ts("dve", y, ang, 1.0 / TWO_PI, off, ALU.mult, ALU.add)
        S.copy("dve", ki, y)
        S.copy("dve", kf, ki)
        S.tt("dve", y, y, kf, ALU.subtract)
        S.ts("dve", kf, y, 0.0, None, ALU.is_lt)
        S.tt("dve", y, y, kf, ALU.add)
        S.ts("dve", y, y, TWO_PI, -math.pi, ALU.mult, ALU.add)
        S.act(out, y, AF.Sin)


def phase_A0(S, C, I, T, xin, uT, qT, kT, vtm):
    win_v = I["ev_w_in_h"].h.rearrange("(kc p) n -> p kc n", p=128)
    with Phase(S):
        C.ps_ss = S.ps([128, 512], F32, "ss")
        psA = [S.ps([128, 512], F32, "psA%d" % i) for i in range(3)]
        psR = S.ps([128, 512], F32, "psR")
        win = S.sb([128, 8, 1024], BF16, "win")
        wst = [S.sb([128, 8, 256], F32, "wst%d" % i) for i in range(2)]
        for i in range(4):
            S.dma("sp" if i % 2 == 0 else "act", wst[i % 2][:], Ref(I["_w"], win_v[:, :, i * 256:(i + 1) * 256], None))
            S.copy("pool", win[:, :, i * 256:(i + 1) * 256], wst[i % 2][:])
        rotm = S.sb([128, 128], F32, "rotm")
        S.dma("sp", rotm[:], I["rotm"][:])
        invf = S.sb([128, 1], F32, "invf")
        S.dma("sp", invf[:], I["invf_col"][:])
        gq = S.sb([128, 2], F32, "gq")
        S.dma("sp", gq[:], I["qk_gain_col"][:])
        S.ts("dve", gq[:, 0:1], gq[:, 0:1], 0.125, None, ALU.mult)
        x = S.sb([128, 8, 512], F32, "x")
        sq = S.sb([128, 8, 512], BF16, "sq")
        h = S.sb([128, 8, 512], BF16, "h")
        rstd = S.sb([128, 512], F32, "rstd")
        tmp = [S.sb([128, 512], F32, "tmp%d" % i) for i in range(2)]
        posi = S.sb([128, 512], I32, "posi")
        ang = S.sb([128, 512], F32, "ang")
        cosT = S.sb([128, 512], F32, "cosT")
        sinT = S.sb([128, 512], F32, "sinT")
        sq2 = S.sb([128, 512], F32, "sq2")
        qn = S.sb([128, 512], F32, "qn")
        r2 = S.sb([128, 512], F32, "r2")
        ob = [S.sb([128, 512], BF16, "ob%d" % i) for i in range(3)]
        oi = 0
        pi = 0
        for t0 in range(0, T, 512):
            S.dma("sp", x[:], xin.r(xin.h[:, t0:t0 + 512].rearrange("(kc p) t -> p kc t", p=128)))
            S.dma("act", posi[:], I["pos"].r(I["pos"].h[t0:t0 + 512].partition_broadcast(128)))
            S.copy("dve", ang[:], posi[:])
            S.ts("dve", ang[:], ang[:], invf[:, 0:1], None, ALU.mult)
            sincos(S, sinT[:], cosT[:], ang[:], (tmp[0][:], tmp[1][:], posi[:]))
            rmsnorm_mod(S, C, x, h, C.gsm, 0, 0, 512, tmp, sq, rstd)
            for n in range(6):
                p = psA[pi % 3]
                pi += 1
                for kc in range(8):
                    S.mm(p[:, :], win[:, kc, n * 128:(n + 1) * 128], h[:, kc, :], start=(kc == 0), stop=(kc == 7))
                o = ob[oi % 3]
                oi += 1
                if n < 2:
                    S.copy("act", o[:], p[:, :])
                    S.dma("pool", uT.r(uT.h[n * 128:(n + 1) * 128, t0:t0 + 512]), o[:])
                    continue
                isq = n < 4
                S.act(sq2[:], p[:, :], AF.Square)
                S.mm(C.ps_ss[:, :], C.blk_f[:], sq2[:])
                rsqrt_to(S, rstd[:], C.ps_ss[:, :], 1.0 / 64, EPS)
                S.tt("dve", qn[:], p[:, :], rstd[:], ALU.mult)
                S.ts("dve", qn[:], qn[:], gq[:, 0:1] if isq else gq[:, 1:2], None, ALU.mult)
                S.mm(psR[:, :], rotm[:], qn[:])
                S.tt("dve", r2[:], psR[:, :], sinT[:], ALU.mult)
                S.tt("pool", qn[:], qn[:], cosT[:], ALU.add if False else ALU.mult)
                S.tt("pool", o[:], qn[:], r2[:], ALU.add)
                dst = qT if isq else kT
                hd = (n - 2) % 2
                S.dma("pool", dst.r(dst.h[hd * 128:(hd + 1) * 128, t0:t0 + 512]), o[:])
            for tt_ in range(4):
                p = psA[pi % 3]
                pi += 1
                for kc in range(8):
                    S.mm(p[:, 0:256], h[:, kc, tt_ * 128:(tt_ + 1) * 128], win[:, kc, 768:1024],
                         start=(kc == 0), stop=(kc == 7))
                o = ob[oi % 3]
                oi += 1
                S.copy("act", o[:, 0:256], p[:, 0:256])
                S.dma("pool", vtm.r(vtm.h[t0 + tt_ * 128:t0 + (tt_ + 1) * 128, :]), o[:, 0:256])


def phase_attn(S, C, I, T, qT, kT, vtm, moT, lam_init):
    NQ = T // 512
    NK = T // 128
    with Phase(S):
        C.ps_ss = S.ps([128, 512], F32, "ss")
        psS = [S.ps([128, 512], F32, "psS%d" % i) for i in range(2)]
        psO = [S.ps([128, 512], F32, "psO%d" % i) for i in range(2)]
        psZ = [S.ps([128, 512], F32, "psZ%d" % i) for i in range(2)]
        masks = S.sb([128, 4, 512], BF16, "masks")
        S.dma("sp", masks[:], I["masks"][:])
        lv = S.sb([128, 4, 64], F32, "lv")
        S.dma("sp", lv[:], I["lam_vecs"].r(I["lam_vecs"].h[:, :].partition_broadcast(128)))
        lp = S.sb([128, 2, 64], F32, "lp")
        S.tt("dve", lp[:, 0, :], lv[:, 0, :], lv[:, 1, :], ALU.mult)
        S.tt("dve", lp[:, 1, :], lv[:, 2, :], lv[:, 3, :], ALU.mult)
        ls = S.sb([128, 2], F32, "ls")
        S.op("dve", lambda g: g.reduce_sum(ls.h[:, :], lp.h[:, :, :], axis=mybir.AxisListType.X), [lp[:]], [ls[:]])
        S.act(ls[:], ls[:], AF.Exp)
        nlam = S.sb([128, 1], F32, "nlam")
        S.tt("dve", nlam[:], ls[:, 1:2], ls[:, 0:1], ALU.subtract)
        S.ts("dve", nlam[:], nlam[:], -lam_init, None, ALU.add)
        sg = S.sb([128, 1], F32, "sg")
        S.dma("sp", sg[:], I["subln_col"][:])
        S.ts("dve", sg[:], sg[:], 1.0 - lam_init, None, ALU.mult)
        q = S.sb([128, T], BF16, "q")
        k = S.sb([128, T], BF16, "k")
        v = S.sb([128, NK, 128], BF16, "v")
        pt = [S.sb([128, 512], BF16, "pt%d" % i) for i in range(4)]
        rs = [S.sb([128, 512], F32, "rs%d" % i) for i in range(2)]
        o0 = S.sb([128, 512], F32, "o0")
        o1 = S.sb([128, 512], F32, "o1")
        sq2 = S.sb([128, 512], F32, "sq2")
        rstd = S.sb([128, 512], F32, "rstd")
        ob = [S.sb([128, 512], BF16, "ob%d" % i) for i in range(2)]
        it = 0
        for hd in range(2):
            S.dma("sp", q[:], qT.r(qT.h[hd * 128:(hd + 1) * 128, :]))
            S.dma("act", k[:], kT.r(kT.h[hd * 128:(hd + 1) * 128, :]))
            S.dma("sp", v[:], vtm.r(vtm.h[:, hd * 128:(hd + 1) * 128].rearrange("(kb p) e -> p kb e", p=128)))
            for qb in range(NQ):
                nkb = 4 * (qb + 1)
                for kb in range(nkb):
                    for c in range(2):
                        ps = psS[it % 2]
                        p = pt[it % 4]
                        it += 1
                        S.mm(ps[:, :], k[c * 64:(c + 1) * 64, kb * 128:(kb + 1) * 128],
                             q[c * 64:(c + 1) * 64, qb * 512:(qb + 1) * 512])
                        S.act(p[:], ps[:, :], AF.Exp)
                        j = kb - 4 * qb
                        if j >= 0:
                            S.tt("pool", p[:], p[:], masks[:, j, :], ALU.mult)
                        S.mm(psO[c][:, :], v[:, kb, :], p[:], start=(kb == 0), stop=(kb == nkb - 1))
                        S.mm(psZ[c][:, :], C.ones_bf[:], p[:], start=(kb == 0), stop=(kb == nkb - 1))
                for c in range(2):
                    S.op("dve", lambda g, c=c: g.reciprocal(rs[c].h[:, :], psZ[c].h[:, :]), [psZ[c][:]], [rs[c][:]])
                S.tt("dve", o0[:], psO[0][:, :], rs[0][:], ALU.mult)
                S.tt("dve", o1[:], psO[1][:, :], rs[1][:], ALU.mult)
                S.stt("dve", o0[:], o1[:], nlam[:, 0:1], o0[:], ALU.mult, ALU.add)
                S.act(sq2[:], o0[:], AF.Square)
                S.mm(C.ps_ss[:, :], C.ones_f[:], sq2[:])
                rsqrt_to(S, rstd[:], C.ps_ss[:, :], 1.0 / 128, EPS)
                S.tt("pool", o0[:], o0[:], rstd[:], ALU.mult)
                o = ob[qb % 2]
                S.ts("dve", o[:], o0[:], sg[:, 0:1], None, ALU.mult)
                S.dma("pool", moT.r(moT.h[256 + hd * 128:256 + (hd + 1) * 128, qb * 512:(qb + 1) * 512]), o[:])


def phase_S5(S, C, I, T, uT, ygT):
    NB = T // 512
    with Phase(S):
        psA = [S.ps([128, 512], F32, "psA%d" % i) for i in range(2)]
        psB = [S.ps([128, 512], F32, "psB%d" % i) for i in range(2)]
        psY = S.ps([128, 512], F32, "psY")
        psW = S.ps([128, 8], F32, "psW")
        swap = S.sb([128, 128], F32, "swap")
        S.dma("sp", swap[:], I["swapm"][:])
        rowmask = S.sb([128, 8], F32, "rowmask")
        S.dma("sp", rowmask[:], I["rowmask"][:])
        dcol = S.sb([128, 2], F32, "dcol")
        S.dma("sp", dcol[:], I["s5_d_pk"][:])
        lr = S.sb([128, 2, 64], F32, "lr")
        li = S.sb([128, 2, 64], F32, "li")
        ldt = S.sb([128, 2], F32, "ldt")
        S.dma("sp", lr[:], I["lamre_row"][:])
        S.dma("act", li[:], I["lamim_row"][:])
        S.dma("sp", ldt[:], I["logdt_row"][:])
        S.act(ldt[:], ldt[:], AF.Exp)
        dtb = ldt.h[:, :].unsqueeze(2).to_broadcast([128, 2, 64])
        th = S.sb([128, 2, 64], F32, "th")
        mag = S.sb([128, 2, 64], F32, "mag")
        S.tt("dve", th[:], li[:], ldt.r(dtb), ALU.mult)
        S.tt("dve", mag[:], lr[:], ldt.r(dtb), ALU.mult)
        S.act(mag[:], mag[:], AF.Exp)
        cs = S.sb([128, 2, 64], F32, "cs")
        sn = S.sb([128, 2, 64], F32, "sn")
        t4 = S.sb([128, 2, 64], F32, "t4")
        t4b = S.sb([128, 2, 64], F32, "t4b")
        t4i = S.sb([128, 2, 64], I32, "t4i")
        sincos(S, sn[:], cs[:], th[:], (t4[:], t4b[:], t4i[:]))
        ar = S.sb([128, 2, 64], F32, "ar")
        ai = S.sb([128, 2, 64], F32, "ai")
        S.tt("dve", ar[:], mag[:], cs[:], ALU.mult)
        S.tt("dve", ai[:], mag[:], sn[:], ALU.mult)
        S.ts("dve", ar[:], ar[:], -1.0, None, ALU.add)
        den = S.sb([128, 2, 64], F32, "den")
        S.tt("dve", den[:], lr[:], lr[:], ALU.mult)
        S.tt("dve", t4[:], li[:], li[:], ALU.mult)
        S.tt("dve", den[:], den[:], t4[:], ALU.add)
        S.op("dve", lambda g: g.reciprocal(den.h[:], den.h[:]), [den[:]], [den[:]])
        qr = S.sb([128, 2, 64], F32, "qr")
        qi = S.sb([128, 2, 64], F32, "qi")
        S.tt("dve", qr[:], ar[:], lr[:], ALU.mult)
        S.tt("dve", t4[:], ai[:], li[:], ALU.mult)
        S.tt("dve", qr[:], qr[:], t4[:], ALU.add)
        S.tt("dve", qr[:], qr[:], den[:], ALU.mult)
        S.tt("dve", qi[:], ai[:], lr[:], ALU.mult)
        S.tt("dve", t4[:], ar[:], li[:], ALU.mult)
        S.tt("dve", qi[:], qi[:], t4[:], ALU.subtract)
        S.tt("dve", qi[:], qi[:], den[:], ALU.mult)
        btr = S.sb([128, 2, 64], F32, "btr")
        bti = S.sb([128, 2, 64], F32, "bti")
        S.dma("sp", btr[:], I["bT_re"][:])
        S.dma("act", bti[:], I["bT_im"][:])
        bbr = S.sb([128, 2, 64], F32, "bbr")
        bbi = S.sb([128, 2, 64], F32, "bbi")
        S.tt("dve", bbr[:], qr[:], btr[:], ALU.mult)
        S.tt("dve", t4[:], qi[:], bti[:], ALU.mult)
        S.tt("dve", bbr[:], bbr[:], t4[:], ALU.subtract)
        S.tt("dve", bbi[:], qr[:], bti[:], ALU.mult)
        S.tt("dve", t4[:], qi[:], btr[:], ALU.mult)
        S.tt("dve", bbi[:], bbi[:], t4[:], ALU.add)
        nbbr = S.sb([128, 2, 64], F32, "nbbr")
        S.ts("dve", nbbr[:], bbr[:], -1.0, None, ALU.mult)
        lrc = S.sb([128, 16], F32, "lrc")
        lic = S.sb([128, 16], F32, "lic")
        dtc = S.sb([128, 16], F32, "dtc")
        S.dma("sp", lrc[:], I["lamre_col"][:])
        S.dma("act", lic[:], I["lamim_col"][:])
        S.dma("sp", dtc[:], I["logdt_col"][:])
        S.act(dtc[:], dtc[:], AF.Exp)
        thc = S.sb([128, 16], F32, "thc")
        rc = S.sb([128, 16], F32, "rc")
        S.tt("dve", thc[:], lic[:], dtc[:], ALU.mult)
        S.tt("dve", rc[:], lrc[:], dtc[:], ALU.mult)
        S.act(rc[:], rc[:], AF.Exp)
        c1 = S.sb([128, 16], F32, "c1")
        s1 = S.sb([128, 16], F32, "s1")
        t32 = S.sb([128, 16], F32, "t32")
        t32b = S.sb([128, 16], F32, "t32b")
        t32i = S.sb([128, 16], I32, "t32i")
        sincos(S, s1[:], c1[:], thc[:], (t32[:], t32b[:], t32i[:]))
        ctc = S.sb([128, 16, 16], F32, "ctc")
        cti = S.sb([128, 16, 16], F32, "cti")
        S.dma("sp", ctc[:], I["cT_re_col"][:])
        S.dma("act", cti[:], I["cT_im_col"][:])
        S.ts("dve", cti[:], cti[:], -1.0, None, ALU.mult)
        Ct = S.sb([128, 8, 512], F32, "Ct")
        St = S.sb([128, 8, 512], F32, "St")
        Cm = S.sb([128, 8, 512], F32, "Cm")
        Sm = S.sb([128, 8, 512], F32, "Sm")
        Rt = S.sb([128, 8, 512], F32, "Rt")
        tA = S.sb([128, 8, 256], F32, "tA")
        LA = S.sb([128, 8, 128], BF16, "LA")
        LB = S.sb([128, 8, 128], BF16, "LB")
        LC1 = S.sb([128, 8, 128], BF16, "LC1")
        LC2 = S.sb([128, 8, 128], BF16, "LC2")
        W = S.sb([128, 8, 512], F32, "W")
        w0 = S.sb([128, 8], F32, "w0")
        wl = S.sb([128, 8], F32, "wl")
        tw = S.sb([128, 8], F32, "tw")
        u = [S.sb([128, 512], BF16, "u%d" % i) for i in range(2)]
        t1 = [S.sb([128, 512], F32, "t1_%d" % i) for i in range(2)]
        t2 = [S.sb([128, 512], F32, "t2_%d" % i) for i in range(2)]
        R1 = [S.sb([128, 512], BF16, "R1_%d" % i) for i in range(2)]
        R2 = [S.sb([128, 512], BF16, "R2_%d" % i) for i in range(2)]
        yv = S.sb([128, 512], F32, "yv")
        y2 = S.sb([128, 512], F32, "y2")
        yo = [S.sb([128, 512], BF16, "yo%d" % i) for i in range(2)]
        it = 0
        for cc in range(2):
            g0 = cc * 8
            S.copy("dve", Ct[:, :, 0:1], c1.r(c1.h[:, g0:g0 + 8].unsqueeze(2)))
            S.copy("dve", St[:, :, 0:1], s1.r(s1.h[:, g0:g0 + 8].unsqueeze(2)))
            n = 1
            while n < 512:
                cn = Ct.r(Ct.h[:, :, n - 1:n].to_broadcast([128, 8, n]))
                sn_ = St.r(St.h[:, :, n - 1:n].to_broadcast([128, 8, n]))
                ta = tA.r(tA.h[:, :, 0:n])
                S.tt("dve", ta, St[:, :, 0:n], sn_, ALU.mult)
                S.tt("dve", Ct[:, :, n:2 * n], Ct[:, :, 0:n], cn, ALU.mult)
                S.tt("dve", Ct[:, :, n:2 * n], Ct[:, :, n:2 * n], ta, ALU.subtract)
                S.tt("dve", ta, St[:, :, 0:n], cn, ALU.mult)
                S.tt("dve", St[:, :, n:2 * n], Ct[:, :, 0:n], sn_, ALU.mult)
                S.tt("dve", St[:, :, n:2 * n], St[:, :, n:2 * n], ta, ALU.add)
                n *= 2
            S.copy("pool", Cm[0:64, :, :], Ct[0:64, :, :])
            S.ts("pool", Cm[64:128, :, :], St[64:128, :, :], -1.0, None, ALU.mult)
            S.copy("pool", Sm[0:64, :, :], St[0:64, :, :])
            S.copy("pool", Sm[64:128, :, :], Ct[64:128, :, :])
            S.memset("pool", Rt[:], 1.0)
            S.tt("pool", Rt[:], Rt[:], rc.r(rc.h[:, g0:g0 + 8].unsqueeze(2).to_broadcast([128, 8, 512])), ALU.mult)
            S.memset("pool", LC1[:], 0.0)
            S.memset("pool", LC2[:], 0.0)
            for gi in range(8):
                S.ts("dve", LA[:, gi, 0:64], bbr[:, cc, :], rowmask[:, gi:gi + 1], None, ALU.mult)
                S.ts("dve", LA[:, gi, 64:128], bbi[:, cc, :], rowmask[:, gi:gi + 1], None, ALU.mult)
                S.ts("dve", LB[:, gi, 0:64], bbi[:, cc, :], rowmask[:, gi:gi + 1], None, ALU.mult)
                S.ts("dve", LB[:, gi, 64:128], nbbr[:, cc, :], rowmask[:, gi:gi + 1], None, ALU.mult)
                S.copy("pool", LC1[:, gi, gi * 16:(gi + 1) * 16], ctc[:, g0 + gi, :])
                S.copy("pool", LC2[:, gi, gi * 16:(gi + 1) * 16], cti[:, g0 + gi, :])
            S.memset("pool", w0[:], 0.0)
            for tb in range(NB):
                ut = u[tb % 2]
                S.dma("sp", ut[:], uT.r(uT.h[cc * 128:(cc + 1) * 128, tb * 512:(tb + 1) * 512]))
                for gi in range(8):
                    k = it % 2
                    it += 1
                    S.mm(psA[k][:, :], LA[:, gi, :], ut[:])
                    S.mm(psB[k][:, :], LB[:, gi, :], ut[:])
                    S.tt("dve", t1[k][:], psA[k][:, :], Ct[:, gi, :], ALU.mult)
                    S.tt("dve", t2[k][:], psB[k][:, :], St[:, gi, :], ALU.mult)
                    S.tt("pool", t1[k][:], t1[k][:], t2[k][:], ALU.add)
                    S.scan("dve", W[:, gi, :], Rt[:, gi, :], t1[k][:], w0[:, gi:gi + 1])
                    S.tt("pool", R1[k][:], W[:, gi, :], Cm[:, gi, :], ALU.mult)
                    S.tt("pool", R2[k][:], W[:, gi, :], Sm[:, gi, :], ALU.mult)
                    S.mm(psY[:, :], LC1[:, gi, :], R1[k][:], start=(gi == 0), stop=False)
                    S.mm(psY[:, :], LC2[:, gi, :], R2[k][:], start=False, stop=(gi == 7))
                S.copy("dve", wl[:], W.r(W.h[:, :, 511]))
                S.mm(psW[:, :], swap[:], wl[:])
                S.tt("dve", tw[:], psW[:, :], St.r(St.h[:, :, 511]), ALU.mult)
                S.tt("dve", w0[:], wl[:], Ct.r(Ct.h[:, :, 511]), ALU.mult)
                S.tt("dve", w0[:], w0[:], tw[:], ALU.subtract)
                S.stt("dve", yv[:], ut[:], dcol[:, cc:cc + 1], psY[:, :], ALU.mult, ALU.add)
                S.act(y2[:], yv[:], AF.Square)
                S.ts("dve", y2[:], y2[:], 0.044715, 1.0, ALU.mult, ALU.add)
                S.tt("pool", y2[:], y2[:], yv[:], ALU.mult)
                S.act(y2[:], y2[:], AF.Sigmoid, scale=1.5957691216057308)
                o = yo[tb % 2]
                S.tt("dve", o[:], yv[:], y2[:], ALU.mult)
                S.dma("pool", ygT.r(ygT.h[cc * 128:(cc + 1) * 128, tb * 512:(tb + 1) * 512]), o[:])


def phase_glu(S, C, I, T, ygF, ygL, moT):
    wv = I["ev_s5_w_glu_h"].h.rearrange("(kc p) n -> p kc n", p=128)
    with Phase(S):
        ps = [S.ps([128, 512], F32, "ps%d" % i) for i in range(2)]
        wst = S.sb([128, 4, 256], F32, "wst")
        S.dma("sp", wst[:], Ref(I["_w"], wv, None))
        w = S.sb([128, 4, 256], BF16, "w")
        S.copy("pool", w[:], wst[:])
        yg = [S.sb([128, 4, 512], BF16, "yg%d" % i) for i in range(2)]
        yl = [S.sb([128, 2, 512], BF16, "yl%d" % i) for i in range(2)]
        sg = [S.sb([128, 512], F32, "sg%d" % i) for i in range(2)]
        ob = [S.sb([128, 512], BF16, "ob%d" % i) for i in range(2)]
        it = 0
        for tb in range(T // 512):
            y = yg[tb % 2]
            yy = yl[tb % 2]
            S.dma("sp", y[:], ygF.r(ygF.h[:, tb * 512:(tb + 1) * 512].rearrange("(kc p) t -> p kc t", p=128)))
            S.dma("act", yy[:], ygL.r(ygL.h[:, tb * 512:(tb + 1) * 512].rearrange("(kc p) t -> p kc t", p=128)))
            for n in range(2):
                p = ps[it % 2]
                for kc in range(4):
                    S.mm(p[:, :], w[:, kc, n * 128:(n + 1) * 128], y[:, kc, :], start=(kc == 0), stop=(kc == 3))
                S.act(sg[it % 2][:], p[:, :], AF.Sigmoid)
                S.tt("dve", ob[it % 2][:], yy[:, n, :], sg[it % 2][:], ALU.mult)
                S.dma("pool", moT.r(moT.h[n * 128:(n + 1) * 128, tb * 512:(tb + 1) * 512]), ob[it % 2][:])
                it += 1


def phase_B0(S, C, I, T, xin, R):
    TB = 256
    NCH = TB // 64
    wv_ = {n: I[n].h[0].rearrange("(kc p) n -> p kc n", p=128) for n in ("od_w1", "od_a1", "od_g1")}
    for n in ("od_w_r", "od_w_k", "od_w_v"):
        wv_[n] = I[n + "_h"].h.rearrange("(kc p) n -> p kc n", p=128)
    with Phase(S):
        C.ps_ss = S.ps([128, 512], F32, "ss")
        psA_ = [S.ps([128, 512], F32, "psA%d" % i) for i in range(4)]
        psT_ = [S.ps([128, 512], F32, "psT%d" % i) for i in range(2)]

        class _V:
            def __init__(self, t, w):
                self.t, self.w = t, w

            def __getitem__(self, idx):
                return Ref(self.t, self.t.h[:, 0:self.w][idx], None)
        psA = [_V(t, TB) for t in psA_]
        psT = [_V(t, 128) for t in psT_]
        wst = [S.sb([128, 8, 128], F32, "wst%d" % i) for i in range(2)]
        W3 = [S.sb([128, 8, 512], BF16, "W%d" % i) for i in range(3)]
        wi = 0
        for m, nm in enumerate(("od_w_r", "od_w_k", "od_w_v")):
            for n in range(4):
                S.dma("sp" if wi % 2 == 0 else "act", wst[wi % 2][:], Ref(I["_w"], wv_[nm][:, :, n * 128:(n + 1) * 128], None))
                S.copy("pool", W3[m][:, :, n * 128:(n + 1) * 128], wst[wi % 2][:])
                wi += 1
        L1 = S.sb([128, 8, 256], BF16, "L1")
        for nm, lo, wd in (("od_w1", 0, 64), ("od_a1", 64, 64), ("od_g1", 128, 128)):
            S.dma("sp", wst[wi % 2][:, :, 0:wd], Ref(I["_w"], wv_[nm], None))
            S.copy("pool", L1[:, :, lo:lo + wd], wst[wi % 2][:, :, 0:wd])
            wi += 1
        L2st = S.sb([128, 3, 512], F32, "L2st")
        S.memset("pool", L2st[:], 0.0)
        S.dma("sp", L2st[0:64, 0, :], I["od_w2_h"][:])
        S.dma("sp", L2st[0:64, 1, :], I["od_a2_h"][:])
        S.dma("sp", L2st[:, 2, :], I["od_g2_h"][:])
        L2 = S.sb([128, 3, 512], BF16, "L2")
        S.copy("pool", L2[:], L2st[:])
        pv = S.sb([128, 7, 4], F32, "pv")
        S.dma("sp", pv[:, 0:5, :], I["od_vecs_pk"][:])
        S.ts("dve", pv[:, 5, :], pv[:, 3, :], -1.0, 1.0, ALU.mult, ALU.add)
        S.ts("dve", pv[:, 6, :], pv[:, 0, :], -1.0, None, ALU.mult)
        mu = S.sb([128, 48], F32, "mu")
        S.dma("sp", mu[:], I["od_mu_pk"][:])
        ident = S.sb([128, 128], F32, "ident")
        S.dma("sp", ident[:], I["ident"][:])
        rmask = S.sb([128, TB], F32, "rmask")
        S.memset("pool", rmask[:], 1.0)
        for c in range(NCH):
            S.memset("pool", rmask[:, c * 64:c * 64 + 1], 0.0)
        x = S.sb([128, 8, TB], F32, "x")
        sq = S.sb([128, 8, TB], BF16, "sq")
        hs = S.sb([128, 8, TB + 1], F32, "hs")
        S.memset("pool", hs[:], 0.0)
        hh = S.sb([128, 8, TB], F32, "hh")
        dx = S.sb([128, 8, TB], F32, "dx")
        xm = [S.sb([128, 8, TB], BF16, "xm%d" % i) for i in range(6)]
        rstd = S.sb([128, TB], F32, "rstd")
        tmp = [S.sb([128, TB], F32, "tmp%d" % i) for i in range(2)]
        lo1 = S.sb([128, 3, TB], BF16, "lo1")
        S.memset("pool", lo1[:], 0.0)
        E = {k: S.sb([128, TB], F32, k) for k in ("r", "k", "v", "lw", "cum", "ep", "em", "ex", "a", "kk", "t0", "t1", "kmod", "b", "o0", "o1", "o2")}
        tq = [S.sb([128, 128], F32, "tq%d" % i) for i in range(2)]
        gl = S.sb([128, NCH], F32, "gl")
        pi = 0
        ti = 0
        for t0 in range(0, T, TB):
            S.dma("sp", x[:], xin.r(xin.h[:, t0:t0 + TB].rearrange("(kc p) t -> p kc t", p=128)))
            for kc in range(8):
                S.act(sq[:, kc, :], x[:, kc, :], AF.Square)
            for kc in range(8):
                S.mm(C.ps_ss[:, 0:TB], C.ones_bf[:], sq[:, kc, :], start=(kc == 0), stop=(kc == 7))
            rsqrt_to(S, rstd[:], C.ps_ss[:, 0:TB], 1.0 / D, EPS)
            for kc in range(8):
                S.tt("dve", tmp[kc % 2][:], x[:, kc, :], rstd[:], ALU.mult)
                S.act(hh[:, kc, :], tmp[kc % 2][:], AF.Identity, bias=modcol(C, 1, 0, kc), scale=C.gsm[:, 8 + kc:9 + kc])
            S.copy("pool", hs[:, :, 1:TB + 1], hh[:])
            S.tt("dve", dx[:], hs[:, :, 0:TB], hh[:], ALU.subtract)
            S.copy("pool", hs[:, :, 0:1], hh[:, :, TB - 1:TB])
            for m in range(6):
                for kc in range(8):
                    S.stt("dve", xm[m][:, kc, :], dx[:, kc, :],
                          mu[:, m * 8 + kc:m * 8 + kc + 1], hh[:, kc, :], ALU.mult, ALU.add)
            for j, (mx, lo, wd, fn) in enumerate(((1, 0, 64, AF.Tanh), (4, 64, 64, AF.Identity), (5, 128, 128, AF.Sigmoid))):
                p = psA[pi % 4]
                pi += 1
                for kc in range(8):
                    S.mm(p[0:wd, :], L1[:, kc, lo:lo + wd], xm[mx][:, kc, :], start=(kc == 0), stop=(kc == 7))
                S.act(lo1[0:wd, j, :], p[0:wd, :], fn)
            for n in range(4):
                f0 = n * 128
                pr, pk_, pv_, pw = [psA[(pi + i) % 4] for i in range(4)]
                for (p, m, mx) in ((pr, 0, 0), (pk_, 1, 2), (pv_, 2, 3)):
                    for kc in range(8):
                        S.mm(p[:, :], W3[m][:, kc, f0:f0 + 128], xm[mx][:, kc, :], start=(kc == 0), stop=(kc == 7))
                S.copy("act", E["r"][:], pr[:, :])
                S.copy("act", E["k"][:], pk_[:, :])
                S.copy("act", E["v"][:], pv_[:, :])
                S.mm(pw[:, :], L2[:, 0, f0:f0 + 128], lo1[:, 0, :])
                S.act(E["t0"][:], pw[:, :], AF.Exp, bias=pv[:, 6, n:n + 1], scale=-1.0)
                S.ts("dve", E["t0"][:], E["t0"][:], 1.0, None, ALU.add)
                S.act(E["t0"][:], E["t0"][:], AF.Ln)
                S.ts("dve", E["t0"][:], E["t0"][:], -1.0, -0.5, ALU.mult, ALU.add)
                S.act(E["t0"][:], E["t0"][:], AF.Exp)
                S.ts("dve", E["lw"][:], E["t0"][:], -1.0, None, ALU.mult)
                S.scan("dve", E["cum"][:], rmask[:], E["lw"][:], 0.0)
                S.act(E["ep"][:], E["cum"][:], AF.Exp)
                S.act(E["em"][:], E["cum"][:], AF.Exp, scale=-1.0)
                S.tt("pool", E["t1"][:], E["cum"][:], E["lw"][:], ALU.subtract)
                S.act(E["ex"][:], E["t1"][:], AF.Exp)
                S.mm(pw[:, :], L2[:, 1, f0:f0 + 128], lo1[:, 1, :])
                S.act(E["a"][:], pw[:, :], AF.Sigmoid, bias=pv[:, 1, n:n + 1])
                S.ts("dve", E["kk"][:], E["k"][:], pv[:, 2, n:n + 1], None, ALU.mult)
                S.tt("pool", E["t0"][:], E["kk"][:], E["kk"][:], ALU.mult)
                S.mm(C.ps_ss[:, 0:TB], C.blk_f[:], E["t0"][:])
                S.act(E["t0"][:], C.ps_ss[:, 0:TB], AF.Sqrt)
                S.ts("dve", E["t0"][:], E["t0"][:], 1e-12, None, ALU.max)
                S.op("dve", lambda g: g.reciprocal(E["t0"].h[:], E["t0"].h[:]), [E["t0"][:]], [E["t0"][:]])
                S.tt("dve", E["kk"][:], E["kk"][:], E["t0"][:], ALU.mult)
                S.ts("dve", E["t1"][:], E["a"][:], pv[:, 3, n:n + 1], pv[:, 5, n:n + 1], ALU.mult, ALU.add)
                S.tt("dve", E["kmod"][:], E["k"][:], E["t1"][:], ALU.mult)
                S.tt("pool", E["b"][:], E["kk"][:], E["a"][:], ALU.mult)
                S.mm(pw[:, :], L2[:, 2, f0:f0 + 128], lo1[:, 2, :])
                S.copy("act", E["o2"][:], pw[:, :])
                S.dma("pool", R["GG"].r(R["GG"].h[f0:f0 + 128, t0:t0 + TB]), E["o2"][:])
                pi += 4
                S.tt("dve", E["t0"][:], E["r"][:], E["kmod"][:], ALU.mult)
                S.ts("dve", E["t0"][:], E["t0"][:], pv[:, 4, n:n + 1], None, ALU.mult)
                S.mm(C.ps_ss[:, 0:TB], C.blk_f[:], E["t0"][:])
                S.tt("dve", E["o0"][:], C.ps_ss[:, 0:TB], E["v"][:], ALU.mult)
                S.dma("pool", R["BON"].r(R["BON"].h[f0:f0 + 128, t0:t0 + TB]), E["o0"][:])
                S.tt("dve", E["o1"][:], E["r"][:], E["ep"][:], ALU.mult)
                S.dma("pool", R["RR"].r(R["RR"].h[f0:f0 + 128, t0:t0 + TB]), E["o1"][:])
                S.stt("dve", E["o0"][:], E["kk"][:], -1.0, E["ex"][:], ALU.mult, ALU.mult)
                S.dma("pool", R["AR"].r(R["AR"].h[f0:f0 + 128, t0:t0 + TB]), E["o0"][:])
                S.tt("dve", E["kmod"][:], E["kmod"][:], E["em"][:], ALU.mult)
                S.dma("pool", R["KB"].r(R["KB"].h[f0:f0 + 128, t0:t0 + TB]), E["kmod"][:])
                S.tt("dve", E["b"][:], E["b"][:], E["em"][:], ALU.mult)
                S.dma("pool", R["BB"].r(R["BB"].h[f0:f0 + 128, t0:t0 + TB]), E["b"][:])
                S.copy("dve", gl[:], E["ep"].r(E["ep"].h[:, 63:TB:64]))
                S.dma("pool", R["GL"].r(R["GL"].h[f0:f0 + 128, t0 // 64:t0 // 64 + NCH]), gl[:])
                elb = gl.r(gl.h[:, :].unsqueeze(2).to_broadcast([128, NCH, 64]))
                S.tt("dve", E["kmod"].r(E["kmod"].h[:, :].rearrange("p (c t) -> p c t", t=64)),
                     E["kmod"].r(E["kmod"].h[:, :].rearrange("p (c t) -> p c t", t=64)), elb, ALU.mult)
                S.tt("dve", E["b"].r(E["b"].h[:, :].rearrange("p (c t) -> p c t", t=64)),
                     E["b"].r(E["b"].h[:, :].rearrange("p (c t) -> p c t", t=64)), elb, ALU.mult)
                for (src, dst) in ((E["kmod"], R["KT"]), (E["b"], R["BT"]), (E["v"], R["VT"])):
                    for s in range(TB // 128):
                        pt_ = psT[ti % 2]
                        q_ = tq[ti % 2]
                        ti += 1
                        S.transpose(pt_[:, :], src[:, s * 128:(s + 1) * 128], ident[:])
                        S.copy("act", q_[:], pt_[:, :])
                        S.dma("sp", dst.r(dst.h[t0 + s * 128:t0 + (s + 1) * 128, f0:f0 + 128]), q_[:])


def phase_B1(S, C, I, T, R, moT):
    SC = 256
    NCS = SC // 64
    NCH = T // 64
    GN_EPS = 64e-5
    with Phase(S):
        banks = [S.ps([64, 8, 64], F32, "bk%d" % i) for i in range(7)]
        psE = S.ps([64, 512], F32, "psE")
        bstate = [0]

        def bank():
            b = banks[bstate[0] % 7]
            bstate[0] += 1
            return b
        cm = S.sb([64, 4, 8, 64], F32, "cmask")
        S.dma("sp", cm[:], I["rw_masks"][:])
        MS, MI, MST, I8 = [cm.r(cm.h[:, i]) for i in range(4)]
        lng = S.sb([64, 2, 8], F32, "lng")
        S.dma("sp", lng[:], I["od_ln_pk64"][:])
        ones64 = C.ones_f[0:64, 0:64]
        fm = {k: [S.sb([64, 8, SC], F32, "%s%d" % (k, i)) for i in range(2)] for k in ("AR", "RR", "KB", "BB")}
        tm = {k: [S.sb([64, NCS, 512], F32, "%s%d" % (k, i)) for i in range(2)] for k in ("KT", "BT", "VT")}
        ep = {k: S.sb([64, 8, SC], F32, k) for k in ("BON", "GG")}
        GLt = S.sb([64, 8, NCH], F32, "GLt")
        ST = S.sb([64, 8, 64], F32, "ST")
        Y = S.sb([64, 8, SC], F32, "Y")
        Yc = S.sb([64, 8, SC], F32, "Yc")
        Ysq = S.sb([64, 8, SC], F32, "Ysq")
        rstd = S.sb([64, 512], F32, "rstd")
        ob = S.sb([64, 8, SC], BF16, "ob")
        A = {k: S.sb([64, 8, 64], F32, k) for k in ("N", "NT", "ak", "kr", "br", "Tm", "X", "UT")}
        Mp = [S.sb([64, 8, 64], F32, "M%d" % i) for i in range(2)]
        MTp = [S.sb([64, 8, 64], F32, "MT%d" % i) for i in range(2)]
        for g in range(1):
            rows = slice(g * 512, (g + 1) * 512)
            S.dma("sp", GLt[:], R["GL"].r(R["GL"].h[rows, :].rearrange("(h i) c -> i h c", i=64)))
            S.memset("pool", ST[:], 0.0)
            for si, t0 in enumerate(range(0, T, SC)):
                F = {}
                for k in fm:
                    F[k] = fm[k][si % 2]
                    S.dma("sp" if k in ("AR", "KB") else "act", F[k][:],
                          R[k].r(R[k].h[rows, t0:t0 + SC].rearrange("(h i) t -> i h t", i=64)))
                for k in tm:
                    F[k] = tm[k][si % 2]
                    S.dma("sp", F[k][:], R[k].r(R[k].h[t0:t0 + SC, rows].rearrange("(c s) f -> s c f", s=64)))
                for k in ep:
                    S.dma("act", ep[k][:], R[k].r(R[k].h[rows, t0:t0 + SC].rearrange("(h i) t -> i h t", i=64)))
                for c in range(NCS):
                    cs = slice(c * 64, (c + 1) * 64)
                    cg = t0 // 64 + c

                    def hv(name, h):
                        return F[name][:, c, h * 64:(h + 1) * 64]
                    for (dst, l, r_, msk) in ((A["N"], "BB", "AR", MS), (A["NT"], "AR", "BB", MST),
                                              (A["ak"], "KB", "AR", MS), (A["kr"], "KB", "RR", MI),
                                              (A["br"], "BB", "RR", MI)):
                        p = bank()
                        for h in range(8):
                            S.mm(p[:, h, :], F[l][:, h, cs], F[r_][:, h, cs])
                        S.tt("dve", dst[:], p[:], msk, ALU.mult)
                    S.tt("pool", A["Tm"][:], A["N"][:], I8, ALU.add)
                    M, MT = A["N"], A["NT"]
                    for lv in range(5):
                        p1, p2 = bank(), bank()
                        for h in range(8):
                            S.mm(p1[:, h, :], MT[:, h, :], M[:, h, :])
                            S.mm(p2[:, h, :], M[:, h, :], MT[:, h, :])
                        M2, MT2 = Mp[lv % 2], MTp[lv % 2]
                        S.copy("act", M2[:], p1[:])
                        S.copy("dve", MT2[:], p2[:])
                        p3 = bank()
                        for h in range(8):
                            S.mm(p3[:, h, :], MT2[:, h, :], A["Tm"][:, h, :])
                        S.tt("dve", A["Tm"][:], A["Tm"][:], p3[:], ALU.add)
                        M, MT = M2, MT2
                    px = bank()
                    for h in range(8):
                        S.mm(px[:, h, :], F["AR"][:, h, cs], ST[:, h, :], start=True, stop=False)
                        S.mm(px[:, h, :], A["ak"][:, h, :], hv("VT", h), start=False, stop=True)
                    S.copy("act", A["X"][:], px[:])
                    pu = bank()
                    for h in range(8):
                        S.mm(pu[:, h, :], A["Tm"][:, h, :], A["X"][:, h, :])
                    S.copy("act", A["UT"][:], pu[:])
                    py = bank()
                    for h in range(8):
                        S.mm(py[:, h, :], ST[:, h, :], F["RR"][:, h, cs], start=True, stop=False)
                        S.mm(py[:, h, :], hv("VT", h), A["kr"][:, h, :], start=False, stop=False)
                        S.mm(py[:, h, :], A["UT"][:, h, :], A["br"][:, h, :], start=False, stop=True)
                    S.copy("act", Y[:, :, cs], py[:])
                    pst = bank()
                    for h in range(8):
                        S.mm(pst[:, h, :], hv("KT", h), hv("VT", h), start=True, stop=False)
                        S.mm(pst[:, h, :], hv("BT", h), A["UT"][:, h, :], start=False, stop=True)
                    S.tt("dve", ST[:], ST[:], GLt.r(GLt.h[:, :, cg:cg + 1].to_broadcast([64, 8, 64])), ALU.mult)
                    S.tt("dve", ST[:], ST[:], pst[:], ALU.add)
                for q4 in range(8 * SC // 512):
                    hs_ = slice(q4 * (512 // SC), (q4 + 1) * (512 // SC))
                    S.mm(psE[:, :], ones64, Y[:, hs_, :])
                    S.stt("dve", Yc[:, hs_, :], psE[:, :], -1.0 / 64, Y[:, hs_, :], ALU.mult, ALU.add)
                    S.act(Ysq[:, hs_, :], Yc[:, hs_, :], AF.Square)
                    S.mm(psE[:, :], ones64, Ysq[:, hs_, :])
                    rsqrt_to(S, rstd[:], psE[:, :], 1.0 / 64, GN_EPS)
                    S.tt("dve", Yc[:, hs_, :], Yc[:, hs_, :], rstd[:], ALU.mult)
                for h in range(8):
                    hg = g * 8 + h
                    S.act(Yc[:, h, :], Yc[:, h, :], AF.Identity, bias=lng[:, 1, hg:hg + 1], scale=lng[:, 0, hg:hg + 1])
                S.tt("pool", Yc[:], Yc[:], ep["BON"][:], ALU.add)
                S.tt("dve", ob[:], Yc[:], ep["GG"][:], ALU.mult)
                S.dma("pool", moT.r(moT.h[rows, t0:t0 + SC].rearrange("(h i) t -> i h t", i=64)), ob[:])


_NC_CACHE = {}


def kernel(**inputs):
    T = 4096
    if "nc" not in _NC_CACHE:
        _NC_CACHE["nc"] = build(T, "full")
    nc = _NC_CACHE["nc"]
    maps = [host_inputs(inputs, c % 4, c // 4, T) for c in range(8)]
    res = run_bass_kernel_spmd(nc, maps, core_ids=list(range(8)))
    out = np.stack([np.ascontiguousarray(np.asarray(res.results[b]["yT"]).T) for b in range(4)], axis=0)
    return out.astype(np.float32)
```

```python
import contextlib
import math
import numpy as np
import concourse.bass as bass
import concourse.mybir as mybir
from concourse.bass_utils import run_bass_kernel_spmd

F32 = mybir.dt.float32
BF16 = mybir.dt.bfloat16
I32 = mybir.dt.int32
AF = mybir.ActivationFunctionType
ALU = mybir.AluOpType

D = 1024
KC = 8
FFN = 2816
HC = 22
EPS = 1e-6


class _State:
    __slots__ = ("w", "r")

    def __init__(self):
        self.w = None
        self.r = {}


class Tile:
    def __init__(self, h, name):
        self.h = h
        self.name = name
        self.st = {None: _State()}

    def __getitem__(self, idx):
        return Ref(self, self.h[idx], None)

    def sub(self, key, ap):
        return Ref(self, ap, key)

    def r(self, ap):
        return Ref(self, ap, None)

    def states(self, key):
        if key is None:
            return list(self.st.values())
        if key not in self.st:
            self.st[key] = _State()
        return [self.st[key], self.st[None]]

    def wstate(self, key):
        if key not in self.st:
            self.st[key] = _State()
        return self.st[key]


class Ref:
    __slots__ = ("t", "ap", "key")

    def __init__(self, t, ap, key):
        self.t, self.ap, self.key = t, ap, key


class Sched:
    NDMA = 32

    def __init__(self, nc):
        self.nc = nc
        self.es = contextlib.ExitStack()
        self.eng = {"pe": nc.tensor, "act": nc.scalar, "dve": nc.vector,
                    "pool": nc.gpsimd, "sp": nc.sync}
        self.root_es = self.es
        self.semmap = {}
        self.ekey = {}
        self.gen = 0
        self.cnt = {}
        self.known = {}
        for e in self.eng:
            self.known[e] = {}
        self.new_engine_sems()
        self.lastclk = {e: {} for e in self.eng}
        self.dsem = [self.es.enter_context(nc.semaphore("d%d" % i)) for i in range(self.NDMA)]
        self.dtarget = [0] * self.NDMA
        self.dclock = [None] * self.NDMA
        self.dnext = 0
        self.ntile = 0
        self.nwait = 0
        self.ninstr = 0

    def new_engine_sems(self):
        self.gen += 1
        for e in self.eng:
            key = "%s#%d" % (e, self.gen)
            self.ekey[e] = key
            self.semmap[key] = self.root_es.enter_context(self.nc.semaphore("s_%s_%d" % (e, self.gen)))
            self.cnt[e] = 0

    def sb(self, shape, dt, name=None):
        self.ntile += 1
        name = "%s_%d" % (name or "t", self.ntile)
        h = self.es.enter_context(self.nc.sbuf_tensor(name, list(shape), dt))
        return Tile(h, name)

    def ps(self, shape, dt=F32, name=None):
        self.ntile += 1
        name = "%s_%d" % (name or "p", self.ntile)
        h = self.es.enter_context(self.nc.psum_tensor(name, list(shape), dt))
        return Tile(h, name)

    def dram(self, shape, dt, name, kind="Internal"):
        h = self.nc.dram_tensor(name, list(shape), dt, kind=kind).ap()
        return Tile(h, name)

    def _semobj(self, key):
        return self.semmap[key] if isinstance(key, str) else self.dsem[key]

    def _wait(self, e, ev):
        if ev is None:
            return
        key, val, clock = ev
        kn = self.known[e]
        if kn.get(key, 0) >= val:
            return
        self.eng[e].wait_ge(self._semobj(key), val)
        self.nwait += 1
        new = dict(kn)
        if clock:
            for k2, v2 in clock.items():
                if new.get(k2, 0) < v2:
                    new[k2] = v2
        new[key] = val
        self.known[e] = new

    def _deps(self, e, reads, writes, is_dma):
        for rf in reads:
            for st in rf.t.states(rf.key):
                if st.w is not None:
                    self._wait(e, st.w)
        for rf in writes:
            for st in rf.t.states(rf.key):
                if st.w is not None and (is_dma or st.w[0] != self.ekey[e]):
                    self._wait(e, st.w)
                for k, (v, c) in st.r.items():
                    if is_dma or k != self.ekey[e]:
                        self._wait(e, (k, v, c))

    def _record(self, ev, reads, writes):
        key, val, clock = ev
        for rf in reads:
            st = rf.t.wstate(rf.key)
            st.r[key] = (val, clock)
        for rf in writes:
            if rf.key is None:
                for k in list(rf.t.st.keys()):
                    if k is not None:
                        del rf.t.st[k]
            st = rf.t.wstate(rf.key)
            st.w = ev
            st.r = {}

    def op(self, e, fn, reads, writes):
        reads = [r for r in reads if isinstance(r, Ref)]
        self._deps(e, reads, writes, False)
        ins = fn(self.eng[e])
        self.cnt[e] += 1
        ins.then_inc(self.semmap[self.ekey[e]], 1)
        ev = (self.ekey[e], self.cnt[e], self.known[e])
        self.lastclk[e] = self.known[e]
        self._record(ev, reads, writes)
        self.ninstr += 1
        return ev

    def dma(self, q, out, in_, **kw):
        i = self.dnext
        self.dnext = (self.dnext + 1) % self.NDMA
        if self.dtarget[i] > 0:
            self._wait(q, (i, self.dtarget[i], self.dclock[i]))
        self._deps(q, [in_], [out], True)
        ins = self.eng[q].dma_start(out=out.ap, in_=in_.ap, **kw)
        ins.then_inc(self.dsem[i], 16)
        self.dtarget[i] += 16
        self.dclock[i] = self.known[q]
        ev = (i, self.dtarget[i], self.known[q])
        self._record(ev, [in_], [out])
        self.ninstr += 1
        return ev

    def collective(self, kind, alu, groups, in_, out):
        sem = self.root_es.enter_context(self.nc.semaphore("cc%d" % len(self.dsem)))
        self.dsem.append(sem)
        self.dtarget.append(0)
        self.dclock.append(None)
        i = len(self.dsem) - 1
        self._deps("pool", [in_], [out], True)
        ins = self.eng["pool"].collective_compute(kind, alu, replica_groups=groups,
                                                  ins=[in_.ap.opt()], outs=[out.ap.opt()])
        ins.then_inc(sem, 1)
        self.dtarget[i] = 1
        self.dclock[i] = self.known["pool"]
        ev = (i, 1, self.known["pool"])
        self._record(ev, [in_], [out])
        self.ninstr += 1
        return ev

    def wait_all(self, e, refs):
        for rf in refs:
            for st in rf.t.states(rf.key):
                self._wait(e, st.w)

    def close(self):
        self.es.close()
        if self.root_es is not self.es:
            self.root_es.close()

    def mm(self, out, lhsT, rhs, start=True, stop=True):
        return self.op("pe", lambda g: g.matmul(out.ap, lhsT.ap, rhs.ap, start=start, stop=stop),
                       [lhsT, rhs], [out])

    def transpose(self, out, in_, ident):
        return self.op("pe", lambda g: g.transpose(out.ap, in_.ap, ident.ap), [in_, ident], [out])

    def act(self, out, in_, func, bias=None, scale=None, e="act"):
        kw = {}
        if bias is not None:
            kw["bias"] = bias.ap if isinstance(bias, Ref) else bias
        if scale is not None:
            kw["scale"] = scale.ap if isinstance(scale, Ref) else scale
        return self.op(e, lambda g: g.activation(out.ap, in_.ap, func, **kw),
                       [in_, bias, scale], [out])

    def tt(self, e, out, in0, in1, op):
        return self.op(e, lambda g: g.tensor_tensor(out.ap, in0.ap, in1.ap, op), [in0, in1], [out])

    def ts(self, e, out, in0, s1, s2, op0, op1=None):
        a1 = s1.ap if isinstance(s1, Ref) else s1
        a2 = s2.ap if isinstance(s2, Ref) else s2
        if op1 is None:
            return self.op(e, lambda g: g.tensor_scalar(out.ap, in0.ap, a1, None, op0), [in0, s1], [out])
        return self.op(e, lambda g: g.tensor_scalar(out.ap, in0.ap, a1, a2, op0, op1), [in0, s1, s2], [out])

    def stt(self, e, out, in0, sc, in1, op0, op1):
        a = sc.ap if isinstance(sc, Ref) else sc
        return self.op(e, lambda g: g.scalar_tensor_tensor(out.ap, in0.ap, a, in1.ap, op0, op1),
                       [in0, sc, in1], [out])

    def scan(self, e, out, d0, d1, init, op0=ALU.mult, op1=ALU.add):
        a = init.ap if isinstance(init, Ref) else init
        return self.op(e, lambda g: g.tensor_tensor_scan(out.ap, d0.ap, d1.ap, a, op0, op1),
                       [d0, d1, init], [out])

    def copy(self, e, out, in_):
        if e == "act":
            return self.op(e, lambda g: g.copy(out.ap, in_.ap), [in_], [out])
        return self.op(e, lambda g: g.tensor_copy(out.ap, in_.ap), [in_], [out])

    def memset(self, e, out, val):
        return self.op(e, lambda g: g.memset(out.ap, val), [], [out])


def _barrier(S):
    for e in S.eng:
        for f in S.eng:
            if f != e and S.cnt[f] > 0:
                S._wait(e, (S.ekey[f], S.cnt[f], S.lastclk[f]))
        for i in range(len(S.dsem)):
            if S.dtarget[i] > 0:
                S._wait(e, (i, S.dtarget[i], S.dclock[i]))


class Phase:
    def __init__(self, S):
        self.S = S

    def __enter__(self):
        self.saved = self.S.es
        self.S.es = contextlib.ExitStack()
        return self

    def __exit__(self, *a):
        _barrier(self.S)
        self.S.es.close()
        self.S.es = self.saved
        if max(self.S.cnt.values()) > 16000:
            self.S.new_engine_sems()
        return False


def pk(v):
    v = np.asarray(v, dtype=np.float32)
    lead = v.shape[:-1]
    n = v.shape[-1] // 128
    v = v.reshape(lead + (n, 128))
    v = np.moveaxis(v, -1, 0)
    return np.ascontiguousarray(v.reshape(128, -1))


class Ctx:
    pass


def load_cast(S, C, dram_ref, shape, q="sp", ceng="pool", name="w"):
    st = S.sb(shape, F32, name + "_st")
    S.dma(q, st[:], dram_ref)
    wt = S.sb(shape, BF16, name + "_bf")
    S.copy(ceng, wt[:], st[:])
    return wt


def setup_consts(S, C):
    C.ones_bf = S.sb([128, 128], BF16, "ones")
    S.memset("pool", C.ones_bf[:], 1.0)
    C.ones_f = S.sb([128, 128], F32, "onesf")
    S.memset("pool", C.ones_f[:], 1.0)
    C.blk_f = S.sb([128, 128], F32, "blkf")
    S.memset("pool", C.blk_f[:], 0.0)
    S.memset("pool", C.blk_f[0:64, 0:64], 1.0)
    S.memset("pool", C.blk_f[64:128, 64:128], 1.0)
    C.blk_bf = S.sb([128, 128], BF16, "blkbf")
    S.copy("pool", C.blk_bf[:], C.blk_f[:])


def setup_adaln(S, C, I):
    C.mod = S.sb([128, 96], F32, "mod")
    C.gsm = S.sb([128, 16], F32, "gsm")
    C.gsf = S.sb([128, 16], F32, "gsf")
    with Phase(S):
        cact = S.sb([128, 8], F32, "cact")
        S.dma("sp", cact[:], I["c_pk"][:])
        S.act(cact[:], cact[:], AF.Silu)
        bada = S.sb([128, 96], F32, "bada")
        S.dma("sp", bada[:], I["b_ada_pk"][:])
        nm = S.sb([128, 16], F32, "nm")
        S.dma("sp", nm[:], I["norm_mix_pk"][:])
        nf = S.sb([128, 16], F32, "nf")
        S.dma("sp", nf[:], I["norm_ffn_pk"][:])
        pm = S.ps([128, 96], F32, "pmod")
        wt = [S.sb([128, 8, 512], F32, "wada%d" % i) for i in range(2)]
        it = 0
        for l in range(2):
            wl = I["w_ada"].h[l].rearrange("(kc p) n -> p kc n", p=128)
            for ng in range(12):
                w = wt[it % 2]
                it += 1
                S.dma("sp" if it % 2 else "act", w[:], I["w_ada"].r(wl[:, :, ng * 512:(ng + 1) * 512]))
                for j in range(4):
                    col = l * 48 + ng * 4 + j
                    for kc in range(8):
                        S.mm(pm[:, col:col + 1], w[:, kc, j * 128:(j + 1) * 128], cact[:, kc:kc + 1],
                             start=(kc == 0), stop=(kc == 7))
        S.tt("dve", C.mod[:], pm[:], bada[:], ALU.add)
        for l in range(2):
            S.stt("dve", C.gsm[:, l * 8:(l + 1) * 8], C.mod[:, l * 48 + 8:l * 48 + 16], 1.0,
                  nm[:, l * 8:(l + 1) * 8], ALU.add, ALU.mult)
            S.stt("dve", C.gsf[:, l * 8:(l + 1) * 8], C.mod[:, l * 48 + 32:l * 48 + 40], 1.0,
                  nf[:, l * 8:(l + 1) * 8], ALU.add, ALU.mult)


def rsqrt_to(S, out, in_, scale, eps):
    S.ts("dve", out, in_, scale, eps, ALU.mult, ALU.add)
    S.act(out, out, AF.Sqrt)
    S.op("dve", lambda g: g.reciprocal(out.ap, out.ap), [out], [out])


def modcol(C, l, part, kc):
    c = l * 48 + part * 8 + kc
    return C.mod[:, c:c + 1]


def rmsnorm_mod(S, C, x, h, gs, l, shift_part, W, tmp, sq, rstd):
    for kc in range(8):
        S.act(sq[:, kc, :], x[:, kc, :], AF.Square)
    for s0 in range(0, W, 512):
        ss = C.ps_ss
        for kc in range(8):
            S.mm(ss[:, :], C.ones_bf[:], sq[:, kc, s0:s0 + 512], start=(kc == 0), stop=(kc == 7))
        rsqrt_to(S, rstd[:, s0:s0 + 512], ss[:, :], 1.0 / D, EPS)
    for kc in range(8):
        e = "dve" if kc % 2 == 0 else "pool"
        S.tt(e, tmp[kc % 2][:, :], x[:, kc, :], rstd[:, :], ALU.mult)
        S.act(h[:, kc, :], tmp[kc % 2][:, :], AF.Identity, bias=modcol(C, l, shift_part, kc),
              scale=gs[:, l * 8 + kc:l * 8 + kc + 1])


def phase_F(S, C, I, l, T, mo, xin, xmid, xout, wo, PD, PDS, groups, TBS=1024):
    HCL = HC // 2
    wg = I["ffn_w_gate_h"].h[l].rearrange("(kc p) n -> p kc n", p=128)
    wu = I["ffn_w_up_h"].h[l].rearrange("(kc p) n -> p kc n", p=128)
    wd = I["ffn_w_down_h"].h[l].rearrange("(j p) n -> p j n", p=128)
    wov = wo.rearrange("(kc p) n -> p kc n", p=128)
    NS = TBS // 512
    with Phase(S):
        C.ps_ss = S.ps([128, 512], F32, "ss")
        psA = [S.ps([128, 512], F32, "psA%d" % i) for i in range(3)]
        psB = [S.ps([128, 512], F32, "psB%d" % i) for i in range(3)]
        x = S.sb([128, 8, TBS], F32, "x")
        mot = S.sb([128, 8, TBS], BF16, "mo")
        sq = S.sb([128, 8, TBS], BF16, "sq")
        h = S.sb([128, 8, TBS], BF16, "h")
        a = S.sb([128, HCL, TBS], BF16, "a")
        rstd = S.sb([128, TBS], F32, "rstd")
        tmp = [S.sb([128, TBS], F32, "tmp%d" % i) for i in range(2)]
        sg = [S.sb([128, 512], F32, "sg%d" % i) for i in range(2)]
        pdt = [S.sb([128, 512], F32, "pdt%d" % i) for i in range(3)]
        wst = [S.sb([128, 8, 128], F32, "wst%d" % i) for i in range(4)]
        wbf = [S.sb([128, 8, 128], BF16, "wbf%d" % i) for i in range(4)]
        wdst = [S.sb([128, HCL, 128], F32, "wdst%d" % i) for i in range(2)]
        wdbf = [S.sb([128, HCL, 128], BF16, "wdbf%d" % i) for i in range(2)]
        wi = 0
        pi = 0
        di = 0
        for t0 in range(0, T, TBS):
            S.dma("sp", x[:], xin.r(xin.h[:, t0:t0 + TBS].rearrange("(kc p) t -> p kc t", p=128)))
            S.dma("act", mot[:], mo.r(mo.h[:, t0:t0 + TBS].rearrange("(kc p) t -> p kc t", p=128)))
            for n in range(8):
                k = wi % 4
                wi += 1
                S.dma("sp", wst[k][:], Ref(I["_w"], wov[:, :, n * 128:(n + 1) * 128], None))
                S.copy("pool", wbf[k][:], wst[k][:])
                for s in range(NS):
                    p = psA[pi % 3]
                    pi += 1
                    for kc in range(8):
                        S.mm(p[:, :], wbf[k][:, kc, :], mot[:, kc, s * 512:(s + 1) * 512],
                             start=(kc == 0), stop=(kc == 7))
                    S.stt("dve", x[:, n, s * 512:(s + 1) * 512], p[:, :], modcol(C, l, 2, n),
                          x[:, n, s * 512:(s + 1) * 512], ALU.mult, ALU.add)
            S.dma("act", xmid.r(xmid.h[:, t0:t0 + TBS].rearrange("(kc p) t -> p kc t", p=128)), x[:])
            rmsnorm_mod(S, C, x, h, C.gsf, l, 3, TBS, tmp, sq, rstd)
            for j in range(HCL):
                k0 = wi % 4
                k1 = (wi + 1) % 4
                wi += 2
                S.dma("sp", wst[k0][:], Ref(I["_w"], wg[:, :, j * 128:(j + 1) * 128], None))
                S.dma("act", wst[k1][:], Ref(I["_w"], wu[:, :, j * 128:(j + 1) * 128], None))
                S.copy("pool", wbf[k0][:], wst[k0][:])
                S.copy("pool", wbf[k1][:], wst[k1][:])
                for s in range(NS):
                    pg = psA[pi % 3]
                    pu = psB[pi % 3]
                    pi += 1
                    for kc in range(8):
                        S.mm(pg[:, :], wbf[k0][:, kc, :], h[:, kc, s * 512:(s + 1) * 512],
                             start=(kc == 0), stop=(kc == 7))
                    for kc in range(8):
                        S.mm(pu[:, :], wbf[k1][:, kc, :], h[:, kc, s * 512:(s + 1) * 512],
                             start=(kc == 0), stop=(kc == 7))
                    g = sg[pi % 2]
                    S.act(g[:, :], pg[:, :], AF.Silu)
                    S.tt("dve", a[:, j, s * 512:(s + 1) * 512], g[:, :], pu[:, :], ALU.mult)
            for n in range(8):
                k = n % 2
                S.dma("sp" if n % 2 == 0 else "act", wdst[k][:],
                      Ref(I["_w"], wd[:, :, n * 128:(n + 1) * 128], None))
                S.copy("pool", wdbf[k][:], wdst[k][:])
                for s in range(NS):
                    p = psA[pi % 3]
                    pi += 1
                    for j in range(HCL):
                        S.mm(p[:, :], wdbf[k][:, j, :], a[:, j, s * 512:(s + 1) * 512],
                             start=(j == 0), stop=(j == HCL - 1))
                    o = pdt[di % 3]
                    di += 1
                    S.copy("act", o[:], p[:, :])
                    S.dma("pool", PD.r(PD.h[n * 128:(n + 1) * 128, t0 + s * 512:t0 + (s + 1) * 512]), o[:])
    for n in range(8):
        S.collective("AllReduce", ALU.add, groups, PD.r(PD.h[n * 128:(n + 1) * 128, :]),
                     PDS.r(PDS.h[n * 128:(n + 1) * 128, :]))
    with Phase(S):
        xs = [S.sb([128, 8, 512], F32, "xs%d" % i) for i in range(2)]
        ps_ = [S.sb([128, 8, 512], F32, "pds%d" % i) for i in range(2)]
        for bi, t0 in enumerate(range(0, T, 512)):
            xx, pp = xs[bi % 2], ps_[bi % 2]
            S.dma("sp", xx[:], xmid.r(xmid.h[:, t0:t0 + 512].rearrange("(kc p) t -> p kc t", p=128)))
            S.dma("act", pp[:], PDS.r(PDS.h[:, t0:t0 + 512].rearrange("(kc p) t -> p kc t", p=128)))
            for n in range(8):
                S.stt("dve", xx[:, n, :], pp[:, n, :], modcol(C, l, 5, n), xx[:, n, :], ALU.mult, ALU.add)
            S.dma("pool", xout.r(xout.h[:, t0:t0 + 512].rearrange("(kc p) t -> p kc t", p=128)), xx[:])


def declare_inputs(nc, specs):
    I = {}
    for name, (shape, dt) in specs.items():
        ap = nc.dram_tensor(name, list(shape), dt, kind="ExternalInput").ap()
        I[name] = Tile(ap, name)
    I["_w"] = Tile(None, "_w")
    return I


def input_specs(T):
    HH = FFN // 2
    sp = {
        "xT": ([D, T], F32), "c_pk": ([128, 8], F32),
        "w_ada": ([2, D, 6 * D], F32), "b_ada_pk": ([128, 96], F32),
        "norm_mix_pk": ([128, 16], F32), "norm_ffn_pk": ([128, 16], F32),
        "ffn_w_gate_h": ([2, D, HH], F32), "ffn_w_up_h": ([2, D, HH], F32),
        "ffn_w_down_h": ([2, HH, D], F32),
        "ev_w_out_p": ([D, D], F32), "od_w_o": ([D, D], F32),
        "ev_w_in_h": ([D, 1024], F32), "ev_s5_w_glu_h": ([512, 256], F32),
        "pos": ([T], I32), "rotm": ([128, 128], F32), "invf_col": ([128, 1], F32),
        "qk_gain_col": ([128, 2], F32), "masks": ([128, 4, 512], BF16),
        "lam_vecs": ([4, 64], F32), "subln_col": ([128, 1], F32),
        "swapm": ([128, 128], F32), "rowmask": ([128, 8], F32), "s5_d_pk": ([128, 2], F32),
        "lamre_row": ([128, 2, 64], F32), "lamim_row": ([128, 2, 64], F32), "logdt_row": ([128, 2], F32),
        "bT_re": ([128, 2, 64], F32), "bT_im": ([128, 2, 64], F32),
        "lamre_col": ([128, 16], F32), "lamim_col": ([128, 16], F32), "logdt_col": ([128, 16], F32),
        "cT_re_col": ([128, 16, 16], F32), "cT_im_col": ([128, 16, 16], F32),
        "od_w_r_h": ([D, 512], F32), "od_w_k_h": ([D, 512], F32), "od_w_v_h": ([D, 512], F32),
        "od_w1": ([1, D, 64], F32), "od_a1": ([1, D, 64], F32), "od_g1": ([1, D, 128], F32),
        "od_w2_h": ([64, 512], F32), "od_a2_h": ([64, 512], F32), "od_g2_h": ([128, 512], F32),
        "od_vecs_pk": ([128, 5, 4], F32), "od_mu_pk": ([128, 48], F32), "ident": ([128, 128], F32),
        "rw_masks": ([64, 4, 8, 64], F32), "od_ln_pk64": ([64, 2, 8], F32),
    }
    return sp


GROUPS = [[0, 4], [1, 5], [2, 6], [3, 7]]


def build(T=4096, mode="full"):
    nc = bass.Bass("TRN2", target_bir_lowering=False)
    I = declare_inputs(nc, input_specs(T))
    out = Tile(nc.dram_tensor("yT", [D, T], F32, kind="ExternalOutput").ap(), "yT")
    S = Sched(nc)
    C = Ctx()
    setup_consts(S, C)
    setup_adaln(S, C, I)
    dr = lambda shape, dt, name: S.dram(shape, dt, name, "Internal")
    uT = dr([256, T], BF16, "uT")
    qT = dr([256, T], BF16, "qT")
    kT = dr([256, T], BF16, "kT")
    vtm = dr([T, 256], BF16, "vtm")
    ygT = dr([256, T], BF16, "ygT")
    ygF = dr([512, T], BF16, "ygF")
    moL = dr([512, T], BF16, "moL")
    mo0 = dr([D, T], BF16, "mo0")
    phase_A0(S, C, I, T, I["xT"], uT, qT, kT, vtm)
    phase_S5(S, C, I, T, uT, ygT)
    S.collective("AllGather", ALU.bypass, GROUPS, ygT[:], ygF[:])
    phase_glu(S, C, I, T, ygF, ygT, moL)
    phase_attn(S, C, I, T, qT, kT, vtm, moL, 0.8 - 0.6 * math.exp(-0.3 * 0))
    for i in range(2):
        S.collective("AllGather", ALU.bypass, GROUPS, moL.r(moL.h[i * 256:(i + 1) * 256, :]),
                     mo0.r(mo0.h[i * 512:(i + 1) * 512, :]))
    xm0 = dr([D, T], F32, "xm0")
    x1 = dr([D, T], F32, "x1T")
    PD0 = dr([D, T], F32, "PD0")
    PS0 = dr([D, T], F32, "PS0")
    phase_F(S, C, I, 0, T, mo0, I["xT"], xm0, x1, I["ev_w_out_p"].h, PD0, PS0, GROUPS)
    R = {k: dr([512, T], F32, k) for k in ("AR", "RR", "KB", "BB", "BON", "GG")}
    for k in ("KT", "BT", "VT"):
        R[k] = dr([T, 512], F32, k)
    R["GL"] = dr([512, T // 64], F32, "GL")
    mo1L = dr([512, T], BF16, "mo1L")
    mo1 = dr([D, T], BF16, "mo1")
    phase_B0(S, C, I, T, x1, R)
    phase_B1(S, C, I, T, R, mo1L)
    for i in range(2):
        S.collective("AllGather", ALU.bypass, GROUPS, mo1L.r(mo1L.h[i * 256:(i + 1) * 256, :]),
                     mo1.r(mo1.h[i * 512:(i + 1) * 512, :]))
    xm1 = dr([D, T], F32, "xm1")
    PD1 = dr([D, T], F32, "PD1")
    PS1 = dr([D, T], F32, "PS1")
    phase_F(S, C, I, 1, T, mo1, x1, xm1, out, I["od_w_o"].h, PD1, PS1, GROUPS)
    S.wait_all("sp", [out[:]])
    _barrier(S)
    print("instrs", S.ninstr, "waits", S.nwait)
    S.close()
    return nc


def host_inputs(inp, b, hf, T):
    f = lambda a: np.ascontiguousarray(np.asarray(a, dtype=np.float32))
    HH = FFN // 2
    hs = slice(hf * HH, (hf + 1) * HH)
    q4 = slice(hf * 256, (hf + 1) * 256)
    win = f(inp["ev_w_in"])[0]
    wout = f(inp["ev_w_out"])[0]
    perm = np.concatenate([np.arange(0, 256), np.arange(512, 768), np.arange(256, 512), np.arange(768, 1024)])
    m = {
        "xT": np.ascontiguousarray(f(inp["x"][b, :T]).T),
        "c_pk": pk(f(inp["c"])[b]),
        "w_ada": f(inp["w_ada"]),
        "b_ada_pk": pk(f(inp["b_ada"]).reshape(-1)),
        "norm_mix_pk": pk(f(inp["norm_mix"]).reshape(-1)),
        "norm_ffn_pk": pk(f(inp["norm_ffn"]).reshape(-1)),
        "ffn_w_gate_h": f(f(inp["ffn_w_gate"])[:, :, hs]), "ffn_w_up_h": f(f(inp["ffn_w_up"])[:, :, hs]),
        "ffn_w_down_h": f(f(inp["ffn_w_down"])[:, hs, :]),
        "ev_w_out_p": f(wout), "od_w_o": f(f(inp["od_w_o"])[0][perm, :]),
        "ev_w_in_h": f(np.concatenate([win[:, 0 + hf * 256:0 + (hf + 1) * 256], win[:, 512 + hf * 256:512 + (hf + 1) * 256],
                                       win[:, 1024 + hf * 256:1024 + (hf + 1) * 256], win[:, 1536 + hf * 256:1536 + (hf + 1) * 256]], axis=1)),
        "ev_s5_w_glu_h": f(f(inp["ev_s5_w_glu"])[0][:, q4]),
        "pos": np.ascontiguousarray(np.asarray(inp["positions"])[b, :T].astype(np.int32)),
    }
    m.update(const_inputs())
    qg = f(inp["ev_q_norm"])[0]; kg = f(inp["ev_k_norm"])[0]
    m["qk_gain_col"] = np.ascontiguousarray(np.stack([np.tile(qg, 2), np.tile(kg, 2)], axis=1))
    m["lam_vecs"] = np.ascontiguousarray(np.stack([f(inp["ev_lambda_q1"])[0], f(inp["ev_lambda_k1"])[0],
                                                   f(inp["ev_lambda_q2"])[0], f(inp["ev_lambda_k2"])[0]]))
    m["subln_col"] = np.ascontiguousarray(f(inp["ev_subln"])[0].reshape(128, 1))
    gs = slice(hf * 16, (hf + 1) * 16)
    m["s5_d_pk"] = pk(f(inp["ev_s5_d"])[0][gs].reshape(-1))
    lre = f(inp["ev_s5_lam_re"])[0][gs]; lim = f(inp["ev_s5_lam_im"])[0][gs]; ldt = f(inp["ev_s5_log_dt"])[0][gs]
    rowg = (np.arange(128) // 16)[:, None] + 8 * np.arange(2)[None, :]
    m["lamre_row"] = np.ascontiguousarray(lre[rowg]); m["lamim_row"] = np.ascontiguousarray(lim[rowg])
    m["logdt_row"] = np.ascontiguousarray(ldt[rowg])
    hh = (np.arange(128) % 16)
    bre = f(inp["ev_s5_b_re"])[0][gs]; bim = f(inp["ev_s5_b_im"])[0][gs]
    m["bT_re"] = np.ascontiguousarray(bre[rowg, :, hh[:, None]]); m["bT_im"] = np.ascontiguousarray(bim[rowg, :, hh[:, None]])
    m["lamre_col"] = np.ascontiguousarray(np.concatenate([lre.T, lre.T], 0)); m["lamim_col"] = np.ascontiguousarray(np.concatenate([lim.T, lim.T], 0))
    m["logdt_col"] = np.ascontiguousarray(np.broadcast_to(ldt[None, :], (128, 16)))
    cre = np.transpose(f(inp["ev_s5_c_re"])[0][gs], (2, 0, 1)); cim = np.transpose(f(inp["ev_s5_c_im"])[0][gs], (2, 0, 1))
    m["cT_re_col"] = np.ascontiguousarray(np.concatenate([cre, cre], 0)); m["cT_im_col"] = np.ascontiguousarray(np.concatenate([cim, cim], 0))
    fs = slice(hf * 512, (hf + 1) * 512)
    for nm in ("od_w_r", "od_w_k", "od_w_v"):
        m[nm + "_h"] = f(f(inp[nm])[0][:, fs])
    for nm in ("od_w1", "od_a1", "od_g1"):
        m[nm] = f(inp[nm])
    for nm in ("od_w2", "od_a2", "od_g2"):
        m[nm + "_h"] = f(f(inp[nm])[0][:, fs])
    vecs = np.stack([f(inp["od_w0"])[0], f(inp["od_a0"])[0], f(inp["od_k_k"])[0], f(inp["od_k_a"])[0],
                     f(inp["od_r_k"])[0].reshape(-1)])[:, fs]
    m["od_vecs_pk"] = np.ascontiguousarray(pk(vecs).reshape(128, 5, 4))
    m["od_mu_pk"] = pk(f(inp["od_mu"])[0])
    ln = np.stack([f(inp["od_ln_g"])[0], f(inp["od_ln_b"])[0]])[:, fs]
    m["od_ln_pk64"] = np.ascontiguousarray(np.transpose(ln.reshape(2, 8, 64), (2, 0, 1)))
    return m


def const_inputs():
    import ml_dtypes
    c = {}
    rotm = np.zeros((128, 128), np.float32)
    for base in (0, 64):
        for i in range(8):
            rotm[base + i + 8, base + i] = -1.0
            rotm[base + i, base + i + 8] = 1.0
    c["rotm"] = rotm
    invf = np.zeros((128, 1), np.float32)
    fr = (500000.0 ** (-np.arange(0, 16, 2, dtype=np.float32) / 16)).astype(np.float32)
    for base in (0, 64):
        invf[base:base + 8, 0] = fr
        invf[base + 8:base + 16, 0] = fr
    c["invf_col"] = invf
    kk = np.arange(128)[:, None]; qq = np.arange(512)[None, :]
    c["masks"] = np.stack([(qq >= 128 * j + kk) for j in range(4)], axis=1).astype(np.float32).astype(ml_dtypes.bfloat16)
    sw = np.zeros((128, 128), np.float32)
    for p in range(64):
        sw[64 + p, p] = 1.0
        sw[p, 64 + p] = -1.0
    c["swapm"] = sw
    c["ident"] = np.eye(128, dtype=np.float32)
    ss = np.arange(64)[:, None]; tt = np.arange(64)[None, :]
    m4 = np.stack([(ss < tt), (ss <= tt), (tt < ss), (ss == tt)]).astype(np.float32)
    c["rw_masks"] = np.ascontiguousarray(np.broadcast_to(np.transpose(m4, (1, 0, 2))[:, :, None, :], (64, 4, 8, 64)))
    c["rowmask"] = (np.arange(128)[:, None] // 16 == np.arange(8)[None, :]).astype(np.float32)
    return c


TWO_PI = 2.0 * math.pi


def sincos(S, sin_out, cos_out, ang, tmps):
    y, kf, ki = tmps
    for out, off in ((sin_out, 0.5), (cos_out, 0.75)):
        if out is None:
            continue
        S.ts("dve", y, ang, 1.0 / TWO_PI, off, ALU.mult, ALU.add)
        S.copy("dve", ki, y)
        S.copy("dve", kf, ki)
        S.tt("dve", y, y, kf, ALU.subtract)
        S.ts("dve", kf, y, 0.0, None, ALU.is_lt)
        S.tt("dve", y, y, kf, ALU.add)
        S.ts("dve", y, y, TWO_PI, -math.pi, ALU.mult, ALU.add)
        S.act(out, y, AF.Sin)


def phase_A0(S, C, I, T, xin, uT, qT, kT, vtm):
    win_v = I["ev_w_in_h"].h.rearrange("(kc p) n -> p kc n", p=128)
    with Phase(S):
        C.ps_ss = S.ps([128, 512], F32, "ss")
        psA = [S.ps([128, 512], F32, "psA%d" % i) for i in range(3)]
        psR = S.ps([128, 512], F32, "psR")
        win = S.sb([128, 8, 1024], BF16, "win")
        wst = [S.sb([128, 8, 256], F32, "wst%d" % i) for i in range(2)]
        for i in range(4):
            S.dma("sp" if i % 2 == 0 else "act", wst[i % 2][:], Ref(I["_w"], win_v[:, :, i * 256:(i + 1) * 256], None))
            S.copy("pool", win[:, :, i * 256:(i + 1) * 256], wst[i % 2][:])
        rotm = S.sb([128, 128], F32, "rotm")
        S.dma("sp", rotm[:], I["rotm"][:])
        invf = S.sb([128, 1], F32, "invf")
        S.dma("sp", invf[:], I["invf_col"][:])
        gq = S.sb([128, 2], F32, "gq")
        S.dma("sp", gq[:], I["qk_gain_col"][:])
        S.ts("dve", gq[:, 0:1], gq[:, 0:1], 0.125, None, ALU.mult)
        x = S.sb([128, 8, 512], F32, "x")
        sq = S.sb([128, 8, 512], BF16, "sq")
        h = S.sb([128, 8, 512], BF16, "h")
        rstd = S.sb([128, 512], F32, "rstd")
        tmp = [S.sb([128, 512], F32, "tmp%d" % i) for i in range(2)]
        posi = S.sb([128, 512], I32, "posi")
        ang = S.sb([128, 512], F32, "ang")
        cosT = S.sb([128, 512], F32, "cosT")
        sinT = S.sb([128, 512], F32, "sinT")
        sq2 = S.sb([128, 512], F32, "sq2")
        qn = S.sb([128, 512], F32, "qn")
        r2 = S.sb([128, 512], F32, "r2")
        ob = [S.sb([128, 512], BF16, "ob%d" % i) for i in range(3)]
        oi = 0
        pi = 0
        for t0 in range(0, T, 512):
            S.dma("sp", x[:], xin.r(xin.h[:, t0:t0 + 512].rearrange("(kc p) t -> p kc t", p=128)))
            S.dma("act", posi[:], I["pos"].r(I["pos"].h[t0:t0 + 512].partition_broadcast(128)))
            S.copy("dve", ang[:], posi[:])
            S.ts("dve", ang[:], ang[:], invf[:, 0:1], None, ALU.mult)
            sincos(S, sinT[:], cosT[:], ang[:], (tmp[0][:], tmp[1][:], posi[:]))
            rmsnorm_mod(S, C, x, h, C.gsm, 0, 0, 512, tmp, sq, rstd)
            for n in range(6):
                p = psA[pi % 3]
                pi += 1
                for kc in range(8):
                    S.mm(p[:, :], win[:, kc, n * 128:(n + 1) * 128], h[:, kc, :], start=(kc == 0), stop=(kc == 7))
                o = ob[oi % 3]
                oi += 1
                if n < 2:
                    S.copy("act", o[:], p[:, :])
                    S.dma("pool", uT.r(uT.h[n * 128:(n + 1) * 128, t0:t0 + 512]), o[:])
                    continue
                isq = n < 4
                S.act(sq2[:], p[:, :], AF.Square)
                S.mm(C.ps_ss[:, :], C.blk_f[:], sq2[:])
                rsqrt_to(S, rstd[:], C.ps_ss[:, :], 1.0 / 64, EPS)
                S.tt("dve", qn[:], p[:, :], rstd[:], ALU.mult)
                S.ts("dve", qn[:], qn[:], gq[:, 0:1] if isq else gq[:, 1:2], None, ALU.mult)
                S.mm(psR[:, :], rotm[:], qn[:])
                S.tt("dve", r2[:], psR[:, :], sinT[:], ALU.mult)
                S.tt("pool", qn[:], qn[:], cosT[:], ALU.add if False else ALU.mult)
                S.tt("pool", o[:], qn[:], r2[:], ALU.add)
                dst = qT if isq else kT
                hd = (n - 2) % 2
                S.dma("pool", dst.r(dst.h[hd * 128:(hd + 1) * 128, t0:t0 + 512]), o[:])
            for tt_ in range(4):
                p = psA[pi % 3]
                pi += 1
                for kc in range(8):
                    S.mm(p[:, 0:256], h[:, kc, tt_ * 128:(tt_ + 1) * 128], win[:, kc, 768:1024],
                         start=(kc == 0), stop=(kc == 7))
                o = ob[oi % 3]
                oi += 1
                S.copy("act", o[:, 0:256], p[:, 0:256])
                S.dma("pool", vtm.r(vtm.h[t0 + tt_ * 128:t0 + (tt_ + 1) * 128, :]), o[:, 0:256])


def phase_attn(S, C, I, T, qT, kT, vtm, moT, lam_init):
    NQ = T // 512
    NK = T // 128
    with Phase(S):
        C.ps_ss = S.ps([128, 512], F32, "ss")
        psS = [S.ps([128, 512], F32, "psS%d" % i) for i in range(3)]
        psO = [S.ps([128, 512], F32, "psO%d" % i) for i in range(2)]
        psZ = [S.ps([128, 512], F32, "psZ%d" % i) for i in range(2)]
        masks = S.sb([128, 4, 512], BF16, "masks")
        S.dma("sp", masks[:], I["masks"][:])
        lv = S.sb([128, 4, 64], F32, "lv")
        S.dma("sp", lv[:], I["lam_vecs"].r(I["lam_vecs"].h[:, :].partition_broadcast(128)))
        lp = S.sb([128, 2, 64], F32, "lp")
        S.tt("dve", lp[:, 0, :], lv[:, 0, :], lv[:, 1, :], ALU.mult)
        S.tt("dve", lp[:, 1, :], lv[:, 2, :], lv[:, 3, :], ALU.mult)
        ls = S.sb([128, 2], F32, "ls")
        S.op("dve", lambda g: g.reduce_sum(ls.h[:, :], lp.h[:, :, :], axis=mybir.AxisListType.X), [lp[:]], [ls[:]])
        S.act(ls[:], ls[:], AF.Exp)
        nlam = S.sb([128, 1], F32, "nlam")
        S.tt("dve", nlam[:], ls[:, 1:2], ls[:, 0:1], ALU.subtract)
        S.ts("dve", nlam[:], nlam[:], -lam_init, None, ALU.add)
        sg = S.sb([128, 1], F32, "sg")
        S.dma("sp", sg[:], I["subln_col"][:])
        S.ts("dve", sg[:], sg[:], 1.0 - lam_init, None, ALU.mult)
        q = S.sb([128, T], BF16, "q")
        k = S.sb([128, T], BF16, "k")
        v = S.sb([128, NK, 128], BF16, "v")
        pt = [S.sb([128, 512], BF16, "pt%d" % i) for i in range(8)]
        rs = [S.sb([128, 512], F32, "rs%d" % i) for i in range(2)]
        o0 = S.sb([128, 512], F32, "o0")
        o1 = S.sb([128, 512], F32, "o1")
        sq2 = S.sb([128, 512], F32, "sq2")
        rstd = S.sb([128, 512], F32, "rstd")
        ob = [S.sb([128, 512], BF16, "ob%d" % i) for i in range(2)]
        it = 0
        for hd in range(2):
            S.dma("sp", q[:], qT.r(qT.h[hd * 128:(hd + 1) * 128, :]))
            S.dma("act", k[:], kT.r(kT.h[hd * 128:(hd + 1) * 128, :]))
            S.dma("sp", v[:], vtm.r(vtm.h[:, hd * 128:(hd + 1) * 128].rearrange("(kb p) e -> p kb e", p=128)))
            for qb in range(NQ):
                nkb = 4 * (qb + 1)
                for kb in range(nkb):
                    for c in range(2):
                        ps = psS[it % 3]
                        p = pt[it % 8]
                        it += 1
                        S.mm(ps[:, :], k[c * 64:(c + 1) * 64, kb * 128:(kb + 1) * 128],
                             q[c * 64:(c + 1) * 64, qb * 512:(qb + 1) * 512])
                        S.act(p[:], ps[:, :], AF.Exp)
                        j = kb - 4 * qb
                        if j >= 0:
                            S.tt("pool", p[:], p[:], masks[:, j, :], ALU.mult)
                        S.mm(psO[c][:, :], v[:, kb, :], p[:], start=(kb == 0), stop=(kb == nkb - 1))
                        S.mm(psZ[c][:, :], C.ones_bf[:], p[:], start=(kb == 0), stop=(kb == nkb - 1))
                for c in range(2):
                    S.op("dve", lambda g, c=c: g.reciprocal(rs[c].h[:, :], psZ[c].h[:, :]), [psZ[c][:]], [rs[c][:]])
                S.tt("dve", o0[:], psO[0][:, :], rs[0][:], ALU.mult)
                S.tt("dve", o1[:], psO[1][:, :], rs[1][:], ALU.mult)
                S.stt("dve", o0[:], o1[:], nlam[:, 0:1], o0[:], ALU.mult, ALU.add)
                S.act(sq2[:], o0[:], AF.Square)
                S.mm(C.ps_ss[:, :], C.ones_f[:], sq2[:])
                rsqrt_to(S, rstd[:], C.ps_ss[:, :], 1.0 / 128, EPS)
                S.tt("pool", o0[:], o0[:], rstd[:], ALU.mult)
                o = ob[qb % 2]
                S.ts("dve", o[:], o0[:], sg[:, 0:1], None, ALU.mult)
                S.dma("pool", moT.r(moT.h[256 + hd * 128:256 + (hd + 1) * 128, qb * 512:(qb + 1) * 512]), o[:])


def phase_S5(S, C, I, T, uT, ygT):
    NB = T // 512
    with Phase(S):
        psA = [S.ps([128, 512], F32, "psA%d" % i) for i in range(2)]
        psB = [S.ps([128, 512], F32, "psB%d" % i) for i in range(2)]
        psY = S.ps([128, 512], F32, "psY")
        psW = S.ps([128, 8], F32, "psW")
        swap = S.sb([128, 128], F32, "swap")
        S.dma("sp", swap[:], I["swapm"][:])
        rowmask = S.sb([128, 8], F32, "rowmask")
        S.dma("sp", rowmask[:], I["rowmask"][:])
        dcol = S.sb([128, 2], F32, "dcol")
        S.dma("sp", dcol[:], I["s5_d_pk"][:])
        lr = S.sb([128, 2, 64], F32, "lr")
        li = S.sb([128, 2, 64], F32, "li")
        ldt = S.sb([128, 2], F32, "ldt")
        S.dma("sp", lr[:], I["lamre_row"][:])
        S.dma("act", li[:], I["lamim_row"][:])
        S.dma("sp", ldt[:], I["logdt_row"][:])
        S.act(ldt[:], ldt[:], AF.Exp)
        dtb = ldt.h[:, :].unsqueeze(2).to_broadcast([128, 2, 64])
        th = S.sb([128, 2, 64], F32, "th")
        mag = S.sb([128, 2, 64], F32, "mag")
        S.tt("dve", th[:], li[:], ldt.r(dtb), ALU.mult)
        S.tt("dve", mag[:], lr[:], ldt.r(dtb), ALU.mult)
        S.act(mag[:], mag[:], AF.Exp)
        cs = S.sb([128, 2, 64], F32, "cs")
        sn = S.sb([128, 2, 64], F32, "sn")
        t4 = S.sb([128, 2, 64], F32, "t4")
        t4b = S.sb([128, 2, 64], F32, "t4b")
        t4i = S.sb([128, 2, 64], I32, "t4i")
        sincos(S, sn[:], cs[:], th[:], (t4[:], t4b[:], t4i[:]))
        ar = S.sb([128, 2, 64], F32, "ar")
        ai = S.sb([128, 2, 64], F32, "ai")
        S.tt("dve", ar[:], mag[:], cs[:], ALU.mult)
        S.tt("dve", ai[:], mag[:], sn[:], ALU.mult)
        S.ts("dve", ar[:], ar[:], -1.0, None, ALU.add)
        den = S.sb([128, 2, 64], F32, "den")
        S.tt("dve", den[:], lr[:], lr[:], ALU.mult)
        S.tt("dve", t4[:], li[:], li[:], ALU.mult)
        S.tt("dve", den[:], den[:], t4[:], ALU.add)
        S.op("dve", lambda g: g.reciprocal(den.h[:], den.h[:]), [den[:]], [den[:]])
        qr = S.sb([128, 2, 64], F32, "qr")
        qi = S.sb([128, 2, 64], F32, "qi")
        S.tt("dve", qr[:], ar[:], lr[:], ALU.mult)
        S.tt("dve", t4[:], ai[:], li[:], ALU.mult)
        S.tt("dve", qr[:], qr[:], t4[:], ALU.add)
        S.tt("dve", qr[:], qr[:], den[:], ALU.mult)
        S.tt("dve", qi[:], ai[:], lr[:], ALU.mult)
        S.tt("dve", t4[:], ar[:], li[:], ALU.mult)
        S.tt("dve", qi[:], qi[:], t4[:], ALU.subtract)
        S.tt("dve", qi[:], qi[:], den[:], ALU.mult)
        btr = S.sb([128, 2, 64], F32, "btr")
        bti = S.sb([128, 2, 64], F32, "bti")
        S.dma("sp", btr[:], I["bT_re"][:])
        S.dma("act", bti[:], I["bT_im"][:])
        bbr = S.sb([128, 2, 64], F32, "bbr")
        bbi = S.sb([128, 2, 64], F32, "bbi")
        S.tt("dve", bbr[:], qr[:], btr[:], ALU.mult)
        S.tt("dve", t4[:], qi[:], bti[:], ALU.mult)
        S.tt("dve", bbr[:], bbr[:], t4[:], ALU.subtract)
        S.tt("dve", bbi[:], qr[:], bti[:], ALU.mult)
        S.tt("dve", t4[:], qi[:], btr[:], ALU.mult)
        S.tt("dve", bbi[:], bbi[:], t4[:], ALU.add)
        nbbr = S.sb([128, 2, 64], F32, "nbbr")
        S.ts("dve", nbbr[:], bbr[:], -1.0, None, ALU.mult)
        lrc = S.sb([128, 16], F32, "lrc")
        lic = S.sb([128, 16], F32, "lic")
        dtc = S.sb([128, 16], F32, "dtc")
        S.dma("sp", lrc[:], I["lamre_col"][:])
        S.dma("act", lic[:], I["lamim_col"][:])
        S.dma("sp", dtc[:], I["logdt_col"][:])
        S.act(dtc[:], dtc[:], AF.Exp)
        thc = S.sb([128, 16], F32, "thc")
        rc = S.sb([128, 16], F32, "rc")
        S.tt("dve", thc[:], lic[:], dtc[:], ALU.mult)
        S.tt("dve", rc[:], lrc[:], dtc[:], ALU.mult)
        S.act(rc[:], rc[:], AF.Exp)
        c1 = S.sb([128, 16], F32, "c1")
        s1 = S.sb([128, 16], F32, "s1")
        t32 = S.sb([128, 16], F32, "t32")
        t32b = S.sb([128, 16], F32, "t32b")
        t32i = S.sb([128, 16], I32, "t32i")
        sincos(S, s1[:], c1[:], thc[:], (t32[:], t32b[:], t32i[:]))
        ctc = S.sb([128, 16, 16], F32, "ctc")
        cti = S.sb([128, 16, 16], F32, "cti")
        S.dma("sp", ctc[:], I["cT_re_col"][:])
        S.dma("act", cti[:], I["cT_im_col"][:])
        S.ts("dve", cti[:], cti[:], -1.0, None, ALU.mult)
        Ct = S.sb([128, 8, 512], F32, "Ct")
        St = S.sb([128, 8, 512], F32, "St")
        Cm = S.sb([128, 8, 512], F32, "Cm")
        Sm = S.sb([128, 8, 512], F32, "Sm")
        Rt = S.sb([128, 8, 512], F32, "Rt")
        tA = S.sb([128, 8, 256], F32, "tA")
        LA = S.sb([128, 8, 128], BF16, "LA")
        LB = S.sb([128, 8, 128], BF16, "LB")
        LC1 = S.sb([128, 8, 128], BF16, "LC1")
        LC2 = S.sb([128, 8, 128], BF16, "LC2")
        W = S.sb([128, 8, 512], F32, "W")
        w0 = S.sb([128, 8], F32, "w0")
        wl = S.sb([128, 8], F32, "wl")
        tw = S.sb([128, 8], F32, "tw")
        u = [S.sb([128, 512], BF16, "u%d" % i) for i in range(2)]
        t1 = [S.sb([128, 512], F32, "t1_%d" % i) for i in range(2)]
        t2 = [S.sb([128, 512], F32, "t2_%d" % i) for i in range(2)]
        R1 = [S.sb([128, 512], BF16, "R1_%d" % i) for i in range(2)]
        R2 = [S.sb([128, 512], BF16, "R2_%d" % i) for i in range(2)]
        yv = S.sb([128, 512], F32, "yv")
        y2 = S.sb([128, 512], F32, "y2")
        yo = [S.sb([128, 512], BF16, "yo%d" % i) for i in range(2)]
        it = 0
        for cc in range(2):
            g0 = cc * 8
            S.copy("dve", Ct[:, :, 0:1], c1.r(c1.h[:, g0:g0 + 8].unsqueeze(2)))
            S.copy("dve", St[:, :, 0:1], s1.r(s1.h[:, g0:g0 + 8].unsqueeze(2)))
            n = 1
            while n < 512:
                cn = Ct.r(Ct.h[:, :, n - 1:n].to_broadcast([128, 8, n]))
                sn_ = St.r(St.h[:, :, n - 1:n].to_broadcast([128, 8, n]))
                ta = tA.r(tA.h[:, :, 0:n])
                S.tt("dve", ta, St[:, :, 0:n], sn_, ALU.mult)
                S.tt("dve", Ct[:, :, n:2 * n], Ct[:, :, 0:n], cn, ALU.mult)
                S.tt("dve", Ct[:, :, n:2 * n], Ct[:, :, n:2 * n], ta, ALU.subtract)
                S.tt("dve", ta, St[:, :, 0:n], cn, ALU.mult)
                S.tt("dve", St[:, :, n:2 * n], Ct[:, :, 0:n], sn_, ALU.mult)
                S.tt("dve", St[:, :, n:2 * n], St[:, :, n:2 * n], ta, ALU.add)
                n *= 2
            S.copy("pool", Cm[0:64, :, :], Ct[0:64, :, :])
            S.ts("pool", Cm[64:128, :, :], St[64:128, :, :], -1.0, None, ALU.mult)
            S.copy("pool", Sm[0:64, :, :], St[0:64, :, :])
            S.copy("pool", Sm[64:128, :, :], Ct[64:128, :, :])
            S.memset("pool", Rt[:], 1.0)
            S.tt("pool", Rt[:], Rt[:], rc.r(rc.h[:, g0:g0 + 8].unsqueeze(2).to_broadcast([128, 8, 512])), ALU.mult)
            S.memset("pool", LC1[:], 0.0)
            S.memset("pool", LC2[:], 0.0)
            for gi in range(8):
                S.ts("dve", LA[:, gi, 0:64], bbr[:, cc, :], rowmask[:, gi:gi + 1], None, ALU.mult)
                S.ts("dve", LA[:, gi, 64:128], bbi[:, cc, :], rowmask[:, gi:gi + 1], None, ALU.mult)
                S.ts("dve", LB[:, gi, 0:64], bbi[:, cc, :], rowmask[:, gi:gi + 1], None, ALU.mult)
                S.ts("dve", LB[:, gi, 64:128], nbbr[:, cc, :], rowmask[:, gi:gi + 1], None, ALU.mult)
                S.copy("pool", LC1[:, gi, gi * 16:(gi + 1) * 16], ctc[:, g0 + gi, :])
                S.copy("pool", LC2[:, gi, gi * 16:(gi + 1) * 16], cti[:, g0 + gi, :])
            S.memset("pool", w0[:], 0.0)
            for tb in range(NB):
                ut = u[tb % 2]
                S.dma("sp", ut[:], uT.r(uT.h[cc * 128:(cc + 1) * 128, tb * 512:(tb + 1) * 512]))
                for gi in range(8):
                    k = it % 2
                    it += 1
                    S.mm(psA[k][:, :], LA[:, gi, :], ut[:])
                    S.mm(psB[k][:, :], LB[:, gi, :], ut[:])
                    S.tt("dve", t1[k][:], psA[k][:, :], Ct[:, gi, :], ALU.mult)
                    S.tt("dve", t2[k][:], psB[k][:, :], St[:, gi, :], ALU.mult)
                    S.tt("pool", t1[k][:], t1[k][:], t2[k][:], ALU.add)
                    S.scan("dve", W[:, gi, :], Rt[:, gi, :], t1[k][:], w0[:, gi:gi + 1])
                    S.tt("pool", R1[k][:], W[:, gi, :], Cm[:, gi, :], ALU.mult)
                    S.tt("pool", R2[k][:], W[:, gi, :], Sm[:, gi, :], ALU.mult)
                    S.mm(psY[:, :], LC1[:, gi, :], R1[k][:], start=(gi == 0), stop=False)
                    S.mm(psY[:, :], LC2[:, gi, :], R2[k][:], start=False, stop=(gi == 7))
                S.copy("dve", wl[:], W.r(W.h[:, :, 511]))
                S.mm(psW[:, :], swap[:], wl[:])
                S.tt("dve", tw[:], psW[:, :], St.r(St.h[:, :, 511]), ALU.mult)
                S.tt("dve", w0[:], wl[:], Ct.r(Ct.h[:, :, 511]), ALU.mult)
                S.tt("dve", w0[:], w0[:], tw[:], ALU.subtract)
                S.stt("dve", yv[:], ut[:], dcol[:, cc:cc + 1], psY[:, :], ALU.mult, ALU.add)
                S.act(y2[:], yv[:], AF.Square)
                S.ts("dve", y2[:], y2[:], 0.044715, 1.0, ALU.mult, ALU.add)
                S.tt("pool", y2[:], y2[:], yv[:], ALU.mult)
                S.act(y2[:], y2[:], AF.Sigmoid, scale=1.5957691216057308)
                o = yo[tb % 2]
                S.tt("dve", o[:], yv[:], y2[:], ALU.mult)
                S.dma("pool", ygT.r(ygT.h[cc * 128:(cc + 1) * 128, tb * 512:(tb + 1) * 512]), o[:])


def phase_glu(S, C, I, T, ygF, ygL, moT):
    wv = I["ev_s5_w_glu_h"].h.rearrange("(kc p) n -> p kc n", p=128)
    with Phase(S):
        ps = [S.ps([128, 512], F32, "ps%d" % i) for i in range(2)]
        wst = S.sb([128, 4, 256], F32, "wst")
        S.dma("sp", wst[:], Ref(I["_w"], wv, None))
        w = S.sb([128, 4, 256], BF16, "w")
        S.copy("pool", w[:], wst[:])
        yg = [S.sb([128, 4, 512], BF16, "yg%d" % i) for i in range(2)]
        yl = [S.sb([128, 2, 512], BF16, "yl%d" % i) for i in range(2)]
        sg = [S.sb([128, 512], F32, "sg%d" % i) for i in range(2)]
        ob = [S.sb([128, 512], BF16, "ob%d" % i) for i in range(2)]
        it = 0
        for tb in range(T // 512):
            y = yg[tb % 2]
            yy = yl[tb % 2]
            S.dma("sp", y[:], ygF.r(ygF.h[:, tb * 512:(tb + 1) * 512].rearrange("(kc p) t -> p kc t", p=128)))
            S.dma("act", yy[:], ygL.r(ygL.h[:, tb * 512:(tb + 1) * 512].rearrange("(kc p) t -> p kc t", p=128)))
            for n in range(2):
                p = ps[it % 2]
                for kc in range(4):
                    S.mm(p[:, :], w[:, kc, n * 128:(n + 1) * 128], y[:, kc, :], start=(kc == 0), stop=(kc == 3))
                S.act(sg[it % 2][:], p[:, :], AF.Sigmoid)
                S.tt("dve", ob[it % 2][:], yy[:, n, :], sg[it % 2][:], ALU.mult)
                S.dma("pool", moT.r(moT.h[n * 128:(n + 1) * 128, tb * 512:(tb + 1) * 512]), ob[it % 2][:])
                it += 1


def phase_B0(S, C, I, T, xin, R):
    TB = 256
    NCH = TB // 64
    wv_ = {n: I[n].h[0].rearrange("(kc p) n -> p kc n", p=128) for n in ("od_w1", "od_a1", "od_g1")}
    for n in ("od_w_r", "od_w_k", "od_w_v"):
        wv_[n] = I[n + "_h"].h.rearrange("(kc p) n -> p kc n", p=128)
    with Phase(S):
        C.ps_ss = S.ps([128, 512], F32, "ss")
        psA_ = [S.ps([128, 512], F32, "psA%d" % i) for i in range(4)]
        psT_ = [S.ps([128, 512], F32, "psT%d" % i) for i in range(2)]

        class _V:
            def __init__(self, t, w):
                self.t, self.w = t, w

            def __getitem__(self, idx):
                return Ref(self.t, self.t.h[:, 0:self.w][idx], None)
        psA = [_V(t, TB) for t in psA_]
        psT = [_V(t, 128) for t in psT_]
        wst = [S.sb([128, 8, 128], F32, "wst%d" % i) for i in range(2)]
        W3 = [S.sb([128, 8, 512], BF16, "W%d" % i) for i in range(3)]
        wi = 0
        for m, nm in enumerate(("od_w_r", "od_w_k", "od_w_v")):
            for n in range(4):
                S.dma("sp" if wi % 2 == 0 else "act", wst[wi % 2][:], Ref(I["_w"], wv_[nm][:, :, n * 128:(n + 1) * 128], None))
                S.copy("pool", W3[m][:, :, n * 128:(n + 1) * 128], wst[wi % 2][:])
                wi += 1
        L1 = S.sb([128, 8, 256], BF16, "L1")
        for nm, lo, wd in (("od_w1", 0, 64), ("od_a1", 64, 64), ("od_g1", 128, 128)):
            S.dma("sp", wst[wi % 2][:, :, 0:wd], Ref(I["_w"], wv_[nm], None))
            S.copy("pool", L1[:, :, lo:lo + wd], wst[wi % 2][:, :, 0:wd])
            wi += 1
        L2st = S.sb([128, 3, 512], F32, "L2st")
        S.memset("pool", L2st[:], 0.0)
        S.dma("sp", L2st[0:64, 0, :], I["od_w2_h"][:])
        S.dma("sp", L2st[0:64, 1, :], I["od_a2_h"][:])
        S.dma("sp", L2st[:, 2, :], I["od_g2_h"][:])
        L2 = S.sb([128, 3, 512], BF16, "L2")
        S.copy("pool", L2[:], L2st[:])
        pv = S.sb([128, 7, 4], F32, "pv")
        S.dma("sp", pv[:, 0:5, :], I["od_vecs_pk"][:])
        S.ts("dve", pv[:, 5, :], pv[:, 3, :], -1.0, 1.0, ALU.mult, ALU.add)
        S.ts("dve", pv[:, 6, :], pv[:, 0, :], -1.0, None, ALU.mult)
        mu = S.sb([128, 48], F32, "mu")
        S.dma("sp", mu[:], I["od_mu_pk"][:])
        ident = S.sb([128, 128], F32, "ident")
        S.dma("sp", ident[:], I["ident"][:])
        rmask = S.sb([128, TB], F32, "rmask")
        S.memset("pool", rmask[:], 1.0)
        for c in range(NCH):
            S.memset("pool", rmask[:, c * 64:c * 64 + 1], 0.0)
        x = S.sb([128, 8, TB], F32, "x")
        sq = S.sb([128, 8, TB], BF16, "sq")
        hs = S.sb([128, 8, TB + 1], F32, "hs")
        S.memset("pool", hs[:], 0.0)
        hh = S.sb([128, 8, TB], F32, "hh")
        dx = S.sb([128, 8, TB], F32, "dx")
        xm = [S.sb([128, 8, TB], BF16, "xm%d" % i) for i in range(6)]
        rstd = S.sb([128, TB], F32, "rstd")
        tmp = [S.sb([128, TB], F32, "tmp%d" % i) for i in range(2)]
        lo1 = S.sb([128, 3, TB], BF16, "lo1")
        S.memset("pool", lo1[:], 0.0)
        E = {k: S.sb([128, TB], F32, k) for k in ("r", "k", "v", "lw", "cum", "ep", "em", "ex", "a", "kk", "t0", "t1", "kmod", "b", "o0", "o1", "o2")}
        tq = [S.sb([128, 128], F32, "tq%d" % i) for i in range(2)]
        gl = S.sb([128, NCH], F32, "gl")
        pi = 0
        ti = 0
        for t0 in range(0, T, TB):
            S.dma("sp", x[:], xin.r(xin.h[:, t0:t0 + TB].rearrange("(kc p) t -> p kc t", p=128)))
            for kc in range(8):
                S.act(sq[:, kc, :], x[:, kc, :], AF.Square)
            for kc in range(8):
                S.mm(C.ps_ss[:, 0:TB], C.ones_bf[:], sq[:, kc, :], start=(kc == 0), stop=(kc == 7))
            rsqrt_to(S, rstd[:], C.ps_ss[:, 0:TB], 1.0 / D, EPS)
            for kc in range(8):
                S.tt("dve", tmp[kc % 2][:], x[:, kc, :], rstd[:], ALU.mult)
                S.act(hh[:, kc, :], tmp[kc % 2][:], AF.Identity, bias=modcol(C, 1, 0, kc), scale=C.gsm[:, 8 + kc:9 + kc])
            S.copy("pool", hs[:, :, 1:TB + 1], hh[:])
            S.tt("dve", dx[:], hs[:, :, 0:TB], hh[:], ALU.subtract)
            S.copy("pool", hs[:, :, 0:1], hh[:, :, TB - 1:TB])
            for m in range(6):
                for kc in range(8):
                    S.stt("dve", xm[m][:, kc, :], dx[:, kc, :],
                          mu[:, m * 8 + kc:m * 8 + kc + 1], hh[:, kc, :], ALU.mult, ALU.add)
            for j, (mx, lo, wd, fn) in enumerate(((1, 0, 64, AF.Tanh), (4, 64, 64, AF.Identity), (5, 128, 128, AF.Sigmoid))):
                p = psA[pi % 4]
                pi += 1
                for kc in range(8):
                    S.mm(p[0:wd, :], L1[:, kc, lo:lo + wd], xm[mx][:, kc, :], start=(kc == 0), stop=(kc == 7))
                S.act(lo1[0:wd, j, :], p[0:wd, :], fn)
            for n in range(4):
                f0 = n * 128
                pr, pk_, pv_, pw = [psA[(pi + i) % 4] for i in range(4)]
                for (p, m, mx) in ((pr, 0, 0), (pk_, 1, 2), (pv_, 2, 3)):
                    for kc in range(8):
                        S.mm(p[:, :], W3[m][:, kc, f0:f0 + 128], xm[mx][:, kc, :], start=(kc == 0), stop=(kc == 7))
                S.copy("act", E["r"][:], pr[:, :])
                S.copy("act", E["k"][:], pk_[:, :])
                S.copy("act", E["v"][:], pv_[:, :])
                S.mm(pw[:, :], L2[:, 0, f0:f0 + 128], lo1[:, 0, :])
                S.act(E["t0"][:], pw[:, :], AF.Exp, bias=pv[:, 6, n:n + 1], scale=-1.0)
                S.ts("dve", E["t0"][:], E["t0"][:], 1.0, None, ALU.add)
                S.act(E["t0"][:], E["t0"][:], AF.Ln)
                S.ts("dve", E["t0"][:], E["t0"][:], -1.0, -0.5, ALU.mult, ALU.add)
                S.act(E["t0"][:], E["t0"][:], AF.Exp)
                S.ts("dve", E["lw"][:], E["t0"][:], -1.0, None, ALU.mult)
                S.scan("dve", E["cum"][:], rmask[:], E["lw"][:], 0.0)
                S.act(E["ep"][:], E["cum"][:], AF.Exp)
                S.act(E["em"][:], E["cum"][:], AF.Exp, scale=-1.0)
                S.tt("pool", E["t1"][:], E["cum"][:], E["lw"][:], ALU.subtract)
                S.act(E["ex"][:], E["t1"][:], AF.Exp)
                S.mm(pw[:, :], L2[:, 1, f0:f0 + 128], lo1[:, 1, :])
                S.act(E["a"][:], pw[:, :], AF.Sigmoid, bias=pv[:, 1, n:n + 1])
                S.ts("dve", E["kk"][:], E["k"][:], pv[:, 2, n:n + 1], None, ALU.mult)
                S.tt("pool", E["t0"][:], E["kk"][:], E["kk"][:], ALU.mult)
                S.mm(C.ps_ss[:, 0:TB], C.blk_f[:], E["t0"][:])
                S.act(E["t0"][:], C.ps_ss[:, 0:TB], AF.Sqrt)
                S.ts("dve", E["t0"][:], E["t0"][:], 1e-12, None, ALU.max)
                S.op("dve", lambda g: g.reciprocal(E["t0"].h[:], E["t0"].h[:]), [E["t0"][:]], [E["t0"][:]])
                S.tt("dve", E["kk"][:], E["kk"][:], E["t0"][:], ALU.mult)
                S.ts("dve", E["t1"][:], E["a"][:], pv[:, 3, n:n + 1], pv[:, 5, n:n + 1], ALU.mult, ALU.add)
                S.tt("dve", E["kmod"][:], E["k"][:], E["t1"][:], ALU.mult)
                S.tt("pool", E["b"][:], E["kk"][:], E["a"][:], ALU.mult)
                S.mm(pw[:, :], L2[:, 2, f0:f0 + 128], lo1[:, 2, :])
                S.copy("act", E["o2"][:], pw[:, :])
                S.dma("pool", R["GG"].r(R["GG"].h[f0:f0 + 128, t0:t0 + TB]), E["o2"][:])
                pi += 4
                S.tt("dve", E["t0"][:], E["r"][:], E["kmod"][:], ALU.mult)
                S.ts("dve", E["t0"][:], E["t0"][:], pv[:, 4, n:n + 1], None, ALU.mult)
                S.mm(C.ps_ss[:, 0:TB], C.blk_f[:], E["t0"][:])
                S.tt("dve", E["o0"][:], C.ps_ss[:, 0:TB], E["v"][:], ALU.mult)
                S.dma("pool", R["BON"].r(R["BON"].h[f0:f0 + 128, t0:t0 + TB]), E["o0"][:])
                S.tt("dve", E["o1"][:], E["r"][:], E["ep"][:], ALU.mult)
                S.dma("pool", R["RR"].r(R["RR"].h[f0:f0 + 128, t0:t0 + TB]), E["o1"][:])
                S.stt("dve", E["o0"][:], E["kk"][:], -1.0, E["ex"][:], ALU.mult, ALU.mult)
                S.dma("pool", R["AR"].r(R["AR"].h[f0:f0 + 128, t0:t0 + TB]), E["o0"][:])
                S.tt("dve", E["kmod"][:], E["kmod"][:], E["em"][:], ALU.mult)
                S.dma("pool", R["KB"].r(R["KB"].h[f0:f0 + 128, t0:t0 + TB]), E["kmod"][:])
                S.tt("dve", E["b"][:], E["b"][:], E["em"][:], ALU.mult)
                S.dma("pool", R["BB"].r(R["BB"].h[f0:f0 + 128, t0:t0 + TB]), E["b"][:])
                S.copy("dve", gl[:], E["ep"].r(E["ep"].h[:, 63:TB:64]))
                S.dma("pool", R["GL"].r(R["GL"].h[f0:f0 + 128, t0 // 64:t0 // 64 + NCH]), gl[:])
                elb = gl.r(gl.h[:, :].unsqueeze(2).to_broadcast([128, NCH, 64]))
                S.tt("dve", E["kmod"].r(E["kmod"].h[:, :].rearrange("p (c t) -> p c t", t=64)),
                     E["kmod"].r(E["kmod"].h[:, :].rearrange("p (c t) -> p c t", t=64)), elb, ALU.mult)
                S.tt("dve", E["b"].r(E["b"].h[:, :].rearrange("p (c t) -> p c t", t=64)),
                     E["b"].r(E["b"].h[:, :].rearrange("p (c t) -> p c t", t=64)), elb, ALU.mult)
                for (src, dst) in ((E["kmod"], R["KT"]), (E["b"], R["BT"]), (E["v"], R["VT"])):
                    for s in range(TB // 128):
                        pt_ = psT[ti % 2]
                        q_ = tq[ti % 2]
                        ti += 1
                        S.transpose(pt_[:, :], src[:, s * 128:(s + 1) * 128], ident[:])
                        S.copy("act", q_[:], pt_[:, :])
                        S.dma("sp", dst.r(dst.h[t0 + s * 128:t0 + (s + 1) * 128, f0:f0 + 128]), q_[:])


def phase_B1(S, C, I, T, R, moT):
    SC = 256
    NCS = SC // 64
    NCH = T // 64
    GN_EPS = 64e-5
    F32R = mybir.dt.float32r
    with Phase(S):
        banks = [S.ps([64, 8, 64], F32, "bk%d" % i) for i in range(7)]
        psE = S.ps([64, 512], F32, "psE")
        bstate = [0]

        def mmr(out, lhsT, rhs, start=True, stop=True):
            return S.mm(out, lhsT, rhs, start=start, stop=stop)

        def bank():
            b = banks[bstate[0] % 7]
            bstate[0] += 1
            return b
        cm = S.sb([64, 4, 8, 64], F32, "cmask")
        S.dma("sp", cm[:], I["rw_masks"][:])
        MS, MI, MST, I8 = [cm.r(cm.h[:, i]) for i in range(4)]
        lng = S.sb([64, 2, 8], F32, "lng")
        S.dma("sp", lng[:], I["od_ln_pk64"][:])
        ones64 = C.ones_f[0:64, 0:64]
        fm = {k: [S.sb([64, 8, SC], F32R, "%s%d" % (k, i)) for i in range(2)] for k in ("AR", "RR", "KB", "BB")}
        tm = {k: [S.sb([64, NCS, 512], F32R, "%s%d" % (k, i)) for i in range(2)] for k in ("KT", "BT", "VT")}
        ep = {k: S.sb([64, 8, SC], F32, k) for k in ("BON", "GG")}
        GLt = S.sb([64, 8, NCH], F32, "GLt")
        ST = S.sb([64, 8, 64], F32R, "ST")
        Y = S.sb([64, 8, SC], F32, "Y")
        Yc = S.sb([64, 8, SC], F32, "Yc")
        Ysq = S.sb([64, 8, SC], F32, "Ysq")
        rstd = S.sb([64, 512], F32, "rstd")
        ob = S.sb([64, 8, SC], BF16, "ob")
        A = {k: S.sb([64, 8, 64], F32R, k) for k in ("N", "NT", "ak", "kr", "br", "Tm", "X", "UT")}
        Mp = [S.sb([64, 8, 64], F32R, "M%d" % i) for i in range(2)]
        MTp = [S.sb([64, 8, 64], F32R, "MT%d" % i) for i in range(2)]
        for g in range(1):
            rows = slice(g * 512, (g + 1) * 512)
            S.dma("sp", GLt[:], R["GL"].r(R["GL"].h[rows, :].rearrange("(h i) c -> i h c", i=64)))
            S.ts("dve", ST[:], I8, 0.0, None, ALU.mult)
            for si, t0 in enumerate(range(0, T, SC)):
                F = {}
                for k in fm:
                    F[k] = fm[k][si % 2]
                    S.dma("sp" if k in ("AR", "KB") else "act", F[k][:],
                          R[k].r(R[k].h[rows, t0:t0 + SC].rearrange("(h i) t -> i h t", i=64).bitcast(F32R)))
                for k in tm:
                    F[k] = tm[k][si % 2]
                    S.dma("sp", F[k][:], R[k].r(R[k].h[t0:t0 + SC, rows].rearrange("(c s) f -> s c f", s=64).bitcast(F32R)))
                for k in ep:
                    S.dma("act", ep[k][:], R[k].r(R[k].h[rows, t0:t0 + SC].rearrange("(h i) t -> i h t", i=64)))
                for c in range(NCS):
                    cs = slice(c * 64, (c + 1) * 64)
                    cg = t0 // 64 + c

                    def hv(name, h):
                        return F[name][:, c, h * 64:(h + 1) * 64]
                    for (dst, l, r_, msk) in ((A["N"], "BB", "AR", MS), (A["NT"], "AR", "BB", MST),
                                              (A["ak"], "KB", "AR", MS), (A["kr"], "KB", "RR", MI),
                                              (A["br"], "BB", "RR", MI)):
                        p = bank()
                        for h in range(8):
                            mmr(p[:, h, :], F[l][:, h, cs], F[r_][:, h, cs])
                        S.tt("dve", dst[:], p[:], msk, ALU.mult)
                    S.tt("dve", A["Tm"][:], A["N"][:], I8, ALU.add)
                    M, MT = A["N"], A["NT"]
                    for lv in range(5):
                        p1, p2 = bank(), bank()
                        for h in range(8):
                            mmr(p1[:, h, :], MT[:, h, :], M[:, h, :])
                            mmr(p2[:, h, :], M[:, h, :], MT[:, h, :])
                        M2, MT2 = Mp[lv % 2], MTp[lv % 2]
                        S.copy("act", M2[:], p1[:])
                        S.copy("dve", MT2[:], p2[:])
                        p3 = bank()
                        for h in range(8):
                            mmr(p3[:, h, :], MT2[:, h, :], A["Tm"][:, h, :])
                        S.tt("dve", A["Tm"][:], A["Tm"][:], p3[:], ALU.add)
                        M, MT = M2, MT2
                    px = bank()
                    for h in range(8):
                        mmr(px[:, h, :], F["AR"][:, h, cs], ST[:, h, :], start=True, stop=False)
                        mmr(px[:, h, :], A["ak"][:, h, :], hv("VT", h), start=False, stop=True)
                    S.copy("act", A["X"][:], px[:])
                    pu = bank()
                    for h in range(8):
                        mmr(pu[:, h, :], A["Tm"][:, h, :], A["X"][:, h, :])
                    S.copy("act", A["UT"][:], pu[:])
                    py = bank()
                    for h in range(8):
                        mmr(py[:, h, :], ST[:, h, :], F["RR"][:, h, cs], start=True, stop=False)
                        mmr(py[:, h, :], hv("VT", h), A["kr"][:, h, :], start=False, stop=False)
                        mmr(py[:, h, :], A["UT"][:, h, :], A["br"][:, h, :], start=False, stop=True)
                    S.copy("act", Y[:, :, cs], py[:])
                    pst = bank()
                    for h in range(8):
                        mmr(pst[:, h, :], hv("KT", h), hv("VT", h), start=True, stop=False)
                        mmr(pst[:, h, :], hv("BT", h), A["UT"][:, h, :], start=False, stop=True)
                    S.tt("dve", ST[:], ST[:], GLt.r(GLt.h[:, :, cg:cg + 1].to_broadcast([64, 8, 64])), ALU.mult)
                    S.tt("dve", ST[:], ST[:], pst[:], ALU.add)
                for q4 in range(8 * SC // 512):
                    hs_ = slice(q4 * (512 // SC), (q4 + 1) * (512 // SC))
                    S.mm(psE[:, :], ones64, Y[:, hs_, :])
                    S.stt("dve", Yc[:, hs_, :], psE[:, :], -1.0 / 64, Y[:, hs_, :], ALU.mult, ALU.add)
                    S.act(Ysq[:, hs_, :], Yc[:, hs_, :], AF.Square)
                    S.mm(psE[:, :], ones64, Ysq[:, hs_, :])
                    rsqrt_to(S, rstd[:], psE[:, :], 1.0 / 64, GN_EPS)
                    S.tt("dve", Yc[:, hs_, :], Yc[:, hs_, :], rstd[:], ALU.mult)
                for h in range(8):
                    hg = g * 8 + h
                    S.act(Yc[:, h, :], Yc[:, h, :], AF.Identity, bias=lng[:, 1, hg:hg + 1], scale=lng[:, 0, hg:hg + 1])
                S.tt("pool", Yc[:], Yc[:], ep["BON"][:], ALU.add)
                S.tt("dve", ob[:], Yc[:], ep["GG"][:], ALU.mult)
                S.dma("pool", moT.r(moT.h[rows, t0:t0 + SC].rearrange("(h i) t -> i h t", i=64)), ob[:])


_NC_CACHE = {}


def kernel(**inputs):
    T = 4096
    if "nc" not in _NC_CACHE:
        _NC_CACHE["nc"] = build(T, "full")
    nc = _NC_CACHE["nc"]
    maps = [host_inputs(inputs, c % 4, c // 4, T) for c in range(8)]
    res = run_bass_kernel_spmd(nc, maps, core_ids=list(range(8)))
    out = np.stack([np.ascontiguousarray(np.asarray(res.results[b]["yT"]).T) for b in range(4)], axis=0)
    return out.astype(np.float32)
```

```python
import contextlib
import math
import numpy as np
import concourse.bass as bass
import concourse.mybir as mybir
from concourse.bass_utils import run_bass_kernel_spmd

F32 = mybir.dt.float32
BF16 = mybir.dt.bfloat16
I32 = mybir.dt.int32
AF = mybir.ActivationFunctionType
ALU = mybir.AluOpType

D = 1024
KC = 8
FFN = 2816
HC = 22
EPS = 1e-6


class _State:
    __slots__ = ("w", "r")

    def __init__(self):
        self.w = None
        self.r = {}


class Tile:
    def __init__(self, h, name):
        self.h = h
        self.name = name
        self.st = {None: _State()}

    def __getitem__(self, idx):
        return Ref(self, self.h[idx], None)

    def sub(self, key, ap):
        return Ref(self, ap, key)

    def r(self, ap):
        return Ref(self, ap, None)

    def states(self, key):
        if key is None:
            return list(self.st.values())
        if key not in self.st:
            self.st[key] = _State()
        return [self.st[key], self.st[None]]

    def wstate(self, key):
        if key not in self.st:
            self.st[key] = _State()
        return self.st[key]


class Ref:
    __slots__ = ("t", "ap", "key")

    def __init__(self, t, ap, key):
        self.t, self.ap, self.key = t, ap, key


class Sched:
    NDMA = 32

    def __init__(self, nc):
        self.nc = nc
        self.es = contextlib.ExitStack()
        self.eng = {"pe": nc.tensor, "act": nc.scalar, "dve": nc.vector,
                    "pool": nc.gpsimd, "sp": nc.sync}
        self.root_es = self.es
        self.semmap = {}
        self.ekey = {}
        self.gen = 0
        self.cnt = {}
        self.known = {}
        for e in self.eng:
            self.known[e] = {}
        self.new_engine_sems()
        self.lastclk = {e: {} for e in self.eng}
        self.dsem = [self.es.enter_context(nc.semaphore("d%d" % i)) for i in range(self.NDMA)]
        self.dtarget = [0] * self.NDMA
        self.dclock = [None] * self.NDMA
        self.dnext = 0
        self.ntile = 0
        self.nwait = 0
        self.ninstr = 0

    def new_engine_sems(self):
        self.gen += 1
        for e in self.eng:
            key = "%s#%d" % (e, self.gen)
            self.ekey[e] = key
            self.semmap[key] = self.root_es.enter_context(self.nc.semaphore("s_%s_%d" % (e, self.gen)))
            self.cnt[e] = 0

    def sb(self, shape, dt, name=None):
        self.ntile += 1
        name = "%s_%d" % (name or "t", self.ntile)
        h = self.es.enter_context(self.nc.sbuf_tensor(name, list(shape), dt))
        return Tile(h, name)

    def ps(self, shape, dt=F32, name=None):
        self.ntile += 1
        name = "%s_%d" % (name or "p", self.ntile)
        h = self.es.enter_context(self.nc.psum_tensor(name, list(shape), dt))
        return Tile(h, name)

    def dram(self, shape, dt, name, kind="Internal"):
        h = self.nc.dram_tensor(name, list(shape), dt, kind=kind).ap()
        return Tile(h, name)

    def _semobj(self, key):
        return self.semmap[key] if isinstance(key, str) else self.dsem[key]

    def _wait(self, e, ev):
        if ev is None:
            return
        key, val, clock = ev
        kn = self.known[e]
        if kn.get(key, 0) >= val:
            return
        self.eng[e].wait_ge(self._semobj(key), val)
        self.nwait += 1
        new = dict(kn)
        if clock:
            for k2, v2 in clock.items():
                if new.get(k2, 0) < v2:
                    new[k2] = v2
        new[key] = val
        self.known[e] = new

    def _deps(self, e, reads, writes, is_dma):
        for rf in reads:
            for st in rf.t.states(rf.key):
                if st.w is not None:
                    self._wait(e, st.w)
        for rf in writes:
            for st in rf.t.states(rf.key):
                if st.w is not None and (is_dma or st.w[0] != self.ekey[e]):
                    self._wait(e, st.w)
                for k, (v, c) in st.r.items():
                    if is_dma or k != self.ekey[e]:
                        self._wait(e, (k, v, c))

    def _record(self, ev, reads, writes):
        key, val, clock = ev
        for rf in reads:
            st = rf.t.wstate(rf.key)
            st.r[key] = (val, clock)
        for rf in writes:
            if rf.key is None:
                for k in list(rf.t.st.keys()):
                    if k is not None:
                        del rf.t.st[k]
            st = rf.t.wstate(rf.key)
            st.w = ev
            st.r = {}

    def op(self, e, fn, reads, writes):
        reads = [r for r in reads if isinstance(r, Ref)]
        self._deps(e, reads, writes, False)
        ins = fn(self.eng[e])
        self.cnt[e] += 1
        ins.then_inc(self.semmap[self.ekey[e]], 1)
        ev = (self.ekey[e], self.cnt[e], self.known[e])
        self.lastclk[e] = self.known[e]
        self._record(ev, reads, writes)
        self.ninstr += 1
        return ev

    def dma(self, q, out, in_, **kw):
        i = self.dnext
        self.dnext = (self.dnext + 1) % self.NDMA
        if self.dtarget[i] > 0:
            self._wait(q, (i, self.dtarget[i], self.dclock[i]))
        self._deps(q, [in_], [out], True)
        ins = self.eng[q].dma_start(out=out.ap, in_=in_.ap, **kw)
        ins.then_inc(self.dsem[i], 16)
        self.dtarget[i] += 16
        self.dclock[i] = self.known[q]
        ev = (i, self.dtarget[i], self.known[q])
        self._record(ev, [in_], [out])
        self.ninstr += 1
        return ev

    def collective(self, kind, alu, groups, in_, out):
        sem = self.root_es.enter_context(self.nc.semaphore("cc%d" % len(self.dsem)))
        self.dsem.append(sem)
        self.dtarget.append(0)
        self.dclock.append(None)
        i = len(self.dsem) - 1
        self._deps("pool", [in_], [out], True)
        ins = self.eng["pool"].collective_compute(kind, alu, replica_groups=groups,
                                                  ins=[in_.ap.opt()], outs=[out.ap.opt()])
        ins.then_inc(sem, 1)
        self.dtarget[i] = 1
        self.dclock[i] = self.known["pool"]
        ev = (i, 1, self.known["pool"])
        self._record(ev, [in_], [out])
        self.ninstr += 1
        return ev

    def wait_all(self, e, refs):
        for rf in refs:
            for st in rf.t.states(rf.key):
                self._wait(e, st.w)

    def close(self):
        self.es.close()
        if self.root_es is not self.es:
            self.root_es.close()

    def mm(self, out, lhsT, rhs, start=True, stop=True):
        return self.op("pe", lambda g: g.matmul(out.ap, lhsT.ap, rhs.ap, start=start, stop=stop),
                       [lhsT, rhs], [out])

    def transpose(self, out, in_, ident):
        return self.op("pe", lambda g: g.transpose(out.ap, in_.ap, ident.ap), [in_, ident], [out])

    def act(self, out, in_, func, bias=None, scale=None, e="act"):
        kw = {}
        if bias is not None:
            kw["bias"] = bias.ap if isinstance(bias, Ref) else bias
        if scale is not None:
            kw["scale"] = scale.ap if isinstance(scale, Ref) else scale
        return self.op(e, lambda g: g.activation(out.ap, in_.ap, func, **kw),
                       [in_, bias, scale], [out])

    def tt(self, e, out, in0, in1, op):
        return self.op(e, lambda g: g.tensor_tensor(out.ap, in0.ap, in1.ap, op), [in0, in1], [out])

    def ts(self, e, out, in0, s1, s2, op0, op1=None):
        a1 = s1.ap if isinstance(s1, Ref) else s1
        a2 = s2.ap if isinstance(s2, Ref) else s2
        if op1 is None:
            return self.op(e, lambda g: g.tensor_scalar(out.ap, in0.ap, a1, None, op0), [in0, s1], [out])
        return self.op(e, lambda g: g.tensor_scalar(out.ap, in0.ap, a1, a2, op0, op1), [in0, s1, s2], [out])

    def stt(self, e, out, in0, sc, in1, op0, op1):
        a = sc.ap if isinstance(sc, Ref) else sc
        return self.op(e, lambda g: g.scalar_tensor_tensor(out.ap, in0.ap, a, in1.ap, op0, op1),
                       [in0, sc, in1], [out])

    def scan(self, e, out, d0, d1, init, op0=ALU.mult, op1=ALU.add):
        a = init.ap if isinstance(init, Ref) else init
        return self.op(e, lambda g: g.tensor_tensor_scan(out.ap, d0.ap, d1.ap, a, op0, op1),
                       [d0, d1, init], [out])

    def copy(self, e, out, in_):
        if e == "act":
            return self.op(e, lambda g: g.copy(out.ap, in_.ap), [in_], [out])
        return self.op(e, lambda g: g.tensor_copy(out.ap, in_.ap), [in_], [out])

    def memset(self, e, out, val):
        return self.op(e, lambda g: g.memset(out.ap, val), [], [out])


def _barrier(S):
    for e in S.eng:
        for f in S.eng:
            if f != e and S.cnt[f] > 0:
                S._wait(e, (S.ekey[f], S.cnt[f], S.lastclk[f]))
        for i in range(len(S.dsem)):
            if S.dtarget[i] > 0:
                S._wait(e, (i, S.dtarget[i], S.dclock[i]))


class Phase:
    def __init__(self, S):
        self.S = S

    def __enter__(self):
        self.saved = self.S.es
        self.S.es = contextlib.ExitStack()
        return self

    def __exit__(self, *a):
        _barrier(self.S)
        self.S.es.close()
        self.S.es = self.saved
        if max(self.S.cnt.values()) > 16000:
            self.S.new_engine_sems()
        return False


def pk(v):
    v = np.asarray(v, dtype=np.float32)
    lead = v.shape[:-1]
    n = v.shape[-1] // 128
    v = v.reshape(lead + (n, 128))
    v = np.moveaxis(v, -1, 0)
    return np.ascontiguousarray(v.reshape(128, -1))


class Ctx:
    pass


def load_cast(S, C, dram_ref, shape, q="sp", ceng="pool", name="w"):
    st = S.sb(shape, F32, name + "_st")
    S.dma(q, st[:], dram_ref)
    wt = S.sb(shape, BF16, name + "_bf")
    S.copy(ceng, wt[:], st[:])
    return wt


def setup_consts(S, C):
    C.ones_bf = S.sb([128, 128], BF16, "ones")
    S.memset("pool", C.ones_bf[:], 1.0)
    C.ones_f = S.sb([128, 128], F32, "onesf")
    S.memset("pool", C.ones_f[:], 1.0)
    C.blk_f = S.sb([128, 128], F32, "blkf")
    S.memset("pool", C.blk_f[:], 0.0)
    S.memset("pool", C.blk_f[0:64, 0:64], 1.0)
    S.memset("pool", C.blk_f[64:128, 64:128], 1.0)
    C.blk_bf = S.sb([128, 128], BF16, "blkbf")
    S.copy("pool", C.blk_bf[:], C.blk_f[:])


def setup_adaln(S, C, I):
    C.mod = S.sb([128, 96], F32, "mod")
    C.gsm = S.sb([128, 16], F32, "gsm")
    C.gsf = S.sb([128, 16], F32, "gsf")
    with Phase(S):
        cact = S.sb([128, 8], F32, "cact")
        S.dma("sp", cact[:], I["c_pk"][:])
        S.act(cact[:], cact[:], AF.Silu)
        bada = S.sb([128, 96], F32, "bada")
        S.dma("sp", bada[:], I["b_ada_pk"][:])
        nm = S.sb([128, 16], F32, "nm")
        S.dma("sp", nm[:], I["norm_mix_pk"][:])
        nf = S.sb([128, 16], F32, "nf")
        S.dma("sp", nf[:], I["norm_ffn_pk"][:])
        pm = S.ps([128, 96], F32, "pmod")
        wt = [S.sb([128, 8, 512], F32, "wada%d" % i) for i in range(2)]
        it = 0
        for l in range(2):
            wl = I["w_ada"].h[l].rearrange("(kc p) n -> p kc n", p=128)
            for ng in range(12):
                w = wt[it % 2]
                it += 1
                S.dma("sp" if it % 2 else "act", w[:], I["w_ada"].r(wl[:, :, ng * 512:(ng + 1) * 512]))
                for j in range(4):
                    col = l * 48 + ng * 4 + j
                    for kc in range(8):
                        S.mm(pm[:, col:col + 1], w[:, kc, j * 128:(j + 1) * 128], cact[:, kc:kc + 1],
                             start=(kc == 0), stop=(kc == 7))
        S.tt("dve", C.mod[:], pm[:], bada[:], ALU.add)
        for l in range(2):
            S.stt("dve", C.gsm[:, l * 8:(l + 1) * 8], C.mod[:, l * 48 + 8:l * 48 + 16], 1.0,
                  nm[:, l * 8:(l + 1) * 8], ALU.add, ALU.mult)
            S.stt("dve", C.gsf[:, l * 8:(l + 1) * 8], C.mod[:, l * 48 + 32:l * 48 + 40], 1.0,
                  nf[:, l * 8:(l + 1) * 8], ALU.add, ALU.mult)


def rsqrt_to(S, out, in_, scale, eps):
    S.ts("dve", out, in_, scale, eps, ALU.mult, ALU.add)
    S.act(out, out, AF.Sqrt)
    S.op("dve", lambda g: g.reciprocal(out.ap, out.ap), [out], [out])


def modcol(C, l, part, kc):
    c = l * 48 + part * 8 + kc
    return C.mod[:, c:c + 1]


def rmsnorm_mod(S, C, x, h, gs, l, shift_part, W, tmp, sq, rstd):
    for kc in range(8):
        S.act(sq[:, kc, :], x[:, kc, :], AF.Square)
    for s0 in range(0, W, 512):
        ss = C.ps_ss
        for kc in range(8):
            S.mm(ss[:, :], C.ones_bf[:], sq[:, kc, s0:s0 + 512], start=(kc == 0), stop=(kc == 7))
        rsqrt_to(S, rstd[:, s0:s0 + 512], ss[:, :], 1.0 / D, EPS)
    for kc in range(8):
        e = "dve" if kc % 2 == 0 else "pool"
        S.tt(e, tmp[kc % 2][:, :], x[:, kc, :], rstd[:, :], ALU.mult)
        S.act(h[:, kc, :], tmp[kc % 2][:, :], AF.Identity, bias=modcol(C, l, shift_part, kc),
              scale=gs[:, l * 8 + kc:l * 8 + kc + 1])


def phase_F(S, C, I, l, T, mo, xin, xmid, xout, wo, PD, PDS, groups, TBS=1024):
    HCL = HC // 2
    wg = I["ffn_w_gate_h"].h[l].rearrange("(kc p) n -> p kc n", p=128)
    wu = I["ffn_w_up_h"].h[l].rearrange("(kc p) n -> p kc n", p=128)
    wd = I["ffn_w_down_h"].h[l].rearrange("(j p) n -> p j n", p=128)
    wov = wo.rearrange("(kc p) n -> p kc n", p=128)
    NS = TBS // 512
    with Phase(S):
        C.ps_ss = S.ps([128, 512], F32, "ss")
        psA = [S.ps([128, 512], F32, "psA%d" % i) for i in range(3)]
        psB = [S.ps([128, 512], F32, "psB%d" % i) for i in range(3)]
        x = S.sb([128, 8, TBS], F32, "x")
        mot = S.sb([128, 8, TBS], BF16, "mo")
        sq = S.sb([128, 8, TBS], BF16, "sq")
        h = S.sb([128, 8, TBS], BF16, "h")
        a = S.sb([128, HCL, TBS], BF16, "a")
        rstd = S.sb([128, TBS], F32, "rstd")
        tmp = [S.sb([128, TBS], F32, "tmp%d" % i) for i in range(2)]
        sg = [S.sb([128, 512], F32, "sg%d" % i) for i in range(2)]
        pdt = [S.sb([128, 512], F32, "pdt%d" % i) for i in range(3)]
        wst = [S.sb([128, 8, 128], F32, "wst%d" % i) for i in range(4)]
        wbf = [S.sb([128, 8, 128], BF16, "wbf%d" % i) for i in range(4)]
        wdst = [S.sb([128, HCL, 128], F32, "wdst%d" % i) for i in range(2)]
        wdbf = [S.sb([128, HCL, 128], BF16, "wdbf%d" % i) for i in range(2)]
        wi = 0
        pi = 0
        di = 0
        for t0 in range(0, T, TBS):
            sb = t0 // TBS
            S.dma("sp", x[:], xin.r(xin.h[:, t0:t0 + TBS].rearrange("(kc p) t -> p kc t", p=128)))
            S.dma("act", mot[:], mo.r(mo.h[:, t0:t0 + TBS].rearrange("(kc p) t -> p kc t", p=128)))
            for n in range(8):
                k = wi % 4
                wi += 1
                S.dma("sp", wst[k][:], Ref(I["_w"], wov[:, :, n * 128:(n + 1) * 128], None))
                S.copy("pool", wbf[k][:], wst[k][:])
                for s in range(NS):
                    p = psA[pi % 3]
                    pi += 1
                    for kc in range(8):
                        S.mm(p[:, :], wbf[k][:, kc, :], mot[:, kc, s * 512:(s + 1) * 512],
                             start=(kc == 0), stop=(kc == 7))
                    S.stt("dve", x[:, n, s * 512:(s + 1) * 512], p[:, :], modcol(C, l, 2, n),
                          x[:, n, s * 512:(s + 1) * 512], ALU.mult, ALU.add)
            S.dma("act", xmid.r(xmid.h[:, t0:t0 + TBS].rearrange("(kc p) t -> p kc t", p=128)), x[:])
            rmsnorm_mod(S, C, x, h, C.gsf, l, 3, TBS, tmp, sq, rstd)
            for j in range(HCL):
                k0 = wi % 4
                k1 = (wi + 1) % 4
                wi += 2
                S.dma("sp", wst[k0][:], Ref(I["_w"], wg[:, :, j * 128:(j + 1) * 128], None))
                S.dma("act", wst[k1][:], Ref(I["_w"], wu[:, :, j * 128:(j + 1) * 128], None))
                S.copy("pool", wbf[k0][:], wst[k0][:])
                S.copy("pool", wbf[k1][:], wst[k1][:])
                for s in range(NS):
                    pg = psA[pi % 3]
                    pu = psB[pi % 3]
                    pi += 1
                    for kc in range(8):
                        S.mm(pg[:, :], wbf[k0][:, kc, :], h[:, kc, s * 512:(s + 1) * 512],
                             start=(kc == 0), stop=(kc == 7))
                    for kc in range(8):
                        S.mm(pu[:, :], wbf[k1][:, kc, :], h[:, kc, s * 512:(s + 1) * 512],
                             start=(kc == 0), stop=(kc == 7))
                    g = sg[pi % 2]
                    S.act(g[:, :], pg[:, :], AF.Silu)
                    S.tt("dve", a[:, j, s * 512:(s + 1) * 512], g[:, :], pu[:, :], ALU.mult)
            for n in range(8):
                k = n % 2
                S.dma("sp" if n % 2 == 0 else "act", wdst[k][:],
                      Ref(I["_w"], wd[:, :, n * 128:(n + 1) * 128], None))
                S.copy("pool", wdbf[k][:], wdst[k][:])
                for s in range(NS):
                    p = psA[pi % 3]
                    pi += 1
                    for j in range(HCL):
                        S.mm(p[:, :], wdbf[k][:, j, :], a[:, j, s * 512:(s + 1) * 512],
                             start=(j == 0), stop=(j == HCL - 1))
                    o = pdt[di % 3]
                    di += 1
                    S.copy("act", o[:], p[:, :])
                    S.dma("pool", PD.sub(sb, PD.h[sb, n * 128:(n + 1) * 128, s * 512:(s + 1) * 512]), o[:])
            for hh_ in range(2):
                S.collective("AllReduce", ALU.add, groups, PD.sub(sb, PD.h[sb, hh_ * 512:(hh_ + 1) * 512, :]),
                             PDS.sub(sb, PDS.h[sb, hh_ * 512:(hh_ + 1) * 512, :]))
    with Phase(S):
        xs = [S.sb([128, 8, 512], F32, "xs%d" % i) for i in range(2)]
        ps_ = [S.sb([128, 8, 512], F32, "pds%d" % i) for i in range(2)]
        for bi, t0 in enumerate(range(0, T, 512)):
            xx, pp = xs[bi % 2], ps_[bi % 2]
            S.dma("sp", xx[:], xmid.r(xmid.h[:, t0:t0 + 512].rearrange("(kc p) t -> p kc t", p=128)))
            sb, so = t0 // TBS, t0 % TBS
            S.dma("act", pp[:], PDS.r(PDS.h[sb, :, so:so + 512].rearrange("(kc p) t -> p kc t", p=128)))
            for n in range(8):
                S.stt("dve", xx[:, n, :], pp[:, n, :], modcol(C, l, 5, n), xx[:, n, :], ALU.mult, ALU.add)
            S.dma("pool", xout.r(xout.h[:, t0:t0 + 512].rearrange("(kc p) t -> p kc t", p=128)), xx[:])


def declare_inputs(nc, specs):
    I = {}
    for name, (shape, dt) in specs.items():
        ap = nc.dram_tensor(name, list(shape), dt, kind="ExternalInput").ap()
        I[name] = Tile(ap, name)
    I["_w"] = Tile(None, "_w")
    return I


def input_specs(T):
    HH = FFN // 2
    sp = {
        "xT": ([D, T], F32), "c_pk": ([128, 8], F32),
        "w_ada": ([2, D, 6 * D], F32), "b_ada_pk": ([128, 96], F32),
        "norm_mix_pk": ([128, 16], F32), "norm_ffn_pk": ([128, 16], F32),
        "ffn_w_gate_h": ([2, D, HH], F32), "ffn_w_up_h": ([2, D, HH], F32),
        "ffn_w_down_h": ([2, HH, D], F32),
        "ev_w_out_p": ([D, D], F32), "od_w_o": ([D, D], F32),
        "ev_w_in_h": ([D, 1024], F32), "ev_s5_w_glu_h": ([512, 256], F32),
        "pos": ([T], I32), "rotm": ([128, 128], F32), "invf_col": ([128, 1], F32),
        "qk_gain_col": ([128, 2], F32), "masks": ([128, 4, 512], BF16),
        "lam_vecs": ([4, 64], F32), "subln_col": ([128, 1], F32),
        "swapm": ([128, 128], F32), "rowmask": ([128, 8], F32), "s5_d_pk": ([128, 2], F32),
        "lamre_row": ([128, 2, 64], F32), "lamim_row": ([128, 2, 64], F32), "logdt_row": ([128, 2], F32),
        "bT_re": ([128, 2, 64], F32), "bT_im": ([128, 2, 64], F32),
        "lamre_col": ([128, 16], F32), "lamim_col": ([128, 16], F32), "logdt_col": ([128, 16], F32),
        "cT_re_col": ([128, 16, 16], F32), "cT_im_col": ([128, 16, 16], F32),
        "od_w_r_h": ([D, 512], F32), "od_w_k_h": ([D, 512], F32), "od_w_v_h": ([D, 512], F32),
        "od_w1": ([1, D, 64], F32), "od_a1": ([1, D, 64], F32), "od_g1": ([1, D, 128], F32),
        "od_w2_h": ([64, 512], F32), "od_a2_h": ([64, 512], F32), "od_g2_h": ([128, 512], F32),
        "od_vecs_pk": ([128, 5, 4], F32), "od_mu_pk": ([128, 48], F32), "ident": ([128, 128], F32),
        "rw_masks": ([64, 4, 8, 64], F32), "od_ln_pk64": ([64, 2, 8], F32),
    }
    return sp


GROUPS = [[0, 4], [1, 5], [2, 6], [3, 7]]


def build(T=4096, mode="full"):
    nc = bass.Bass("TRN2", target_bir_lowering=False)
    I = declare_inputs(nc, input_specs(T))
    out = Tile(nc.dram_tensor("yT", [D, T], F32, kind="ExternalOutput").ap(), "yT")
    S = Sched(nc)
    C = Ctx()
    setup_consts(S, C)
    setup_adaln(S, C, I)
    dr = lambda shape, dt, name: S.dram(shape, dt, name, "Internal")
    uT = dr([256, T], BF16, "uT")
    qT = dr([256, T], BF16, "qT")
    kT = dr([256, T], BF16, "kT")
    vtm = dr([T, 256], BF16, "vtm")
    ygT = dr([256, T], BF16, "ygT")
    ygF = dr([512, T], BF16, "ygF")
    moL = dr([512, T], BF16, "moL")
    mo0 = dr([D, T], BF16, "mo0")
    phase_A0(S, C, I, T, I["xT"], uT, qT, kT, vtm)
    phase_S5(S, C, I, T, uT, ygT)
    S.collective("AllGather", ALU.bypass, GROUPS, ygT[:], ygF[:])
    phase_glu(S, C, I, T, ygF, ygT, moL)
    phase_attn(S, C, I, T, qT, kT, vtm, moL, 0.8 - 0.6 * math.exp(-0.3 * 0))
    for i in range(2):
        S.collective("AllGather", ALU.bypass, GROUPS, moL.r(moL.h[i * 256:(i + 1) * 256, :]),
                     mo0.r(mo0.h[i * 512:(i + 1) * 512, :]))
    xm0 = dr([D, T], F32, "xm0")
    x1 = dr([D, T], F32, "x1T")
    PD0 = dr([T // 1024, D, 1024], F32, "PD0")
    PS0 = dr([T // 1024, D, 1024], F32, "PS0")
    phase_F(S, C, I, 0, T, mo0, I["xT"], xm0, x1, I["ev_w_out_p"].h, PD0, PS0, GROUPS)
    R = {k: dr([512, T], F32, k) for k in ("AR", "RR", "KB", "BB", "BON", "GG")}
    for k in ("KT", "BT", "VT"):
        R[k] = dr([T, 512], F32, k)
    R["GL"] = dr([512, T // 64], F32, "GL")
    mo1L = dr([512, T], BF16, "mo1L")
    mo1 = dr([D, T], BF16, "mo1")
    phase_B0(S, C, I, T, x1, R)
    phase_B1(S, C, I, T, R, mo1L)
    for i in range(2):
        S.collective("AllGather", ALU.bypass, GROUPS, mo1L.r(mo1L.h[i * 256:(i + 1) * 256, :]),
                     mo1.r(mo1.h[i * 512:(i + 1) * 512, :]))
    xm1 = dr([D, T], F32, "xm1")
    PD1 = dr([T // 1024, D, 1024], F32, "PD1")
    PS1 = dr([T // 1024, D, 1024], F32, "PS1")
    phase_F(S, C, I, 1, T, mo1, x1, xm1, out, I["od_w_o"].h, PD1, PS1, GROUPS)
    S.wait_all("sp", [out[:]])
    _barrier(S)
    print("instrs", S.ninstr, "waits", S.nwait)
    S.close()
    return nc


def host_inputs(inp, b, hf, T):
    f = lambda a: np.ascontiguousarray(np.asarray(a, dtype=np.float32))
    HH = FFN // 2
    hs = slice(hf * HH, (hf + 1) * HH)
    q4 = slice(hf * 256, (hf + 1) * 256)
    win = f(inp["ev_w_in"])[0]
    wout = f(inp["ev_w_out"])[0]
    perm = np.concatenate([np.arange(0, 256), np.arange(512, 768), np.arange(256, 512), np.arange(768, 1024)])
    m = {
        "xT": np.ascontiguousarray(f(inp["x"][b, :T]).T),
        "c_pk": pk(f(inp["c"])[b]),
        "w_ada": f(inp["w_ada"]),
        "b_ada_pk": pk(f(inp["b_ada"]).reshape(-1)),
        "norm_mix_pk": pk(f(inp["norm_mix"]).reshape(-1)),
        "norm_ffn_pk": pk(f(inp["norm_ffn"]).reshape(-1)),
        "ffn_w_gate_h": f(f(inp["ffn_w_gate"])[:, :, hs]), "ffn_w_up_h": f(f(inp["ffn_w_up"])[:, :, hs]),
        "ffn_w_down_h": f(f(inp["ffn_w_down"])[:, hs, :]),
        "ev_w_out_p": f(wout), "od_w_o": f(f(inp["od_w_o"])[0][perm, :]),
        "ev_w_in_h": f(np.concatenate([win[:, 0 + hf * 256:0 + (hf + 1) * 256], win[:, 512 + hf * 256:512 + (hf + 1) * 256],
                                       win[:, 1024 + hf * 256:1024 + (hf + 1) * 256], win[:, 1536 + hf * 256:1536 + (hf + 1) * 256]], axis=1)),
        "ev_s5_w_glu_h": f(f(inp["ev_s5_w_glu"])[0][:, q4]),
        "pos": np.ascontiguousarray(np.asarray(inp["positions"])[b, :T].astype(np.int32)),
    }
    m.update(const_inputs())
    qg = f(inp["ev_q_norm"])[0]; kg = f(inp["ev_k_norm"])[0]
    m["qk_gain_col"] = np.ascontiguousarray(np.stack([np.tile(qg, 2), np.tile(kg, 2)], axis=1))
    m["lam_vecs"] = np.ascontiguousarray(np.stack([f(inp["ev_lambda_q1"])[0], f(inp["ev_lambda_k1"])[0],
                                                   f(inp["ev_lambda_q2"])[0], f(inp["ev_lambda_k2"])[0]]))
    m["subln_col"] = np.ascontiguousarray(f(inp["ev_subln"])[0].reshape(128, 1))
    gs = slice(hf * 16, (hf + 1) * 16)
    m["s5_d_pk"] = pk(f(inp["ev_s5_d"])[0][gs].reshape(-1))
    lre = f(inp["ev_s5_lam_re"])[0][gs]; lim = f(inp["ev_s5_lam_im"])[0][gs]; ldt = f(inp["ev_s5_log_dt"])[0][gs]
    rowg = (np.arange(128) // 16)[:, None] + 8 * np.arange(2)[None, :]
    m["lamre_row"] = np.ascontiguousarray(lre[rowg]); m["lamim_row"] = np.ascontiguousarray(lim[rowg])
    m["logdt_row"] = np.ascontiguousarray(ldt[rowg])
    hh = (np.arange(128) % 16)
    bre = f(inp["ev_s5_b_re"])[0][gs]; bim = f(inp["ev_s5_b_im"])[0][gs]
    m["bT_re"] = np.ascontiguousarray(bre[rowg, :, hh[:, None]]); m["bT_im"] = np.ascontiguousarray(bim[rowg, :, hh[:, None]])
    m["lamre_col"] = np.ascontiguousarray(np.concatenate([lre.T, lre.T], 0)); m["lamim_col"] = np.ascontiguousarray(np.concatenate([lim.T, lim.T], 0))
    m["logdt_col"] = np.ascontiguousarray(np.broadcast_to(ldt[None, :], (128, 16)))
    cre = np.transpose(f(inp["ev_s5_c_re"])[0][gs], (2, 0, 1)); cim = np.transpose(f(inp["ev_s5_c_im"])[0][gs], (2, 0, 1))
    m["cT_re_col"] = np.ascontiguousarray(np.concatenate([cre, cre], 0)); m["cT_im_col"] = np.ascontiguousarray(np.concatenate([cim, cim], 0))
    fs = slice(hf * 512, (hf + 1) * 512)
    for nm in ("od_w_r", "od_w_k", "od_w_v"):
        m[nm + "_h"] = f(f(inp[nm])[0][:, fs])
    for nm in ("od_w1", "od_a1", "od_g1"):
        m[nm] = f(inp[nm])
    for nm in ("od_w2", "od_a2", "od_g2"):
        m[nm + "_h"] = f(f(inp[nm])[0][:, fs])
    vecs = np.stack([f(inp["od_w0"])[0], f(inp["od_a0"])[0], f(inp["od_k_k"])[0], f(inp["od_k_a"])[0],
                     f(inp["od_r_k"])[0].reshape(-1)])[:, fs]
    m["od_vecs_pk"] = np.ascontiguousarray(pk(vecs).reshape(128, 5, 4))
    m["od_mu_pk"] = pk(f(inp["od_mu"])[0])
    ln = np.stack([f(inp["od_ln_g"])[0], f(inp["od_ln_b"])[0]])[:, fs]
    m["od_ln_pk64"] = np.ascontiguousarray(np.transpose(ln.reshape(2, 8, 64), (2, 0, 1)))
    return m


def const_inputs():
    import ml_dtypes
    c = {}
    rotm = np.zeros((128, 128), np.float32)
    for base in (0, 64):
        for i in range(8):
            rotm[base + i + 8, base + i] = -1.0
            rotm[base + i, base + i + 8] = 1.0
    c["rotm"] = rotm
    invf = np.zeros((128, 1), np.float32)
    fr = (500000.0 ** (-np.arange(0, 16, 2, dtype=np.float32) / 16)).astype(np.float32)
    for base in (0, 64):
        invf[base:base + 8, 0] = fr
        invf[base + 8:base + 16, 0] = fr
    c["invf_col"] = invf
    kk = np.arange(128)[:, None]; qq = np.arange(512)[None, :]
    c["masks"] = np.stack([(qq >= 128 * j + kk) for j in range(4)], axis=1).astype(np.float32).astype(ml_dtypes.bfloat16)
    sw = np.zeros((128, 128), np.float32)
    for p in range(64):
        sw[64 + p, p] = 1.0
        sw[p, 64 + p] = -1.0
    c["swapm"] = sw
    c["ident"] = np.eye(128, dtype=np.float32)
    ss = np.arange(64)[:, None]; tt = np.arange(64)[None, :]
    m4 = np.stack([(ss < tt), (ss <= tt), (tt < ss), (ss == tt)]).astype(np.float32)
    c["rw_masks"] = np.ascontiguousarray(np.broadcast_to(np.transpose(m4, (1, 0, 2))[:, :, None, :], (64, 4, 8, 64)))
    c["rowmask"] = (np.arange(128)[:, None] // 16 == np.arange(8)[None, :]).astype(np.float32)
    return c


TWO_PI = 2.0 * math.pi


def sincos(S, sin_out, cos_out, ang, tmps):
    y, kf, ki = tmps
    for out, off in ((sin_out, 0.5), (cos_out, 0.75)):
        if out is None:
            continue
        S.ts("dve", y, ang, 1.0 / TWO_PI, off, ALU.mult, ALU.add)
        S.copy("dve", ki, y)
        S.copy("dve", kf, ki)
        S.tt("dve", y, y, kf, ALU.subtract)
        S.ts("dve", kf, y, 0.0, None, ALU.is_lt)
        S.tt("dve", y, y, kf, ALU.add)
        S.ts("dve", y, y, TWO_PI, -math.pi, ALU.mult, ALU.add)
        S.act(out, y, AF.Sin)


def phase_A0(S, C, I, T, xin, uT, qT, kT, vtm):
    win_v = I["ev_w_in_h"].h.rearrange("(kc p) n -> p kc n", p=128)
    with Phase(S):
        C.ps_ss = S.ps([128, 512], F32, "ss")
        psA = [S.ps([128, 512], F32, "psA%d" % i) for i in range(3)]
        psR = S.ps([128, 512], F32, "psR")
        win = S.sb([128, 8, 1024], BF16, "win")
        wst = [S.sb([128, 8, 256], F32, "wst%d" % i) for i in range(2)]
        for i in range(4):
            S.dma("sp" if i % 2 == 0 else "act", wst[i % 2][:], Ref(I["_w"], win_v[:, :, i * 256:(i + 1) * 256], None))
            S.copy("pool", win[:, :, i * 256:(i + 1) * 256], wst[i % 2][:])
        rotm = S.sb([128, 128], F32, "rotm")
        S.dma("sp", rotm[:], I["rotm"][:])
        invf = S.sb([128, 1], F32, "invf")
        S.dma("sp", invf[:], I["invf_col"][:])
        gq = S.sb([128, 2], F32, "gq")
        S.dma("sp", gq[:], I["qk_gain_col"][:])
        S.ts("dve", gq[:, 0:1], gq[:, 0:1], 0.125, None, ALU.mult)
        x = S.sb([128, 8, 512], F32, "x")
        sq = S.sb([128, 8, 512], BF16, "sq")
        h = S.sb([128, 8, 512], BF16, "h")
        rstd = S.sb([128, 512], F32, "rstd")
        tmp = [S.sb([128, 512], F32, "tmp%d" % i) for i in range(2)]
        posi = S.sb([128, 512], I32, "posi")
        ang = S.sb([128, 512], F32, "ang")
        cosT = S.sb([128, 512], F32, "cosT")
        sinT = S.sb([128, 512], F32, "sinT")
        sq2 = S.sb([128, 512], F32, "sq2")
        qn = S.sb([128, 512], F32, "qn")
        r2 = S.sb([128, 512], F32, "r2")
        ob = [S.sb([128, 512], BF16, "ob%d" % i) for i in range(3)]
        oi = 0
        pi = 0
        for t0 in range(0, T, 512):
            S.dma("sp", x[:], xin.r(xin.h[:, t0:t0 + 512].rearrange("(kc p) t -> p kc t", p=128)))
            S.dma("act", posi[:], I["pos"].r(I["pos"].h[t0:t0 + 512].partition_broadcast(128)))
            S.copy("dve", ang[:], posi[:])
            S.ts("dve", ang[:], ang[:], invf[:, 0:1], None, ALU.mult)
            sincos(S, sinT[:], cosT[:], ang[:], (tmp[0][:], tmp[1][:], posi[:]))
            rmsnorm_mod(S, C, x, h, C.gsm, 0, 0, 512, tmp, sq, rstd)
            for n in range(6):
                p = psA[pi % 3]
                pi += 1
                for kc in range(8):
                    S.mm(p[:, :], win[:, kc, n * 128:(n + 1) * 128], h[:, kc, :], start=(kc == 0), stop=(kc == 7))
                o = ob[oi % 3]
                oi += 1
                if n < 2:
                    S.copy("act", o[:], p[:, :])
                    S.dma("pool", uT.r(uT.h[n * 128:(n + 1) * 128, t0:t0 + 512]), o[:])
                    continue
                isq = n < 4
                S.act(sq2[:], p[:, :], AF.Square)
                S.mm(C.ps_ss[:, :], C.blk_f[:], sq2[:])
                rsqrt_to(S, rstd[:], C.ps_ss[:, :], 1.0 / 64, EPS)
                S.tt("dve", qn[:], p[:, :], rstd[:], ALU.mult)
                S.ts("dve", qn[:], qn[:], gq[:, 0:1] if isq else gq[:, 1:2], None, ALU.mult)
                S.mm(psR[:, :], rotm[:], qn[:])
                S.tt("dve", r2[:], psR[:, :], sinT[:], ALU.mult)
                S.tt("pool", qn[:], qn[:], cosT[:], ALU.add if False else ALU.mult)
                S.tt("pool", o[:], qn[:], r2[:], ALU.add)
                dst = qT if isq else kT
                hd = (n - 2) % 2
                S.dma("pool", dst.r(dst.h[hd * 128:(hd + 1) * 128, t0:t0 + 512]), o[:])
            for tt_ in range(4):
                p = psA[pi % 3]
                pi += 1
                for kc in range(8):
                    S.mm(p[:, 0:256], h[:, kc, tt_ * 128:(tt_ + 1) * 128], win[:, kc, 768:1024],
                         start=(kc == 0), stop=(kc == 7))
                o = ob[oi % 3]
                oi += 1
                S.copy("act", o[:, 0:256], p[:, 0:256])
                S.dma("pool", vtm.r(vtm.h[t0 + tt_ * 128:t0 + (tt_ + 1) * 128, :]), o[:, 0:256])


def phase_attn(S, C, I, T, qT, kT, vtm, moT, lam_init):
    NQ = T // 512
    NK = T // 128
    with Phase(S):
        C.ps_ss = S.ps([128, 512], F32, "ss")
        psS = [S.ps([128, 512], F32, "psS%d" % i) for i in range(3)]
        psO = [S.ps([128, 512], F32, "psO%d" % i) for i in range(2)]
        psZ = [S.ps([128, 512], F32, "psZ%d" % i) for i in range(2)]
        masks = S.sb([128, 4, 512], BF16, "masks")
        S.dma("sp", masks[:], I["masks"][:])
        lv = S.sb([128, 4, 64], F32, "lv")
        S.dma("sp", lv[:], I["lam_vecs"].r(I["lam_vecs"].h[:, :].partition_broadcast(128)))
        lp = S.sb([128, 2, 64], F32, "lp")
        S.tt("dve", lp[:, 0, :], lv[:, 0, :], lv[:, 1, :], ALU.mult)
        S.tt("dve", lp[:, 1, :], lv[:, 2, :], lv[:, 3, :], ALU.mult)
        ls = S.sb([128, 2], F32, "ls")
        S.op("dve", lambda g: g.reduce_sum(ls.h[:, :], lp.h[:, :, :], axis=mybir.AxisListType.X), [lp[:]], [ls[:]])
        S.act(ls[:], ls[:], AF.Exp)
        nlam = S.sb([128, 1], F32, "nlam")
        S.tt("dve", nlam[:], ls[:, 1:2], ls[:, 0:1], ALU.subtract)
        S.ts("dve", nlam[:], nlam[:], -lam_init, None, ALU.add)
        sg = S.sb([128, 1], F32, "sg")
        S.dma("sp", sg[:], I["subln_col"][:])
        S.ts("dve", sg[:], sg[:], 1.0 - lam_init, None, ALU.mult)
        q = S.sb([128, T], BF16, "q")
        k = S.sb([128, T], BF16, "k")
        v = S.sb([128, NK, 128], BF16, "v")
        pt = [S.sb([128, 512], BF16, "pt%d" % i) for i in range(8)]
        rs = [S.sb([128, 512], F32, "rs%d" % i) for i in range(2)]
        o0 = S.sb([128, 512], F32, "o0")
        o1 = S.sb([128, 512], F32, "o1")
        sq2 = S.sb([128, 512], F32, "sq2")
        rstd = S.sb([128, 512], F32, "rstd")
        ob = [S.sb([128, 512], BF16, "ob%d" % i) for i in range(2)]
        it = 0
        for hd in range(2):
            S.dma("sp", q[:], qT.r(qT.h[hd * 128:(hd + 1) * 128, :]))
            S.dma("act", k[:], kT.r(kT.h[hd * 128:(hd + 1) * 128, :]))
            S.dma("sp", v[:], vtm.r(vtm.h[:, hd * 128:(hd + 1) * 128].rearrange("(kb p) e -> p kb e", p=128)))
            for qb in range(NQ):
                nkb = 4 * (qb + 1)
                for kb in range(nkb):
                    for c in range(2):
                        ps = psS[it % 3]
                        p = pt[it % 8]
                        it += 1
                        S.mm(ps[:, :], k[c * 64:(c + 1) * 64, kb * 128:(kb + 1) * 128],
                             q[c * 64:(c + 1) * 64, qb * 512:(qb + 1) * 512])
                        S.act(p[:], ps[:, :], AF.Exp)
                        j = kb - 4 * qb
                        if j >= 0:
                            S.tt("pool", p[:], p[:], masks[:, j, :], ALU.mult)
                        S.mm(psO[c][:, :], v[:, kb, :], p[:], start=(kb == 0), stop=(kb == nkb - 1))
                        S.mm(psZ[c][:, :], C.ones_bf[:], p[:], start=(kb == 0), stop=(kb == nkb - 1))
                for c in range(2):
                    S.op("dve", lambda g, c=c: g.reciprocal(rs[c].h[:, :], psZ[c].h[:, :]), [psZ[c][:]], [rs[c][:]])
                S.tt("dve", o0[:], psO[0][:, :], rs[0][:], ALU.mult)
                S.tt("dve", o1[:], psO[1][:, :], rs[1][:], ALU.mult)
                S.stt("dve", o0[:], o1[:], nlam[:, 0:1], o0[:], ALU.mult, ALU.add)
                S.act(sq2[:], o0[:], AF.Square)
                S.mm(C.ps_ss[:, :], C.ones_f[:], sq2[:])
                rsqrt_to(S, rstd[:], C.ps_ss[:, :], 1.0 / 128, EPS)
                S.tt("pool", o0[:], o0[:], rstd[:], ALU.mult)
                o = ob[qb % 2]
                S.ts("dve", o[:], o0[:], sg[:, 0:1], None, ALU.mult)
                S.dma("pool", moT.r(moT.h[256 + hd * 128:256 + (hd + 1) * 128, qb * 512:(qb + 1) * 512]), o[:])


def phase_S5(S, C, I, T, uT, ygT):
    NB = T // 512
    with Phase(S):
        psA = [S.ps([128, 512], F32, "psA%d" % i) for i in range(2)]
        psB = [S.ps([128, 512], F32, "psB%d" % i) for i in range(2)]
        psY = S.ps([128, 512], F32, "psY")
        psW = S.ps([128, 8], F32, "psW")
        swap = S.sb([128, 128], F32, "swap")
        S.dma("sp", swap[:], I["swapm"][:])
        rowmask = S.sb([128, 8], F32, "rowmask")
        S.dma("sp", rowmask[:], I["rowmask"][:])
        dcol = S.sb([128, 2], F32, "dcol")
        S.dma("sp", dcol[:], I["s5_d_pk"][:])
        lr = S.sb([128, 2, 64], F32, "lr")
        li = S.sb([128, 2, 64], F32, "li")
        ldt = S.sb([128, 2], F32, "ldt")
        S.dma("sp", lr[:], I["lamre_row"][:])
        S.dma("act", li[:], I["lamim_row"][:])
        S.dma("sp", ldt[:], I["logdt_row"][:])
        S.act(ldt[:], ldt[:], AF.Exp)
        dtb = ldt.h[:, :].unsqueeze(2).to_broadcast([128, 2, 64])
        th = S.sb([128, 2, 64], F32, "th")
        mag = S.sb([128, 2, 64], F32, "mag")
        S.tt("dve", th[:], li[:], ldt.r(dtb), ALU.mult)
        S.tt("dve", mag[:], lr[:], ldt.r(dtb), ALU.mult)
        S.act(mag[:], mag[:], AF.Exp)
        cs = S.sb([128, 2, 64], F32, "cs")
        sn = S.sb([128, 2, 64], F32, "sn")
        t4 = S.sb([128, 2, 64], F32, "t4")
        t4b = S.sb([128, 2, 64], F32, "t4b")
        t4i = S.sb([128, 2, 64], I32, "t4i")
        sincos(S, sn[:], cs[:], th[:], (t4[:], t4b[:], t4i[:]))
        ar = S.sb([128, 2, 64], F32, "ar")
        ai = S.sb([128, 2, 64], F32, "ai")
        S.tt("dve", ar[:], mag[:], cs[:], ALU.mult)
        S.tt("dve", ai[:], mag[:], sn[:], ALU.mult)
        S.ts("dve", ar[:], ar[:], -1.0, None, ALU.add)
        den = S.sb([128, 2, 64], F32, "den")
        S.tt("dve", den[:], lr[:], lr[:], ALU.mult)
        S.tt("dve", t4[:], li[:], li[:], ALU.mult)
        S.tt("dve", den[:], den[:], t4[:], ALU.add)
        S.op("dve", lambda g: g.reciprocal(den.h[:], den.h[:]), [den[:]], [den[:]])
        qr = S.sb([128, 2, 64], F32, "qr")
        qi = S.sb([128, 2, 64], F32, "qi")
        S.tt("dve", qr[:], ar[:], lr[:], ALU.mult)
        S.tt("dve", t4[:], ai[:], li[:], ALU.mult)
        S.tt("dve", qr[:], qr[:], t4[:], ALU.add)
        S.tt("dve", qr[:], qr[:], den[:], ALU.mult)
        S.tt("dve", qi[:], ai[:], lr[:], ALU.mult)
        S.tt("dve", t4[:], ar[:], li[:], ALU.mult)
        S.tt("dve", qi[:], qi[:], t4[:], ALU.subtract)
        S.tt("dve", qi[:], qi[:], den[:], ALU.mult)
        btr = S.sb([128, 2, 64], F32, "btr")
        bti = S.sb([128, 2, 64], F32, "bti")
        S.dma("sp", btr[:], I["bT_re"][:])
        S.dma("act", bti[:], I["bT_im"][:])
        bbr = S.sb([128, 2, 64], F32, "bbr")
        bbi = S.sb([128, 2, 64], F32, "bbi")
        S.tt("dve", bbr[:], qr[:], btr[:], ALU.mult)
        S.tt("dve", t4[:], qi[:], bti[:], ALU.mult)
        S.tt("dve", bbr[:], bbr[:], t4[:], ALU.subtract)
        S.tt("dve", bbi[:], qr[:], bti[:], ALU.mult)
        S.tt("dve", t4[:], qi[:], btr[:], ALU.mult)
        S.tt("dve", bbi[:], bbi[:], t4[:], ALU.add)
        nbbr = S.sb([128, 2, 64], F32, "nbbr")
        S.ts("dve", nbbr[:], bbr[:], -1.0, None, ALU.mult)
        lrc = S.sb([128, 16], F32, "lrc")
        lic = S.sb([128, 16], F32, "lic")
        dtc = S.sb([128, 16], F32, "dtc")
        S.dma("sp", lrc[:], I["lamre_col"][:])
        S.dma("act", lic[:], I["lamim_col"][:])
        S.dma("sp", dtc[:], I["logdt_col"][:])
        S.act(dtc[:], dtc[:], AF.Exp)
        thc = S.sb([128, 16], F32, "thc")
        rc = S.sb([128, 16], F32, "rc")
        S.tt("dve", thc[:], lic[:], dtc[:], ALU.mult)
        S.tt("dve", rc[:], lrc[:], dtc[:], ALU.mult)
        S.act(rc[:], rc[:], AF.Exp)
        c1 = S.sb([128, 16], F32, "c1")
        s1 = S.sb([128, 16], F32, "s1")
        t32 = S.sb([128, 16], F32, "t32")
        t32b = S.sb([128, 16], F32, "t32b")
        t32i = S.sb([128, 16], I32, "t32i")
        sincos(S, s1[:], c1[:], thc[:], (t32[:], t32b[:], t32i[:]))
        ctc = S.sb([128, 16, 16], F32, "ctc")
        cti = S.sb([128, 16, 16], F32, "cti")
        S.dma("sp", ctc[:], I["cT_re_col"][:])
        S.dma("act", cti[:], I["cT_im_col"][:])
        S.ts("dve", cti[:], cti[:], -1.0, None, ALU.mult)
        Ct = S.sb([128, 8, 512], F32, "Ct")
        St = S.sb([128, 8, 512], F32, "St")
        Cm = S.sb([128, 8, 512], F32, "Cm")
        Sm = S.sb([128, 8, 512], F32, "Sm")
        Rt = S.sb([128, 8, 512], F32, "Rt")
        tA = S.sb([128, 8, 256], F32, "tA")
        LA = S.sb([128, 8, 128], BF16, "LA")
        LB = S.sb([128, 8, 128], BF16, "LB")
        LC1 = S.sb([128, 8, 128], BF16, "LC1")
        LC2 = S.sb([128, 8, 128], BF16, "LC2")
        W = S.sb([128, 8, 512], F32, "W")
        w0 = S.sb([128, 8], F32, "w0")
        wl = S.sb([128, 8], F32, "wl")
        tw = S.sb([128, 8], F32, "tw")
        u = [S.sb([128, 512], BF16, "u%d" % i) for i in range(2)]
        t1 = [S.sb([128, 512], F32, "t1_%d" % i) for i in range(2)]
        t2 = [S.sb([128, 512], F32, "t2_%d" % i) for i in range(2)]
        R1 = [S.sb([128, 512], BF16, "R1_%d" % i) for i in range(2)]
        R2 = [S.sb([128, 512], BF16, "R2_%d" % i) for i in range(2)]
        yv = S.sb([128, 512], F32, "yv")
        y2 = S.sb([128, 512], F32, "y2")
        yo = [S.sb([128, 512], BF16, "yo%d" % i) for i in range(2)]
        it = 0
        for cc in range(2):
            g0 = cc * 8
            S.copy("dve", Ct[:, :, 0:1], c1.r(c1.h[:, g0:g0 + 8].unsqueeze(2)))
            S.copy("dve", St[:, :, 0:1], s1.r(s1.h[:, g0:g0 + 8].unsqueeze(2)))
            n = 1
            while n < 512:
                cn = Ct.r(Ct.h[:, :, n - 1:n].to_broadcast([128, 8, n]))
                sn_ = St.r(St.h[:, :, n - 1:n].to_broadcast([128, 8, n]))
                ta = tA.r(tA.h[:, :, 0:n])
                S.tt("dve", ta, St[:, :, 0:n], sn_, ALU.mult)
                S.tt("dve", Ct[:, :, n:2 * n], Ct[:, :, 0:n], cn, ALU.mult)
                S.tt("dve", Ct[:, :, n:2 * n], Ct[:, :, n:2 * n], ta, ALU.subtract)
                S.tt("dve", ta, St[:, :, 0:n], cn, ALU.mult)
                S.tt("dve", St[:, :, n:2 * n], Ct[:, :, 0:n], sn_, ALU.mult)
                S.tt("dve", St[:, :, n:2 * n], St[:, :, n:2 * n], ta, ALU.add)
                n *= 2
            S.copy("pool", Cm[0:64, :, :], Ct[0:64, :, :])
            S.ts("pool", Cm[64:128, :, :], St[64:128, :, :], -1.0, None, ALU.mult)
            S.copy("pool", Sm[0:64, :, :], St[0:64, :, :])
            S.copy("pool", Sm[64:128, :, :], Ct[64:128, :, :])
            S.memset("pool", Rt[:], 1.0)
            S.tt("pool", Rt[:], Rt[:], rc.r(rc.h[:, g0:g0 + 8].unsqueeze(2).to_broadcast([128, 8, 512])), ALU.mult)
            S.memset("pool", LC1[:], 0.0)
            S.memset("pool", LC2[:], 0.0)
            for gi in range(8):
                S.ts("dve", LA[:, gi, 0:64], bbr[:, cc, :], rowmask[:, gi:gi + 1], None, ALU.mult)
                S.ts("dve", LA[:, gi, 64:128], bbi[:, cc, :], rowmask[:, gi:gi + 1], None, ALU.mult)
                S.ts("dve", LB[:, gi, 0:64], bbi[:, cc, :], rowmask[:, gi:gi + 1], None, ALU.mult)
                S.ts("dve", LB[:, gi, 64:128], nbbr[:, cc, :], rowmask[:, gi:gi + 1], None, ALU.mult)
                S.copy("pool", LC1[:, gi, gi * 16:(gi + 1) * 16], ctc[:, g0 + gi, :])
                S.copy("pool", LC2[:, gi, gi * 16:(gi + 1) * 16], cti[:, g0 + gi, :])
            S.memset("pool", w0[:], 0.0)
            for tb in range(NB):
                ut = u[tb % 2]
                S.dma("sp", ut[:], uT.r(uT.h[cc * 128:(cc + 1) * 128, tb * 512:(tb + 1) * 512]))
                for gi in range(8):
                    k = it % 2
                    it += 1
                    S.mm(psA[k][:, :], LA[:, gi, :], ut[:])
                    S.mm(psB[k][:, :], LB[:, gi, :], ut[:])
                    S.tt("dve", t1[k][:], psA[k][:, :], Ct[:, gi, :], ALU.mult)
                    S.tt("dve", t2[k][:], psB[k][:, :], St[:, gi, :], ALU.mult)
                    S.tt("pool", t1[k][:], t1[k][:], t2[k][:], ALU.add)
                    S.scan("dve", W[:, gi, :], Rt[:, gi, :], t1[k][:], w0[:, gi:gi + 1])
                    S.tt("pool", R1[k][:], W[:, gi, :], Cm[:, gi, :], ALU.mult)
                    S.tt("pool", R2[k][:], W[:, gi, :], Sm[:, gi, :], ALU.mult)
                    S.mm(psY[:, :], LC1[:, gi, :], R1[k][:], start=(gi == 0), stop=False)
                    S.mm(psY[:, :], LC2[:, gi, :], R2[k][:], start=False, stop=(gi == 7))
                S.copy("dve", wl[:], W.r(W.h[:, :, 511]))
                S.mm(psW[:, :], swap[:], wl[:])
                S.tt("dve", tw[:], psW[:, :], St.r(St.h[:, :, 511]), ALU.mult)
                S.tt("dve", w0[:], wl[:], Ct.r(Ct.h[:, :, 511]), ALU.mult)
                S.tt("dve", w0[:], w0[:], tw[:], ALU.subtract)
                S.stt("dve", yv[:], ut[:], dcol[:, cc:cc + 1], psY[:, :], ALU.mult, ALU.add)
                S.act(y2[:], yv[:], AF.Square)
                S.ts("dve", y2[:], y2[:], 0.044715, 1.0, ALU.mult, ALU.add)
                S.tt("pool", y2[:], y2[:], yv[:], ALU.mult)
                S.act(y2[:], y2[:], AF.Sigmoid, scale=1.5957691216057308)
                o = yo[tb % 2]
                S.tt("dve", o[:], yv[:], y2[:], ALU.mult)
                S.dma("pool", ygT.r(ygT.h[cc * 128:(cc + 1) * 128, tb * 512:(tb + 1) * 512]), o[:])


def phase_glu(S, C, I, T, ygF, ygL, moT):
    wv = I["ev_s5_w_glu_h"].h.rearrange("(kc p) n -> p kc n", p=128)
    with Phase(S):
        ps = [S.ps([128, 512], F32, "ps%d" % i) for i in range(2)]
        wst = S.sb([128, 4, 256], F32, "wst")
        S.dma("sp", wst[:], Ref(I["_w"], wv, None))
        w = S.sb([128, 4, 256], BF16, "w")
        S.copy("pool", w[:], wst[:])
        yg = [S.sb([128, 4, 512], BF16, "yg%d" % i) for i in range(2)]
        yl = [S.sb([128, 2, 512], BF16, "yl%d" % i) for i in range(2)]
        sg = [S.sb([128, 512], F32, "sg%d" % i) for i in range(2)]
        ob = [S.sb([128, 512], BF16, "ob%d" % i) for i in range(2)]
        it = 0
        for tb in range(T // 512):
            y = yg[tb % 2]
            yy = yl[tb % 2]
            S.dma("sp", y[:], ygF.r(ygF.h[:, tb * 512:(tb + 1) * 512].rearrange("(kc p) t -> p kc t", p=128)))
            S.dma("act", yy[:], ygL.r(ygL.h[:, tb * 512:(tb + 1) * 512].rearrange("(kc p) t -> p kc t", p=128)))
            for n in range(2):
                p = ps[it % 2]
                for kc in range(4):
                    S.mm(p[:, :], w[:, kc, n * 128:(n + 1) * 128], y[:, kc, :], start=(kc == 0), stop=(kc == 3))
                S.act(sg[it % 2][:], p[:, :], AF.Sigmoid)
                S.tt("dve", ob[it % 2][:], yy[:, n, :], sg[it % 2][:], ALU.mult)
                S.dma("pool", moT.r(moT.h[n * 128:(n + 1) * 128, tb * 512:(tb + 1) * 512]), ob[it % 2][:])
                it += 1


def phase_B0(S, C, I, T, xin, R):
    TB = 256
    NCH = TB // 64
    wv_ = {n: I[n].h[0].rearrange("(kc p) n -> p kc n", p=128) for n in ("od_w1", "od_a1", "od_g1")}
    for n in ("od_w_r", "od_w_k", "od_w_v"):
        wv_[n] = I[n + "_h"].h.rearrange("(kc p) n -> p kc n", p=128)
    with Phase(S):
        C.ps_ss = S.ps([128, 512], F32, "ss")
        psA_ = [S.ps([128, 512], F32, "psA%d" % i) for i in range(4)]
        psT_ = [S.ps([128, 512], F32, "psT%d" % i) for i in range(2)]

        class _V:
            def __init__(self, t, w):
                self.t, self.w = t, w

            def __getitem__(self, idx):
                return Ref(self.t, self.t.h[:, 0:self.w][idx], None)
        psA = [_V(t, TB) for t in psA_]
        psT = [_V(t, 128) for t in psT_]
        wst = [S.sb([128, 8, 128], F32, "wst%d" % i) for i in range(2)]
        W3 = [S.sb([128, 8, 512], BF16, "W%d" % i) for i in range(3)]
        wi = 0
        for m, nm in enumerate(("od_w_r", "od_w_k", "od_w_v")):
            for n in range(4):
                S.dma("sp" if wi % 2 == 0 else "act", wst[wi % 2][:], Ref(I["_w"], wv_[nm][:, :, n * 128:(n + 1) * 128], None))
                S.copy("pool", W3[m][:, :, n * 128:(n + 1) * 128], wst[wi % 2][:])
                wi += 1
        L1 = S.sb([128, 8, 256], BF16, "L1")
        for nm, lo, wd in (("od_w1", 0, 64), ("od_a1", 64, 64), ("od_g1", 128, 128)):
            S.dma("sp", wst[wi % 2][:, :, 0:wd], Ref(I["_w"], wv_[nm], None))
            S.copy("pool", L1[:, :, lo:lo + wd], wst[wi % 2][:, :, 0:wd])
            wi += 1
        L2st = S.sb([128, 3, 512], F32, "L2st")
        S.memset("pool", L2st[:], 0.0)
        S.dma("sp", L2st[0:64, 0, :], I["od_w2_h"][:])
        S.dma("sp", L2st[0:64, 1, :], I["od_a2_h"][:])
        S.dma("sp", L2st[:, 2, :], I["od_g2_h"][:])
        L2 = S.sb([128, 3, 512], BF16, "L2")
        S.copy("pool", L2[:], L2st[:])
        pv = S.sb([128, 7, 4], F32, "pv")
        S.dma("sp", pv[:, 0:5, :], I["od_vecs_pk"][:])
        S.ts("dve", pv[:, 5, :], pv[:, 3, :], -1.0, 1.0, ALU.mult, ALU.add)
        S.ts("dve", pv[:, 6, :], pv[:, 0, :], -1.0, None, ALU.mult)
        mu = S.sb([128, 48], F32, "mu")
        S.dma("sp", mu[:], I["od_mu_pk"][:])
        ident = S.sb([128, 128], F32, "ident")
        S.dma("sp", ident[:], I["ident"][:])
        rmask = S.sb([128, TB], F32, "rmask")
        S.memset("pool", rmask[:], 1.0)
        for c in range(NCH):
            S.memset("pool", rmask[:, c * 64:c * 64 + 1], 0.0)
        x = S.sb([128, 8, TB], F32, "x")
        sq = S.sb([128, 8, TB], BF16, "sq")
        hs = S.sb([128, 8, TB + 1], F32, "hs")
        S.memset("pool", hs[:], 0.0)
        hh = S.sb([128, 8, TB], F32, "hh")
        dx = S.sb([128, 8, TB], F32, "dx")
        xm = [S.sb([128, 8, TB], BF16, "xm%d" % i) for i in range(6)]
        rstd = S.sb([128, TB], F32, "rstd")
        tmp = [S.sb([128, TB], F32, "tmp%d" % i) for i in range(2)]
        lo1 = S.sb([128, 3, TB], BF16, "lo1")
        S.memset("pool", lo1[:], 0.0)
        Es = [{k: S.sb([128, TB], F32, k + str(i)) for k in ("r", "k", "v", "lw", "cum", "ep", "em", "ex", "a", "kk", "t0", "t1", "kmod", "b", "o0", "o1", "o2")} for i in range(2)]
        tq = [S.sb([128, 128], F32, "tq%d" % i) for i in range(2)]
        gls = [S.sb([128, NCH], F32, "gl%d" % i) for i in range(2)]
        pi = 0
        ti = 0
        for t0 in range(0, T, TB):
            S.dma("sp", x[:], xin.r(xin.h[:, t0:t0 + TB].rearrange("(kc p) t -> p kc t", p=128)))
            for kc in range(8):
                S.act(sq[:, kc, :], x[:, kc, :], AF.Square)
            for kc in range(8):
                S.mm(C.ps_ss[:, 0:TB], C.ones_bf[:], sq[:, kc, :], start=(kc == 0), stop=(kc == 7))
            rsqrt_to(S, rstd[:], C.ps_ss[:, 0:TB], 1.0 / D, EPS)
            for kc in range(8):
                S.tt("dve", tmp[kc % 2][:], x[:, kc, :], rstd[:], ALU.mult)
                S.act(hh[:, kc, :], tmp[kc % 2][:], AF.Identity, bias=modcol(C, 1, 0, kc), scale=C.gsm[:, 8 + kc:9 + kc])
            S.copy("pool", hs[:, :, 1:TB + 1], hh[:])
            S.tt("dve", dx[:], hs[:, :, 0:TB], hh[:], ALU.subtract)
            S.copy("pool", hs[:, :, 0:1], hh[:, :, TB - 1:TB])
            for m in range(6):
                for kc in range(8):
                    S.stt("dve", xm[m][:, kc, :], dx[:, kc, :],
                          mu[:, m * 8 + kc:m * 8 + kc + 1], hh[:, kc, :], ALU.mult, ALU.add)
            for j, (mx, lo, wd, fn) in enumerate(((1, 0, 64, AF.Tanh), (4, 64, 64, AF.Identity), (5, 128, 128, AF.Sigmoid))):
                p = psA[pi % 4]
                pi += 1
                for kc in range(8):
                    S.mm(p[0:wd, :], L1[:, kc, lo:lo + wd], xm[mx][:, kc, :], start=(kc == 0), stop=(kc == 7))
                S.act(lo1[0:wd, j, :], p[0:wd, :], fn)
            for n in range(4):
                f0 = n * 128
                E = Es[n % 2]
                gl = gls[n % 2]
                pr, pk_, pv_, pw = [psA[(pi + i) % 4] for i in range(4)]
                for (p, m, mx) in ((pr, 0, 0), (pk_, 1, 2), (pv_, 2, 3)):
                    for kc in range(8):
                        S.mm(p[:, :], W3[m][:, kc, f0:f0 + 128], xm[mx][:, kc, :], start=(kc == 0), stop=(kc == 7))
                S.copy("act", E["r"][:], pr[:, :])
                S.copy("act", E["k"][:], pk_[:, :])
                S.copy("act", E["v"][:], pv_[:, :])
                S.mm(pw[:, :], L2[:, 0, f0:f0 + 128], lo1[:, 0, :])
                S.act(E["t0"][:], pw[:, :], AF.Exp, bias=pv[:, 6, n:n + 1], scale=-1.0)
                S.ts("dve", E["t0"][:], E["t0"][:], 1.0, None, ALU.add)
                S.act(E["t0"][:], E["t0"][:], AF.Ln)
                S.ts("dve", E["t0"][:], E["t0"][:], -1.0, -0.5, ALU.mult, ALU.add)
                S.act(E["t0"][:], E["t0"][:], AF.Exp)
                S.ts("dve", E["lw"][:], E["t0"][:], -1.0, None, ALU.mult)
                S.scan("dve", E["cum"][:], rmask[:], E["lw"][:], 0.0)
                S.act(E["ep"][:], E["cum"][:], AF.Exp)
                S.act(E["em"][:], E["cum"][:], AF.Exp, scale=-1.0)
                S.tt("pool", E["t1"][:], E["cum"][:], E["lw"][:], ALU.subtract)
                S.act(E["ex"][:], E["t1"][:], AF.Exp)
                S.mm(pw[:, :], L2[:, 1, f0:f0 + 128], lo1[:, 1, :])
                S.act(E["a"][:], pw[:, :], AF.Sigmoid, bias=pv[:, 1, n:n + 1])
                S.ts("dve", E["kk"][:], E["k"][:], pv[:, 2, n:n + 1], None, ALU.mult)
                S.tt("pool", E["t0"][:], E["kk"][:], E["kk"][:], ALU.mult)
                S.mm(C.ps_ss[:, 0:TB], C.blk_f[:], E["t0"][:])
                S.act(E["t0"][:], C.ps_ss[:, 0:TB], AF.Sqrt)
                S.ts("dve", E["t0"][:], E["t0"][:], 1e-12, None, ALU.max)
                S.op("dve", lambda g: g.reciprocal(E["t0"].h[:], E["t0"].h[:]), [E["t0"][:]], [E["t0"][:]])
                S.tt("dve", E["kk"][:], E["kk"][:], E["t0"][:], ALU.mult)
                S.ts("dve", E["t1"][:], E["a"][:], pv[:, 3, n:n + 1], pv[:, 5, n:n + 1], ALU.mult, ALU.add)
                S.tt("dve", E["kmod"][:], E["k"][:], E["t1"][:], ALU.mult)
                S.tt("pool", E["b"][:], E["kk"][:], E["a"][:], ALU.mult)
                S.mm(pw[:, :], L2[:, 2, f0:f0 + 128], lo1[:, 2, :])
                S.copy("act", E["o2"][:], pw[:, :])
                S.dma("pool", R["GG"].r(R["GG"].h[f0:f0 + 128, t0:t0 + TB]), E["o2"][:])
                pi += 4
                S.tt("dve", E["t0"][:], E["r"][:], E["kmod"][:], ALU.mult)
                S.ts("dve", E["t0"][:], E["t0"][:], pv[:, 4, n:n + 1], None, ALU.mult)
                S.mm(C.ps_ss[:, 0:TB], C.blk_f[:], E["t0"][:])
                S.tt("dve", E["o0"][:], C.ps_ss[:, 0:TB], E["v"][:], ALU.mult)
                S.dma("pool", R["BON"].r(R["BON"].h[f0:f0 + 128, t0:t0 + TB]), E["o0"][:])
                S.tt("dve", E["o1"][:], E["r"][:], E["ep"][:], ALU.mult)
                S.dma("pool", R["RR"].r(R["RR"].h[f0:f0 + 128, t0:t0 + TB]), E["o1"][:])
                S.stt("dve", E["o0"][:], E["kk"][:], -1.0, E["ex"][:], ALU.mult, ALU.mult)
                S.dma("pool", R["AR"].r(R["AR"].h[f0:f0 + 128, t0:t0 + TB]), E["o0"][:])
                S.tt("dve", E["kmod"][:], E["kmod"][:], E["em"][:], ALU.mult)
                S.dma("pool", R["KB"].r(R["KB"].h[f0:f0 + 128, t0:t0 + TB]), E["kmod"][:])
                S.tt("dve", E["b"][:], E["b"][:], E["em"][:], ALU.mult)
                S.dma("pool", R["BB"].r(R["BB"].h[f0:f0 + 128, t0:t0 + TB]), E["b"][:])
                S.copy("dve", gl[:], E["ep"].r(E["ep"].h[:, 63:TB:64]))
                S.dma("pool", R["GL"].r(R["GL"].h[f0:f0 + 128, t0 // 64:t0 // 64 + NCH]), gl[:])
                elb = gl.r(gl.h[:, :].unsqueeze(2).to_broadcast([128, NCH, 64]))
                S.tt("dve", E["kmod"].r(E["kmod"].h[:, :].rearrange("p (c t) -> p c t", t=64)),
                     E["kmod"].r(E["kmod"].h[:, :].rearrange("p (c t) -> p c t", t=64)), elb, ALU.mult)
                S.tt("dve", E["b"].r(E["b"].h[:, :].rearrange("p (c t) -> p c t", t=64)),
                     E["b"].r(E["b"].h[:, :].rearrange("p (c t) -> p c t", t=64)), elb, ALU.mult)
                for (src, dst) in ((E["kmod"], R["KT"]), (E["b"], R["BT"]), (E["v"], R["VT"])):
                    for s in range(TB // 128):
                        pt_ = psT[ti % 2]
                        q_ = tq[ti % 2]
                        ti += 1
                        S.transpose(pt_[:, :], src[:, s * 128:(s + 1) * 128], ident[:])
                        S.copy("act", q_[:], pt_[:, :])
                        S.dma("sp", dst.r(dst.h[t0 + s * 128:t0 + (s + 1) * 128, f0:f0 + 128]), q_[:])


def phase_B1(S, C, I, T, R, moT):
    SC = 256
    NCS = SC // 64
    NCH = T // 64
    GN_EPS = 64e-5
    F32R = mybir.dt.float32r
    with Phase(S):
        banks = [S.ps([64, 8, 64], F32, "bk%d" % i) for i in range(7)]
        psE = S.ps([64, 512], F32, "psE")
        bstate = [0]

        def mmr(out, lhsT, rhs, start=True, stop=True):
            return S.mm(out, lhsT, rhs, start=start, stop=stop)

        def bank():
            b = banks[bstate[0] % 7]
            bstate[0] += 1
            return b
        cm = S.sb([64, 4, 8, 64], F32, "cmask")
        S.dma("sp", cm[:], I["rw_masks"][:])
        MS, MI, MST, I8 = [cm.r(cm.h[:, i]) for i in range(4)]
        lng = S.sb([64, 2, 8], F32, "lng")
        S.dma("sp", lng[:], I["od_ln_pk64"][:])
        ones64 = C.ones_f[0:64, 0:64]
        fm = {k: [S.sb([64, 8, SC], F32R, "%s%d" % (k, i)) for i in range(2)] for k in ("AR", "RR", "KB", "BB")}
        tm = {k: [S.sb([64, NCS, 512], F32R, "%s%d" % (k, i)) for i in range(2)] for k in ("KT", "BT", "VT")}
        ep = {k: S.sb([64, 8, SC], F32, k) for k in ("BON", "GG")}
        GLt = S.sb([64, 8, NCH], F32, "GLt")
        ST = S.sb([64, 8, 64], F32R, "ST")
        Y = S.sb([64, 8, SC], F32, "Y")
        Yc = S.sb([64, 8, SC], F32, "Yc")
        Ysq = S.sb([64, 8, SC], F32, "Ysq")
        rstd = S.sb([64, 512], F32, "rstd")
        ob = S.sb([64, 8, SC], BF16, "ob")
        A = {k: S.sb([64, 8, 64], F32R, k) for k in ("N", "NT", "ak", "kr", "br", "Tm", "X", "UT")}
        Mp = [S.sb([64, 8, 64], F32R, "M%d" % i) for i in range(2)]
        MTp = [S.sb([64, 8, 64], F32R, "MT%d" % i) for i in range(2)]
        for g in range(1):
            rows = slice(g * 512, (g + 1) * 512)
            S.dma("sp", GLt[:], R["GL"].r(R["GL"].h[rows, :].rearrange("(h i) c -> i h c", i=64)))
            S.ts("dve", ST[:], I8, 0.0, None, ALU.mult)
            for si, t0 in enumerate(range(0, T, SC)):
                F = {}
                for k in fm:
                    F[k] = fm[k][si % 2]
                    S.dma("sp" if k in ("AR", "KB") else "act", F[k][:],
                          R[k].r(R[k].h[rows, t0:t0 + SC].rearrange("(h i) t -> i h t", i=64).bitcast(F32R)))
                for k in tm:
                    F[k] = tm[k][si % 2]
                    S.dma("sp", F[k][:], R[k].r(R[k].h[t0:t0 + SC, rows].rearrange("(c s) f -> s c f", s=64).bitcast(F32R)))
                for k in ep:
                    S.dma("act", ep[k][:], R[k].r(R[k].h[rows, t0:t0 + SC].rearrange("(h i) t -> i h t", i=64)))
                for c in range(NCS):
                    cs = slice(c * 64, (c + 1) * 64)
                    cg = t0 // 64 + c

                    def hv(name, h):
                        return F[name][:, c, h * 64:(h + 1) * 64]
                    for (dst, l, r_, msk) in ((A["N"], "BB", "AR", MS), (A["NT"], "AR", "BB", MST),
                                              (A["ak"], "KB", "AR", MS), (A["kr"], "KB", "RR", MI),
                                              (A["br"], "BB", "RR", MI)):
                        p = bank()
                        for h in range(8):
                            mmr(p[:, h, :], F[l][:, h, cs], F[r_][:, h, cs])
                        S.tt("dve", dst[:], p[:], msk, ALU.mult)
                    S.tt("dve", A["Tm"][:], A["N"][:], I8, ALU.add)
                    M, MT = A["N"], A["NT"]
                    for lv in range(5):
                        p1, p2 = bank(), bank()
                        for h in range(8):
                            mmr(p1[:, h, :], MT[:, h, :], M[:, h, :])
                            mmr(p2[:, h, :], M[:, h, :], MT[:, h, :])
                        M2, MT2 = Mp[lv % 2], MTp[lv % 2]
                        S.copy("act", M2[:], p1[:])
                        S.copy("dve", MT2[:], p2[:])
                        p3 = bank()
                        for h in range(8):
                            mmr(p3[:, h, :], MT2[:, h, :], A["Tm"][:, h, :])
                        S.tt("dve", A["Tm"][:], A["Tm"][:], p3[:], ALU.add)
                        M, MT = M2, MT2
                    px = bank()
                    for h in range(8):
                        mmr(px[:, h, :], F["AR"][:, h, cs], ST[:, h, :], start=True, stop=False)
                        mmr(px[:, h, :], A["ak"][:, h, :], hv("VT", h), start=False, stop=True)
                    S.copy("act", A["X"][:], px[:])
                    pu = bank()
                    for h in range(8):
                        mmr(pu[:, h, :], A["Tm"][:, h, :], A["X"][:, h, :])
                    S.copy("act", A["UT"][:], pu[:])
                    py = bank()
                    for h in range(8):
                        mmr(py[:, h, :], ST[:, h, :], F["RR"][:, h, cs], start=True, stop=False)
                        mmr(py[:, h, :], hv("VT", h), A["kr"][:, h, :], start=False, stop=False)
                        mmr(py[:, h, :], A["UT"][:, h, :], A["br"][:, h, :], start=False, stop=True)
                    S.copy("act", Y[:, :, cs], py[:])
                    pst = bank()
                    for h in range(8):
                        mmr(pst[:, h, :], hv("KT", h), hv("VT", h), start=True, stop=False)
                        mmr(pst[:, h, :], hv("BT", h), A["UT"][:, h, :], start=False, stop=True)
                    S.tt("dve", ST[:], ST[:], GLt.r(GLt.h[:, :, cg:cg + 1].to_broadcast([64, 8, 64])), ALU.mult)
                    S.tt("dve", ST[:], ST[:], pst[:], ALU.add)
                for q4 in range(8 * SC // 512):
                    hs_ = slice(q4 * (512 // SC), (q4 + 1) * (512 // SC))
                    S.mm(psE[:, :], ones64, Y[:, hs_, :])
                    S.stt("dve", Yc[:, hs_, :], psE[:, :], -1.0 / 64, Y[:, hs_, :], ALU.mult, ALU.add)
                    S.act(Ysq[:, hs_, :], Yc[:, hs_, :], AF.Square)
                    S.mm(psE[:, :], ones64, Ysq[:, hs_, :])
                    rsqrt_to(S, rstd[:], psE[:, :], 1.0 / 64, GN_EPS)
                    S.tt("dve", Yc[:, hs_, :], Yc[:, hs_, :], rstd[:], ALU.mult)
                for h in range(8):
                    hg = g * 8 + h
                    S.act(Yc[:, h, :], Yc[:, h, :], AF.Identity, bias=lng[:, 1, hg:hg + 1], scale=lng[:, 0, hg:hg + 1])
                S.tt("pool", Yc[:], Yc[:], ep["BON"][:], ALU.add)
                S.tt("dve", ob[:], Yc[:], ep["GG"][:], ALU.mult)
                S.dma("pool", moT.r(moT.h[rows, t0:t0 + SC].rearrange("(h i) t -> i h t", i=64)), ob[:])


_NC_CACHE = {}


def kernel(**inputs):
    T = 4096
    if "nc" not in _NC_CACHE:
        _NC_CACHE["nc"] = build(T, "full")
    nc = _NC_CACHE["nc"]
    maps = [host_inputs(inputs, c % 4, c // 4, T) for c in range(8)]
    res = run_bass_kernel_spmd(nc, maps, core_ids=list(range(8)))
    out = np.stack([np.ascontiguousarray(np.asarray(res.results[b]["yT"]).T) for b in range(4)], axis=0)
    return out.astype(np.float32)
```

```python
import contextlib
import math
import numpy as np
import concourse.bass as bass
import concourse.mybir as mybir
from concourse.bass_utils import run_bass_kernel_spmd

F32 = mybir.dt.float32
BF16 = mybir.dt.bfloat16
I32 = mybir.dt.int32
AF = mybir.ActivationFunctionType
ALU = mybir.AluOpType

D = 1024
KC = 8
FFN = 2816
HC = 22
EPS = 1e-6


class _State:
    __slots__ = ("w", "r")

    def __init__(self):
        self.w = None
        self.r = {}


class Tile:
    def __init__(self, h, name):
        self.h = h
        self.name = name
        self.st = {None: _State()}

    def __getitem__(self, idx):
        return Ref(self, self.h[idx], None)

    def sub(self, key, ap):
        return Ref(self, ap, key)

    def r(self, ap):
        return Ref(self, ap, None)

    def states(self, key):
        if key is None:
            return list(self.st.values())
        if key not in self.st:
            self.st[key] = _State()
        return [self.st[key], self.st[None]]

    def wstate(self, key):
        if key not in self.st:
            self.st[key] = _State()
        return self.st[key]


class Ref:
    __slots__ = ("t", "ap", "key")

    def __init__(self, t, ap, key):
        self.t, self.ap, self.key = t, ap, key


class Sched:
    NDMA = 32

    def __init__(self, nc):
        self.nc = nc
        self.es = contextlib.ExitStack()
        self.eng = {"pe": nc.tensor, "act": nc.scalar, "dve": nc.vector,
                    "pool": nc.gpsimd, "sp": nc.sync}
        self.root_es = self.es
        self.semmap = {}
        self.ekey = {}
        self.gen = 0
        self.cnt = {}
        self.known = {}
        for e in self.eng:
            self.known[e] = {}
        self.new_engine_sems()
        self.lastclk = {e: {} for e in self.eng}
        self.dsem = [self.es.enter_context(nc.semaphore("d%d" % i)) for i in range(self.NDMA)]
        self.dtarget = [0] * self.NDMA
        self.dclock = [None] * self.NDMA
        self.dnext = 0
        self.ntile = 0
        self.nwait = 0
        self.ninstr = 0

    def new_engine_sems(self):
        self.gen += 1
        for e in self.eng:
            key = "%s#%d" % (e, self.gen)
            self.ekey[e] = key
            self.semmap[key] = self.root_es.enter_context(self.nc.semaphore("s_%s_%d" % (e, self.gen)))
            self.cnt[e] = 0

    def sb(self, shape, dt, name=None):
        self.ntile += 1
        name = "%s_%d" % (name or "t", self.ntile)
        h = self.es.enter_context(self.nc.sbuf_tensor(name, list(shape), dt))
        return Tile(h, name)

    def ps(self, shape, dt=F32, name=None):
        self.ntile += 1
        name = "%s_%d" % (name or "p", self.ntile)
        h = self.es.enter_context(self.nc.psum_tensor(name, list(shape), dt))
        return Tile(h, name)

    def dram(self, shape, dt, name, kind="Internal"):
        h = self.nc.dram_tensor(name, list(shape), dt, kind=kind).ap()
        return Tile(h, name)

    def _semobj(self, key):
        return self.semmap[key] if isinstance(key, str) else self.dsem[key]

    def _wait(self, e, ev):
        if ev is None:
            return
        key, val, clock = ev
        kn = self.known[e]
        if kn.get(key, 0) >= val:
            return
        self.eng[e].wait_ge(self._semobj(key), val)
        self.nwait += 1
        new = dict(kn)
        if clock:
            for k2, v2 in clock.items():
                if new.get(k2, 0) < v2:
                    new[k2] = v2
        new[key] = val
        self.known[e] = new

    def _deps(self, e, reads, writes, is_dma):
        for rf in reads:
            for st in rf.t.states(rf.key):
                if st.w is not None:
                    self._wait(e, st.w)
        for rf in writes:
            for st in rf.t.states(rf.key):
                if st.w is not None and (is_dma or st.w[0] != self.ekey[e]):
                    self._wait(e, st.w)
                for k, (v, c) in st.r.items():
                    if is_dma or k != self.ekey[e]:
                        self._wait(e, (k, v, c))

    def _record(self, ev, reads, writes):
        key, val, clock = ev
        for rf in reads:
            st = rf.t.wstate(rf.key)
            st.r[key] = (val, clock)
        for rf in writes:
            if rf.key is None:
                for k in list(rf.t.st.keys()):
                    if k is not None:
                        del rf.t.st[k]
            st = rf.t.wstate(rf.key)
            st.w = ev
            st.r = {}

    def op(self, e, fn, reads, writes):
        reads = [r for r in reads if isinstance(r, Ref)]
        self._deps(e, reads, writes, False)
        ins = fn(self.eng[e])
        self.cnt[e] += 1
        ins.then_inc(self.semmap[self.ekey[e]], 1)
        ev = (self.ekey[e], self.cnt[e], self.known[e])
        self.lastclk[e] = self.known[e]
        self._record(ev, reads, writes)
        self.ninstr += 1
        return ev

    def dma(self, q, out, in_, **kw):
        i = self.dnext
        self.dnext = (self.dnext + 1) % self.NDMA
        if self.dtarget[i] > 0:
            self._wait(q, (i, self.dtarget[i], self.dclock[i]))
        self._deps(q, [in_], [out], True)
        ins = self.eng[q].dma_start(out=out.ap, in_=in_.ap, **kw)
        ins.then_inc(self.dsem[i], 16)
        self.dtarget[i] += 16
        self.dclock[i] = self.known[q]
        ev = (i, self.dtarget[i], self.known[q])
        self._record(ev, [in_], [out])
        self.ninstr += 1
        return ev

    def collective(self, kind, alu, groups, in_, out):
        sem = self.root_es.enter_context(self.nc.semaphore("cc%d" % len(self.dsem)))
        self.dsem.append(sem)
        self.dtarget.append(0)
        self.dclock.append(None)
        i = len(self.dsem) - 1
        self._deps("pool", [in_], [out], True)
        ins = self.eng["pool"].collective_compute(kind, alu, replica_groups=groups,
                                                  ins=[in_.ap.opt()], outs=[out.ap.opt()])
        ins.then_inc(sem, 1)
        self.dtarget[i] = 1
        self.dclock[i] = self.known["pool"]
        ev = (i, 1, self.known["pool"])
        self._record(ev, [in_], [out])
        self.ninstr += 1
        return ev

    def wait_all(self, e, refs):
        for rf in refs:
            for st in rf.t.states(rf.key):
                self._wait(e, st.w)

    def close(self):
        self.es.close()
        if self.root_es is not self.es:
            self.root_es.close()

    def mm(self, out, lhsT, rhs, start=True, stop=True):
        return self.op("pe", lambda g: g.matmul(out.ap, lhsT.ap, rhs.ap, start=start, stop=stop),
                       [lhsT, rhs], [out])

    def transpose(self, out, in_, ident):
        return self.op("pe", lambda g: g.transpose(out.ap, in_.ap, ident.ap), [in_, ident], [out])

    def act(self, out, in_, func, bias=None, scale=None, e="act"):
        kw = {}
        if bias is not None:
            kw["bias"] = bias.ap if isinstance(bias, Ref) else bias
        if scale is not None:
            kw["scale"] = scale.ap if isinstance(scale, Ref) else scale
        return self.op(e, lambda g: g.activation(out.ap, in_.ap, func, **kw),
                       [in_, bias, scale], [out])

    def tt(self, e, out, in0, in1, op):
        return self.op(e, lambda g: g.tensor_tensor(out.ap, in0.ap, in1.ap, op), [in0, in1], [out])

    def ts(self, e, out, in0, s1, s2, op0, op1=None):
        a1 = s1.ap if isinstance(s1, Ref) else s1
        a2 = s2.ap if isinstance(s2, Ref) else s2
        if op1 is None:
            return self.op(e, lambda g: g.tensor_scalar(out.ap, in0.ap, a1, None, op0), [in0, s1], [out])
        return self.op(e, lambda g: g.tensor_scalar(out.ap, in0.ap, a1, a2, op0, op1), [in0, s1, s2], [out])

    def stt(self, e, out, in0, sc, in1, op0, op1):
        a = sc.ap if isinstance(sc, Ref) else sc
        return self.op(e, lambda g: g.scalar_tensor_tensor(out.ap, in0.ap, a, in1.ap, op0, op1),
                       [in0, sc, in1], [out])

    def scan(self, e, out, d0, d1, init, op0=ALU.mult, op1=ALU.add):
        a = init.ap if isinstance(init, Ref) else init
        return self.op(e, lambda g: g.tensor_tensor_scan(out.ap, d0.ap, d1.ap, a, op0, op1),
                       [d0, d1, init], [out])

    def copy(self, e, out, in_):
        if e == "act":
            return self.op(e, lambda g: g.copy(out.ap, in_.ap), [in_], [out])
        return self.op(e, lambda g: g.tensor_copy(out.ap, in_.ap), [in_], [out])

    def memset(self, e, out, val):
        return self.op(e, lambda g: g.memset(out.ap, val), [], [out])


def _barrier(S):
    for e in S.eng:
        for f in S.eng:
            if f != e and S.cnt[f] > 0:
                S._wait(e, (S.ekey[f], S.cnt[f], S.lastclk[f]))
        for i in range(len(S.dsem)):
            if S.dtarget[i] > 0:
                S._wait(e, (i, S.dtarget[i], S.dclock[i]))


class Phase:
    def __init__(self, S):
        self.S = S

    def __enter__(self):
        self.saved = self.S.es
        self.S.es = contextlib.ExitStack()
        return self

    def __exit__(self, *a):
        _barrier(self.S)
        self.S.es.close()
        self.S.es = self.saved
        if max(self.S.cnt.values()) > 16000:
            self.S.new_engine_sems()
        return False


def pk(v):
    v = np.asarray(v, dtype=np.float32)
    lead = v.shape[:-1]
    n = v.shape[-1] // 128
    v = v.reshape(lead + (n, 128))
    v = np.moveaxis(v, -1, 0)
    return np.ascontiguousarray(v.reshape(128, -1))


class Ctx:
    pass


def load_cast(S, C, dram_ref, shape, q="sp", ceng="pool", name="w"):
    st = S.sb(shape, F32, name + "_st")
    S.dma(q, st[:], dram_ref)
    wt = S.sb(shape, BF16, name + "_bf")
    S.copy(ceng, wt[:], st[:])
    return wt


def setup_consts(S, C):
    C.ones_bf = S.sb([128, 128], BF16, "ones")
    S.memset("pool", C.ones_bf[:], 1.0)
    C.ones_f = S.sb([128, 128], F32, "onesf")
    S.memset("pool", C.ones_f[:], 1.0)
    C.blk_f = S.sb([128, 128], F32, "blkf")
    S.memset("pool", C.blk_f[:], 0.0)
    S.memset("pool", C.blk_f[0:64, 0:64], 1.0)
    S.memset("pool", C.blk_f[64:128, 64:128], 1.0)
    C.blk_bf = S.sb([128, 128], BF16, "blkbf")
    S.copy("pool", C.blk_bf[:], C.blk_f[:])


def setup_adaln(S, C, I):
    C.mod = S.sb([128, 96], F32, "mod")
    C.gsm = S.sb([128, 16], F32, "gsm")
    C.gsf = S.sb([128, 16], F32, "gsf")
    with Phase(S):
        cact = S.sb([128, 8], F32, "cact")
        S.dma("sp", cact[:], I["c_pk"][:])
        S.act(cact[:], cact[:], AF.Silu)
        bada = S.sb([128, 96], F32, "bada")
        S.dma("sp", bada[:], I["b_ada_pk"][:])
        nm = S.sb([128, 16], F32, "nm")
        S.dma("sp", nm[:], I["norm_mix_pk"][:])
        nf = S.sb([128, 16], F32, "nf")
        S.dma("sp", nf[:], I["norm_ffn_pk"][:])
        pm = S.ps([128, 96], F32, "pmod")
        wt = [S.sb([128, 8, 512], F32, "wada%d" % i) for i in range(2)]
        it = 0
        for l in range(2):
            wl = I["w_ada"].h[l].rearrange("(kc p) n -> p kc n", p=128)
            for ng in range(12):
                w = wt[it % 2]
                it += 1
                S.dma("sp" if it % 2 else "act", w[:], I["w_ada"].r(wl[:, :, ng * 512:(ng + 1) * 512]))
                for j in range(4):
                    col = l * 48 + ng * 4 + j
                    for kc in range(8):
                        S.mm(pm[:, col:col + 1], w[:, kc, j * 128:(j + 1) * 128], cact[:, kc:kc + 1],
                             start=(kc == 0), stop=(kc == 7))
        S.tt("dve", C.mod[:], pm[:], bada[:], ALU.add)
        for l in range(2):
            S.stt("dve", C.gsm[:, l * 8:(l + 1) * 8], C.mod[:, l * 48 + 8:l * 48 + 16], 1.0,
                  nm[:, l * 8:(l + 1) * 8], ALU.add, ALU.mult)
            S.stt("dve", C.gsf[:, l * 8:(l + 1) * 8], C.mod[:, l * 48 + 32:l * 48 + 40], 1.0,
                  nf[:, l * 8:(l + 1) * 8], ALU.add, ALU.mult)


def rsqrt_to(S, out, in_, scale, eps):
    S.ts("dve", out, in_, scale, eps, ALU.mult, ALU.add)
    S.act(out, out, AF.Sqrt)
    S.op("dve", lambda g: g.reciprocal(out.ap, out.ap), [out], [out])


def modcol(C, l, part, kc):
    c = l * 48 + part * 8 + kc
    return C.mod[:, c:c + 1]


def rmsnorm_mod(S, C, x, h, gs, l, shift_part, W, tmp, sq, rstd):
    for kc in range(8):
        S.act(sq[:, kc, :], x[:, kc, :], AF.Square)
    for s0 in range(0, W, 512):
        ss = C.ps_ss
        for kc in range(8):
            S.mm(ss[:, :], C.ones_bf[:], sq[:, kc, s0:s0 + 512], start=(kc == 0), stop=(kc == 7))
        rsqrt_to(S, rstd[:, s0:s0 + 512], ss[:, :], 1.0 / D, EPS)
    for kc in range(8):
        e = "dve" if kc % 2 == 0 else "pool"
        S.tt(e, tmp[kc % 2][:, :], x[:, kc, :], rstd[:, :], ALU.mult)
        S.act(h[:, kc, :], tmp[kc % 2][:, :], AF.Identity, bias=modcol(C, l, shift_part, kc),
              scale=gs[:, l * 8 + kc:l * 8 + kc + 1])


def phase_F(S, C, I, l, T, mo, xin, xmid, xout, wo, PD, PDS, groups, TBS=1024):
    HCL = HC // 2
    wg = I["ffn_w_gate_h"].h[l].rearrange("(kc p) n -> p kc n", p=128)
    wu = I["ffn_w_up_h"].h[l].rearrange("(kc p) n -> p kc n", p=128)
    wd = I["ffn_w_down_h"].h[l].rearrange("(j p) n -> p j n", p=128)
    wov = wo.rearrange("(kc p) n -> p kc n", p=128)
    NS = TBS // 512
    with Phase(S):
        C.ps_ss = S.ps([128, 512], F32, "ss")
        psA = [S.ps([128, 512], F32, "psA%d" % i) for i in range(3)]
        psB = [S.ps([128, 512], F32, "psB%d" % i) for i in range(3)]
        x = S.sb([128, 8, TBS], F32, "x")
        mot = S.sb([128, 8, TBS], BF16, "mo")
        sq = S.sb([128, 8, TBS], BF16, "sq")
        h = S.sb([128, 8, TBS], BF16, "h")
        a = S.sb([128, HCL, TBS], BF16, "a")
        rstd = S.sb([128, TBS], F32, "rstd")
        tmp = [S.sb([128, TBS], F32, "tmp%d" % i) for i in range(2)]
        sg = [S.sb([128, 512], F32, "sg%d" % i) for i in range(2)]
        pdt = [S.sb([128, 512], F32, "pdt%d" % i) for i in range(3)]
        wst = [S.sb([128, 8, 128], F32, "wst%d" % i) for i in range(4)]
        wbf = [S.sb([128, 8, 128], BF16, "wbf%d" % i) for i in range(4)]
        wdst = [S.sb([128, HCL, 128], F32, "wdst%d" % i) for i in range(2)]
        wdbf = [S.sb([128, HCL, 128], BF16, "wdbf%d" % i) for i in range(2)]
        wi = 0
        pi = 0
        di = 0
        for t0 in range(0, T, TBS):
            sb = t0 // TBS
            S.dma("sp", x[:], xin.r(xin.h[:, t0:t0 + TBS].rearrange("(kc p) t -> p kc t", p=128)))
            S.dma("act", mot[:], mo.r(mo.h[:, t0:t0 + TBS].rearrange("(kc p) t -> p kc t", p=128)))
            for n in range(8):
                k = wi % 4
                wi += 1
                S.dma("sp", wst[k][:], Ref(I["_w"], wov[:, :, n * 128:(n + 1) * 128], None))
                S.copy("pool", wbf[k][:], wst[k][:])
                for s in range(NS):
                    p = psA[pi % 3]
                    pi += 1
                    for kc in range(8):
                        S.mm(p[:, :], wbf[k][:, kc, :], mot[:, kc, s * 512:(s + 1) * 512],
                             start=(kc == 0), stop=(kc == 7))
                    S.stt("dve", x[:, n, s * 512:(s + 1) * 512], p[:, :], modcol(C, l, 2, n),
                          x[:, n, s * 512:(s + 1) * 512], ALU.mult, ALU.add)
            S.dma("act", xmid.r(xmid.h[:, t0:t0 + TBS].rearrange("(kc p) t -> p kc t", p=128)), x[:])
            rmsnorm_mod(S, C, x, h, C.gsf, l, 3, TBS, tmp, sq, rstd)
            for j in range(HCL):
                k0 = wi % 4
                k1 = (wi + 1) % 4
                wi += 2
                S.dma("sp", wst[k0][:], Ref(I["_w"], wg[:, :, j * 128:(j + 1) * 128], None))
                S.dma("act", wst[k1][:], Ref(I["_w"], wu[:, :, j * 128:(j + 1) * 128], None))
                S.copy("pool", wbf[k0][:], wst[k0][:])
                S.copy("pool", wbf[k1][:], wst[k1][:])
                for s in range(NS):
                    pg = psA[pi % 3]
                    pu = psB[pi % 3]
                    pi += 1
                    for kc in range(8):
                        S.mm(pg[:, :], wbf[k0][:, kc, :], h[:, kc, s * 512:(s + 1) * 512],
                             start=(kc == 0), stop=(kc == 7))
                    for kc in range(8):
                        S.mm(pu[:, :], wbf[k1][:, kc, :], h[:, kc, s * 512:(s + 1) * 512],
                             start=(kc == 0), stop=(kc == 7))
                    g = sg[pi % 2]
                    S.act(g[:, :], pg[:, :], AF.Silu)
                    S.tt("dve", a[:, j, s * 512:(s + 1) * 512], g[:, :], pu[:, :], ALU.mult)
            for n in range(8):
                k = n % 2
                S.dma("sp" if n % 2 == 0 else "act", wdst[k][:],
                      Ref(I["_w"], wd[:, :, n * 128:(n + 1) * 128], None))
                S.copy("pool", wdbf[k][:], wdst[k][:])
                for s in range(NS):
                    p = psA[pi % 3]
                    pi += 1
                    for j in range(HCL):
                        S.mm(p[:, :], wdbf[k][:, j, :], a[:, j, s * 512:(s + 1) * 512],
                             start=(j == 0), stop=(j == HCL - 1))
                    o = pdt[di % 3]
                    di += 1
                    S.copy("act", o[:], p[:, :])
                    S.dma("pool", PD.sub(sb, PD.h[sb, n * 128:(n + 1) * 128, s * 512:(s + 1) * 512]), o[:])
            for hh_ in range(2):
                S.collective("AllReduce", ALU.add, groups, PD.sub(sb, PD.h[sb, hh_ * 512:(hh_ + 1) * 512, :]),
                             PDS.sub(sb, PDS.h[sb, hh_ * 512:(hh_ + 1) * 512, :]))
    with Phase(S):
        xs = [S.sb([128, 8, 512], F32, "xs%d" % i) for i in range(2)]
        ps_ = [S.sb([128, 8, 512], F32, "pds%d" % i) for i in range(2)]
        for bi, t0 in enumerate(range(0, T, 512)):
            xx, pp = xs[bi % 2], ps_[bi % 2]
            S.dma("sp", xx[:], xmid.r(xmid.h[:, t0:t0 + 512].rearrange("(kc p) t -> p kc t", p=128)))
            sb, so = t0 // TBS, t0 % TBS
            S.dma("act", pp[:], PDS.r(PDS.h[sb, :, so:so + 512].rearrange("(kc p) t -> p kc t", p=128)))
            for n in range(8):
                S.stt("dve", xx[:, n, :], pp[:, n, :], modcol(C, l, 5, n), xx[:, n, :], ALU.mult, ALU.add)
            S.dma("pool", xout.r(xout.h[:, t0:t0 + 512].rearrange("(kc p) t -> p kc t", p=128)), xx[:])


def declare_inputs(nc, specs):
    I = {}
    for name, (shape, dt) in specs.items():
        ap = nc.dram_tensor(name, list(shape), dt, kind="ExternalInput").ap()
        I[name] = Tile(ap, name)
    I["_w"] = Tile(None, "_w")
    return I


def input_specs(T):
    HH = FFN // 2
    sp = {
        "xT": ([D, T], F32), "c_pk": ([128, 8], F32),
        "w_ada": ([2, D, 6 * D], F32), "b_ada_pk": ([128, 96], F32),
        "norm_mix_pk": ([128, 16], F32), "norm_ffn_pk": ([128, 16], F32),
        "ffn_w_gate_h": ([2, D, HH], F32), "ffn_w_up_h": ([2, D, HH], F32),
        "ffn_w_down_h": ([2, HH, D], F32),
        "ev_w_out_p": ([D, D], F32), "od_w_o": ([D, D], F32),
        "ev_w_in_h": ([D, 1024], F32), "ev_s5_w_glu_h": ([512, 256], F32),
        "pos": ([T], I32), "rotm": ([128, 128], F32), "invf_col": ([128, 1], F32),
        "qk_gain_col": ([128, 2], F32), "masks": ([128, 4, 512], BF16),
        "lam_vecs": ([4, 64], F32), "subln_col": ([128, 1], F32),
        "swapm": ([128, 128], F32), "rowmask": ([128, 8], F32), "s5_d_pk": ([128, 2], F32),
        "lamre_row": ([128, 2, 64], F32), "lamim_row": ([128, 2, 64], F32), "logdt_row": ([128, 2], F32),
        "bT_re": ([128, 2, 64], F32), "bT_im": ([128, 2, 64], F32),
        "lamre_col": ([128, 16], F32), "lamim_col": ([128, 16], F32), "logdt_col": ([128, 16], F32),
        "cT_re_col": ([128, 16, 16], F32), "cT_im_col": ([128, 16, 16], F32),
        "od_w_r_h": ([D, 512], F32), "od_w_k_h": ([D, 512], F32), "od_w_v_h": ([D, 512], F32),
        "od_w1": ([1, D, 64], F32), "od_a1": ([1, D, 64], F32), "od_g1": ([1, D, 128], F32),
        "od_w2_h": ([64, 512], F32), "od_a2_h": ([64, 512], F32), "od_g2_h": ([128, 512], F32),
        "od_vecs_pk": ([128, 5, 4], F32), "od_mu_pk": ([128, 48], F32), "ident": ([128, 128], F32),
        "rw_masks": ([64, 4, 8, 64], F32), "od_ln_pk64": ([64, 2, 8], F32),
    }
    return sp


GROUPS = [[0, 4], [1, 5], [2, 6], [3, 7]]


def build(T=4096, mode="full"):
    nc = bass.Bass("TRN2", target_bir_lowering=False)
    I = declare_inputs(nc, input_specs(T))
    out = Tile(nc.dram_tensor("yT", [D, T], F32, kind="ExternalOutput").ap(), "yT")
    S = Sched(nc)
    C = Ctx()
    setup_consts(S, C)
    setup_adaln(S, C, I)
    dr = lambda shape, dt, name: S.dram(shape, dt, name, "Internal")
    uT = dr([256, T], BF16, "uT")
    qT = dr([256, T], BF16, "qT")
    kT = dr([256, T], BF16, "kT")
    vtm = dr([T, 256], BF16, "vtm")
    ygT = dr([256, T], BF16, "ygT")
    ygF = dr([512, T], BF16, "ygF")
    moL = dr([512, T], BF16, "moL")
    mo0 = dr([D, T], BF16, "mo0")
    phase_A0(S, C, I, T, I["xT"], uT, qT, kT, vtm)
    phase_S5(S, C, I, T, uT, ygT)
    S.collective("AllGather", ALU.bypass, GROUPS, ygT[:], ygF[:])
    phase_glu(S, C, I, T, ygF, ygT, moL)
    phase_attn(S, C, I, T, qT, kT, vtm, moL, 0.8 - 0.6 * math.exp(-0.3 * 0))
    for i in range(2):
        S.collective("AllGather", ALU.bypass, GROUPS, moL.r(moL.h[i * 256:(i + 1) * 256, :]),
                     mo0.r(mo0.h[i * 512:(i + 1) * 512, :]))
    xm0 = dr([D, T], F32, "xm0")
    x1 = dr([D, T], F32, "x1T")
    PD0 = dr([T // 1024, D, 1024], F32, "PD0")
    PS0 = dr([T // 1024, D, 1024], F32, "PS0")
    phase_F(S, C, I, 0, T, mo0, I["xT"], xm0, x1, I["ev_w_out_p"].h, PD0, PS0, GROUPS)
    R = {k: dr([512, T], F32, k) for k in ("AR", "RR", "KB", "BB", "BON", "GG")}
    for k in ("KT", "BT", "VT"):
        R[k] = dr([T, 512], F32, k)
    R["GL"] = dr([512, T // 64], F32, "GL")
    mo1L = dr([512, T], BF16, "mo1L")
    mo1 = dr([D, T], BF16, "mo1")
    phase_B0(S, C, I, T, x1, R)
    phase_B1(S, C, I, T, R, mo1L)
    for i in range(2):
        S.collective("AllGather", ALU.bypass, GROUPS, mo1L.r(mo1L.h[i * 256:(i + 1) * 256, :]),
                     mo1.r(mo1.h[i * 512:(i + 1) * 512, :]))
    xm1 = dr([D, T], F32, "xm1")
    PD1 = dr([T // 1024, D, 1024], F32, "PD1")
    PS1 = dr([T // 1024, D, 1024], F32, "PS1")
    phase_F(S, C, I, 1, T, mo1, x1, xm1, out, I["od_w_o"].h, PD1, PS1, GROUPS)
    S.wait_all("sp", [out[:]])
    _barrier(S)
    print("instrs", S.ninstr, "waits", S.nwait)
    S.close()
    return nc


def host_inputs(inp, b, hf, T):
    f = lambda a: np.ascontiguousarray(np.asarray(a, dtype=np.float32))
    HH = FFN // 2
    hs = slice(hf * HH, (hf + 1) * HH)
    q4 = slice(hf * 256, (hf + 1) * 256)
    win = f(inp["ev_w_in"])[0]
    wout = f(inp["ev_w_out"])[0]
    perm = np.concatenate([np.arange(0, 256), np.arange(512, 768), np.arange(256, 512), np.arange(768, 1024)])
    m = {
        "xT": np.ascontiguousarray(f(inp["x"][b, :T]).T),
        "c_pk": pk(f(inp["c"])[b]),
        "w_ada": f(inp["w_ada"]),
        "b_ada_pk": pk(f(inp["b_ada"]).reshape(-1)),
        "norm_mix_pk": pk(f(inp["norm_mix"]).reshape(-1)),
        "norm_ffn_pk": pk(f(inp["norm_ffn"]).reshape(-1)),
        "ffn_w_gate_h": f(f(inp["ffn_w_gate"])[:, :, hs]), "ffn_w_up_h": f(f(inp["ffn_w_up"])[:, :, hs]),
        "ffn_w_down_h": f(f(inp["ffn_w_down"])[:, hs, :]),
        "ev_w_out_p": f(wout), "od_w_o": f(f(inp["od_w_o"])[0][perm, :]),
        "ev_w_in_h": f(np.concatenate([win[:, 0 + hf * 256:0 + (hf + 1) * 256], win[:, 512 + hf * 256:512 + (hf + 1) * 256],
                                       win[:, 1024 + hf * 256:1024 + (hf + 1) * 256], win[:, 1536 + hf * 256:1536 + (hf + 1) * 256]], axis=1)),
        "ev_s5_w_glu_h": f(f(inp["ev_s5_w_glu"])[0][:, q4]),
        "pos": np.ascontiguousarray(np.asarray(inp["positions"])[b, :T].astype(np.int32)),
    }
    m.update(const_inputs())
    qg = f(inp["ev_q_norm"])[0]; kg = f(inp["ev_k_norm"])[0]
    m["qk_gain_col"] = np.ascontiguousarray(np.stack([np.tile(qg, 2), np.tile(kg, 2)], axis=1))
    m["lam_vecs"] = np.ascontiguousarray(np.stack([f(inp["ev_lambda_q1"])[0], f(inp["ev_lambda_k1"])[0],
                                                   f(inp["ev_lambda_q2"])[0], f(inp["ev_lambda_k2"])[0]]))
    m["subln_col"] = np.ascontiguousarray(f(inp["ev_subln"])[0].reshape(128, 1))
    gs = slice(hf * 16, (hf + 1) * 16)
    m["s5_d_pk"] = pk(f(inp["ev_s5_d"])[0][gs].reshape(-1))
    lre = f(inp["ev_s5_lam_re"])[0][gs]; lim = f(inp["ev_s5_lam_im"])[0][gs]; ldt = f(inp["ev_s5_log_dt"])[0][gs]
    rowg = (np.arange(128) // 16)[:, None] + 8 * np.arange(2)[None, :]
    m["lamre_row"] = np.ascontiguousarray(lre[rowg]); m["lamim_row"] = np.ascontiguousarray(lim[rowg])
    m["logdt_row"] = np.ascontiguousarray(ldt[rowg])
    hh = (np.arange(128) % 16)
    bre = f(inp["ev_s5_b_re"])[0][gs]; bim = f(inp["ev_s5_b_im"])[0][gs]
    m["bT_re"] = np.ascontiguousarray(bre[rowg, :, hh[:, None]]); m["bT_im"] = np.ascontiguousarray(bim[rowg, :, hh[:, None]])
    m["lamre_col"] = np.ascontiguousarray(np.concatenate([lre.T, lre.T], 0)); m["lamim_col"] = np.ascontiguousarray(np.concatenate([lim.T, lim.T], 0))
    m["logdt_col"] = np.ascontiguousarray(np.broadcast_to(ldt[None, :], (128, 16)))
    cre = np.transpose(f(inp["ev_s5_c_re"])[0][gs], (2, 0, 1)); cim = np.transpose(f(inp["ev_s5_c_im"])[0][gs], (2, 0, 1))
    m["cT_re_col"] = np.ascontiguousarray(np.concatenate([cre, cre], 0)); m["cT_im_col"] = np.ascontiguousarray(np.concatenate([cim, cim], 0))
    fs = slice(hf * 512, (hf + 1) * 512)
    for nm in ("od_w_r", "od_w_k", "od_w_v"):
        m[nm + "_h"] = f(f(inp[nm])[0][:, fs])
    for nm in ("od_w1", "od_a1", "od_g1"):
        m[nm] = f(inp[nm])
    for nm in ("od_w2", "od_a2", "od_g2"):
        m[nm + "_h"] = f(f(inp[nm])[0][:, fs])
    vecs = np.stack([f(inp["od_w0"])[0], f(inp["od_a0"])[0], f(inp["od_k_k"])[0], f(inp["od_k_a"])[0],
                     f(inp["od_r_k"])[0].reshape(-1)])[:, fs]
    m["od_vecs_pk"] = np.ascontiguousarray(pk(vecs).reshape(128, 5, 4))
    m["od_mu_pk"] = pk(f(inp["od_mu"])[0])
    ln = np.stack([f(inp["od_ln_g"])[0], f(inp["od_ln_b"])[0]])[:, fs]
    m["od_ln_pk64"] = np.ascontiguousarray(np.transpose(ln.reshape(2, 8, 64), (2, 0, 1)))
    return m


def const_inputs():
    import ml_dtypes
    c = {}
    rotm = np.zeros((128, 128), np.float32)
    for base in (0, 64):
        for i in range(8):
            rotm[base + i + 8, base + i] = -1.0
            rotm[base + i, base + i + 8] = 1.0
    c["rotm"] = rotm
    invf = np.zeros((128, 1), np.float32)
    fr = (500000.0 ** (-np.arange(0, 16, 2, dtype=np.float32) / 16)).astype(np.float32)
    for base in (0, 64):
        invf[base:base + 8, 0] = fr
        invf[base + 8:base + 16, 0] = fr
    c["invf_col"] = invf
    kk = np.arange(128)[:, None]; qq = np.arange(512)[None, :]
    c["masks"] = np.stack([(qq >= 128 * j + kk) for j in range(4)], axis=1).astype(np.float32).astype(ml_dtypes.bfloat16)
    sw = np.zeros((128, 128), np.float32)
    for p in range(64):
        sw[64 + p, p] = 1.0
        sw[p, 64 + p] = -1.0
    c["swapm"] = sw
    c["ident"] = np.eye(128, dtype=np.float32)
    ss = np.arange(64)[:, None]; tt = np.arange(64)[None, :]
    m4 = np.stack([(ss < tt), (ss <= tt), (tt < ss), (ss == tt)]).astype(np.float32)
    c["rw_masks"] = np.ascontiguousarray(np.broadcast_to(np.transpose(m4, (1, 0, 2))[:, :, None, :], (64, 4, 8, 64)))
    c["rowmask"] = (np.arange(128)[:, None] // 16 == np.arange(8)[None, :]).astype(np.float32)
    return c


TWO_PI = 2.0 * math.pi


def sincos(S, sin_out, cos_out, ang, tmps):
    y, kf, ki = tmps
    for out, off in ((sin_out, 0.5), (cos_out, 0.75)):
        if out is None:
            continue
        S.ts("dve", y, ang, 1.0 / TWO_PI, off, ALU.mult, ALU.add)
        S.copy("dve", ki, y)
        S.copy("dve", kf, ki)
        S.tt("dve", y, y, kf, ALU.subtract)
        S.ts("dve", kf, y, 0.0, None, ALU.is_lt)
        S.tt("dve", y, y, kf, ALU.add)
        S.ts("dve", y, y, TWO_PI, -math.pi, ALU.mult, ALU.add)
        S.act(out, y, AF.Sin)


def phase_A0(S, C, I, T, xin, uT, qT, kT, vtm):
    win_v = I["ev_w_in_h"].h.rearrange("(kc p) n -> p kc n", p=128)
    with Phase(S):
        C.ps_ss = S.ps([128, 512], F32, "ss")
        psA = [S.ps([128, 512], F32, "psA%d" % i) for i in range(3)]
        psR = S.ps([128, 512], F32, "psR")
        win = S.sb([128, 8, 1024], BF16, "win")
        wst = [S.sb([128, 8, 256], F32, "wst%d" % i) for i in range(2)]
        for i in range(4):
            S.dma("sp" if i % 2 == 0 else "act", wst[i % 2][:], Ref(I["_w"], win_v[:, :, i * 256:(i + 1) * 256], None))
            S.copy("pool", win[:, :, i * 256:(i + 1) * 256], wst[i % 2][:])
        rotm = S.sb([128, 128], F32, "rotm")
        S.dma("sp", rotm[:], I["rotm"][:])
        invf = S.sb([128, 1], F32, "invf")
        S.dma("sp", invf[:], I["invf_col"][:])
        gq = S.sb([128, 2], F32, "gq")
        S.dma("sp", gq[:], I["qk_gain_col"][:])
        S.ts("dve", gq[:, 0:1], gq[:, 0:1], 0.125, None, ALU.mult)
        x = S.sb([128, 8, 512], F32, "x")
        sq = S.sb([128, 8, 512], BF16, "sq")
        h = S.sb([128, 8, 512], BF16, "h")
        rstd = S.sb([128, 512], F32, "rstd")
        tmp = [S.sb([128, 512], F32, "tmp%d" % i) for i in range(2)]
        posi = S.sb([128, 512], I32, "posi")
        ang = S.sb([128, 512], F32, "ang")
        cosT = S.sb([128, 512], F32, "cosT")
        sinT = S.sb([128, 512], F32, "sinT")
        sq2 = S.sb([128, 512], F32, "sq2")
        qn = S.sb([128, 512], F32, "qn")
        r2 = S.sb([128, 512], F32, "r2")
        ob = [S.sb([128, 512], BF16, "ob%d" % i) for i in range(3)]
        oi = 0
        pi = 0
        for t0 in range(0, T, 512):
            S.dma("sp", x[:], xin.r(xin.h[:, t0:t0 + 512].rearrange("(kc p) t -> p kc t", p=128)))
            S.dma("act", posi[:], I["pos"].r(I["pos"].h[t0:t0 + 512].partition_broadcast(128)))
            S.copy("dve", ang[:], posi[:])
            S.ts("dve", ang[:], ang[:], invf[:, 0:1], None, ALU.mult)
            sincos(S, sinT[:], cosT[:], ang[:], (tmp[0][:], tmp[1][:], posi[:]))
            rmsnorm_mod(S, C, x, h, C.gsm, 0, 0, 512, tmp, sq, rstd)
            for n in range(6):
                p = psA[pi % 3]
                pi += 1
                for kc in range(8):
                    S.mm(p[:, :], win[:, kc, n * 128:(n + 1) * 128], h[:, kc, :], start=(kc == 0), stop=(kc == 7))
                o = ob[oi % 3]
                oi += 1
                if n < 2:
                    S.copy("act", o[:], p[:, :])
                    S.dma("pool", uT.r(uT.h[n * 128:(n + 1) * 128, t0:t0 + 512]), o[:])
                    continue
                isq = n < 4
                S.act(sq2[:], p[:, :], AF.Square)
                S.mm(C.ps_ss[:, :], C.blk_f[:], sq2[:])
                rsqrt_to(S, rstd[:], C.ps_ss[:, :], 1.0 / 64, EPS)
                S.tt("dve", qn[:], p[:, :], rstd[:], ALU.mult)
                S.ts("dve", qn[:], qn[:], gq[:, 0:1] if isq else gq[:, 1:2], None, ALU.mult)
                S.mm(psR[:, :], rotm[:], qn[:])
                S.tt("dve", r2[:], psR[:, :], sinT[:], ALU.mult)
                S.tt("pool", qn[:], qn[:], cosT[:], ALU.add if False else ALU.mult)
                S.tt("pool", o[:], qn[:], r2[:], ALU.add)
                dst = qT if isq else kT
                hd = (n - 2) % 2
                S.dma("pool", dst.r(dst.h[hd * 128:(hd + 1) * 128, t0:t0 + 512]), o[:])
            for tt_ in range(4):
                p = psA[pi % 3]
                pi += 1
                for kc in range(8):
                    S.mm(p[:, 0:256], h[:, kc, tt_ * 128:(tt_ + 1) * 128], win[:, kc, 768:1024],
                         start=(kc == 0), stop=(kc == 7))
                o = ob[oi % 3]
                oi += 1
                S.copy("act", o[:, 0:256], p[:, 0:256])
                S.dma("pool", vtm.r(vtm.h[t0 + tt_ * 128:t0 + (tt_ + 1) * 128, :]), o[:, 0:256])


def phase_attn(S, C, I, T, qT, kT, vtm, moT, lam_init):
    NQ = T // 512
    NK = T // 128
    with Phase(S):
        C.ps_ss = S.ps([128, 512], F32, "ss")
        psS = [S.ps([128, 512], F32, "psS%d" % i) for i in range(3)]
        psO = [S.ps([128, 512], F32, "psO%d" % i) for i in range(2)]
        psZ = [S.ps([128, 512], F32, "psZ%d" % i) for i in range(2)]
        masks = S.sb([128, 4, 512], BF16, "masks")
        S.dma("sp", masks[:], I["masks"][:])
        lv = S.sb([128, 4, 64], F32, "lv")
        S.dma("sp", lv[:], I["lam_vecs"].r(I["lam_vecs"].h[:, :].partition_broadcast(128)))
        lp = S.sb([128, 2, 64], F32, "lp")
        S.tt("dve", lp[:, 0, :], lv[:, 0, :], lv[:, 1, :], ALU.mult)
        S.tt("dve", lp[:, 1, :], lv[:, 2, :], lv[:, 3, :], ALU.mult)
        ls = S.sb([128, 2], F32, "ls")
        S.op("dve", lambda g: g.reduce_sum(ls.h[:, :], lp.h[:, :, :], axis=mybir.AxisListType.X), [lp[:]], [ls[:]])
        S.act(ls[:], ls[:], AF.Exp)
        nlam = S.sb([128, 1], F32, "nlam")
        S.tt("dve", nlam[:], ls[:, 1:2], ls[:, 0:1], ALU.subtract)
        S.ts("dve", nlam[:], nlam[:], -lam_init, None, ALU.add)
        sg = S.sb([128, 1], F32, "sg")
        S.dma("sp", sg[:], I["subln_col"][:])
        S.ts("dve", sg[:], sg[:], 1.0 - lam_init, None, ALU.mult)
        q = S.sb([128, T], BF16, "q")
        k = S.sb([128, T], BF16, "k")
        v = S.sb([128, NK, 128], BF16, "v")
        pt = [S.sb([128, 512], BF16, "pt%d" % i) for i in range(8)]
        rs = [S.sb([128, 512], F32, "rs%d" % i) for i in range(2)]
        o0 = S.sb([128, 512], F32, "o0")
        o1 = S.sb([128, 512], F32, "o1")
        sq2 = S.sb([128, 512], F32, "sq2")
        rstd = S.sb([128, 512], F32, "rstd")
        ob = [S.sb([128, 512], BF16, "ob%d" % i) for i in range(2)]
        it = 0
        for hd in range(2):
            S.dma("sp", q[:], qT.r(qT.h[hd * 128:(hd + 1) * 128, :]))
            S.dma("act", k[:], kT.r(kT.h[hd * 128:(hd + 1) * 128, :]))
            S.dma("sp", v[:], vtm.r(vtm.h[:, hd * 128:(hd + 1) * 128].rearrange("(kb p) e -> p kb e", p=128)))
            for qb in range(NQ):
                nkb = 4 * (qb + 1)
                tiles = [(kb, c) for kb in range(nkb) for c in range(2)]
                LA = 2
                pts = {}
                for idx in range(len(tiles) + LA):
                    if idx < len(tiles):
                        kb, c = tiles[idx]
                        ps = psS[it % 3]
                        p = pt[it % 8]
                        it += 1
                        pts[idx] = p
                        S.mm(ps[:, :], k[c * 64:(c + 1) * 64, kb * 128:(kb + 1) * 128],
                             q[c * 64:(c + 1) * 64, qb * 512:(qb + 1) * 512])
                        S.act(p[:], ps[:, :], AF.Exp)
                        j = kb - 4 * qb
                        if j >= 0:
                            S.tt("pool", p[:], p[:], masks[:, j, :], ALU.mult)
                    if idx - LA >= 0:
                        kb, c = tiles[idx - LA]
                        p = pts.pop(idx - LA)
                        S.mm(psO[c][:, :], v[:, kb, :], p[:], start=(kb == 0), stop=(kb == nkb - 1))
                        S.mm(psZ[c][:, :], C.ones_bf[:], p[:], start=(kb == 0), stop=(kb == nkb - 1))
                for c in range(2):
                    S.op("dve", lambda g, c=c: g.reciprocal(rs[c].h[:, :], psZ[c].h[:, :]), [psZ[c][:]], [rs[c][:]])
                S.tt("dve", o0[:], psO[0][:, :], rs[0][:], ALU.mult)
                S.tt("dve", o1[:], psO[1][:, :], rs[1][:], ALU.mult)
                S.stt("dve", o0[:], o1[:], nlam[:, 0:1], o0[:], ALU.mult, ALU.add)
                S.act(sq2[:], o0[:], AF.Square)
                S.mm(C.ps_ss[:, :], C.ones_f[:], sq2[:])
                rsqrt_to(S, rstd[:], C.ps_ss[:, :], 1.0 / 128, EPS)
                S.tt("pool", o0[:], o0[:], rstd[:], ALU.mult)
                o = ob[qb % 2]
                S.ts("dve", o[:], o0[:], sg[:, 0:1], None, ALU.mult)
                S.dma("pool", moT.r(moT.h[256 + hd * 128:256 + (hd + 1) * 128, qb * 512:(qb + 1) * 512]), o[:])


def phase_S5(S, C, I, T, uT, ygT):
    NB = T // 512
    with Phase(S):
        psA = [S.ps([128, 512], F32, "psA%d" % i) for i in range(2)]
        psB = [S.ps([128, 512], F32, "psB%d" % i) for i in range(2)]
        psY = S.ps([128, 512], F32, "psY")
        psW = S.ps([128, 8], F32, "psW")
        swap = S.sb([128, 128], F32, "swap")
        S.dma("sp", swap[:], I["swapm"][:])
        rowmask = S.sb([128, 8], F32, "rowmask")
        S.dma("sp", rowmask[:], I["rowmask"][:])
        dcol = S.sb([128, 2], F32, "dcol")
        S.dma("sp", dcol[:], I["s5_d_pk"][:])
        lr = S.sb([128, 2, 64], F32, "lr")
        li = S.sb([128, 2, 64], F32, "li")
        ldt = S.sb([128, 2], F32, "ldt")
        S.dma("sp", lr[:], I["lamre_row"][:])
        S.dma("act", li[:], I["lamim_row"][:])
        S.dma("sp", ldt[:], I["logdt_row"][:])
        S.act(ldt[:], ldt[:], AF.Exp)
        dtb = ldt.h[:, :].unsqueeze(2).to_broadcast([128, 2, 64])
        th = S.sb([128, 2, 64], F32, "th")
        mag = S.sb([128, 2, 64], F32, "mag")
        S.tt("dve", th[:], li[:], ldt.r(dtb), ALU.mult)
        S.tt("dve", mag[:], lr[:], ldt.r(dtb), ALU.mult)
        S.act(mag[:], mag[:], AF.Exp)
        cs = S.sb([128, 2, 64], F32, "cs")
        sn = S.sb([128, 2, 64], F32, "sn")
        t4 = S.sb([128, 2, 64], F32, "t4")
        t4b = S.sb([128, 2, 64], F32, "t4b")
        t4i = S.sb([128, 2, 64], I32, "t4i")
        sincos(S, sn[:], cs[:], th[:], (t4[:], t4b[:], t4i[:]))
        ar = S.sb([128, 2, 64], F32, "ar")
        ai = S.sb([128, 2, 64], F32, "ai")
        S.tt("dve", ar[:], mag[:], cs[:], ALU.mult)
        S.tt("dve", ai[:], mag[:], sn[:], ALU.mult)
        S.ts("dve", ar[:], ar[:], -1.0, None, ALU.add)
        den = S.sb([128, 2, 64], F32, "den")
        S.tt("dve", den[:], lr[:], lr[:], ALU.mult)
        S.tt("dve", t4[:], li[:], li[:], ALU.mult)
        S.tt("dve", den[:], den[:], t4[:], ALU.add)
        S.op("dve", lambda g: g.reciprocal(den.h[:], den.h[:]), [den[:]], [den[:]])
        qr = S.sb([128, 2, 64], F32, "qr")
        qi = S.sb([128, 2, 64], F32, "qi")
        S.tt("dve", qr[:], ar[:], lr[:], ALU.mult)
        S.tt("dve", t4[:], ai[:], li[:], ALU.mult)
        S.tt("dve", qr[:], qr[:], t4[:], ALU.add)
        S.tt("dve", qr[:], qr[:], den[:], ALU.mult)
        S.tt("dve", qi[:], ai[:], lr[:], ALU.mult)
        S.tt("dve", t4[:], ar[:], li[:], ALU.mult)
        S.tt("dve", qi[:], qi[:], t4[:], ALU.subtract)
        S.tt("dve", qi[:], qi[:], den[:], ALU.mult)
        btr = S.sb([128, 2, 64], F32, "btr")
        bti = S.sb([128, 2, 64], F32, "bti")
        S.dma("sp", btr[:], I["bT_re"][:])
        S.dma("act", bti[:], I["bT_im"][:])
        bbr = S.sb([128, 2, 64], F32, "bbr")
        bbi = S.sb([128, 2, 64], F32, "bbi")
        S.tt("dve", bbr[:], qr[:], btr[:], ALU.mult)
        S.tt("dve", t4[:], qi[:], bti[:], ALU.mult)
        S.tt("dve", bbr[:], bbr[:], t4[:], ALU.subtract)
        S.tt("dve", bbi[:], qr[:], bti[:], ALU.mult)
        S.tt("dve", t4[:], qi[:], btr[:], ALU.mult)
        S.tt("dve", bbi[:], bbi[:], t4[:], ALU.add)
        nbbr = S.sb([128, 2, 64], F32, "nbbr")
        S.ts("dve", nbbr[:], bbr[:], -1.0, None, ALU.mult)
        lrc = S.sb([128, 16], F32, "lrc")
        lic = S.sb([128, 16], F32, "lic")
        dtc = S.sb([128, 16], F32, "dtc")
        S.dma("sp", lrc[:], I["lamre_col"][:])
        S.dma("act", lic[:], I["lamim_col"][:])
        S.dma("sp", dtc[:], I["logdt_col"][:])
        S.act(dtc[:], dtc[:], AF.Exp)
        thc = S.sb([128, 16], F32, "thc")
        rc = S.sb([128, 16], F32, "rc")
        S.tt("dve", thc[:], lic[:], dtc[:], ALU.mult)
        S.tt("dve", rc[:], lrc[:], dtc[:], ALU.mult)
        S.act(rc[:], rc[:], AF.Exp)
        c1 = S.sb([128, 16], F32, "c1")
        s1 = S.sb([128, 16], F32, "s1")
        t32 = S.sb([128, 16], F32, "t32")
        t32b = S.sb([128, 16], F32, "t32b")
        t32i = S.sb([128, 16], I32, "t32i")
        sincos(S, s1[:], c1[:], thc[:], (t32[:], t32b[:], t32i[:]))
        ctc = S.sb([128, 16, 16], F32, "ctc")
        cti = S.sb([128, 16, 16], F32, "cti")
        S.dma("sp", ctc[:], I["cT_re_col"][:])
        S.dma("act", cti[:], I["cT_im_col"][:])
        S.ts("dve", cti[:], cti[:], -1.0, None, ALU.mult)
        Ct = S.sb([128, 8, 512], F32, "Ct")
        St = S.sb([128, 8, 512], F32, "St")
        Cm = S.sb([128, 8, 512], F32, "Cm")
        Sm = S.sb([128, 8, 512], F32, "Sm")
        Rt = S.sb([128, 8, 512], F32, "Rt")
        tA = S.sb([128, 8, 256], F32, "tA")
        LA = S.sb([128, 8, 128], BF16, "LA")
        LB = S.sb([128, 8, 128], BF16, "LB")
        LC1 = S.sb([128, 8, 128], BF16, "LC1")
        LC2 = S.sb([128, 8, 128], BF16, "LC2")
        W = S.sb([128, 8, 512], F32, "W")
        w0 = S.sb([128, 8], F32, "w0")
        wl = S.sb([128, 8], F32, "wl")
        tw = S.sb([128, 8], F32, "tw")
        u = [S.sb([128, 512], BF16, "u%d" % i) for i in range(2)]
        t1 = [S.sb([128, 512], F32, "t1_%d" % i) for i in range(2)]
        t2 = [S.sb([128, 512], F32, "t2_%d" % i) for i in range(2)]
        R1 = [S.sb([128, 512], BF16, "R1_%d" % i) for i in range(2)]
        R2 = [S.sb([128, 512], BF16, "R2_%d" % i) for i in range(2)]
        yv = S.sb([128, 512], F32, "yv")
        y2 = S.sb([128, 512], F32, "y2")
        yo = [S.sb([128, 512], BF16, "yo%d" % i) for i in range(2)]
        it = 0
        for cc in range(2):
            g0 = cc * 8
            S.copy("dve", Ct[:, :, 0:1], c1.r(c1.h[:, g0:g0 + 8].unsqueeze(2)))
            S.copy("dve", St[:, :, 0:1], s1.r(s1.h[:, g0:g0 + 8].unsqueeze(2)))
            n = 1
            while n < 512:
                cn = Ct.r(Ct.h[:, :, n - 1:n].to_broadcast([128, 8, n]))
                sn_ = St.r(St.h[:, :, n - 1:n].to_broadcast([128, 8, n]))
                ta = tA.r(tA.h[:, :, 0:n])
                S.tt("dve", ta, St[:, :, 0:n], sn_, ALU.mult)
                S.tt("dve", Ct[:, :, n:2 * n], Ct[:, :, 0:n], cn, ALU.mult)
                S.tt("dve", Ct[:, :, n:2 * n], Ct[:, :, n:2 * n], ta, ALU.subtract)
                S.tt("dve", ta, St[:, :, 0:n], cn, ALU.mult)
                S.tt("dve", St[:, :, n:2 * n], Ct[:, :, 0:n], sn_, ALU.mult)
                S.tt("dve", St[:, :, n:2 * n], St[:, :, n:2 * n], ta, ALU.add)
                n *= 2
            S.copy("pool", Cm[0:64, :, :], Ct[0:64, :, :])
            S.ts("pool", Cm[64:128, :, :], St[64:128, :, :], -1.0, None, ALU.mult)
            S.copy("pool", Sm[0:64, :, :], St[0:64, :, :])
            S.copy("pool", Sm[64:128, :, :], Ct[64:128, :, :])
            S.memset("pool", Rt[:], 1.0)
            S.tt("pool", Rt[:], Rt[:], rc.r(rc.h[:, g0:g0 + 8].unsqueeze(2).to_broadcast([128, 8, 512])), ALU.mult)
            S.memset("pool", LC1[:], 0.0)
            S.memset("pool", LC2[:], 0.0)
            for gi in range(8):
                S.ts("dve", LA[:, gi, 0:64], bbr[:, cc, :], rowmask[:, gi:gi + 1], None, ALU.mult)
                S.ts("dve", LA[:, gi, 64:128], bbi[:, cc, :], rowmask[:, gi:gi + 1], None, ALU.mult)
                S.ts("dve", LB[:, gi, 0:64], bbi[:, cc, :], rowmask[:, gi:gi + 1], None, ALU.mult)
                S.ts("dve", LB[:, gi, 64:128], nbbr[:, cc, :], rowmask[:, gi:gi + 1], None, ALU.mult)
                S.copy("pool", LC1[:, gi, gi * 16:(gi + 1) * 16], ctc[:, g0 + gi, :])
                S.copy("pool", LC2[:, gi, gi * 16:(gi + 1) * 16], cti[:, g0 + gi, :])
            S.memset("pool", w0[:], 0.0)
            for tb in range(NB):
                ut = u[tb % 2]
                S.dma("sp", ut[:], uT.r(uT.h[cc * 128:(cc + 1) * 128, tb * 512:(tb + 1) * 512]))
                for gi in range(8):
                    k = it % 2
                    it += 1
                    if gi == 0:
                        S.mm(psA[k][:, :], LA[:, gi, :], ut[:])
                        S.mm(psB[k][:, :], LB[:, gi, :], ut[:])
                    if gi + 1 < 8:
                        S.mm(psA[1 - k][:, :], LA[:, gi + 1, :], ut[:])
                        S.mm(psB[1 - k][:, :], LB[:, gi + 1, :], ut[:])
                    S.tt("dve", t1[k][:], psA[k][:, :], Ct[:, gi, :], ALU.mult)
                    S.tt("dve", t2[k][:], psB[k][:, :], St[:, gi, :], ALU.mult)
                    S.tt("pool", t1[k][:], t1[k][:], t2[k][:], ALU.add)
                    S.scan("dve", W[:, gi, :], Rt[:, gi, :], t1[k][:], w0[:, gi:gi + 1])
                    S.tt("pool", R1[k][:], W[:, gi, :], Cm[:, gi, :], ALU.mult)
                    S.tt("pool", R2[k][:], W[:, gi, :], Sm[:, gi, :], ALU.mult)
                    S.mm(psY[:, :], LC1[:, gi, :], R1[k][:], start=(gi == 0), stop=False)
                    S.mm(psY[:, :], LC2[:, gi, :], R2[k][:], start=False, stop=(gi == 7))
                S.copy("dve", wl[:], W.r(W.h[:, :, 511]))
                S.mm(psW[:, :], swap[:], wl[:])
                S.tt("dve", tw[:], psW[:, :], St.r(St.h[:, :, 511]), ALU.mult)
                S.tt("dve", w0[:], wl[:], Ct.r(Ct.h[:, :, 511]), ALU.mult)
                S.tt("dve", w0[:], w0[:], tw[:], ALU.subtract)
                S.stt("dve", yv[:], ut[:], dcol[:, cc:cc + 1], psY[:, :], ALU.mult, ALU.add)
                S.act(y2[:], yv[:], AF.Square)
                S.ts("dve", y2[:], y2[:], 0.044715, 1.0, ALU.mult, ALU.add)
                S.tt("pool", y2[:], y2[:], yv[:], ALU.mult)
                S.act(y2[:], y2[:], AF.Sigmoid, scale=1.5957691216057308)
                o = yo[tb % 2]
                S.tt("dve", o[:], yv[:], y2[:], ALU.mult)
                S.dma("pool", ygT.r(ygT.h[cc * 128:(cc + 1) * 128, tb * 512:(tb + 1) * 512]), o[:])


def phase_glu(S, C, I, T, ygF, ygL, moT):
    wv = I["ev_s5_w_glu_h"].h.rearrange("(kc p) n -> p kc n", p=128)
    with Phase(S):
        ps = [S.ps([128, 512], F32, "ps%d" % i) for i in range(2)]
        wst = S.sb([128, 4, 256], F32, "wst")
        S.dma("sp", wst[:], Ref(I["_w"], wv, None))
        w = S.sb([128, 4, 256], BF16, "w")
        S.copy("pool", w[:], wst[:])
        yg = [S.sb([128, 4, 512], BF16, "yg%d" % i) for i in range(2)]
        yl = [S.sb([128, 2, 512], BF16, "yl%d" % i) for i in range(2)]
        sg = [S.sb([128, 512], F32, "sg%d" % i) for i in range(2)]
        ob = [S.sb([128, 512], BF16, "ob%d" % i) for i in range(2)]
        it = 0
        for tb in range(T // 512):
            y = yg[tb % 2]
            yy = yl[tb % 2]
            S.dma("sp", y[:], ygF.r(ygF.h[:, tb * 512:(tb + 1) * 512].rearrange("(kc p) t -> p kc t", p=128)))
            S.dma("act", yy[:], ygL.r(ygL.h[:, tb * 512:(tb + 1) * 512].rearrange("(kc p) t -> p kc t", p=128)))
            for n in range(2):
                p = ps[it % 2]
                for kc in range(4):
                    S.mm(p[:, :], w[:, kc, n * 128:(n + 1) * 128], y[:, kc, :], start=(kc == 0), stop=(kc == 3))
                S.act(sg[it % 2][:], p[:, :], AF.Sigmoid)
                S.tt("dve", ob[it % 2][:], yy[:, n, :], sg[it % 2][:], ALU.mult)
                S.dma("pool", moT.r(moT.h[n * 128:(n + 1) * 128, tb * 512:(tb + 1) * 512]), ob[it % 2][:])
                it += 1


def phase_B0(S, C, I, T, xin, R):
    TB = 256
    NCH = TB // 64
    wv_ = {n: I[n].h[0].rearrange("(kc p) n -> p kc n", p=128) for n in ("od_w1", "od_a1", "od_g1")}
    for n in ("od_w_r", "od_w_k", "od_w_v"):
        wv_[n] = I[n + "_h"].h.rearrange("(kc p) n -> p kc n", p=128)
    with Phase(S):
        C.ps_ss = S.ps([128, 512], F32, "ss")
        psA_ = [S.ps([128, 512], F32, "psA%d" % i) for i in range(4)]
        psT_ = [S.ps([128, 512], F32, "psT%d" % i) for i in range(2)]

        class _V:
            def __init__(self, t, w):
                self.t, self.w = t, w

            def __getitem__(self, idx):
                return Ref(self.t, self.t.h[:, 0:self.w][idx], None)
        psA = [_V(t, TB) for t in psA_]
        psT = [_V(t, 128) for t in psT_]
        wst = [S.sb([128, 8, 128], F32, "wst%d" % i) for i in range(2)]
        W3 = [S.sb([128, 8, 512], BF16, "W%d" % i) for i in range(3)]
        wi = 0
        for m, nm in enumerate(("od_w_r", "od_w_k", "od_w_v")):
            for n in range(4):
                S.dma("sp" if wi % 2 == 0 else "act", wst[wi % 2][:], Ref(I["_w"], wv_[nm][:, :, n * 128:(n + 1) * 128], None))
                S.copy("pool", W3[m][:, :, n * 128:(n + 1) * 128], wst[wi % 2][:])
                wi += 1
        L1 = S.sb([128, 8, 256], BF16, "L1")
        for nm, lo, wd in (("od_w1", 0, 64), ("od_a1", 64, 64), ("od_g1", 128, 128)):
            S.dma("sp", wst[wi % 2][:, :, 0:wd], Ref(I["_w"], wv_[nm], None))
            S.copy("pool", L1[:, :, lo:lo + wd], wst[wi % 2][:, :, 0:wd])
            wi += 1
        L2st = S.sb([128, 3, 512], F32, "L2st")
        S.memset("pool", L2st[:], 0.0)
        S.dma("sp", L2st[0:64, 0, :], I["od_w2_h"][:])
        S.dma("sp", L2st[0:64, 1, :], I["od_a2_h"][:])
        S.dma("sp", L2st[:, 2, :], I["od_g2_h"][:])
        L2 = S.sb([128, 3, 512], BF16, "L2")
        S.copy("pool", L2[:], L2st[:])
        pv = S.sb([128, 7, 4], F32, "pv")
        S.dma("sp", pv[:, 0:5, :], I["od_vecs_pk"][:])
        S.ts("dve", pv[:, 5, :], pv[:, 3, :], -1.0, 1.0, ALU.mult, ALU.add)
        S.ts("dve", pv[:, 6, :], pv[:, 0, :], -1.0, None, ALU.mult)
        mu = S.sb([128, 48], F32, "mu")
        S.dma("sp", mu[:], I["od_mu_pk"][:])
        ident = S.sb([128, 128], F32, "ident")
        S.dma("sp", ident[:], I["ident"][:])
        rmask = S.sb([128, TB], F32, "rmask")
        S.memset("pool", rmask[:], 1.0)
        for c in range(NCH):
            S.memset("pool", rmask[:, c * 64:c * 64 + 1], 0.0)
        x = S.sb([128, 8, TB], F32, "x")
        sq = S.sb([128, 8, TB], BF16, "sq")
        hs = S.sb([128, 8, TB + 1], F32, "hs")
        S.memset("pool", hs[:], 0.0)
        hh = S.sb([128, 8, TB], F32, "hh")
        dx = S.sb([128, 8, TB], F32, "dx")
        xm = [S.sb([128, 8, TB], BF16, "xm%d" % i) for i in range(6)]
        rstd = S.sb([128, TB], F32, "rstd")
        tmp = [S.sb([128, TB], F32, "tmp%d" % i) for i in range(2)]
        lo1 = S.sb([128, 3, TB], BF16, "lo1")
        S.memset("pool", lo1[:], 0.0)
        Es = [{k: S.sb([128, TB], F32, k + str(i)) for k in ("r", "k", "v", "lw", "cum", "ep", "em", "ex", "a", "kk", "t0", "t1", "kmod", "b", "o0", "o1", "o2")} for i in range(2)]
        tq = [S.sb([128, 128], F32, "tq%d" % i) for i in range(2)]
        gls = [S.sb([128, NCH], F32, "gl%d" % i) for i in range(2)]
        pi = 0
        ti = 0
        for t0 in range(0, T, TB):
            S.dma("sp", x[:], xin.r(xin.h[:, t0:t0 + TB].rearrange("(kc p) t -> p kc t", p=128)))
            for kc in range(8):
                S.act(sq[:, kc, :], x[:, kc, :], AF.Square)
            for kc in range(8):
                S.mm(C.ps_ss[:, 0:TB], C.ones_bf[:], sq[:, kc, :], start=(kc == 0), stop=(kc == 7))
            rsqrt_to(S, rstd[:], C.ps_ss[:, 0:TB], 1.0 / D, EPS)
            for kc in range(8):
                S.tt("dve", tmp[kc % 2][:], x[:, kc, :], rstd[:], ALU.mult)
                S.act(hh[:, kc, :], tmp[kc % 2][:], AF.Identity, bias=modcol(C, 1, 0, kc), scale=C.gsm[:, 8 + kc:9 + kc])
            S.copy("pool", hs[:, :, 1:TB + 1], hh[:])
            S.tt("dve", dx[:], hs[:, :, 0:TB], hh[:], ALU.subtract)
            S.copy("pool", hs[:, :, 0:1], hh[:, :, TB - 1:TB])
            for m in range(6):
                for kc in range(8):
                    S.stt("dve", xm[m][:, kc, :], dx[:, kc, :],
                          mu[:, m * 8 + kc:m * 8 + kc + 1], hh[:, kc, :], ALU.mult, ALU.add)
            for j, (mx, lo, wd, fn) in enumerate(((1, 0, 64, AF.Tanh), (4, 64, 64, AF.Identity), (5, 128, 128, AF.Sigmoid))):
                p = psA[pi % 4]
                pi += 1
                for kc in range(8):
                    S.mm(p[0:wd, :], L1[:, kc, lo:lo + wd], xm[mx][:, kc, :], start=(kc == 0), stop=(kc == 7))
                S.act(lo1[0:wd, j, :], p[0:wd, :], fn)
            for n in range(4):
                f0 = n * 128
                E = Es[n % 2]
                gl = gls[n % 2]
                pr, pk_, pv_, pw = [psA[(pi + i) % 4] for i in range(4)]
                for (p, m, mx) in ((pr, 0, 0), (pk_, 1, 2), (pv_, 2, 3)):
                    for kc in range(8):
                        S.mm(p[:, :], W3[m][:, kc, f0:f0 + 128], xm[mx][:, kc, :], start=(kc == 0), stop=(kc == 7))
                S.copy("act", E["r"][:], pr[:, :])
                S.copy("act", E["k"][:], pk_[:, :])
                S.copy("act", E["v"][:], pv_[:, :])
                S.mm(pw[:, :], L2[:, 0, f0:f0 + 128], lo1[:, 0, :])
                S.act(E["t0"][:], pw[:, :], AF.Exp, bias=pv[:, 6, n:n + 1], scale=-1.0)
                S.ts("dve", E["t0"][:], E["t0"][:], 1.0, None, ALU.add)
                S.act(E["t0"][:], E["t0"][:], AF.Ln)
                S.ts("dve", E["t0"][:], E["t0"][:], -1.0, -0.5, ALU.mult, ALU.add)
                S.act(E["t0"][:], E["t0"][:], AF.Exp)
                S.ts("dve", E["lw"][:], E["t0"][:], -1.0, None, ALU.mult)
                S.scan("dve", E["cum"][:], rmask[:], E["lw"][:], 0.0)
                S.act(E["ep"][:], E["cum"][:], AF.Exp)
                S.act(E["em"][:], E["cum"][:], AF.Exp, scale=-1.0)
                S.tt("pool", E["t1"][:], E["cum"][:], E["lw"][:], ALU.subtract)
                S.act(E["ex"][:], E["t1"][:], AF.Exp)
                S.mm(pw[:, :], L2[:, 1, f0:f0 + 128], lo1[:, 1, :])
                S.act(E["a"][:], pw[:, :], AF.Sigmoid, bias=pv[:, 1, n:n + 1])
                S.ts("dve", E["kk"][:], E["k"][:], pv[:, 2, n:n + 1], None, ALU.mult)
                S.tt("pool", E["t0"][:], E["kk"][:], E["kk"][:], ALU.mult)
                S.mm(C.ps_ss[:, 0:TB], C.blk_f[:], E["t0"][:])
                S.act(E["t0"][:], C.ps_ss[:, 0:TB], AF.Sqrt)
                S.ts("dve", E["t0"][:], E["t0"][:], 1e-12, None, ALU.max)
                S.op("dve", lambda g: g.reciprocal(E["t0"].h[:], E["t0"].h[:]), [E["t0"][:]], [E["t0"][:]])
                S.tt("dve", E["kk"][:], E["kk"][:], E["t0"][:], ALU.mult)
                S.ts("dve", E["t1"][:], E["a"][:], pv[:, 3, n:n + 1], pv[:, 5, n:n + 1], ALU.mult, ALU.add)
                S.tt("dve", E["kmod"][:], E["k"][:], E["t1"][:], ALU.mult)
                S.tt("pool", E["b"][:], E["kk"][:], E["a"][:], ALU.mult)
                S.mm(pw[:, :], L2[:, 2, f0:f0 + 128], lo1[:, 2, :])
                S.copy("act", E["o2"][:], pw[:, :])
                S.dma("pool", R["GG"].r(R["GG"].h[f0:f0 + 128, t0:t0 + TB]), E["o2"][:])
                pi += 4
                S.tt("dve", E["t0"][:], E["r"][:], E["kmod"][:], ALU.mult)
                S.ts("dve", E["t0"][:], E["t0"][:], pv[:, 4, n:n + 1], None, ALU.mult)
                S.mm(C.ps_ss[:, 0:TB], C.blk_f[:], E["t0"][:])
                S.tt("dve", E["o0"][:], C.ps_ss[:, 0:TB], E["v"][:], ALU.mult)
                S.dma("pool", R["BON"].r(R["BON"].h[f0:f0 + 128, t0:t0 + TB]), E["o0"][:])
                S.tt("dve", E["o1"][:], E["r"][:], E["ep"][:], ALU.mult)
                S.dma("pool", R["RR"].r(R["RR"].h[f0:f0 + 128, t0:t0 + TB]), E["o1"][:])
                S.stt("dve", E["o0"][:], E["kk"][:], -1.0, E["ex"][:], ALU.mult, ALU.mult)
                S.dma("pool", R["AR"].r(R["AR"].h[f0:f0 + 128, t0:t0 + TB]), E["o0"][:])
                S.tt("dve", E["kmod"][:], E["kmod"][:], E["em"][:], ALU.mult)
                S.dma("pool", R["KB"].r(R["KB"].h[f0:f0 + 128, t0:t0 + TB]), E["kmod"][:])
                S.tt("dve", E["b"][:], E["b"][:], E["em"][:], ALU.mult)
                S.dma("pool", R["BB"].r(R["BB"].h[f0:f0 + 128, t0:t0 + TB]), E["b"][:])
                S.copy("dve", gl[:], E["ep"].r(E["ep"].h[:, 63:TB:64]))
                S.dma("pool", R["GL"].r(R["GL"].h[f0:f0 + 128, t0 // 64:t0 // 64 + NCH]), gl[:])
                elb = gl.r(gl.h[:, :].unsqueeze(2).to_broadcast([128, NCH, 64]))
                S.tt("dve", E["kmod"].r(E["kmod"].h[:, :].rearrange("p (c t) -> p c t", t=64)),
                     E["kmod"].r(E["kmod"].h[:, :].rearrange("p (c t) -> p c t", t=64)), elb, ALU.mult)
                S.tt("dve", E["b"].r(E["b"].h[:, :].rearrange("p (c t) -> p c t", t=64)),
                     E["b"].r(E["b"].h[:, :].rearrange("p (c t) -> p c t", t=64)), elb, ALU.mult)
                for (src, dst) in ((E["kmod"], R["KT"]), (E["b"], R["BT"]), (E["v"], R["VT"])):
                    for s in range(TB // 128):
                        pt_ = psT[ti % 2]
                        q_ = tq[ti % 2]
                        ti += 1
                        S.transpose(pt_[:, :], src[:, s * 128:(s + 1) * 128], ident[:])
                        S.copy("act", q_[:], pt_[:, :])
                        S.dma("sp", dst.r(dst.h[t0 + s * 128:t0 + (s + 1) * 128, f0:f0 + 128]), q_[:])


def phase_B1(S, C, I, T, R, moT):
    SC = 256
    NCS = SC // 64
    NCH = T // 64
    GN_EPS = 64e-5
    F32R = mybir.dt.float32r
    with Phase(S):
        banks = [S.ps([64, 8, 64], F32, "bk%d" % i) for i in range(7)]
        psE = S.ps([64, 512], F32, "psE")
        bstate = [0]

        def mmr(out, lhsT, rhs, start=True, stop=True):
            return S.mm(out, lhsT, rhs, start=start, stop=stop)

        def bank():
            b = banks[bstate[0] % 7]
            bstate[0] += 1
            return b
        cm = S.sb([64, 4, 8, 64], F32, "cmask")
        S.dma("sp", cm[:], I["rw_masks"][:])
        MS, MI, MST, I8 = [cm.r(cm.h[:, i]) for i in range(4)]
        lng = S.sb([64, 2, 8], F32, "lng")
        S.dma("sp", lng[:], I["od_ln_pk64"][:])
        ones64 = C.ones_f[0:64, 0:64]
        fm = {k: [S.sb([64, 8, SC], F32R, "%s%d" % (k, i)) for i in range(2)] for k in ("AR", "RR", "KB", "BB")}
        tm = {k: [S.sb([64, NCS, 512], F32R, "%s%d" % (k, i)) for i in range(2)] for k in ("KT", "BT", "VT")}
        ep = {k: S.sb([64, 8, SC], F32, k) for k in ("BON", "GG")}
        GLt = S.sb([64, 8, NCH], F32, "GLt")
        ST = S.sb([64, 8, 64], F32R, "ST")
        Y = S.sb([64, 8, SC], F32, "Y")
        Yc = S.sb([64, 8, SC], F32, "Yc")
        Ysq = S.sb([64, 8, SC], F32, "Ysq")
        rstd = S.sb([64, 512], F32, "rstd")
        ob = S.sb([64, 8, SC], BF16, "ob")
        A = {k: S.sb([64, 8, 64], F32R, k) for k in ("N", "NT", "ak", "kr", "br", "Tm", "X", "UT")}
        Mp = [S.sb([64, 8, 64], F32R, "M%d" % i) for i in range(2)]
        MTp = [S.sb([64, 8, 64], F32R, "MT%d" % i) for i in range(2)]
        for g in range(1):
            rows = slice(g * 512, (g + 1) * 512)
            S.dma("sp", GLt[:], R["GL"].r(R["GL"].h[rows, :].rearrange("(h i) c -> i h c", i=64)))
            S.ts("dve", ST[:], I8, 0.0, None, ALU.mult)
            for si, t0 in enumerate(range(0, T, SC)):
                F = {}
                for k in fm:
                    F[k] = fm[k][si % 2]
                    S.dma("sp" if k in ("AR", "KB") else "act", F[k][:],
                          R[k].r(R[k].h[rows, t0:t0 + SC].rearrange("(h i) t -> i h t", i=64).bitcast(F32R)))
                for k in tm:
                    F[k] = tm[k][si % 2]
                    S.dma("sp", F[k][:], R[k].r(R[k].h[t0:t0 + SC, rows].rearrange("(c s) f -> s c f", s=64).bitcast(F32R)))
                for k in ep:
                    S.dma("act", ep[k][:], R[k].r(R[k].h[rows, t0:t0 + SC].rearrange("(h i) t -> i h t", i=64)))
                for c in range(NCS):
                    cs = slice(c * 64, (c + 1) * 64)
                    cg = t0 // 64 + c

                    def hv(name, h):
                        return F[name][:, c, h * 64:(h + 1) * 64]
                    for (dst, l, r_, msk) in ((A["N"], "BB", "AR", MS), (A["NT"], "AR", "BB", MST),
                                              (A["ak"], "KB", "AR", MS), (A["kr"], "KB", "RR", MI),
                                              (A["br"], "BB", "RR", MI)):
                        p = bank()
                        for h in range(8):
                            mmr(p[:, h, :], F[l][:, h, cs], F[r_][:, h, cs])
                        S.tt("dve", dst[:], p[:], msk, ALU.mult)
                    S.tt("dve", A["Tm"][:], A["N"][:], I8, ALU.add)
                    M, MT = A["N"], A["NT"]
                    for lv in range(5):
                        p1, p2 = bank(), bank()
                        for h in range(8):
                            mmr(p1[:, h, :], MT[:, h, :], M[:, h, :])
                            mmr(p2[:, h, :], M[:, h, :], MT[:, h, :])
                        M2, MT2 = Mp[lv % 2], MTp[lv % 2]
                        S.copy("act", M2[:], p1[:])
                        S.copy("dve", MT2[:], p2[:])
                        p3 = bank()
                        for h in range(8):
                            mmr(p3[:, h, :], MT2[:, h, :], A["Tm"][:, h, :])
                        S.tt("dve", A["Tm"][:], A["Tm"][:], p3[:], ALU.add)
                        M, MT = M2, MT2
                    px = bank()
                    for h in range(8):
                        mmr(px[:, h, :], F["AR"][:, h, cs], ST[:, h, :], start=True, stop=False)
                        mmr(px[:, h, :], A["ak"][:, h, :], hv("VT", h), start=False, stop=True)
                    S.copy("act", A["X"][:], px[:])
                    pu = bank()
                    for h in range(8):
                        mmr(pu[:, h, :], A["Tm"][:, h, :], A["X"][:, h, :])
                    S.copy("act", A["UT"][:], pu[:])
                    py = bank()
                    for h in range(8):
                        mmr(py[:, h, :], ST[:, h, :], F["RR"][:, h, cs], start=True, stop=False)
                        mmr(py[:, h, :], hv("VT", h), A["kr"][:, h, :], start=False, stop=False)
                        mmr(py[:, h, :], A["UT"][:, h, :], A["br"][:, h, :], start=False, stop=True)
                    S.copy("act", Y[:, :, cs], py[:])
                    pst = bank()
                    for h in range(8):
                        mmr(pst[:, h, :], hv("KT", h), hv("VT", h), start=True, stop=False)
                        mmr(pst[:, h, :], hv("BT", h), A["UT"][:, h, :], start=False, stop=True)
                    S.tt("dve", ST[:], ST[:], GLt.r(GLt.h[:, :, cg:cg + 1].to_broadcast([64, 8, 64])), ALU.mult)
                    S.tt("dve", ST[:], ST[:], pst[:], ALU.add)
                for q4 in range(8 * SC // 512):
                    hs_ = slice(q4 * (512 // SC), (q4 + 1) * (512 // SC))
                    S.mm(psE[:, :], ones64, Y[:, hs_, :])
                    S.stt("dve", Yc[:, hs_, :], psE[:, :], -1.0 / 64, Y[:, hs_, :], ALU.mult, ALU.add)
                    S.act(Ysq[:, hs_, :], Yc[:, hs_, :], AF.Square)
                    S.mm(psE[:, :], ones64, Ysq[:, hs_, :])
                    rsqrt_to(S, rstd[:], psE[:, :], 1.0 / 64, GN_EPS)
                    S.tt("dve", Yc[:, hs_, :], Yc[:, hs_, :], rstd[:], ALU.mult)
                for h in range(8):
                    hg = g * 8 + h
                    S.act(Yc[:, h, :], Yc[:, h, :], AF.Identity, bias=lng[:, 1, hg:hg + 1], scale=lng[:, 0, hg:hg + 1])
                S.tt("pool", Yc[:], Yc[:], ep["BON"][:], ALU.add)
                S.tt("dve", ob[:], Yc[:], ep["GG"][:], ALU.mult)
                S.dma("pool", moT.r(moT.h[rows, t0:t0 + SC].rearrange("(h i) t -> i h t", i=64)), ob[:])


_NC_CACHE = {}


def kernel(**inputs):
    T = 4096
    if "nc" not in _NC_CACHE:
        _NC_CACHE["nc"] = build(T, "full")
    nc = _NC_CACHE["nc"]
    maps = [host_inputs(inputs, c % 4, c // 4, T) for c in range(8)]
    res = run_bass_kernel_spmd(nc, maps, core_ids=list(range(8)))
    out = np.stack([np.ascontiguousarray(np.asarray(res.results[b]["yT"]).T) for b in range(4)], axis=0)
    return out.astype(np.float32)
```

```python
import contextlib
import math
import numpy as np
import concourse.bass as bass
import concourse.mybir as mybir
from concourse.bass_utils import run_bass_kernel_spmd

F32 = mybir.dt.float32
BF16 = mybir.dt.bfloat16
I32 = mybir.dt.int32
AF = mybir.ActivationFunctionType
ALU = mybir.AluOpType

D = 1024
KC = 8
FFN = 2816
HC = 22
EPS = 1e-6


class _State:
    __slots__ = ("w", "r")

    def __init__(self):
        self.w = None
        self.r = {}


class Tile:
    def __init__(self, h, name):
        self.h = h
        self.name = name
        self.st = {None: _State()}

    def __getitem__(self, idx):
        return Ref(self, self.h[idx], None)

    def sub(self, key, ap):
        return Ref(self, ap, key)

    def r(self, ap):
        return Ref(self, ap, None)

    def states(self, key):
        if key is None:
            return list(self.st.values())
        if key not in self.st:
            self.st[key] = _State()
        return [self.st[key], self.st[None]]

    def wstate(self, key):
        if key not in self.st:
            self.st[key] = _State()
        return self.st[key]


class Ref:
    __slots__ = ("t", "ap", "key")

    def __init__(self, t, ap, key):
        self.t, self.ap, self.key = t, ap, key


class Sched:
    NDMA = 32

    def __init__(self, nc):
        self.nc = nc
        self.es = contextlib.ExitStack()
        self.eng = {"pe": nc.tensor, "act": nc.scalar, "dve": nc.vector,
                    "pool": nc.gpsimd, "sp": nc.sync}
        self.root_es = self.es
        self.semmap = {}
        self.ekey = {}
        self.gen = 0
        self.cnt = {}
        self.known = {}
        for e in self.eng:
            self.known[e] = {}
        self.new_engine_sems()
        self.lastclk = {e: {} for e in self.eng}
        self.dsem = [self.es.enter_context(nc.semaphore("d%d" % i)) for i in range(self.NDMA)]
        self.dtarget = [0] * self.NDMA
        self.dclock = [None] * self.NDMA
        self.dnext = 0
        self.ntile = 0
        self.nwait = 0
        self.ninstr = 0

    def new_engine_sems(self):
        self.gen += 1
        for e in self.eng:
            key = "%s#%d" % (e, self.gen)
            self.ekey[e] = key
            self.semmap[key] = self.root_es.enter_context(self.nc.semaphore("s_%s_%d" % (e, self.gen)))
            self.cnt[e] = 0

    def sb(self, shape, dt, name=None):
        self.ntile += 1
        name = "%s_%d" % (name or "t", self.ntile)
        h = self.es.enter_context(self.nc.sbuf_tensor(name, list(shape), dt))
        return Tile(h, name)

    def ps(self, shape, dt=F32, name=None):
        self.ntile += 1
        name = "%s_%d" % (name or "p", self.ntile)
        h = self.es.enter_context(self.nc.psum_tensor(name, list(shape), dt))
        return Tile(h, name)

    def dram(self, shape, dt, name, kind="Internal"):
        h = self.nc.dram_tensor(name, list(shape), dt, kind=kind).ap()
        return Tile(h, name)

    def _semobj(self, key):
        return self.semmap[key] if isinstance(key, str) else self.dsem[key]

    def _wait(self, e, ev):
        if ev is None:
            return
        key, val, clock = ev
        kn = self.known[e]
        if kn.get(key, 0) >= val:
            return
        self.eng[e].wait_ge(self._semobj(key), val)
        self.nwait += 1
        new = dict(kn)
        if clock:
            for k2, v2 in clock.items():
                if new.get(k2, 0) < v2:
                    new[k2] = v2
        new[key] = val
        self.known[e] = new

    def _deps(self, e, reads, writes, is_dma):
        for rf in reads:
            for st in rf.t.states(rf.key):
                if st.w is not None:
                    self._wait(e, st.w)
        for rf in writes:
            for st in rf.t.states(rf.key):
                if st.w is not None and (is_dma or st.w[0] != self.ekey[e]):
                    self._wait(e, st.w)
                for k, (v, c) in st.r.items():
                    if is_dma or k != self.ekey[e]:
                        self._wait(e, (k, v, c))

    def _record(self, ev, reads, writes):
        key, val, clock = ev
        for rf in reads:
            st = rf.t.wstate(rf.key)
            st.r[key] = (val, clock)
        for rf in writes:
            if rf.key is None:
                for k in list(rf.t.st.keys()):
                    if k is not None:
                        del rf.t.st[k]
            st = rf.t.wstate(rf.key)
            st.w = ev
            st.r = {}

    def op(self, e, fn, reads, writes):
        reads = [r for r in reads if isinstance(r, Ref)]
        self._deps(e, reads, writes, False)
        ins = fn(self.eng[e])
        self.cnt[e] += 1
        ins.then_inc(self.semmap[self.ekey[e]], 1)
        ev = (self.ekey[e], self.cnt[e], self.known[e])
        self.lastclk[e] = self.known[e]
        self._record(ev, reads, writes)
        self.ninstr += 1
        return ev

    def dma(self, q, out, in_, **kw):
        i = self.dnext
        self.dnext = (self.dnext + 1) % self.NDMA
        if self.dtarget[i] > 0:
            self._wait(q, (i, self.dtarget[i], self.dclock[i]))
        self._deps(q, [in_], [out], True)
        ins = self.eng[q].dma_start(out=out.ap, in_=in_.ap, **kw)
        ins.then_inc(self.dsem[i], 16)
        self.dtarget[i] += 16
        self.dclock[i] = self.known[q]
        ev = (i, self.dtarget[i], self.known[q])
        self._record(ev, [in_], [out])
        self.ninstr += 1
        return ev

    def collective(self, kind, alu, groups, in_, out):
        sem = self.root_es.enter_context(self.nc.semaphore("cc%d" % len(self.dsem)))
        self.dsem.append(sem)
        self.dtarget.append(0)
        self.dclock.append(None)
        i = len(self.dsem) - 1
        self._deps("pool", [in_], [out], True)
        ins = self.eng["pool"].collective_compute(kind, alu, replica_groups=groups,
                                                  ins=[in_.ap.opt()], outs=[out.ap.opt()])
        ins.then_inc(sem, 1)
        self.dtarget[i] = 1
        self.dclock[i] = self.known["pool"]
        ev = (i, 1, self.known["pool"])
        self._record(ev, [in_], [out])
        self.ninstr += 1
        return ev

    def wait_all(self, e, refs):
        for rf in refs:
            for st in rf.t.states(rf.key):
                self._wait(e, st.w)

    def close(self):
        self.es.close()
        if self.root_es is not self.es:
            self.root_es.close()

    def mm(self, out, lhsT, rhs, start=True, stop=True):
        return self.op("pe", lambda g: g.matmul(out.ap, lhsT.ap, rhs.ap, start=start, stop=stop),
                       [lhsT, rhs], [out])

    def transpose(self, out, in_, ident):
        return self.op("pe", lambda g: g.transpose(out.ap, in_.ap, ident.ap), [in_, ident], [out])

    def act(self, out, in_, func, bias=None, scale=None, e="act"):
        kw = {}
        if bias is not None:
            kw["bias"] = bias.ap if isinstance(bias, Ref) else bias
        if scale is not None:
            kw["scale"] = scale.ap if isinstance(scale, Ref) else scale
        return self.op(e, lambda g: g.activation(out.ap, in_.ap, func, **kw),
                       [in_, bias, scale], [out])

    def tt(self, e, out, in0, in1, op):
        return self.op(e, lambda g: g.tensor_tensor(out.ap, in0.ap, in1.ap, op), [in0, in1], [out])

    def ts(self, e, out, in0, s1, s2, op0, op1=None):
        a1 = s1.ap if isinstance(s1, Ref) else s1
        a2 = s2.ap if isinstance(s2, Ref) else s2
        if op1 is None:
            return self.op(e, lambda g: g.tensor_scalar(out.ap, in0.ap, a1, None, op0), [in0, s1], [out])
        return self.op(e, lambda g: g.tensor_scalar(out.ap, in0.ap, a1, a2, op0, op1), [in0, s1, s2], [out])

    def stt(self, e, out, in0, sc, in1, op0, op1):
        a = sc.ap if isinstance(sc, Ref) else sc
        return self.op(e, lambda g: g.scalar_tensor_tensor(out.ap, in0.ap, a, in1.ap, op0, op1),
                       [in0, sc, in1], [out])

    def scan(self, e, out, d0, d1, init, op0=ALU.mult, op1=ALU.add):
        a = init.ap if isinstance(init, Ref) else init
        return self.op(e, lambda g: g.tensor_tensor_scan(out.ap, d0.ap, d1.ap, a, op0, op1),
                       [d0, d1, init], [out])

    def copy(self, e, out, in_):
        if e == "act":
            return self.op(e, lambda g: g.copy(out.ap, in_.ap), [in_], [out])
        return self.op(e, lambda g: g.tensor_copy(out.ap, in_.ap), [in_], [out])

    def memset(self, e, out, val):
        return self.op(e, lambda g: g.memset(out.ap, val), [], [out])


def _barrier(S):
    for e in S.eng:
        for f in S.eng:
            if f != e and S.cnt[f] > 0:
                S._wait(e, (S.ekey[f], S.cnt[f], S.lastclk[f]))
        for i in range(len(S.dsem)):
            if S.dtarget[i] > 0:
                S._wait(e, (i, S.dtarget[i], S.dclock[i]))


class Phase:
    def __init__(self, S):
        self.S = S

    def __enter__(self):
        self.saved = self.S.es
        self.S.es = contextlib.ExitStack()
        return self

    def __exit__(self, *a):
        _barrier(self.S)
        self.S.es.close()
        self.S.es = self.saved
        if max(self.S.cnt.values()) > 16000:
            self.S.new_engine_sems()
        return False


def pk(v):
    v = np.asarray(v, dtype=np.float32)
    lead = v.shape[:-1]
    n = v.shape[-1] // 128
    v = v.reshape(lead + (n, 128))
    v = np.moveaxis(v, -1, 0)
    return np.ascontiguousarray(v.reshape(128, -1))


class Ctx:
    pass


def load_cast(S, C, dram_ref, shape, q="sp", ceng="pool", name="w"):
    st = S.sb(shape, F32, name + "_st")
    S.dma(q, st[:], dram_ref)
    wt = S.sb(shape, BF16, name + "_bf")
    S.copy(ceng, wt[:], st[:])
    return wt


def setup_consts(S, C):
    C.ones_bf = S.sb([128, 128], BF16, "ones")
    S.memset("pool", C.ones_bf[:], 1.0)
    C.ones_f = S.sb([128, 128], F32, "onesf")
    S.memset("pool", C.ones_f[:], 1.0)
    C.blk_f = S.sb([128, 128], F32, "blkf")
    S.memset("pool", C.blk_f[:], 0.0)
    S.memset("pool", C.blk_f[0:64, 0:64], 1.0)
    S.memset("pool", C.blk_f[64:128, 64:128], 1.0)
    C.blk_bf = S.sb([128, 128], BF16, "blkbf")
    S.copy("pool", C.blk_bf[:], C.blk_f[:])


def setup_adaln(S, C, I):
    C.mod = S.sb([128, 96], F32, "mod")
    C.gsm = S.sb([128, 16], F32, "gsm")
    C.gsf = S.sb([128, 16], F32, "gsf")
    with Phase(S):
        cact = S.sb([128, 8], F32, "cact")
        S.dma("sp", cact[:], I["c_pk"][:])
        S.act(cact[:], cact[:], AF.Silu)
        bada = S.sb([128, 96], F32, "bada")
        S.dma("sp", bada[:], I["b_ada_pk"][:])
        nm = S.sb([128, 16], F32, "nm")
        S.dma("sp", nm[:], I["norm_mix_pk"][:])
        nf = S.sb([128, 16], F32, "nf")
        S.dma("sp", nf[:], I["norm_ffn_pk"][:])
        pm = S.ps([128, 96], F32, "pmod")
        wt = [S.sb([128, 8, 512], F32, "wada%d" % i) for i in range(2)]
        it = 0
        for l in range(2):
            wl = I["w_ada"].h[l].rearrange("(kc p) n -> p kc n", p=128)
            for ng in range(12):
                w = wt[it % 2]
                it += 1
                S.dma("sp" if it % 2 else "act", w[:], I["w_ada"].r(wl[:, :, ng * 512:(ng + 1) * 512]))
                for j in range(4):
                    col = l * 48 + ng * 4 + j
                    for kc in range(8):
                        S.mm(pm[:, col:col + 1], w[:, kc, j * 128:(j + 1) * 128], cact[:, kc:kc + 1],
                             start=(kc == 0), stop=(kc == 7))
        S.tt("dve", C.mod[:], pm[:], bada[:], ALU.add)
        for l in range(2):
            S.stt("dve", C.gsm[:, l * 8:(l + 1) * 8], C.mod[:, l * 48 + 8:l * 48 + 16], 1.0,
                  nm[:, l * 8:(l + 1) * 8], ALU.add, ALU.mult)
            S.stt("dve", C.gsf[:, l * 8:(l + 1) * 8], C.mod[:, l * 48 + 32:l * 48 + 40], 1.0,
                  nf[:, l * 8:(l + 1) * 8], ALU.add, ALU.mult)


def rsqrt_to(S, out, in_, scale, eps):
    S.ts("dve", out, in_, scale, eps, ALU.mult, ALU.add)
    S.act(out, out, AF.Sqrt)
    S.op("dve", lambda g: g.reciprocal(out.ap, out.ap), [out], [out])


def modcol(C, l, part, kc):
    c = l * 48 + part * 8 + kc
    return C.mod[:, c:c + 1]


def rmsnorm_mod(S, C, x, h, gs, l, shift_part, W, tmp, sq, rstd):
    for kc in range(8):
        S.act(sq[:, kc, :], x[:, kc, :], AF.Square)
    for s0 in range(0, W, 512):
        ss = C.ps_ss
        for kc in range(8):
            S.mm(ss[:, :], C.ones_bf[:], sq[:, kc, s0:s0 + 512], start=(kc == 0), stop=(kc == 7))
        rsqrt_to(S, rstd[:, s0:s0 + 512], ss[:, :], 1.0 / D, EPS)
    for kc in range(8):
        e = "dve" if kc % 2 == 0 else "pool"
        S.tt(e, tmp[kc % 2][:, :], x[:, kc, :], rstd[:, :], ALU.mult)
        S.act(h[:, kc, :], tmp[kc % 2][:, :], AF.Identity, bias=modcol(C, l, shift_part, kc),
              scale=gs[:, l * 8 + kc:l * 8 + kc + 1])


def phase_F(S, C, I, l, T, mo, xin, xmid, xout, wo, PD, PDS, groups, WS, TBS=1024):
    HCL = HC // 2
    wg = I["ffn_w_gate_h"].h[l].rearrange("(kc p) n -> p kc n", p=128)
    wu = I["ffn_w_up_h"].h[l].rearrange("(kc p) n -> p kc n", p=128)
    wd = I["ffn_w_down_h"].h[l].rearrange("(j p) n -> p j n", p=128)
    wov = wo.rearrange("(kc p) n -> p kc n", p=128)
    NS = TBS // 512
    with Phase(S):
        C.ps_ss = S.ps([128, 512], F32, "ss")
        psA = [S.ps([128, 512], F32, "psA%d" % i) for i in range(3)]
        psB = [S.ps([128, 512], F32, "psB%d" % i) for i in range(3)]
        x = S.sb([128, 8, TBS], F32, "x")
        mot = S.sb([128, 8, TBS], BF16, "mo")
        sq = S.sb([128, 8, TBS], BF16, "sq")
        h = S.sb([128, 8, TBS], BF16, "h")
        a = S.sb([128, HCL, TBS], BF16, "a")
        rstd = S.sb([128, TBS], F32, "rstd")
        tmp = [S.sb([128, TBS], F32, "tmp%d" % i) for i in range(2)]
        sg = [S.sb([128, 512], F32, "sg%d" % i) for i in range(2)]
        pdt = [S.sb([128, 512], F32, "pdt%d" % i) for i in range(3)]
        wst = [S.sb([128, 8, 128], F32, "wst%d" % i) for i in range(4)]
        wbf = [S.sb([128, 8, 128], BF16, "wbf%d" % i) for i in range(4)]
        wdst = [S.sb([128, HCL, 128], F32, "wdst%d" % i) for i in range(2)]
        wdbf = [S.sb([128, HCL, 128], BF16, "wdbf%d" % i) for i in range(2)]
        wi = 0
        pi = 0
        di = 0

        def getw(kind, idx, src, st, bf, q, sb):
            view = WS[kind].h[idx].rearrange("p (a b) -> p a b", b=128)
            if sb == 0:
                S.dma(q, st[:], Ref(I["_w"], src, None))
                S.copy("pool", bf[:], st[:])
                S.dma("pool", WS[kind].sub(idx, view), bf[:])
            else:
                S.dma(q, bf[:], WS[kind].sub(idx, view))
        for t0 in range(0, T, TBS):
            sb = t0 // TBS
            S.dma("sp", x[:], xin.r(xin.h[:, t0:t0 + TBS].rearrange("(kc p) t -> p kc t", p=128)))
            S.dma("act", mot[:], mo.r(mo.h[:, t0:t0 + TBS].rearrange("(kc p) t -> p kc t", p=128)))
            for n in range(8):
                k = wi % 4
                wi += 1
                getw("o", n, wov[:, :, n * 128:(n + 1) * 128], wst[k], wbf[k], "sp", sb)
                for s in range(NS):
                    p = psA[pi % 3]
                    pi += 1
                    for kc in range(8):
                        S.mm(p[:, :], wbf[k][:, kc, :], mot[:, kc, s * 512:(s + 1) * 512],
                             start=(kc == 0), stop=(kc == 7))
                    S.stt("dve", x[:, n, s * 512:(s + 1) * 512], p[:, :], modcol(C, l, 2, n),
                          x[:, n, s * 512:(s + 1) * 512], ALU.mult, ALU.add)
            S.dma("act", xmid.r(xmid.h[:, t0:t0 + TBS].rearrange("(kc p) t -> p kc t", p=128)), x[:])
            rmsnorm_mod(S, C, x, h, C.gsf, l, 3, TBS, tmp, sq, rstd)
            for j in range(HCL):
                k0 = wi % 4
                k1 = (wi + 1) % 4
                wi += 2
                getw("g", j, wg[:, :, j * 128:(j + 1) * 128], wst[k0], wbf[k0], "sp", sb)
                getw("u", j, wu[:, :, j * 128:(j + 1) * 128], wst[k1], wbf[k1], "act", sb)
                for s in range(NS):
                    pg = psA[pi % 3]
                    pu = psB[pi % 3]
                    pi += 1
                    for kc in range(8):
                        S.mm(pg[:, :], wbf[k0][:, kc, :], h[:, kc, s * 512:(s + 1) * 512],
                             start=(kc == 0), stop=(kc == 7))
                    for kc in range(8):
                        S.mm(pu[:, :], wbf[k1][:, kc, :], h[:, kc, s * 512:(s + 1) * 512],
                             start=(kc == 0), stop=(kc == 7))
                    g = sg[pi % 2]
                    S.act(g[:, :], pg[:, :], AF.Silu)
                    S.tt("dve", a[:, j, s * 512:(s + 1) * 512], g[:, :], pu[:, :], ALU.mult)
            for n in range(8):
                k = n % 2
                getw("d", n, wd[:, :, n * 128:(n + 1) * 128], wdst[k], wdbf[k], "sp" if n % 2 == 0 else "act", sb)
                for s in range(NS):
                    p = psA[pi % 3]
                    pi += 1
                    for j in range(HCL):
                        S.mm(p[:, :], wdbf[k][:, j, :], a[:, j, s * 512:(s + 1) * 512],
                             start=(j == 0), stop=(j == HCL - 1))
                    o = pdt[di % 3]
                    di += 1
                    S.copy("act", o[:], p[:, :])
                    S.dma("pool", PD.sub(sb, PD.h[sb, n * 128:(n + 1) * 128, s * 512:(s + 1) * 512]), o[:])
            for hh_ in range(2):
                S.collective("AllReduce", ALU.add, groups, PD.sub(sb, PD.h[sb, hh_ * 512:(hh_ + 1) * 512, :]),
                             PDS.sub(sb, PDS.h[sb, hh_ * 512:(hh_ + 1) * 512, :]))
    with Phase(S):
        xs = [S.sb([128, 8, 512], F32, "xs%d" % i) for i in range(2)]
        ps_ = [S.sb([128, 8, 512], F32, "pds%d" % i) for i in range(2)]
        for bi, t0 in enumerate(range(0, T, 512)):
            xx, pp = xs[bi % 2], ps_[bi % 2]
            S.dma("sp", xx[:], xmid.r(xmid.h[:, t0:t0 + 512].rearrange("(kc p) t -> p kc t", p=128)))
            sb, so = t0 // TBS, t0 % TBS
            S.dma("act", pp[:], PDS.r(PDS.h[sb, :, so:so + 512].rearrange("(kc p) t -> p kc t", p=128)))
            for n in range(8):
                S.stt("dve", xx[:, n, :], pp[:, n, :], modcol(C, l, 5, n), xx[:, n, :], ALU.mult, ALU.add)
            S.dma("pool", xout.r(xout.h[:, t0:t0 + 512].rearrange("(kc p) t -> p kc t", p=128)), xx[:])


def declare_inputs(nc, specs):
    I = {}
    for name, (shape, dt) in specs.items():
        ap = nc.dram_tensor(name, list(shape), dt, kind="ExternalInput").ap()
        I[name] = Tile(ap, name)
    I["_w"] = Tile(None, "_w")
    return I


def input_specs(T):
    HH = FFN // 2
    sp = {
        "xT": ([D, T], F32), "c_pk": ([128, 8], F32),
        "w_ada": ([2, D, 6 * D], F32), "b_ada_pk": ([128, 96], F32),
        "norm_mix_pk": ([128, 16], F32), "norm_ffn_pk": ([128, 16], F32),
        "ffn_w_gate_h": ([2, D, HH], F32), "ffn_w_up_h": ([2, D, HH], F32),
        "ffn_w_down_h": ([2, HH, D], F32),
        "ev_w_out_p": ([D, D], F32), "od_w_o": ([D, D], F32),
        "ev_w_in_h": ([D, 1024], F32), "ev_s5_w_glu_h": ([512, 256], F32),
        "pos": ([T], I32), "rotm": ([128, 128], F32), "invf_col": ([128, 1], F32),
        "qk_gain_col": ([128, 2], F32), "masks": ([128, 4, 512], BF16),
        "lam_vecs": ([4, 64], F32), "subln_col": ([128, 1], F32),
        "swapm": ([128, 128], F32), "rowmask": ([128, 8], F32), "s5_d_pk": ([128, 2], F32),
        "lamre_row": ([128, 2, 64], F32), "lamim_row": ([128, 2, 64], F32), "logdt_row": ([128, 2], F32),
        "bT_re": ([128, 2, 64], F32), "bT_im": ([128, 2, 64], F32),
        "lamre_col": ([128, 16], F32), "lamim_col": ([128, 16], F32), "logdt_col": ([128, 16], F32),
        "cT_re_col": ([128, 16, 16], F32), "cT_im_col": ([128, 16, 16], F32),
        "od_w_r_h": ([D, 512], F32), "od_w_k_h": ([D, 512], F32), "od_w_v_h": ([D, 512], F32),
        "od_w1": ([1, D, 64], F32), "od_a1": ([1, D, 64], F32), "od_g1": ([1, D, 128], F32),
        "od_w2_h": ([64, 512], F32), "od_a2_h": ([64, 512], F32), "od_g2_h": ([128, 512], F32),
        "od_vecs_pk": ([128, 5, 4], F32), "od_mu_pk": ([128, 48], F32), "ident": ([128, 128], F32),
        "rw_masks": ([64, 4, 8, 64], F32), "od_ln_pk64": ([64, 2, 8], F32),
    }
    return sp


GROUPS = [[0, 4], [1, 5], [2, 6], [3, 7]]


def build(T=4096, mode="full"):
    nc = bass.Bass("TRN2", target_bir_lowering=False)
    I = declare_inputs(nc, input_specs(T))
    out = Tile(nc.dram_tensor("yT", [D, T], F32, kind="ExternalOutput").ap(), "yT")
    S = Sched(nc)
    C = Ctx()
    setup_consts(S, C)
    setup_adaln(S, C, I)
    dr = lambda shape, dt, name: S.dram(shape, dt, name, "Internal")
    uT = dr([256, T], BF16, "uT")
    qT = dr([256, T], BF16, "qT")
    kT = dr([256, T], BF16, "kT")
    vtm = dr([T, 256], BF16, "vtm")
    ygT = dr([256, T], BF16, "ygT")
    ygF = dr([512, T], BF16, "ygF")
    moL = dr([512, T], BF16, "moL")
    mo0 = dr([D, T], BF16, "mo0")
    phase_A0(S, C, I, T, I["xT"], uT, qT, kT, vtm)
    phase_S5(S, C, I, T, uT, ygT)
    S.collective("AllGather", ALU.bypass, GROUPS, ygT[:], ygF[:])
    phase_glu(S, C, I, T, ygF, ygT, moL)
    phase_attn(S, C, I, T, qT, kT, vtm, moL, 0.8 - 0.6 * math.exp(-0.3 * 0))
    for i in range(2):
        S.collective("AllGather", ALU.bypass, GROUPS, moL.r(moL.h[i * 256:(i + 1) * 256, :]),
                     mo0.r(mo0.h[i * 512:(i + 1) * 512, :]))
    xm0 = dr([D, T], F32, "xm0")
    x1 = dr([D, T], F32, "x1T")
    PD0 = dr([T // 1024, D, 1024], F32, "PD0")
    PS0 = dr([T // 1024, D, 1024], F32, "PS0")
    mkws = lambda l: {"o": dr([8, 128, 1024], BF16, "WSo%d" % l), "g": dr([HC // 2, 128, 1024], BF16, "WSg%d" % l),
                      "u": dr([HC // 2, 128, 1024], BF16, "WSu%d" % l), "d": dr([8, 128, (HC // 2) * 128], BF16, "WSd%d" % l)}
    phase_F(S, C, I, 0, T, mo0, I["xT"], xm0, x1, I["ev_w_out_p"].h, PD0, PS0, GROUPS, mkws(0))
    R = {k: dr([512, T], F32, k) for k in ("AR", "RR", "KB", "BB", "BON", "GG")}
    for k in ("KT", "BT", "VT"):
        R[k] = dr([T, 512], F32, k)
    R["GL"] = dr([512, T // 64], F32, "GL")
    mo1L = dr([512, T], BF16, "mo1L")
    mo1 = dr([D, T], BF16, "mo1")
    phase_B0(S, C, I, T, x1, R)
    phase_B1(S, C, I, T, R, mo1L)
    for i in range(2):
        S.collective("AllGather", ALU.bypass, GROUPS, mo1L.r(mo1L.h[i * 256:(i + 1) * 256, :]),
                     mo1.r(mo1.h[i * 512:(i + 1) * 512, :]))
    xm1 = dr([D, T], F32, "xm1")
    PD1 = dr([T // 1024, D, 1024], F32, "PD1")
    PS1 = dr([T // 1024, D, 1024], F32, "PS1")
    phase_F(S, C, I, 1, T, mo1, x1, xm1, out, I["od_w_o"].h, PD1, PS1, GROUPS, mkws(1))
    S.wait_all("sp", [out[:]])
    _barrier(S)
    print("instrs", S.ninstr, "waits", S.nwait)
    S.close()
    return nc


def host_inputs(inp, b, hf, T):
    f = lambda a: np.ascontiguousarray(np.asarray(a, dtype=np.float32))
    HH = FFN // 2
    hs = slice(hf * HH, (hf + 1) * HH)
    q4 = slice(hf * 256, (hf + 1) * 256)
    win = f(inp["ev_w_in"])[0]
    wout = f(inp["ev_w_out"])[0]
    perm = np.concatenate([np.arange(0, 256), np.arange(512, 768), np.arange(256, 512), np.arange(768, 1024)])
    m = {
        "xT": np.ascontiguousarray(f(inp["x"][b, :T]).T),
        "c_pk": pk(f(inp["c"])[b]),
        "w_ada": f(inp["w_ada"]),
        "b_ada_pk": pk(f(inp["b_ada"]).reshape(-1)),
        "norm_mix_pk": pk(f(inp["norm_mix"]).reshape(-1)),
        "norm_ffn_pk": pk(f(inp["norm_ffn"]).reshape(-1)),
        "ffn_w_gate_h": f(f(inp["ffn_w_gate"])[:, :, hs]), "ffn_w_up_h": f(f(inp["ffn_w_up"])[:, :, hs]),
        "ffn_w_down_h": f(f(inp["ffn_w_down"])[:, hs, :]),
        "ev_w_out_p": f(wout), "od_w_o": f(f(inp["od_w_o"])[0][perm, :]),
        "ev_w_in_h": f(np.concatenate([win[:, 0 + hf * 256:0 + (hf + 1) * 256], win[:, 512 + hf * 256:512 + (hf + 1) * 256],
                                       win[:, 1024 + hf * 256:1024 + (hf + 1) * 256], win[:, 1536 + hf * 256:1536 + (hf + 1) * 256]], axis=1)),
        "ev_s5_w_glu_h": f(f(inp["ev_s5_w_glu"])[0][:, q4]),
        "pos": np.ascontiguousarray(np.asarray(inp["positions"])[b, :T].astype(np.int32)),
    }
    m.update(const_inputs())
    qg = f(inp["ev_q_norm"])[0]; kg = f(inp["ev_k_norm"])[0]
    m["qk_gain_col"] = np.ascontiguousarray(np.stack([np.tile(qg, 2), np.tile(kg, 2)], axis=1))
    m["lam_vecs"] = np.ascontiguousarray(np.stack([f(inp["ev_lambda_q1"])[0], f(inp["ev_lambda_k1"])[0],
                                                   f(inp["ev_lambda_q2"])[0], f(inp["ev_lambda_k2"])[0]]))
    m["subln_col"] = np.ascontiguousarray(f(inp["ev_subln"])[0].reshape(128, 1))
    gs = slice(hf * 16, (hf + 1) * 16)
    m["s5_d_pk"] = pk(f(inp["ev_s5_d"])[0][gs].reshape(-1))
    lre = f(inp["ev_s5_lam_re"])[0][gs]; lim = f(inp["ev_s5_lam_im"])[0][gs]; ldt = f(inp["ev_s5_log_dt"])[0][gs]
    rowg = (np.arange(128) // 16)[:, None] + 8 * np.arange(2)[None, :]
    m["lamre_row"] = np.ascontiguousarray(lre[rowg]); m["lamim_row"] = np.ascontiguousarray(lim[rowg])
    m["logdt_row"] = np.ascontiguousarray(ldt[rowg])
    hh = (np.arange(128) % 16)
    bre = f(inp["ev_s5_b_re"])[0][gs]; bim = f(inp["ev_s5_b_im"])[0][gs]
    m["bT_re"] = np.ascontiguousarray(bre[rowg, :, hh[:, None]]); m["bT_im"] = np.ascontiguousarray(bim[rowg, :, hh[:, None]])
    m["lamre_col"] = np.ascontiguousarray(np.concatenate([lre.T, lre.T], 0)); m["lamim_col"] = np.ascontiguousarray(np.concatenate([lim.T, lim.T], 0))
    m["logdt_col"] = np.ascontiguousarray(np.broadcast_to(ldt[None, :], (128, 16)))
    cre = np.transpose(f(inp["ev_s5_c_re"])[0][gs], (2, 0, 1)); cim = np.transpose(f(inp["ev_s5_c_im"])[0][gs], (2, 0, 1))
    m["cT_re_col"] = np.ascontiguousarray(np.concatenate([cre, cre], 0)); m["cT_im_col"] = np.ascontiguousarray(np.concatenate([cim, cim], 0))
    fs = slice(hf * 512, (hf + 1) * 512)
    for nm in ("od_w_r", "od_w_k", "od_w_v"):
        m[nm + "_h"] = f(f(inp[nm])[0][:, fs])
    for nm in ("od_w1", "od_a1", "od_g1"):
        m[nm] = f(inp[nm])
    for nm in ("od_w2", "od_a2", "od_g2"):
        m[nm + "_h"] = f(f(inp[nm])[0][:, fs])
    vecs = np.stack([f(inp["od_w0"])[0], f(inp["od_a0"])[0], f(inp["od_k_k"])[0], f(inp["od_k_a"])[0],
                     f(inp["od_r_k"])[0].reshape(-1)])[:, fs]
    m["od_vecs_pk"] = np.ascontiguousarray(pk(vecs).reshape(128, 5, 4))
    m["od_mu_pk"] = pk(f(inp["od_mu"])[0])
    ln = np.stack([f(inp["od_ln_g"])[0], f(inp["od_ln_b"])[0]])[:, fs]
    m["od_ln_pk64"] = np.ascontiguousarray(np.transpose(ln.reshape(2, 8, 64), (2, 0, 1)))
    return m


def const_inputs():
    import ml_dtypes
    c = {}
    rotm = np.zeros((128, 128), np.float32)
    for base in (0, 64):
        for i in range(8):
            rotm[base + i + 8, base + i] = -1.0
            rotm[base + i, base + i + 8] = 1.0
    c["rotm"] = rotm
    invf = np.zeros((128, 1), np.float32)
    fr = (500000.0 ** (-np.arange(0, 16, 2, dtype=np.float32) / 16)).astype(np.float32)
    for base in (0, 64):
        invf[base:base + 8, 0] = fr
        invf[base + 8:base + 16, 0] = fr
    c["invf_col"] = invf
    kk = np.arange(128)[:, None]; qq = np.arange(512)[None, :]
    c["masks"] = np.stack([(qq >= 128 * j + kk) for j in range(4)], axis=1).astype(np.float32).astype(ml_dtypes.bfloat16)
    sw = np.zeros((128, 128), np.float32)
    for p in range(64):
        sw[64 + p, p] = 1.0
        sw[p, 64 + p] = -1.0
    c["swapm"] = sw
    c["ident"] = np.eye(128, dtype=np.float32)
    ss = np.arange(64)[:, None]; tt = np.arange(64)[None, :]
    m4 = np.stack([(ss < tt), (ss <= tt), (tt < ss), (ss == tt)]).astype(np.float32)
    c["rw_masks"] = np.ascontiguousarray(np.broadcast_to(np.transpose(m4, (1, 0, 2))[:, :, None, :], (64, 4, 8, 64)))
    c["rowmask"] = (np.arange(128)[:, None] // 16 == np.arange(8)[None, :]).astype(np.float32)
    return c


TWO_PI = 2.0 * math.pi


def sincos(S, sin_out, cos_out, ang, tmps):
    y, kf, ki = tmps
    for out, off in ((sin_out, 0.5), (cos_out, 0.75)):
        if out is None:
            continue
        S.ts("dve", y, ang, 1.0 / TWO_PI, off, ALU.mult, ALU.add)
        S.copy("dve", ki, y)
        S.copy("dve", kf, ki)
        S.tt("dve", y, y, kf, ALU.subtract)
        S.ts("dve", kf, y, 0.0, None, ALU.is_lt)
        S.tt("dve", y, y, kf, ALU.add)
        S.ts("dve", y, y, TWO_PI, -math.pi, ALU.mult, ALU.add)
        S.act(out, y, AF.Sin)


def phase_A0(S, C, I, T, xin, uT, qT, kT, vtm):
    win_v = I["ev_w_in_h"].h.rearrange("(kc p) n -> p kc n", p=128)
    with Phase(S):
        C.ps_ss = S.ps([128, 512], F32, "ss")
        psA = [S.ps([128, 512], F32, "psA%d" % i) for i in range(3)]
        psR = S.ps([128, 512], F32, "psR")
        win = S.sb([128, 8, 1024], BF16, "win")
        wst = [S.sb([128, 8, 256], F32, "wst%d" % i) for i in range(2)]
        for i in range(4):
            S.dma("sp" if i % 2 == 0 else "act", wst[i % 2][:], Ref(I["_w"], win_v[:, :, i * 256:(i + 1) * 256], None))
            S.copy("pool", win[:, :, i * 256:(i + 1) * 256], wst[i % 2][:])
        rotm = S.sb([128, 128], F32, "rotm")
        S.dma("sp", rotm[:], I["rotm"][:])
        invf = S.sb([128, 1], F32, "invf")
        S.dma("sp", invf[:], I["invf_col"][:])
        gq = S.sb([128, 2], F32, "gq")
        S.dma("sp", gq[:], I["qk_gain_col"][:])
        S.ts("dve", gq[:, 0:1], gq[:, 0:1], 0.125, None, ALU.mult)
        x = S.sb([128, 8, 512], F32, "x")
        sq = S.sb([128, 8, 512], BF16, "sq")
        h = S.sb([128, 8, 512], BF16, "h")
        rstd = S.sb([128, 512], F32, "rstd")
        tmp = [S.sb([128, 512], F32, "tmp%d" % i) for i in range(2)]
        posi = S.sb([128, 512], I32, "posi")
        ang = S.sb([128, 512], F32, "ang")
        cosT = S.sb([128, 512], F32, "cosT")
        sinT = S.sb([128, 512], F32, "sinT")
        sq2 = S.sb([128, 512], F32, "sq2")
        qn = S.sb([128, 512], F32, "qn")
        r2 = S.sb([128, 512], F32, "r2")
        ob = [S.sb([128, 512], BF16, "ob%d" % i) for i in range(3)]
        oi = 0
        pi = 0
        for t0 in range(0, T, 512):
            S.dma("sp", x[:], xin.r(xin.h[:, t0:t0 + 512].rearrange("(kc p) t -> p kc t", p=128)))
            S.dma("act", posi[:], I["pos"].r(I["pos"].h[t0:t0 + 512].partition_broadcast(128)))
            S.copy("dve", ang[:], posi[:])
            S.ts("dve", ang[:], ang[:], invf[:, 0:1], None, ALU.mult)
            sincos(S, sinT[:], cosT[:], ang[:], (tmp[0][:], tmp[1][:], posi[:]))
            rmsnorm_mod(S, C, x, h, C.gsm, 0, 0, 512, tmp, sq, rstd)
            for n in range(6):
                p = psA[pi % 3]
                pi += 1
                for kc in range(8):
                    S.mm(p[:, :], win[:, kc, n * 128:(n + 1) * 128], h[:, kc, :], start=(kc == 0), stop=(kc == 7))
                o = ob[oi % 3]
                oi += 1
                if n < 2:
                    S.copy("act", o[:], p[:, :])
                    S.dma("pool", uT.r(uT.h[n * 128:(n + 1) * 128, t0:t0 + 512]), o[:])
                    continue
                isq = n < 4
                S.act(sq2[:], p[:, :], AF.Square)
                S.mm(C.ps_ss[:, :], C.blk_f[:], sq2[:])
                rsqrt_to(S, rstd[:], C.ps_ss[:, :], 1.0 / 64, EPS)
                S.tt("dve", qn[:], p[:, :], rstd[:], ALU.mult)
                S.ts("dve", qn[:], qn[:], gq[:, 0:1] if isq else gq[:, 1:2], None, ALU.mult)
                S.mm(psR[:, :], rotm[:], qn[:])
                S.tt("dve", r2[:], psR[:, :], sinT[:], ALU.mult)
                S.tt("pool", qn[:], qn[:], cosT[:], ALU.add if False else ALU.mult)
                S.tt("pool", o[:], qn[:], r2[:], ALU.add)
                dst = qT if isq else kT
                hd = (n - 2) % 2
                S.dma("pool", dst.r(dst.h[hd * 128:(hd + 1) * 128, t0:t0 + 512]), o[:])
            for tt_ in range(4):
                p = psA[pi % 3]
                pi += 1
                for kc in range(8):
                    S.mm(p[:, 0:256], h[:, kc, tt_ * 128:(tt_ + 1) * 128], win[:, kc, 768:1024],
                         start=(kc == 0), stop=(kc == 7))
                o = ob[oi % 3]
                oi += 1
                S.copy("act", o[:, 0:256], p[:, 0:256])
                S.dma("pool", vtm.r(vtm.h[t0 + tt_ * 128:t0 + (tt_ + 1) * 128, :]), o[:, 0:256])


def phase_attn(S, C, I, T, qT, kT, vtm, moT, lam_init):
    NQ = T // 512
    NK = T // 128
    with Phase(S):
        C.ps_ss = S.ps([128, 512], F32, "ss")
        psS = [S.ps([128, 512], F32, "psS%d" % i) for i in range(3)]
        psO = [S.ps([128, 512], F32, "psO%d" % i) for i in range(2)]
        psZ = [S.ps([128, 512], F32, "psZ%d" % i) for i in range(2)]
        masks = S.sb([128, 4, 512], BF16, "masks")
        S.dma("sp", masks[:], I["masks"][:])
        lv = S.sb([128, 4, 64], F32, "lv")
        S.dma("sp", lv[:], I["lam_vecs"].r(I["lam_vecs"].h[:, :].partition_broadcast(128)))
        lp = S.sb([128, 2, 64], F32, "lp")
        S.tt("dve", lp[:, 0, :], lv[:, 0, :], lv[:, 1, :], ALU.mult)
        S.tt("dve", lp[:, 1, :], lv[:, 2, :], lv[:, 3, :], ALU.mult)
        ls = S.sb([128, 2], F32, "ls")
        S.op("dve", lambda g: g.reduce_sum(ls.h[:, :], lp.h[:, :, :], axis=mybir.AxisListType.X), [lp[:]], [ls[:]])
        S.act(ls[:], ls[:], AF.Exp)
        nlam = S.sb([128, 1], F32, "nlam")
        S.tt("dve", nlam[:], ls[:, 1:2], ls[:, 0:1], ALU.subtract)
        S.ts("dve", nlam[:], nlam[:], -lam_init, None, ALU.add)
        sg = S.sb([128, 1], F32, "sg")
        S.dma("sp", sg[:], I["subln_col"][:])
        S.ts("dve", sg[:], sg[:], 1.0 - lam_init, None, ALU.mult)
        q = S.sb([128, T], BF16, "q")
        k = S.sb([128, T], BF16, "k")
        v = S.sb([128, NK, 128], BF16, "v")
        pt = [S.sb([128, 512], BF16, "pt%d" % i) for i in range(8)]
        rs = [S.sb([128, 512], F32, "rs%d" % i) for i in range(2)]
        o0 = S.sb([128, 512], F32, "o0")
        o1 = S.sb([128, 512], F32, "o1")
        sq2 = S.sb([128, 512], F32, "sq2")
        rstd = S.sb([128, 512], F32, "rstd")
        ob = [S.sb([128, 512], BF16, "ob%d" % i) for i in range(2)]
        it = 0
        for hd in range(2):
            S.dma("sp", q[:], qT.r(qT.h[hd * 128:(hd + 1) * 128, :]))
            S.dma("act", k[:], kT.r(kT.h[hd * 128:(hd + 1) * 128, :]))
            S.dma("sp", v[:], vtm.r(vtm.h[:, hd * 128:(hd + 1) * 128].rearrange("(kb p) e -> p kb e", p=128)))
            for qb in range(NQ):
                nkb = 4 * (qb + 1)
                tiles = [(kb, c) for kb in range(nkb) for c in range(2)]
                LA = 2
                pts = {}
                for idx in range(len(tiles) + LA):
                    if idx < len(tiles):
                        kb, c = tiles[idx]
                        ps = psS[it % 3]
                        p = pt[it % 8]
                        it += 1
                        pts[idx] = p
                        S.mm(ps[:, :], k[c * 64:(c + 1) * 64, kb * 128:(kb + 1) * 128],
                             q[c * 64:(c + 1) * 64, qb * 512:(qb + 1) * 512])
                        S.act(p[:], ps[:, :], AF.Exp)
                        j = kb - 4 * qb
                        if j >= 0:
                            S.tt("pool", p[:], p[:], masks[:, j, :], ALU.mult)
                    if idx - LA >= 0:
                        kb, c = tiles[idx - LA]
                        p = pts.pop(idx - LA)
                        S.mm(psO[c][:, :], v[:, kb, :], p[:], start=(kb == 0), stop=(kb == nkb - 1))
                        S.mm(psZ[c][:, :], C.ones_bf[:], p[:], start=(kb == 0), stop=(kb == nkb - 1))
                for c in range(2):
                    S.op("dve", lambda g, c=c: g.reciprocal(rs[c].h[:, :], psZ[c].h[:, :]), [psZ[c][:]], [rs[c][:]])
                S.tt("dve", o0[:], psO[0][:, :], rs[0][:], ALU.mult)
                S.tt("dve", o1[:], psO[1][:, :], rs[1][:], ALU.mult)
                S.stt("dve", o0[:], o1[:], nlam[:, 0:1], o0[:], ALU.mult, ALU.add)
                S.act(sq2[:], o0[:], AF.Square)
                S.mm(C.ps_ss[:, :], C.ones_f[:], sq2[:])
                rsqrt_to(S, rstd[:], C.ps_ss[:, :], 1.0 / 128, EPS)
                S.tt("pool", o0[:], o0[:], rstd[:], ALU.mult)
                o = ob[qb % 2]
                S.ts("dve", o[:], o0[:], sg[:, 0:1], None, ALU.mult)
                S.dma("pool", moT.r(moT.h[256 + hd * 128:256 + (hd + 1) * 128, qb * 512:(qb + 1) * 512]), o[:])


def phase_S5(S, C, I, T, uT, ygT):
    NB = T // 512
    with Phase(S):
        psA = [S.ps([128, 512], F32, "psA%d" % i) for i in range(2)]
        psB = [S.ps([128, 512], F32, "psB%d" % i) for i in range(2)]
        psY = S.ps([128, 512], F32, "psY")
        psW = S.ps([128, 8], F32, "psW")
        swap = S.sb([128, 128], F32, "swap")
        S.dma("sp", swap[:], I["swapm"][:])
        rowmask = S.sb([128, 8], F32, "rowmask")
        S.dma("sp", rowmask[:], I["rowmask"][:])
        dcol = S.sb([128, 2], F32, "dcol")
        S.dma("sp", dcol[:], I["s5_d_pk"][:])
        lr = S.sb([128, 2, 64], F32, "lr")
        li = S.sb([128, 2, 64], F32, "li")
        ldt = S.sb([128, 2], F32, "ldt")
        S.dma("sp", lr[:], I["lamre_row"][:])
        S.dma("act", li[:], I["lamim_row"][:])
        S.dma("sp", ldt[:], I["logdt_row"][:])
        S.act(ldt[:], ldt[:], AF.Exp)
        dtb = ldt.h[:, :].unsqueeze(2).to_broadcast([128, 2, 64])
        th = S.sb([128, 2, 64], F32, "th")
        mag = S.sb([128, 2, 64], F32, "mag")
        S.tt("dve", th[:], li[:], ldt.r(dtb), ALU.mult)
        S.tt("dve", mag[:], lr[:], ldt.r(dtb), ALU.mult)
        S.act(mag[:], mag[:], AF.Exp)
        cs = S.sb([128, 2, 64], F32, "cs")
        sn = S.sb([128, 2, 64], F32, "sn")
        t4 = S.sb([128, 2, 64], F32, "t4")
        t4b = S.sb([128, 2, 64], F32, "t4b")
        t4i = S.sb([128, 2, 64], I32, "t4i")
        sincos(S, sn[:], cs[:], th[:], (t4[:], t4b[:], t4i[:]))
        ar = S.sb([128, 2, 64], F32, "ar")
        ai = S.sb([128, 2, 64], F32, "ai")
        S.tt("dve", ar[:], mag[:], cs[:], ALU.mult)
        S.tt("dve", ai[:], mag[:], sn[:], ALU.mult)
        S.ts("dve", ar[:], ar[:], -1.0, None, ALU.add)
        den = S.sb([128, 2, 64], F32, "den")
        S.tt("dve", den[:], lr[:], lr[:], ALU.mult)
        S.tt("dve", t4[:], li[:], li[:], ALU.mult)
        S.tt("dve", den[:], den[:], t4[:], ALU.add)
        S.op("dve", lambda g: g.reciprocal(den.h[:], den.h[:]), [den[:]], [den[:]])
        qr = S.sb([128, 2, 64], F32, "qr")
        qi = S.sb([128, 2, 64], F32, "qi")
        S.tt("dve", qr[:], ar[:], lr[:], ALU.mult)
        S.tt("dve", t4[:], ai[:], li[:], ALU.mult)
        S.tt("dve", qr[:], qr[:], t4[:], ALU.add)
        S.tt("dve", qr[:], qr[:], den[:], ALU.mult)
        S.tt("dve", qi[:], ai[:], lr[:], ALU.mult)
        S.tt("dve", t4[:], ar[:], li[:], ALU.mult)
        S.tt("dve", qi[:], qi[:], t4[:], ALU.subtract)
        S.tt("dve", qi[:], qi[:], den[:], ALU.mult)
        btr = S.sb([128, 2, 64], F32, "btr")
        bti = S.sb([128, 2, 64], F32, "bti")
        S.dma("sp", btr[:], I["bT_re"][:])
        S.dma("act", bti[:], I["bT_im"][:])
        bbr = S.sb([128, 2, 64], F32, "bbr")
        bbi = S.sb([128, 2, 64], F32, "bbi")
        S.tt("dve", bbr[:], qr[:], btr[:], ALU.mult)
        S.tt("dve", t4[:], qi[:], bti[:], ALU.mult)
        S.tt("dve", bbr[:], bbr[:], t4[:], ALU.subtract)
        S.tt("dve", bbi[:], qr[:], bti[:], ALU.mult)
        S.tt("dve", t4[:], qi[:], btr[:], ALU.mult)
        S.tt("dve", bbi[:], bbi[:], t4[:], ALU.add)
        nbbr = S.sb([128, 2, 64], F32, "nbbr")
        S.ts("dve", nbbr[:], bbr[:], -1.0, None, ALU.mult)
        lrc = S.sb([128, 16], F32, "lrc")
        lic = S.sb([128, 16], F32, "lic")
        dtc = S.sb([128, 16], F32, "dtc")
        S.dma("sp", lrc[:], I["lamre_col"][:])
        S.dma("act", lic[:], I["lamim_col"][:])
        S.dma("sp", dtc[:], I["logdt_col"][:])
        S.act(dtc[:], dtc[:], AF.Exp)
        thc = S.sb([128, 16], F32, "thc")
        rc = S.sb([128, 16], F32, "rc")
        S.tt("dve", thc[:], lic[:], dtc[:], ALU.mult)
        S.tt("dve", rc[:], lrc[:], dtc[:], ALU.mult)
        S.act(rc[:], rc[:], AF.Exp)
        c1 = S.sb([128, 16], F32, "c1")
        s1 = S.sb([128, 16], F32, "s1")
        t32 = S.sb([128, 16], F32, "t32")
        t32b = S.sb([128, 16], F32, "t32b")
        t32i = S.sb([128, 16], I32, "t32i")
        sincos(S, s1[:], c1[:], thc[:], (t32[:], t32b[:], t32i[:]))
        ctc = S.sb([128, 16, 16], F32, "ctc")
        cti = S.sb([128, 16, 16], F32, "cti")
        S.dma("sp", ctc[:], I["cT_re_col"][:])
        S.dma("act", cti[:], I["cT_im_col"][:])
        S.ts("dve", cti[:], cti[:], -1.0, None, ALU.mult)
        Ct = S.sb([128, 8, 512], F32, "Ct")
        St = S.sb([128, 8, 512], F32, "St")
        Cm = S.sb([128, 8, 512], F32, "Cm")
        Sm = S.sb([128, 8, 512], F32, "Sm")
        Rt = S.sb([128, 8, 512], F32, "Rt")
        tA = S.sb([128, 8, 256], F32, "tA")
        LA = S.sb([128, 8, 128], BF16, "LA")
        LB = S.sb([128, 8, 128], BF16, "LB")
        LC1 = S.sb([128, 8, 128], BF16, "LC1")
        LC2 = S.sb([128, 8, 128], BF16, "LC2")
        W = S.sb([128, 8, 512], F32, "W")
        w0 = S.sb([128, 8], F32, "w0")
        wl = S.sb([128, 8], F32, "wl")
        tw = S.sb([128, 8], F32, "tw")
        u = [S.sb([128, 512], BF16, "u%d" % i) for i in range(2)]
        t1 = [S.sb([128, 512], F32, "t1_%d" % i) for i in range(2)]
        t2 = [S.sb([128, 512], F32, "t2_%d" % i) for i in range(2)]
        R1 = [S.sb([128, 512], BF16, "R1_%d" % i) for i in range(2)]
        R2 = [S.sb([128, 512], BF16, "R2_%d" % i) for i in range(2)]
        yv = S.sb([128, 512], F32, "yv")
        y2 = S.sb([128, 512], F32, "y2")
        yo = [S.sb([128, 512], BF16, "yo%d" % i) for i in range(2)]
        it = 0
        for cc in range(2):
            g0 = cc * 8
            S.copy("dve", Ct[:, :, 0:1], c1.r(c1.h[:, g0:g0 + 8].unsqueeze(2)))
            S.copy("dve", St[:, :, 0:1], s1.r(s1.h[:, g0:g0 + 8].unsqueeze(2)))
            n = 1
            while n < 512:
                cn = Ct.r(Ct.h[:, :, n - 1:n].to_broadcast([128, 8, n]))
                sn_ = St.r(St.h[:, :, n - 1:n].to_broadcast([128, 8, n]))
                ta = tA.r(tA.h[:, :, 0:n])
                S.tt("dve", ta, St[:, :, 0:n], sn_, ALU.mult)
                S.tt("dve", Ct[:, :, n:2 * n], Ct[:, :, 0:n], cn, ALU.mult)
                S.tt("dve", Ct[:, :, n:2 * n], Ct[:, :, n:2 * n], ta, ALU.subtract)
                S.tt("dve", ta, St[:, :, 0:n], cn, ALU.mult)
                S.tt("dve", St[:, :, n:2 * n], Ct[:, :, 0:n], sn_, ALU.mult)
                S.tt("dve", St[:, :, n:2 * n], St[:, :, n:2 * n], ta, ALU.add)
                n *= 2
            S.copy("pool", Cm[0:64, :, :], Ct[0:64, :, :])
            S.ts("pool", Cm[64:128, :, :], St[64:128, :, :], -1.0, None, ALU.mult)
            S.copy("pool", Sm[0:64, :, :], St[0:64, :, :])
            S.copy("pool", Sm[64:128, :, :], Ct[64:128, :, :])
            S.memset("pool", Rt[:], 1.0)
            S.tt("pool", Rt[:], Rt[:], rc.r(rc.h[:, g0:g0 + 8].unsqueeze(2).to_broadcast([128, 8, 512])), ALU.mult)
            S.memset("pool", LC1[:], 0.0)
            S.memset("pool", LC2[:], 0.0)
            for gi in range(8):
                S.ts("dve", LA[:, gi, 0:64], bbr[:, cc, :], rowmask[:, gi:gi + 1], None, ALU.mult)
                S.ts("dve", LA[:, gi, 64:128], bbi[:, cc, :], rowmask[:, gi:gi + 1], None, ALU.mult)
                S.ts("dve", LB[:, gi, 0:64], bbi[:, cc, :], rowmask[:, gi:gi + 1], None, ALU.mult)
                S.ts("dve", LB[:, gi, 64:128], nbbr[:, cc, :], rowmask[:, gi:gi + 1], None, ALU.mult)
                S.copy("pool", LC1[:, gi, gi * 16:(gi + 1) * 16], ctc[:, g0 + gi, :])
                S.copy("pool", LC2[:, gi, gi * 16:(gi + 1) * 16], cti[:, g0 + gi, :])
            S.memset("pool", w0[:], 0.0)
            for tb in range(NB):
                ut = u[tb % 2]
                S.dma("sp", ut[:], uT.r(uT.h[cc * 128:(cc + 1) * 128, tb * 512:(tb + 1) * 512]))
                for gi in range(8):
                    k = it % 2
                    it += 1
                    if gi == 0:
                        S.mm(psA[k][:, :], LA[:, gi, :], ut[:])
                        S.mm(psB[k][:, :], LB[:, gi, :], ut[:])
                    if gi + 1 < 8:
                        S.mm(psA[1 - k][:, :], LA[:, gi + 1, :], ut[:])
                        S.mm(psB[1 - k][:, :], LB[:, gi + 1, :], ut[:])
                    S.tt("dve", t1[k][:], psA[k][:, :], Ct[:, gi, :], ALU.mult)
                    S.tt("dve", t2[k][:], psB[k][:, :], St[:, gi, :], ALU.mult)
                    S.tt("pool", t1[k][:], t1[k][:], t2[k][:], ALU.add)
                    S.scan("dve", W[:, gi, :], Rt[:, gi, :], t1[k][:], w0[:, gi:gi + 1])
                    S.tt("pool", R1[k][:], W[:, gi, :], Cm[:, gi, :], ALU.mult)
                    S.tt("pool", R2[k][:], W[:, gi, :], Sm[:, gi, :], ALU.mult)
                    S.mm(psY[:, :], LC1[:, gi, :], R1[k][:], start=(gi == 0), stop=False)
                    S.mm(psY[:, :], LC2[:, gi, :], R2[k][:], start=False, stop=(gi == 7))
                S.copy("dve", wl[:], W.r(W.h[:, :, 511]))
                S.mm(psW[:, :], swap[:], wl[:])
                S.tt("dve", tw[:], psW[:, :], St.r(St.h[:, :, 511]), ALU.mult)
                S.tt("dve", w0[:], wl[:], Ct.r(Ct.h[:, :, 511]), ALU.mult)
                S.tt("dve", w0[:], w0[:], tw[:], ALU.subtract)
                S.stt("dve", yv[:], ut[:], dcol[:, cc:cc + 1], psY[:, :], ALU.mult, ALU.add)
                S.act(y2[:], yv[:], AF.Square)
                S.ts("dve", y2[:], y2[:], 0.044715, 1.0, ALU.mult, ALU.add)
                S.tt("pool", y2[:], y2[:], yv[:], ALU.mult)
                S.act(y2[:], y2[:], AF.Sigmoid, scale=1.5957691216057308)
                o = yo[tb % 2]
                S.tt("dve", o[:], yv[:], y2[:], ALU.mult)
                S.dma("pool", ygT.r(ygT.h[cc * 128:(cc + 1) * 128, tb * 512:(tb + 1) * 512]), o[:])


def phase_glu(S, C, I, T, ygF, ygL, moT):
    wv = I["ev_s5_w_glu_h"].h.rearrange("(kc p) n -> p kc n", p=128)
    with Phase(S):
        ps = [S.ps([128, 512], F32, "ps%d" % i) for i in range(2)]
        wst = S.sb([128, 4, 256], F32, "wst")
        S.dma("sp", wst[:], Ref(I["_w"], wv, None))
        w = S.sb([128, 4, 256], BF16, "w")
        S.copy("pool", w[:], wst[:])
        yg = [S.sb([128, 4, 512], BF16, "yg%d" % i) for i in range(2)]
        yl = [S.sb([128, 2, 512], BF16, "yl%d" % i) for i in range(2)]
        sg = [S.sb([128, 512], F32, "sg%d" % i) for i in range(2)]
        ob = [S.sb([128, 512], BF16, "ob%d" % i) for i in range(2)]
        it = 0
        for tb in range(T // 512):
            y = yg[tb % 2]
            yy = yl[tb % 2]
            S.dma("sp", y[:], ygF.r(ygF.h[:, tb * 512:(tb + 1) * 512].rearrange("(kc p) t -> p kc t", p=128)))
            S.dma("act", yy[:], ygL.r(ygL.h[:, tb * 512:(tb + 1) * 512].rearrange("(kc p) t -> p kc t", p=128)))
            for n in range(2):
                p = ps[it % 2]
                for kc in range(4):
                    S.mm(p[:, :], w[:, kc, n * 128:(n + 1) * 128], y[:, kc, :], start=(kc == 0), stop=(kc == 3))
                S.act(sg[it % 2][:], p[:, :], AF.Sigmoid)
                S.tt("dve", ob[it % 2][:], yy[:, n, :], sg[it % 2][:], ALU.mult)
                S.dma("pool", moT.r(moT.h[n * 128:(n + 1) * 128, tb * 512:(tb + 1) * 512]), ob[it % 2][:])
                it += 1


def phase_B0(S, C, I, T, xin, R):
    TB = 256
    NCH = TB // 64
    wv_ = {n: I[n].h[0].rearrange("(kc p) n -> p kc n", p=128) for n in ("od_w1", "od_a1", "od_g1")}
    for n in ("od_w_r", "od_w_k", "od_w_v"):
        wv_[n] = I[n + "_h"].h.rearrange("(kc p) n -> p kc n", p=128)
    with Phase(S):
        C.ps_ss = S.ps([128, 512], F32, "ss")
        psA_ = [S.ps([128, 512], F32, "psA%d" % i) for i in range(4)]
        psT_ = [S.ps([128, 512], F32, "psT%d" % i) for i in range(2)]

        class _V:
            def __init__(self, t, w):
                self.t, self.w = t, w

            def __getitem__(self, idx):
                return Ref(self.t, self.t.h[:, 0:self.w][idx], None)
        psA = [_V(t, TB) for t in psA_]
        psT = [_V(t, 128) for t in psT_]
        wst = [S.sb([128, 8, 128], F32, "wst%d" % i) for i in range(2)]
        W3 = [S.sb([128, 8, 512], BF16, "W%d" % i) for i in range(3)]
        wi = 0
        for m, nm in enumerate(("od_w_r", "od_w_k", "od_w_v")):
            for n in range(4):
                S.dma("sp" if wi % 2 == 0 else "act", wst[wi % 2][:], Ref(I["_w"], wv_[nm][:, :, n * 128:(n + 1) * 128], None))
                S.copy("pool", W3[m][:, :, n * 128:(n + 1) * 128], wst[wi % 2][:])
                wi += 1
        L1 = S.sb([128, 8, 256], BF16, "L1")
        for nm, lo, wd in (("od_w1", 0, 64), ("od_a1", 64, 64), ("od_g1", 128, 128)):
            S.dma("sp", wst[wi % 2][:, :, 0:wd], Ref(I["_w"], wv_[nm], None))
            S.copy("pool", L1[:, :, lo:lo + wd], wst[wi % 2][:, :, 0:wd])
            wi += 1
        L2st = S.sb([128, 3, 512], F32, "L2st")
        S.memset("pool", L2st[:], 0.0)
        S.dma("sp", L2st[0:64, 0, :], I["od_w2_h"][:])
        S.dma("sp", L2st[0:64, 1, :], I["od_a2_h"][:])
        S.dma("sp", L2st[:, 2, :], I["od_g2_h"][:])
        L2 = S.sb([128, 3, 512], BF16, "L2")
        S.copy("pool", L2[:], L2st[:])
        pv = S.sb([128, 7, 4], F32, "pv")
        S.dma("sp", pv[:, 0:5, :], I["od_vecs_pk"][:])
        S.ts("dve", pv[:, 5, :], pv[:, 3, :], -1.0, 1.0, ALU.mult, ALU.add)
        S.ts("dve", pv[:, 6, :], pv[:, 0, :], -1.0, None, ALU.mult)
        mu = S.sb([128, 48], F32, "mu")
        S.dma("sp", mu[:], I["od_mu_pk"][:])
        ident = S.sb([128, 128], F32, "ident")
        S.dma("sp", ident[:], I["ident"][:])
        rmask = S.sb([128, TB], F32, "rmask")
        S.memset("pool", rmask[:], 1.0)
        for c in range(NCH):
            S.memset("pool", rmask[:, c * 64:c * 64 + 1], 0.0)
        x = S.sb([128, 8, TB], F32, "x")
        sq = S.sb([128, 8, TB], BF16, "sq")
        hs = S.sb([128, 8, TB + 1], F32, "hs")
        S.memset("pool", hs[:], 0.0)
        hh = S.sb([128, 8, TB], F32, "hh")
        dx = S.sb([128, 8, TB], F32, "dx")
        xm = [S.sb([128, 8, TB], BF16, "xm%d" % i) for i in range(6)]
        rstd = S.sb([128, TB], F32, "rstd")
        tmp = [S.sb([128, TB], F32, "tmp%d" % i) for i in range(2)]
        lo1 = S.sb([128, 3, TB], BF16, "lo1")
        S.memset("pool", lo1[:], 0.0)
        Es = [{k: S.sb([128, TB], F32, k + str(i)) for k in ("r", "k", "v", "lw", "cum", "ep", "em", "ex", "a", "kk", "t0", "t1", "kmod", "b", "o0", "o1", "o2")} for i in range(2)]
        tq = [S.sb([128, 128], F32, "tq%d" % i) for i in range(2)]
        gls = [S.sb([128, NCH], F32, "gl%d" % i) for i in range(2)]
        pi = 0
        ti = 0
        for t0 in range(0, T, TB):
            S.dma("sp", x[:], xin.r(xin.h[:, t0:t0 + TB].rearrange("(kc p) t -> p kc t", p=128)))
            for kc in range(8):
                S.act(sq[:, kc, :], x[:, kc, :], AF.Square)
            for kc in range(8):
                S.mm(C.ps_ss[:, 0:TB], C.ones_bf[:], sq[:, kc, :], start=(kc == 0), stop=(kc == 7))
            rsqrt_to(S, rstd[:], C.ps_ss[:, 0:TB], 1.0 / D, EPS)
            for kc in range(8):
                S.tt("dve", tmp[kc % 2][:], x[:, kc, :], rstd[:], ALU.mult)
                S.act(hh[:, kc, :], tmp[kc % 2][:], AF.Identity, bias=modcol(C, 1, 0, kc), scale=C.gsm[:, 8 + kc:9 + kc])
            S.copy("pool", hs[:, :, 1:TB + 1], hh[:])
            S.tt("dve", dx[:], hs[:, :, 0:TB], hh[:], ALU.subtract)
            S.copy("pool", hs[:, :, 0:1], hh[:, :, TB - 1:TB])
            for m in range(6):
                for kc in range(8):
                    S.stt("dve", xm[m][:, kc, :], dx[:, kc, :],
                          mu[:, m * 8 + kc:m * 8 + kc + 1], hh[:, kc, :], ALU.mult, ALU.add)
            for j, (mx, lo, wd, fn) in enumerate(((1, 0, 64, AF.Tanh), (4, 64, 64, AF.Identity), (5, 128, 128, AF.Sigmoid))):
                p = psA[pi % 4]
                pi += 1
                for kc in range(8):
                    S.mm(p[0:wd, :], L1[:, kc, lo:lo + wd], xm[mx][:, kc, :], start=(kc == 0), stop=(kc == 7))
                S.act(lo1[0:wd, j, :], p[0:wd, :], fn)
            for n in range(4):
                f0 = n * 128
                E = Es[n % 2]
                gl = gls[n % 2]
                pr, pk_, pv_, pw = [psA[(pi + i) % 4] for i in range(4)]
                for (p, m, mx) in ((pr, 0, 0), (pk_, 1, 2), (pv_, 2, 3)):
                    for kc in range(8):
                        S.mm(p[:, :], W3[m][:, kc, f0:f0 + 128], xm[mx][:, kc, :], start=(kc == 0), stop=(kc == 7))
                S.copy("act", E["r"][:], pr[:, :])
                S.copy("act", E["k"][:], pk_[:, :])
                S.copy("act", E["v"][:], pv_[:, :])
                S.mm(pw[:, :], L2[:, 0, f0:f0 + 128], lo1[:, 0, :])
                S.act(E["t0"][:], pw[:, :], AF.Exp, bias=pv[:, 6, n:n + 1], scale=-1.0)
                S.ts("dve", E["t0"][:], E["t0"][:], 1.0, None, ALU.add)
                S.act(E["t0"][:], E["t0"][:], AF.Ln)
                S.ts("dve", E["t0"][:], E["t0"][:], -1.0, -0.5, ALU.mult, ALU.add)
                S.act(E["t0"][:], E["t0"][:], AF.Exp)
                S.ts("dve", E["lw"][:], E["t0"][:], -1.0, None, ALU.mult)
                S.scan("dve", E["cum"][:], rmask[:], E["lw"][:], 0.0)
                S.act(E["ep"][:], E["cum"][:], AF.Exp)
                S.act(E["em"][:], E["cum"][:], AF.Exp, scale=-1.0)
                S.tt("pool", E["t1"][:], E["cum"][:], E["lw"][:], ALU.subtract)
                S.act(E["ex"][:], E["t1"][:], AF.Exp)
                S.mm(pw[:, :], L2[:, 1, f0:f0 + 128], lo1[:, 1, :])
                S.act(E["a"][:], pw[:, :], AF.Sigmoid, bias=pv[:, 1, n:n + 1])
                S.ts("dve", E["kk"][:], E["k"][:], pv[:, 2, n:n + 1], None, ALU.mult)
                S.tt("pool", E["t0"][:], E["kk"][:], E["kk"][:], ALU.mult)
                S.mm(C.ps_ss[:, 0:TB], C.blk_f[:], E["t0"][:])
                S.act(E["t0"][:], C.ps_ss[:, 0:TB], AF.Sqrt)
                S.ts("dve", E["t0"][:], E["t0"][:], 1e-12, None, ALU.max)
                S.op("dve", lambda g: g.reciprocal(E["t0"].h[:], E["t0"].h[:]), [E["t0"][:]], [E["t0"][:]])
                S.tt("dve", E["kk"][:], E["kk"][:], E["t0"][:], ALU.mult)
                S.ts("dve", E["t1"][:], E["a"][:], pv[:, 3, n:n + 1], pv[:, 5, n:n + 1], ALU.mult, ALU.add)
                S.tt("dve", E["kmod"][:], E["k"][:], E["t1"][:], ALU.mult)
                S.tt("pool", E["b"][:], E["kk"][:], E["a"][:], ALU.mult)
                S.mm(pw[:, :], L2[:, 2, f0:f0 + 128], lo1[:, 2, :])
                S.copy("act", E["o2"][:], pw[:, :])
                S.dma("pool", R["GG"].r(R["GG"].h[f0:f0 + 128, t0:t0 + TB]), E["o2"][:])
                pi += 4
                S.tt("dve", E["t0"][:], E["r"][:], E["kmod"][:], ALU.mult)
                S.ts("dve", E["t0"][:], E["t0"][:], pv[:, 4, n:n + 1], None, ALU.mult)
                S.mm(C.ps_ss[:, 0:TB], C.blk_f[:], E["t0"][:])
                S.tt("dve", E["o0"][:], C.ps_ss[:, 0:TB], E["v"][:], ALU.mult)
                S.dma("pool", R["BON"].r(R["BON"].h[f0:f0 + 128, t0:t0 + TB]), E["o0"][:])
                S.tt("dve", E["o1"][:], E["r"][:], E["ep"][:], ALU.mult)
                S.dma("pool", R["RR"].r(R["RR"].h[f0:f0 + 128, t0:t0 + TB]), E["o1"][:])
                S.stt("dve", E["o0"][:], E["kk"][:], -1.0, E["ex"][:], ALU.mult, ALU.mult)
                S.dma("pool", R["AR"].r(R["AR"].h[f0:f0 + 128, t0:t0 + TB]), E["o0"][:])
                S.tt("dve", E["kmod"][:], E["kmod"][:], E["em"][:], ALU.mult)
                S.dma("pool", R["KB"].r(R["KB"].h[f0:f0 + 128, t0:t0 + TB]), E["kmod"][:])
                S.tt("dve", E["b"][:], E["b"][:], E["em"][:], ALU.mult)
                S.dma("pool", R["BB"].r(R["BB"].h[f0:f0 + 128, t0:t0 + TB]), E["b"][:])
                S.copy("dve", gl[:], E["ep"].r(E["ep"].h[:, 63:TB:64]))
                S.dma("pool", R["GL"].r(R["GL"].h[f0:f0 + 128, t0 // 64:t0 // 64 + NCH]), gl[:])
                elb = gl.r(gl.h[:, :].unsqueeze(2).to_broadcast([128, NCH, 64]))
                S.tt("dve", E["kmod"].r(E["kmod"].h[:, :].rearrange("p (c t) -> p c t", t=64)),
                     E["kmod"].r(E["kmod"].h[:, :].rearrange("p (c t) -> p c t", t=64)), elb, ALU.mult)
                S.tt("dve", E["b"].r(E["b"].h[:, :].rearrange("p (c t) -> p c t", t=64)),
                     E["b"].r(E["b"].h[:, :].rearrange("p (c t) -> p c t", t=64)), elb, ALU.mult)
                for (src, dst) in ((E["kmod"], R["KT"]), (E["b"], R["BT"]), (E["v"], R["VT"])):
                    for s in range(TB // 128):
                        pt_ = psT[ti % 2]
                        q_ = tq[ti % 2]
                        ti += 1
                        S.transpose(pt_[:, :], src[:, s * 128:(s + 1) * 128], ident[:])
                        S.copy("act", q_[:], pt_[:, :])
                        S.dma("sp", dst.r(dst.h[t0 + s * 128:t0 + (s + 1) * 128, f0:f0 + 128]), q_[:])


def phase_B1(S, C, I, T, R, moT):
    SC = 256
    NCS = SC // 64
    NCH = T // 64
    GN_EPS = 64e-5
    F32R = mybir.dt.float32r
    with Phase(S):
        banks = [S.ps([64, 8, 64], F32, "bk%d" % i) for i in range(7)]
        psE = S.ps([64, 512], F32, "psE")
        bstate = [0]

        def mmr(out, lhsT, rhs, start=True, stop=True):
            return S.mm(out, lhsT, rhs, start=start, stop=stop)

        def bank():
            b = banks[bstate[0] % 7]
            bstate[0] += 1
            return b
        cm = S.sb([64, 4, 8, 64], F32, "cmask")
        S.dma("sp", cm[:], I["rw_masks"][:])
        MS, MI, MST, I8 = [cm.r(cm.h[:, i]) for i in range(4)]
        lng = S.sb([64, 2, 8], F32, "lng")
        S.dma("sp", lng[:], I["od_ln_pk64"][:])
        ones64 = C.ones_f[0:64, 0:64]
        fm = {k: [S.sb([64, 8, SC], F32R, "%s%d" % (k, i)) for i in range(2)] for k in ("AR", "RR", "KB", "BB")}
        tm = {k: [S.sb([64, NCS, 512], F32R, "%s%d" % (k, i)) for i in range(2)] for k in ("KT", "BT", "VT")}
        ep = {k: S.sb([64, 8, SC], F32, k) for k in ("BON", "GG")}
        GLt = S.sb([64, 8, NCH], F32, "GLt")
        ST = S.sb([64, 8, 64], F32R, "ST")
        Y = S.sb([64, 8, SC], F32, "Y")
        Yc = S.sb([64, 8, SC], F32, "Yc")
        Ysq = S.sb([64, 8, SC], F32, "Ysq")
        rstd = S.sb([64, 512], F32, "rstd")
        ob = S.sb([64, 8, SC], BF16, "ob")
        A = {k: S.sb([64, 8, 64], F32R, k) for k in ("N", "NT", "ak", "kr", "br", "Tm", "X", "UT")}
        Mp = [S.sb([64, 8, 64], F32R, "M%d" % i) for i in range(2)]
        MTp = [S.sb([64, 8, 64], F32R, "MT%d" % i) for i in range(2)]
        for g in range(1):
            rows = slice(g * 512, (g + 1) * 512)
            S.dma("sp", GLt[:], R["GL"].r(R["GL"].h[rows, :].rearrange("(h i) c -> i h c", i=64)))
            S.ts("dve", ST[:], I8, 0.0, None, ALU.mult)
            for si, t0 in enumerate(range(0, T, SC)):
                F = {}
                for k in fm:
                    F[k] = fm[k][si % 2]
                    S.dma("sp" if k in ("AR", "KB") else "act", F[k][:],
                          R[k].r(R[k].h[rows, t0:t0 + SC].rearrange("(h i) t -> i h t", i=64).bitcast(F32R)))
                for k in tm:
                    F[k] = tm[k][si % 2]
                    S.dma("sp", F[k][:], R[k].r(R[k].h[t0:t0 + SC, rows].rearrange("(c s) f -> s c f", s=64).bitcast(F32R)))
                for k in ep:
                    S.dma("act", ep[k][:], R[k].r(R[k].h[rows, t0:t0 + SC].rearrange("(h i) t -> i h t", i=64)))
                for c in range(NCS):
                    cs = slice(c * 64, (c + 1) * 64)
                    cg = t0 // 64 + c

                    def hv(name, h):
                        return F[name][:, c, h * 64:(h + 1) * 64]
                    for (dst, l, r_, msk) in ((A["N"], "BB", "AR", MS), (A["NT"], "AR", "BB", MST),
                                              (A["ak"], "KB", "AR", MS), (A["kr"], "KB", "RR", MI),
                                              (A["br"], "BB", "RR", MI)):
                        p = bank()
                        for h in range(8):
                            mmr(p[:, h, :], F[l][:, h, cs], F[r_][:, h, cs])
                        S.tt("dve", dst[:], p[:], msk, ALU.mult)
                    S.tt("dve", A["Tm"][:], A["N"][:], I8, ALU.add)
                    M, MT = A["N"], A["NT"]
                    for lv in range(5):
                        p1, p2 = bank(), bank()
                        for h in range(8):
                            mmr(p1[:, h, :], MT[:, h, :], M[:, h, :])
                            mmr(p2[:, h, :], M[:, h, :], MT[:, h, :])
                        M2, MT2 = Mp[lv % 2], MTp[lv % 2]
                        S.copy("act", M2[:], p1[:])
                        S.copy("dve", MT2[:], p2[:])
                        p3 = bank()
                        for h in range(8):
                            mmr(p3[:, h, :], MT2[:, h, :], A["Tm"][:, h, :])
                        S.tt("dve", A["Tm"][:], A["Tm"][:], p3[:], ALU.add)
                        M, MT = M2, MT2
                    px = bank()
                    for h in range(8):
                        mmr(px[:, h, :], F["AR"][:, h, cs], ST[:, h, :], start=True, stop=False)
                        mmr(px[:, h, :], A["ak"][:, h, :], hv("VT", h), start=False, stop=True)
                    S.copy("act", A["X"][:], px[:])
                    pu = bank()
                    for h in range(8):
                        mmr(pu[:, h, :], A["Tm"][:, h, :], A["X"][:, h, :])
                    S.copy("act", A["UT"][:], pu[:])
                    py = bank()
                    for h in range(8):
                        mmr(py[:, h, :], ST[:, h, :], F["RR"][:, h, cs], start=True, stop=False)
                        mmr(py[:, h, :], hv("VT", h), A["kr"][:, h, :], start=False, stop=False)
                        mmr(py[:, h, :], A["UT"][:, h, :], A["br"][:, h, :], start=False, stop=True)
                    S.copy("act", Y[:, :, cs], py[:])
                    pst = bank()
                    for h in range(8):
                        mmr(pst[:, h, :], hv("KT", h), hv("VT", h), start=True, stop=False)
                        mmr(pst[:, h, :], hv("BT", h), A["UT"][:, h, :], start=False, stop=True)
                    S.tt("dve", ST[:], ST[:], GLt.r(GLt.h[:, :, cg:cg + 1].to_broadcast([64, 8, 64])), ALU.mult)
                    S.tt("dve", ST[:], ST[:], pst[:], ALU.add)
                for q4 in range(8 * SC // 512):
                    hs_ = slice(q4 * (512 // SC), (q4 + 1) * (512 // SC))
                    S.mm(psE[:, :], ones64, Y[:, hs_, :])
                    S.stt("dve", Yc[:, hs_, :], psE[:, :], -1.0 / 64, Y[:, hs_, :], ALU.mult, ALU.add)
                    S.act(Ysq[:, hs_, :], Yc[:, hs_, :], AF.Square)
                    S.mm(psE[:, :], ones64, Ysq[:, hs_, :])
                    rsqrt_to(S, rstd[:], psE[:, :], 1.0 / 64, GN_EPS)
                    S.tt("dve", Yc[:, hs_, :], Yc[:, hs_, :], rstd[:], ALU.mult)
                for h in range(8):
                    hg = g * 8 + h
                    S.act(Yc[:, h, :], Yc[:, h, :], AF.Identity, bias=lng[:, 1, hg:hg + 1], scale=lng[:, 0, hg:hg + 1])
                S.tt("pool", Yc[:], Yc[:], ep["BON"][:], ALU.add)
                S.tt("dve", ob[:], Yc[:], ep["GG"][:], ALU.mult)
                S.dma("pool", moT.r(moT.h[rows, t0:t0 + SC].rearrange("(h i) t -> i h t", i=64)), ob[:])


_NC_CACHE = {}


def kernel(**inputs):
    T = 4096
    if "nc" not in _NC_CACHE:
        _NC_CACHE["nc"] = build(T, "full")
    nc = _NC_CACHE["nc"]
    maps = [host_inputs(inputs, c % 4, c // 4, T) for c in range(8)]
    res = run_bass_kernel_spmd(nc, maps, core_ids=list(range(8)))
    out = np.stack([np.ascontiguousarray(np.asarray(res.results[b]["yT"]).T) for b in range(4)], axis=0)
    return out.astype(np.float32)
```

```python
import contextlib
import math
import numpy as np
import concourse.bass as bass
import concourse.mybir as mybir
from concourse.bass_utils import run_bass_kernel_spmd

F32 = mybir.dt.float32
BF16 = mybir.dt.bfloat16
I32 = mybir.dt.int32
AF = mybir.ActivationFunctionType
ALU = mybir.AluOpType

D = 1024
KC = 8
FFN = 2816
HC = 22
EPS = 1e-6


class _State:
    __slots__ = ("w", "r")

    def __init__(self):
        self.w = None
        self.r = {}


class Tile:
    def __init__(self, h, name):
        self.h = h
        self.name = name
        self.st = {None: _State()}

    def __getitem__(self, idx):
        return Ref(self, self.h[idx], None)

    def sub(self, key, ap):
        return Ref(self, ap, key)

    def r(self, ap):
        return Ref(self, ap, None)

    def states(self, key):
        if key is None:
            return list(self.st.values())
        if key not in self.st:
            self.st[key] = _State()
        return [self.st[key], self.st[None]]

    def wstate(self, key):
        if key not in self.st:
            self.st[key] = _State()
        return self.st[key]


class Ref:
    __slots__ = ("t", "ap", "key")

    def __init__(self, t, ap, key):
        self.t, self.ap, self.key = t, ap, key


class Sched:
    NDMA = 32

    def __init__(self, nc):
        self.nc = nc
        self.es = contextlib.ExitStack()
        self.eng = {"pe": nc.tensor, "act": nc.scalar, "dve": nc.vector,
                    "pool": nc.gpsimd, "sp": nc.sync}
        self.root_es = self.es
        self.semmap = {}
        self.ekey = {}
        self.gen = 0
        self.cnt = {}
        self.known = {}
        for e in self.eng:
            self.known[e] = {}
        self.new_engine_sems()
        self.lastclk = {e: {} for e in self.eng}
        self.dsem = [self.es.enter_context(nc.semaphore("d%d" % i)) for i in range(self.NDMA)]
        self.dtarget = [0] * self.NDMA
        self.dclock = [None] * self.NDMA
        self.dnext = 0
        self.ntile = 0
        self.nwait = 0
        self.ninstr = 0

    def new_engine_sems(self):
        self.gen += 1
        for e in self.eng:
            key = "%s#%d" % (e, self.gen)
            self.ekey[e] = key
            self.semmap[key] = self.root_es.enter_context(self.nc.semaphore("s_%s_%d" % (e, self.gen)))
            self.cnt[e] = 0

    def sb(self, shape, dt, name=None):
        self.ntile += 1
        name = "%s_%d" % (name or "t", self.ntile)
        h = self.es.enter_context(self.nc.sbuf_tensor(name, list(shape), dt))
        return Tile(h, name)

    def ps(self, shape, dt=F32, name=None):
        self.ntile += 1
        name = "%s_%d" % (name or "p", self.ntile)
        h = self.es.enter_context(self.nc.psum_tensor(name, list(shape), dt))
        return Tile(h, name)

    def dram(self, shape, dt, name, kind="Internal"):
        h = self.nc.dram_tensor(name, list(shape), dt, kind=kind).ap()
        return Tile(h, name)

    def _semobj(self, key):
        return self.semmap[key] if isinstance(key, str) else self.dsem[key]

    def _wait(self, e, ev):
        if ev is None:
            return
        key, val, clock = ev
        kn = self.known[e]
        if kn.get(key, 0) >= val:
            return
        self.eng[e].wait_ge(self._semobj(key), val)
        self.nwait += 1
        new = dict(kn)
        if clock:
            for k2, v2 in clock.items():
                if new.get(k2, 0) < v2:
                    new[k2] = v2
        new[key] = val
        self.known[e] = new

    def _deps(self, e, reads, writes, is_dma):
        for rf in reads:
            for st in rf.t.states(rf.key):
                if st.w is not None:
                    self._wait(e, st.w)
        for rf in writes:
            for st in rf.t.states(rf.key):
                if st.w is not None and (is_dma or st.w[0] != self.ekey[e]):
                    self._wait(e, st.w)
                for k, (v, c) in st.r.items():
                    if is_dma or k != self.ekey[e]:
                        self._wait(e, (k, v, c))

    def _record(self, ev, reads, writes):
        key, val, clock = ev
        for rf in reads:
            st = rf.t.wstate(rf.key)
            st.r[key] = (val, clock)
        for rf in writes:
            if rf.key is None:
                for k in list(rf.t.st.keys()):
                    if k is not None:
                        del rf.t.st[k]
            st = rf.t.wstate(rf.key)
            st.w = ev
            st.r = {}

    def op(self, e, fn, reads, writes):
        reads = [r for r in reads if isinstance(r, Ref)]
        self._deps(e, reads, writes, False)
        ins = fn(self.eng[e])
        self.cnt[e] += 1
        ins.then_inc(self.semmap[self.ekey[e]], 1)
        ev = (self.ekey[e], self.cnt[e], self.known[e])
        self.lastclk[e] = self.known[e]
        self._record(ev, reads, writes)
        self.ninstr += 1
        return ev

    def dma(self, q, out, in_, **kw):
        i = self.dnext
        self.dnext = (self.dnext + 1) % self.NDMA
        if self.dtarget[i] > 0:
            self._wait(q, (i, self.dtarget[i], self.dclock[i]))
        self._deps(q, [in_], [out], True)
        ins = self.eng[q].dma_start(out=out.ap, in_=in_.ap, **kw)
        ins.then_inc(self.dsem[i], 16)
        self.dtarget[i] += 16
        self.dclock[i] = self.known[q]
        ev = (i, self.dtarget[i], self.known[q])
        self._record(ev, [in_], [out])
        self.ninstr += 1
        return ev

    def collective(self, kind, alu, groups, in_, out):
        sem = self.root_es.enter_context(self.nc.semaphore("cc%d" % len(self.dsem)))
        self.dsem.append(sem)
        self.dtarget.append(0)
        self.dclock.append(None)
        i = len(self.dsem) - 1
        self._deps("pool", [in_], [out], True)
        ins = self.eng["pool"].collective_compute(kind, alu, replica_groups=groups,
                                                  ins=[in_.ap.opt()], outs=[out.ap.opt()])
        ins.then_inc(sem, 1)
        self.dtarget[i] = 1
        self.dclock[i] = self.known["pool"]
        ev = (i, 1, self.known["pool"])
        self._record(ev, [in_], [out])
        self.ninstr += 1
        return ev

    def wait_all(self, e, refs):
        for rf in refs:
            for st in rf.t.states(rf.key):
                self._wait(e, st.w)

    def close(self):
        self.es.close()
        if self.root_es is not self.es:
            self.root_es.close()

    def mm(self, out, lhsT, rhs, start=True, stop=True):
        return self.op("pe", lambda g: g.matmul(out.ap, lhsT.ap, rhs.ap, start=start, stop=stop),
                       [lhsT, rhs], [out])

    def transpose(self, out, in_, ident):
        return self.op("pe", lambda g: g.transpose(out.ap, in_.ap, ident.ap), [in_, ident], [out])

    def act(self, out, in_, func, bias=None, scale=None, e="act"):
        kw = {}
        if bias is not None:
            kw["bias"] = bias.ap if isinstance(bias, Ref) else bias
        if scale is not None:
            kw["scale"] = scale.ap if isinstance(scale, Ref) else scale
        return self.op(e, lambda g: g.activation(out.ap, in_.ap, func, **kw),
                       [in_, bias, scale], [out])

    def tt(self, e, out, in0, in1, op):
        return self.op(e, lambda g: g.tensor_tensor(out.ap, in0.ap, in1.ap, op), [in0, in1], [out])

    def ts(self, e, out, in0, s1, s2, op0, op1=None):
        a1 = s1.ap if isinstance(s1, Ref) else s1
        a2 = s2.ap if isinstance(s2, Ref) else s2
        if op1 is None:
            return self.op(e, lambda g: g.tensor_scalar(out.ap, in0.ap, a1, None, op0), [in0, s1], [out])
        return self.op(e, lambda g: g.tensor_scalar(out.ap, in0.ap, a1, a2, op0, op1), [in0, s1, s2], [out])

    def stt(self, e, out, in0, sc, in1, op0, op1):
        a = sc.ap if isinstance(sc, Ref) else sc
        return self.op(e, lambda g: g.scalar_tensor_tensor(out.ap, in0.ap, a, in1.ap, op0, op1),
                       [in0, sc, in1], [out])

    def scan(self, e, out, d0, d1, init, op0=ALU.mult, op1=ALU.add):
        a = init.ap if isinstance(init, Ref) else init
        return self.op(e, lambda g: g.tensor_tensor_scan(out.ap, d0.ap, d1.ap, a, op0, op1),
                       [d0, d1, init], [out])

    def copy(self, e, out, in_):
        if e == "act":
            return self.op(e, lambda g: g.copy(out.ap, in_.ap), [in_], [out])
        return self.op(e, lambda g: g.tensor_copy(out.ap, in_.ap), [in_], [out])

    def memset(self, e, out, val):
        return self.op(e, lambda g: g.memset(out.ap, val), [], [out])


def _barrier(S):
    for e in S.eng:
        for f in S.eng:
            if f != e and S.cnt[f] > 0:
                S._wait(e, (S.ekey[f], S.cnt[f], S.lastclk[f]))
        for i in range(len(S.dsem)):
            if S.dtarget[i] > 0:
                S._wait(e, (i, S.dtarget[i], S.dclock[i]))


class Phase:
    def __init__(self, S):
        self.S = S

    def __enter__(self):
        self.saved = self.S.es
        self.S.es = contextlib.ExitStack()
        return self

    def __exit__(self, *a):
        _barrier(self.S)
        self.S.es.close()
        self.S.es = self.saved
        if max(self.S.cnt.values()) > 16000:
            self.S.new_engine_sems()
        return False


def pk(v):
    v = np.asarray(v, dtype=np.float32)
    lead = v.shape[:-1]
    n = v.shape[-1] // 128
    v = v.reshape(lead + (n, 128))
    v = np.moveaxis(v, -1, 0)
    return np.ascontiguousarray(v.reshape(128, -1))


class Ctx:
    pass


def load_cast(S, C, dram_ref, shape, q="sp", ceng="pool", name="w"):
    st = S.sb(shape, F32, name + "_st")
    S.dma(q, st[:], dram_ref)
    wt = S.sb(shape, BF16, name + "_bf")
    S.copy(ceng, wt[:], st[:])
    return wt


def setup_consts(S, C):
    C.ones_bf = S.sb([128, 128], BF16, "ones")
    S.memset("pool", C.ones_bf[:], 1.0)
    C.ones_f = S.sb([128, 128], F32, "onesf")
    S.memset("pool", C.ones_f[:], 1.0)
    C.blk_f = S.sb([128, 128], F32, "blkf")
    S.memset("pool", C.blk_f[:], 0.0)
    S.memset("pool", C.blk_f[0:64, 0:64], 1.0)
    S.memset("pool", C.blk_f[64:128, 64:128], 1.0)
    C.blk_bf = S.sb([128, 128], BF16, "blkbf")
    S.copy("pool", C.blk_bf[:], C.blk_f[:])


def setup_adaln(S, C, I):
    C.mod = S.sb([128, 96], F32, "mod")
    C.gsm = S.sb([128, 16], F32, "gsm")
    C.gsf = S.sb([128, 16], F32, "gsf")
    with Phase(S):
        cact = S.sb([128, 8], F32, "cact")
        S.dma("sp", cact[:], I["c_pk"][:])
        S.act(cact[:], cact[:], AF.Silu)
        bada = S.sb([128, 96], F32, "bada")
        S.dma("sp", bada[:], I["b_ada_pk"][:])
        nm = S.sb([128, 16], F32, "nm")
        S.dma("sp", nm[:], I["norm_mix_pk"][:])
        nf = S.sb([128, 16], F32, "nf")
        S.dma("sp", nf[:], I["norm_ffn_pk"][:])
        pm = S.ps([128, 96], F32, "pmod")
        wt = [S.sb([128, 8, 512], F32, "wada%d" % i) for i in range(2)]
        it = 0
        for l in range(2):
            wl = I["w_ada"].h[l].rearrange("(kc p) n -> p kc n", p=128)
            for ng in range(12):
                w = wt[it % 2]
                it += 1
                S.dma("sp" if it % 2 else "act", w[:], I["w_ada"].r(wl[:, :, ng * 512:(ng + 1) * 512]))
                for j in range(4):
                    col = l * 48 + ng * 4 + j
                    for kc in range(8):
                        S.mm(pm[:, col:col + 1], w[:, kc, j * 128:(j + 1) * 128], cact[:, kc:kc + 1],
                             start=(kc == 0), stop=(kc == 7))
        S.tt("dve", C.mod[:], pm[:], bada[:], ALU.add)
        for l in range(2):
            S.stt("dve", C.gsm[:, l * 8:(l + 1) * 8], C.mod[:, l * 48 + 8:l * 48 + 16], 1.0,
                  nm[:, l * 8:(l + 1) * 8], ALU.add, ALU.mult)
            S.stt("dve", C.gsf[:, l * 8:(l + 1) * 8], C.mod[:, l * 48 + 32:l * 48 + 40], 1.0,
                  nf[:, l * 8:(l + 1) * 8], ALU.add, ALU.mult)


def rsqrt_to(S, out, in_, scale, eps):
    S.ts("dve", out, in_, scale, eps, ALU.mult, ALU.add)
    S.act(out, out, AF.Sqrt)
    S.op("dve", lambda g: g.reciprocal(out.ap, out.ap), [out], [out])


def modcol(C, l, part, kc):
    c = l * 48 + part * 8 + kc
    return C.mod[:, c:c + 1]


def rmsnorm_mod(S, C, x, h, gs, l, shift_part, W, tmp, sq, rstd):
    for kc in range(8):
        S.act(sq[:, kc, :], x[:, kc, :], AF.Square)
    for s0 in range(0, W, 512):
        ss = C.ps_ss
        for kc in range(8):
            S.mm(ss[:, :], C.ones_bf[:], sq[:, kc, s0:s0 + 512], start=(kc == 0), stop=(kc == 7))
        rsqrt_to(S, rstd[:, s0:s0 + 512], ss[:, :], 1.0 / D, EPS)
    for kc in range(8):
        e = "dve" if kc % 2 == 0 else "pool"
        S.tt(e, tmp[kc % 2][:, :], x[:, kc, :], rstd[:, :], ALU.mult)
        S.act(h[:, kc, :], tmp[kc % 2][:, :], AF.Identity, bias=modcol(C, l, shift_part, kc),
              scale=gs[:, l * 8 + kc:l * 8 + kc + 1])


def phase_F(S, C, I, l, T, mo, xin, xmid, xout, wo, PD, PDS, groups, WS, TBS=1024):
    HCL = HC // 2
    wg = I["ffn_w_gate_h"].h[l].rearrange("(kc p) n -> p kc n", p=128)
    wu = I["ffn_w_up_h"].h[l].rearrange("(kc p) n -> p kc n", p=128)
    wd = I["ffn_w_down_h"].h[l].rearrange("(j p) n -> p j n", p=128)
    wov = wo.rearrange("(kc p) n -> p kc n", p=128)
    NS = TBS // 512
    with Phase(S):
        C.ps_ss = S.ps([128, 512], F32, "ss")
        psA = [S.ps([128, 512], F32, "psA%d" % i) for i in range(3)]
        psB = [S.ps([128, 512], F32, "psB%d" % i) for i in range(3)]
        x = S.sb([128, 8, TBS], F32, "x")
        mot = S.sb([128, 8, TBS], BF16, "mo")
        sq = S.sb([128, 8, TBS], BF16, "sq")
        h = S.sb([128, 8, TBS], BF16, "h")
        a = S.sb([128, HCL, TBS], BF16, "a")
        rstd = S.sb([128, TBS], F32, "rstd")
        tmp = [S.sb([128, TBS], F32, "tmp%d" % i) for i in range(2)]
        sg = [S.sb([128, 512], F32, "sg%d" % i) for i in range(2)]
        pdt = [S.sb([128, 512], F32, "pdt%d" % i) for i in range(3)]
        wst = [S.sb([128, 8, 128], F32, "wst%d" % i) for i in range(4)]
        wbf = [S.sb([128, 8, 128], BF16, "wbf%d" % i) for i in range(4)]
        wdst = [S.sb([128, HCL, 128], F32, "wdst%d" % i) for i in range(2)]
        wdbf = [S.sb([128, HCL, 128], BF16, "wdbf%d" % i) for i in range(2)]
        wi = 0
        pi = 0
        di = 0

        def getw(kind, idx, src, st, bf, q, sb):
            view = WS[kind].h[idx].rearrange("p (a b) -> p a b", b=128)
            if sb == 0:
                S.dma(q, st[:], Ref(I["_w"], src, None))
                S.copy("pool", bf[:], st[:])
                S.dma("pool", WS[kind].sub(idx, view), bf[:])
            else:
                S.dma(q, bf[:], WS[kind].sub(idx, view))
        for t0 in range(0, T, TBS):
            sb = t0 // TBS
            S.dma("sp", x[:], xin.r(xin.h[:, t0:t0 + TBS].rearrange("(kc p) t -> p kc t", p=128)))
            S.dma("act", mot[:], mo.r(mo.h[:, t0:t0 + TBS].rearrange("(kc p) t -> p kc t", p=128)))
            for n in range(8):
                k = wi % 4
                wi += 1
                getw("o", n, wov[:, :, n * 128:(n + 1) * 128], wst[k], wbf[k], "sp", sb)
                for s in range(NS):
                    p = psA[pi % 3]
                    pi += 1
                    for kc in range(8):
                        S.mm(p[:, :], wbf[k][:, kc, :], mot[:, kc, s * 512:(s + 1) * 512],
                             start=(kc == 0), stop=(kc == 7))
                    S.stt("dve", x[:, n, s * 512:(s + 1) * 512], p[:, :], modcol(C, l, 2, n),
                          x[:, n, s * 512:(s + 1) * 512], ALU.mult, ALU.add)
            S.dma("act", xmid.r(xmid.h[:, t0:t0 + TBS].rearrange("(kc p) t -> p kc t", p=128)), x[:])
            rmsnorm_mod(S, C, x, h, C.gsf, l, 3, TBS, tmp, sq, rstd)
            for j in range(HCL):
                k0 = wi % 4
                k1 = (wi + 1) % 4
                wi += 2
                getw("g", j, wg[:, :, j * 128:(j + 1) * 128], wst[k0], wbf[k0], "sp", sb)
                getw("u", j, wu[:, :, j * 128:(j + 1) * 128], wst[k1], wbf[k1], "act", sb)
                for s in range(NS):
                    pg = psA[pi % 3]
                    pu = psB[pi % 3]
                    pi += 1
                    for kc in range(8):
                        S.mm(pg[:, :], wbf[k0][:, kc, :], h[:, kc, s * 512:(s + 1) * 512],
                             start=(kc == 0), stop=(kc == 7))
                    for kc in range(8):
                        S.mm(pu[:, :], wbf[k1][:, kc, :], h[:, kc, s * 512:(s + 1) * 512],
                             start=(kc == 0), stop=(kc == 7))
                    g = sg[pi % 2]
                    S.act(g[:, :], pg[:, :], AF.Silu)
                    S.tt("dve", a[:, j, s * 512:(s + 1) * 512], g[:, :], pu[:, :], ALU.mult)
            for n in range(8):
                k = n % 2
                getw("d", n, wd[:, :, n * 128:(n + 1) * 128], wdst[k], wdbf[k], "sp" if n % 2 == 0 else "act", sb)
                for s in range(NS):
                    p = psA[pi % 3]
                    pi += 1
                    for j in range(HCL):
                        S.mm(p[:, :], wdbf[k][:, j, :], a[:, j, s * 512:(s + 1) * 512],
                             start=(j == 0), stop=(j == HCL - 1))
                    o = pdt[di % 3]
                    di += 1
                    S.copy("act", o[:], p[:, :])
                    S.dma("pool", PD.sub(sb, PD.h[sb, n * 128:(n + 1) * 128, s * 512:(s + 1) * 512]), o[:])
            for hh_ in range(2):
                S.collective("AllReduce", ALU.add, groups, PD.sub(sb, PD.h[sb, hh_ * 512:(hh_ + 1) * 512, :]),
                             PDS.sub(sb, PDS.h[sb, hh_ * 512:(hh_ + 1) * 512, :]))
    with Phase(S):
        xs = [S.sb([128, 8, 512], F32, "xs%d" % i) for i in range(2)]
        ps_ = [S.sb([128, 8, 512], F32, "pds%d" % i) for i in range(2)]
        for bi, t0 in enumerate(range(0, T, 512)):
            xx, pp = xs[bi % 2], ps_[bi % 2]
            S.dma("sp", xx[:], xmid.r(xmid.h[:, t0:t0 + 512].rearrange("(kc p) t -> p kc t", p=128)))
            sb, so = t0 // TBS, t0 % TBS
            S.dma("act", pp[:], PDS.r(PDS.h[sb, :, so:so + 512].rearrange("(kc p) t -> p kc t", p=128)))
            for n in range(8):
                S.stt("dve", xx[:, n, :], pp[:, n, :], modcol(C, l, 5, n), xx[:, n, :], ALU.mult, ALU.add)
            S.dma("pool", xout.r(xout.h[:, t0:t0 + 512].rearrange("(kc p) t -> p kc t", p=128)), xx[:])


def declare_inputs(nc, specs):
    I = {}
    for name, (shape, dt) in specs.items():
        ap = nc.dram_tensor(name, list(shape), dt, kind="ExternalInput").ap()
        I[name] = Tile(ap, name)
    I["_w"] = Tile(None, "_w")
    return I


def input_specs(T):
    HH = FFN // 2
    sp = {
        "xT": ([D, T], F32), "c_pk": ([128, 8], F32),
        "w_ada": ([2, D, 6 * D], F32), "b_ada_pk": ([128, 96], F32),
        "norm_mix_pk": ([128, 16], F32), "norm_ffn_pk": ([128, 16], F32),
        "ffn_w_gate_h": ([2, D, HH], F32), "ffn_w_up_h": ([2, D, HH], F32),
        "ffn_w_down_h": ([2, HH, D], F32),
        "ev_w_out_p": ([D, D], F32), "od_w_o": ([D, D], F32),
        "ev_w_in_h": ([D, 1024], F32), "ev_s5_w_glu_h": ([512, 256], F32),
        "pos": ([T], I32), "rotm": ([128, 128], F32), "invf_col": ([128, 1], F32),
        "qk_gain_col": ([128, 2], F32), "masks": ([128, 4, 512], BF16),
        "lam_vecs": ([4, 64], F32), "subln_col": ([128, 1], F32),
        "swapm": ([128, 128], F32), "rowmask": ([128, 8], F32), "s5_d_pk": ([128, 2], F32),
        "lamre_row": ([128, 2, 64], F32), "lamim_row": ([128, 2, 64], F32), "logdt_row": ([128, 2], F32),
        "bT_re": ([128, 2, 64], F32), "bT_im": ([128, 2, 64], F32),
        "lamre_col": ([128, 16], F32), "lamim_col": ([128, 16], F32), "logdt_col": ([128, 16], F32),
        "cT_re_col": ([128, 16, 16], F32), "cT_im_col": ([128, 16, 16], F32),
        "od_w_r_h": ([D, 512], F32), "od_w_k_h": ([D, 512], F32), "od_w_v_h": ([D, 512], F32),
        "od_w1": ([1, D, 64], F32), "od_a1": ([1, D, 64], F32), "od_g1": ([1, D, 128], F32),
        "od_w2_h": ([64, 512], F32), "od_a2_h": ([64, 512], F32), "od_g2_h": ([128, 512], F32),
        "od_vecs_pk": ([128, 5, 4], F32), "od_mu_pk": ([128, 48], F32), "ident": ([128, 128], F32),
        "rw_masks": ([64, 4, 8, 64], F32), "od_ln_pk64": ([64, 2, 8], F32),
    }
    return sp


GROUPS = [[0, 4], [1, 5], [2, 6], [3, 7]]


def build(T=4096, mode="full"):
    nc = bass.Bass("TRN2", target_bir_lowering=False)
    I = declare_inputs(nc, input_specs(T))
    out = Tile(nc.dram_tensor("yT", [D, T], F32, kind="ExternalOutput").ap(), "yT")
    S = Sched(nc)
    C = Ctx()
    setup_consts(S, C)
    setup_adaln(S, C, I)
    dr = lambda shape, dt, name: S.dram(shape, dt, name, "Internal")
    uT = dr([256, T], BF16, "uT")
    qT = dr([256, T], BF16, "qT")
    kT = dr([256, T], BF16, "kT")
    vtm = dr([T, 256], BF16, "vtm")
    ygT = dr([256, T], BF16, "ygT")
    ygF = dr([512, T], BF16, "ygF")
    moL = dr([512, T], BF16, "moL")
    mo0 = dr([D, T], BF16, "mo0")
    phase_A0(S, C, I, T, I["xT"], uT, qT, kT, vtm)
    phase_S5(S, C, I, T, uT, ygT)
    S.collective("AllGather", ALU.bypass, GROUPS, ygT[:], ygF[:])
    phase_glu(S, C, I, T, ygF, ygT, moL)
    phase_attn(S, C, I, T, qT, kT, vtm, moL, 0.8 - 0.6 * math.exp(-0.3 * 0))
    for i in range(2):
        S.collective("AllGather", ALU.bypass, GROUPS, moL.r(moL.h[i * 256:(i + 1) * 256, :]),
                     mo0.r(mo0.h[i * 512:(i + 1) * 512, :]))
    xm0 = dr([D, T], F32, "xm0")
    x1 = dr([D, T], F32, "x1T")
    PD0 = dr([T // 1024, D, 1024], F32, "PD0")
    PS0 = dr([T // 1024, D, 1024], F32, "PS0")
    mkws = lambda l: {"o": dr([8, 128, 1024], BF16, "WSo%d" % l), "g": dr([HC // 2, 128, 1024], BF16, "WSg%d" % l),
                      "u": dr([HC // 2, 128, 1024], BF16, "WSu%d" % l), "d": dr([8, 128, (HC // 2) * 128], BF16, "WSd%d" % l)}
    phase_F(S, C, I, 0, T, mo0, I["xT"], xm0, x1, I["ev_w_out_p"].h, PD0, PS0, GROUPS, mkws(0))
    R = {k: dr([512, T], F32, k) for k in ("AR", "RR", "KB", "BB", "BON", "GG")}
    for k in ("KT", "BT", "VT"):
        R[k] = dr([T, 512], F32, k)
    R["GL"] = dr([512, T // 64], F32, "GL")
    mo1L = dr([512, T], BF16, "mo1L")
    mo1 = dr([D, T], BF16, "mo1")
    phase_B0(S, C, I, T, x1, R)
    phase_B1(S, C, I, T, R, mo1L)
    for i in range(2):
        S.collective("AllGather", ALU.bypass, GROUPS, mo1L.r(mo1L.h[i * 256:(i + 1) * 256, :]),
                     mo1.r(mo1.h[i * 512:(i + 1) * 512, :]))
    xm1 = dr([D, T], F32, "xm1")
    PD1 = dr([T // 1024, D, 1024], F32, "PD1")
    PS1 = dr([T // 1024, D, 1024], F32, "PS1")
    phase_F(S, C, I, 1, T, mo1, x1, xm1, out, I["od_w_o"].h, PD1, PS1, GROUPS, mkws(1))
    S.wait_all("sp", [out[:]])
    _barrier(S)
    print("instrs", S.ninstr, "waits", S.nwait)
    S.close()
    return nc


def host_inputs(inp, b, hf, T):
    f = lambda a: np.ascontiguousarray(np.asarray(a, dtype=np.float32))
    HH = FFN // 2
    hs = slice(hf * HH, (hf + 1) * HH)
    q4 = slice(hf * 256, (hf + 1) * 256)
    win = f(inp["ev_w_in"])[0]
    wout = f(inp["ev_w_out"])[0]
    perm = np.concatenate([np.arange(0, 256), np.arange(512, 768), np.arange(256, 512), np.arange(768, 1024)])
    m = {
        "xT": np.ascontiguousarray(f(inp["x"][b, :T]).T),
        "c_pk": pk(f(inp["c"])[b]),
        "w_ada": f(inp["w_ada"]),
        "b_ada_pk": pk(f(inp["b_ada"]).reshape(-1)),
        "norm_mix_pk": pk(f(inp["norm_mix"]).reshape(-1)),
        "norm_ffn_pk": pk(f(inp["norm_ffn"]).reshape(-1)),
        "ffn_w_gate_h": f(f(inp["ffn_w_gate"])[:, :, hs]), "ffn_w_up_h": f(f(inp["ffn_w_up"])[:, :, hs]),
        "ffn_w_down_h": f(f(inp["ffn_w_down"])[:, hs, :]),
        "ev_w_out_p": f(wout), "od_w_o": f(f(inp["od_w_o"])[0][perm, :]),
        "ev_w_in_h": f(np.concatenate([win[:, 0 + hf * 256:0 + (hf + 1) * 256], win[:, 512 + hf * 256:512 + (hf + 1) * 256],
                                       win[:, 1024 + hf * 256:1024 + (hf + 1) * 256], win[:, 1536 + hf * 256:1536 + (hf + 1) * 256]], axis=1)),
        "ev_s5_w_glu_h": f(f(inp["ev_s5_w_glu"])[0][:, q4]),
        "pos": np.ascontiguousarray(np.asarray(inp["positions"])[b, :T].astype(np.int32)),
    }
    m.update(const_inputs())
    qg = f(inp["ev_q_norm"])[0]; kg = f(inp["ev_k_norm"])[0]
    m["qk_gain_col"] = np.ascontiguousarray(np.stack([np.tile(qg, 2), np.tile(kg, 2)], axis=1))
    m["lam_vecs"] = np.ascontiguousarray(np.stack([f(inp["ev_lambda_q1"])[0], f(inp["ev_lambda_k1"])[0],
                                                   f(inp["ev_lambda_q2"])[0], f(inp["ev_lambda_k2"])[0]]))
    m["subln_col"] = np.ascontiguousarray(f(inp["ev_subln"])[0].reshape(128, 1))
    gs = slice(hf * 16, (hf + 1) * 16)
    m["s5_d_pk"] = pk(f(inp["ev_s5_d"])[0][gs].reshape(-1))
    lre = f(inp["ev_s5_lam_re"])[0][gs]; lim = f(inp["ev_s5_lam_im"])[0][gs]; ldt = f(inp["ev_s5_log_dt"])[0][gs]
    rowg = (np.arange(128) // 16)[:, None] + 8 * np.arange(2)[None, :]
    m["lamre_row"] = np.ascontiguousarray(lre[rowg]); m["lamim_row"] = np.ascontiguousarray(lim[rowg])
    m["logdt_row"] = np.ascontiguousarray(ldt[rowg])
    hh = (np.arange(128) % 16)
    bre = f(inp["ev_s5_b_re"])[0][gs]; bim = f(inp["ev_s5_b_im"])[0][gs]
    m["bT_re"] = np.ascontiguousarray(bre[rowg, :, hh[:, None]]); m["bT_im"] = np.ascontiguousarray(bim[rowg, :, hh[:, None]])
    m["lamre_col"] = np.ascontiguousarray(np.concatenate([lre.T, lre.T], 0)); m["lamim_col"] = np.ascontiguousarray(np.concatenate([lim.T, lim.T], 0))
    m["logdt_col"] = np.ascontiguousarray(np.broadcast_to(ldt[None, :], (128, 16)))
    cre = np.transpose(f(inp["ev_s5_c_re"])[0][gs], (2, 0, 1)); cim = np.transpose(f(inp["ev_s5_c_im"])[0][gs], (2, 0, 1))
    m["cT_re_col"] = np.ascontiguousarray(np.concatenate([cre, cre], 0)); m["cT_im_col"] = np.ascontiguousarray(np.concatenate([cim, cim], 0))
    fs = slice(hf * 512, (hf + 1) * 512)
    for nm in ("od_w_r", "od_w_k", "od_w_v"):
        m[nm + "_h"] = f(f(inp[nm])[0][:, fs])
    for nm in ("od_w1", "od_a1", "od_g1"):
        m[nm] = f(inp[nm])
    for nm in ("od_w2", "od_a2", "od_g2"):
        m[nm + "_h"] = f(f(inp[nm])[0][:, fs])
    vecs = np.stack([f(inp["od_w0"])[0], f(inp["od_a0"])[0], f(inp["od_k_k"])[0], f(inp["od_k_a"])[0],
                     f(inp["od_r_k"])[0].reshape(-1)])[:, fs]
    m["od_vecs_pk"] = np.ascontiguousarray(pk(vecs).reshape(128, 5, 4))
    m["od_mu_pk"] = pk(f(inp["od_mu"])[0])
    ln = np.stack([f(inp["od_ln_g"])[0], f(inp["od_ln_b"])[0]])[:, fs]
    m["od_ln_pk64"] = np.ascontiguousarray(np.transpose(ln.reshape(2, 8, 64), (2, 0, 1)))
    return m


def const_inputs():
    import ml_dtypes
    c = {}
    rotm = np.zeros((128, 128), np.float32)
    for base in (0, 64):
        for i in range(8):
            rotm[base + i + 8, base + i] = -1.0
            rotm[base + i, base + i + 8] = 1.0
    c["rotm"] = rotm
    invf = np.zeros((128, 1), np.float32)
    fr = (500000.0 ** (-np.arange(0, 16, 2, dtype=np.float32) / 16)).astype(np.float32)
    for base in (0, 64):
        invf[base:base + 8, 0] = fr
        invf[base + 8:base + 16, 0] = fr
    c["invf_col"] = invf
    kk = np.arange(128)[:, None]; qq = np.arange(512)[None, :]
    c["masks"] = np.stack([(qq >= 128 * j + kk) for j in range(4)], axis=1).astype(np.float32).astype(ml_dtypes.bfloat16)
    sw = np.zeros((128, 128), np.float32)
    for p in range(64):
        sw[64 + p, p] = 1.0
        sw[p, 64 + p] = -1.0
    c["swapm"] = sw
    c["ident"] = np.eye(128, dtype=np.float32)
    ss = np.arange(64)[:, None]; tt = np.arange(64)[None, :]
    m4 = np.stack([(ss < tt), (ss <= tt), (tt < ss), (ss == tt)]).astype(np.float32)
    c["rw_masks"] = np.ascontiguousarray(np.broadcast_to(np.transpose(m4, (1, 0, 2))[:, :, None, :], (64, 4, 8, 64)))
    c["rowmask"] = (np.arange(128)[:, None] // 16 == np.arange(8)[None, :]).astype(np.float32)
    return c


TWO_PI = 2.0 * math.pi


def sincos(S, sin_out, cos_out, ang, tmps):
    y, kf, ki = tmps
    for out, off in ((sin_out, 0.5), (cos_out, 0.75)):
        if out is None:
            continue
        S.ts("dve", y, ang, 1.0 / TWO_PI, off, ALU.mult, ALU.add)
        S.copy("dve", ki, y)
        S.copy("dve", kf, ki)
        S.tt("dve", y, y, kf, ALU.subtract)
        S.ts("dve", kf, y, 0.0, None, ALU.is_lt)
        S.tt("dve", y, y, kf, ALU.add)
        S.ts("dve", y, y, TWO_PI, -math.pi, ALU.mult, ALU.add)
        S.act(out, y, AF.Sin)


def phase_A0(S, C, I, T, xin, uT, qT, kT, vtm):
    win_v = I["ev_w_in_h"].h.rearrange("(kc p) n -> p kc n", p=128)
    with Phase(S):
        C.ps_ss = S.ps([128, 512], F32, "ss")
        psA = [S.ps([128, 512], F32, "psA%d" % i) for i in range(3)]
        psR = S.ps([128, 512], F32, "psR")
        win = S.sb([128, 8, 1024], BF16, "win")
        wst = [S.sb([128, 8, 256], F32, "wst%d" % i) for i in range(2)]
        for i in range(4):
            S.dma("sp" if i % 2 == 0 else "act", wst[i % 2][:], Ref(I["_w"], win_v[:, :, i * 256:(i + 1) * 256], None))
            S.copy("pool", win[:, :, i * 256:(i + 1) * 256], wst[i % 2][:])
        rotm = S.sb([128, 128], F32, "rotm")
        S.dma("sp", rotm[:], I["rotm"][:])
        invf = S.sb([128, 1], F32, "invf")
        S.dma("sp", invf[:], I["invf_col"][:])
        gq = S.sb([128, 2], F32, "gq")
        S.dma("sp", gq[:], I["qk_gain_col"][:])
        S.ts("dve", gq[:, 0:1], gq[:, 0:1], 0.125, None, ALU.mult)
        x = S.sb([128, 8, 512], F32, "x")
        sq = S.sb([128, 8, 512], BF16, "sq")
        h = S.sb([128, 8, 512], BF16, "h")
        rstd = S.sb([128, 512], F32, "rstd")
        tmp = [S.sb([128, 512], F32, "tmp%d" % i) for i in range(2)]
        posi = S.sb([128, 512], I32, "posi")
        ang = S.sb([128, 512], F32, "ang")
        cosT = S.sb([128, 512], F32, "cosT")
        sinT = S.sb([128, 512], F32, "sinT")
        sq2 = S.sb([128, 512], F32, "sq2")
        qn = S.sb([128, 512], F32, "qn")
        r2 = S.sb([128, 512], F32, "r2")
        ob = [S.sb([128, 512], BF16, "ob%d" % i) for i in range(3)]
        oi = 0
        pi = 0
        for t0 in range(0, T, 512):
            S.dma("sp", x[:], xin.r(xin.h[:, t0:t0 + 512].rearrange("(kc p) t -> p kc t", p=128)))
            S.dma("act", posi[:], I["pos"].r(I["pos"].h[t0:t0 + 512].partition_broadcast(128)))
            S.copy("dve", ang[:], posi[:])
            S.ts("dve", ang[:], ang[:], invf[:, 0:1], None, ALU.mult)
            sincos(S, sinT[:], cosT[:], ang[:], (tmp[0][:], tmp[1][:], posi[:]))
            rmsnorm_mod(S, C, x, h, C.gsm, 0, 0, 512, tmp, sq, rstd)
            for n in range(6):
                p = psA[pi % 3]
                pi += 1
                for kc in range(8):
                    S.mm(p[:, :], win[:, kc, n * 128:(n + 1) * 128], h[:, kc, :], start=(kc == 0), stop=(kc == 7))
                o = ob[oi % 3]
                oi += 1
                if n < 2:
                    S.copy("act", o[:], p[:, :])
                    S.dma("pool", uT.r(uT.h[n * 128:(n + 1) * 128, t0:t0 + 512]), o[:])
                    continue
                isq = n < 4
                S.act(sq2[:], p[:, :], AF.Square)
                S.mm(C.ps_ss[:, :], C.blk_f[:], sq2[:])
                rsqrt_to(S, rstd[:], C.ps_ss[:, :], 1.0 / 64, EPS)
                S.tt("dve", qn[:], p[:, :], rstd[:], ALU.mult)
                S.ts("dve", qn[:], qn[:], gq[:, 0:1] if isq else gq[:, 1:2], None, ALU.mult)
                S.mm(psR[:, :], rotm[:], qn[:])
                S.tt("dve", r2[:], psR[:, :], sinT[:], ALU.mult)
                S.tt("pool", qn[:], qn[:], cosT[:], ALU.add if False else ALU.mult)
                S.tt("pool", o[:], qn[:], r2[:], ALU.add)
                dst = qT if isq else kT
                hd = (n - 2) % 2
                S.dma("pool", dst.r(dst.h[hd * 128:(hd + 1) * 128, t0:t0 + 512]), o[:])
            for tt_ in range(4):
                p = psA[pi % 3]
                pi += 1
                for kc in range(8):
                    S.mm(p[:, 0:256], h[:, kc, tt_ * 128:(tt_ + 1) * 128], win[:, kc, 768:1024],
                         start=(kc == 0), stop=(kc == 7))
                o = ob[oi % 3]
                oi += 1
                S.copy("act", o[:, 0:256], p[:, 0:256])
                S.dma("pool", vtm.r(vtm.h[t0 + tt_ * 128:t0 + (tt_ + 1) * 128, :]), o[:, 0:256])


def phase_attn(S, C, I, T, qT, kT, vtm, moT, lam_init):
    NQ = T // 512
    NK = T // 128
    with Phase(S):
        C.ps_ss = S.ps([128, 512], F32, "ss")
        psS = [S.ps([128, 512], F32, "psS%d" % i) for i in range(3)]
        psO = [S.ps([128, 512], F32, "psO%d" % i) for i in range(2)]
        psZ = [S.ps([128, 512], F32, "psZ%d" % i) for i in range(2)]
        masks = S.sb([128, 4, 512], BF16, "masks")
        S.dma("sp", masks[:], I["masks"][:])
        lv = S.sb([128, 4, 64], F32, "lv")
        S.dma("sp", lv[:], I["lam_vecs"].r(I["lam_vecs"].h[:, :].partition_broadcast(128)))
        lp = S.sb([128, 2, 64], F32, "lp")
        S.tt("dve", lp[:, 0, :], lv[:, 0, :], lv[:, 1, :], ALU.mult)
        S.tt("dve", lp[:, 1, :], lv[:, 2, :], lv[:, 3, :], ALU.mult)
        ls = S.sb([128, 2], F32, "ls")
        S.op("dve", lambda g: g.reduce_sum(ls.h[:, :], lp.h[:, :, :], axis=mybir.AxisListType.X), [lp[:]], [ls[:]])
        S.act(ls[:], ls[:], AF.Exp)
        nlam = S.sb([128, 1], F32, "nlam")
        S.tt("dve", nlam[:], ls[:, 1:2], ls[:, 0:1], ALU.subtract)
        S.ts("dve", nlam[:], nlam[:], -lam_init, None, ALU.add)
        sg = S.sb([128, 1], F32, "sg")
        S.dma("sp", sg[:], I["subln_col"][:])
        S.ts("dve", sg[:], sg[:], 1.0 - lam_init, None, ALU.mult)
        q = S.sb([128, T], BF16, "q")
        k = S.sb([128, T], BF16, "k")
        v = S.sb([128, NK, 128], BF16, "v")
        pt = [S.sb([128, 512], BF16, "pt%d" % i) for i in range(8)]
        rs = [S.sb([128, 512], F32, "rs%d" % i) for i in range(2)]
        o0 = S.sb([128, 512], F32, "o0")
        o1 = S.sb([128, 512], F32, "o1")
        sq2 = S.sb([128, 512], F32, "sq2")
        rstd = S.sb([128, 512], F32, "rstd")
        ob = [S.sb([128, 512], BF16, "ob%d" % i) for i in range(2)]
        it = 0
        for hd in range(2):
            S.dma("sp", q[:], qT.r(qT.h[hd * 128:(hd + 1) * 128, :]))
            S.dma("act", k[:], kT.r(kT.h[hd * 128:(hd + 1) * 128, :]))
            S.dma("sp", v[:], vtm.r(vtm.h[:, hd * 128:(hd + 1) * 128].rearrange("(kb p) e -> p kb e", p=128)))
            for qb in range(NQ):
                nkb = 4 * (qb + 1)
                tiles = [(kb, c) for kb in range(nkb) for c in range(2)]
                LA = 2
                pts = {}
                for idx in range(len(tiles) + LA):
                    if idx < len(tiles):
                        kb, c = tiles[idx]
                        ps = psS[it % 3]
                        p = pt[it % 8]
                        it += 1
                        pts[idx] = p
                        S.mm(ps[:, :], k[c * 64:(c + 1) * 64, kb * 128:(kb + 1) * 128],
                             q[c * 64:(c + 1) * 64, qb * 512:(qb + 1) * 512])
                        S.act(p[:], ps[:, :], AF.Exp)
                        j = kb - 4 * qb
                        if j >= 0:
                            S.tt("pool", p[:], p[:], masks[:, j, :], ALU.mult)
                    if idx - LA >= 0:
                        kb, c = tiles[idx - LA]
                        p = pts.pop(idx - LA)
                        S.mm(psO[c][:, :], v[:, kb, :], p[:], start=(kb == 0), stop=(kb == nkb - 1))
                        S.mm(psZ[c][:, :], C.ones_bf[:], p[:], start=(kb == 0), stop=(kb == nkb - 1))
                for c in range(2):
                    S.op("dve", lambda g, c=c: g.reciprocal(rs[c].h[:, :], psZ[c].h[:, :]), [psZ[c][:]], [rs[c][:]])
                S.tt("dve", o0[:], psO[0][:, :], rs[0][:], ALU.mult)
                S.tt("dve", o1[:], psO[1][:, :], rs[1][:], ALU.mult)
                S.stt("dve", o0[:], o1[:], nlam[:, 0:1], o0[:], ALU.mult, ALU.add)
                S.act(sq2[:], o0[:], AF.Square)
                S.mm(C.ps_ss[:, :], C.ones_f[:], sq2[:])
                rsqrt_to(S, rstd[:], C.ps_ss[:, :], 1.0 / 128, EPS)
                S.tt("pool", o0[:], o0[:], rstd[:], ALU.mult)
                o = ob[qb % 2]
                S.ts("dve", o[:], o0[:], sg[:, 0:1], None, ALU.mult)
                S.dma("pool", moT.r(moT.h[256 + hd * 128:256 + (hd + 1) * 128, qb * 512:(qb + 1) * 512]), o[:])


def phase_S5(S, C, I, T, uT, ygT):
    NB = T // 512
    with Phase(S):
        psA = [S.ps([128, 512], F32, "psA%d" % i) for i in range(2)]
        psB = [S.ps([128, 512], F32, "psB%d" % i) for i in range(2)]
        psY = S.ps([128, 512], F32, "psY")
        psW = S.ps([128, 8], F32, "psW")
        swap = S.sb([128, 128], F32, "swap")
        S.dma("sp", swap[:], I["swapm"][:])
        rowmask = S.sb([128, 8], F32, "rowmask")
        S.dma("sp", rowmask[:], I["rowmask"][:])
        dcol = S.sb([128, 2], F32, "dcol")
        S.dma("sp", dcol[:], I["s5_d_pk"][:])
        lr = S.sb([128, 2, 64], F32, "lr")
        li = S.sb([128, 2, 64], F32, "li")
        ldt = S.sb([128, 2], F32, "ldt")
        S.dma("sp", lr[:], I["lamre_row"][:])
        S.dma("act", li[:], I["lamim_row"][:])
        S.dma("sp", ldt[:], I["logdt_row"][:])
        S.act(ldt[:], ldt[:], AF.Exp)
        dtb = ldt.h[:, :].unsqueeze(2).to_broadcast([128, 2, 64])
        th = S.sb([128, 2, 64], F32, "th")
        mag = S.sb([128, 2, 64], F32, "mag")
        S.tt("dve", th[:], li[:], ldt.r(dtb), ALU.mult)
        S.tt("dve", mag[:], lr[:], ldt.r(dtb), ALU.mult)
        S.act(mag[:], mag[:], AF.Exp)
        cs = S.sb([128, 2, 64], F32, "cs")
        sn = S.sb([128, 2, 64], F32, "sn")
        t4 = S.sb([128, 2, 64], F32, "t4")
        t4b = S.sb([128, 2, 64], F32, "t4b")
        t4i = S.sb([128, 2, 64], I32, "t4i")
        sincos(S, sn[:], cs[:], th[:], (t4[:], t4b[:], t4i[:]))
        ar = S.sb([128, 2, 64], F32, "ar")
        ai = S.sb([128, 2, 64], F32, "ai")
        S.tt("dve", ar[:], mag[:], cs[:], ALU.mult)
        S.tt("dve", ai[:], mag[:], sn[:], ALU.mult)
        S.ts("dve", ar[:], ar[:], -1.0, None, ALU.add)
        den = S.sb([128, 2, 64], F32, "den")
        S.tt("dve", den[:], lr[:], lr[:], ALU.mult)
        S.tt("dve", t4[:], li[:], li[:], ALU.mult)
        S.tt("dve", den[:], den[:], t4[:], ALU.add)
        S.op("dve", lambda g: g.reciprocal(den.h[:], den.h[:]), [den[:]], [den[:]])
        qr = S.sb([128, 2, 64], F32, "qr")
        qi = S.sb([128, 2, 64], F32, "qi")
        S.tt("dve", qr[:], ar[:], lr[:], ALU.mult)
        S.tt("dve", t4[:], ai[:], li[:], ALU.mult)
        S.tt("dve", qr[:], qr[:], t4[:], ALU.add)
        S.tt("dve", qr[:], qr[:], den[:], ALU.mult)
        S.tt("dve", qi[:], ai[:], lr[:], ALU.mult)
        S.tt("dve", t4[:], ar[:], li[:], ALU.mult)
        S.tt("dve", qi[:], qi[:], t4[:], ALU.subtract)
        S.tt("dve", qi[:], qi[:], den[:], ALU.mult)
        btr = S.sb([128, 2, 64], F32, "btr")
        bti = S.sb([128, 2, 64], F32, "bti")
        S.dma("sp", btr[:], I["bT_re"][:])
        S.dma("act", bti[:], I["bT_im"][:])
        bbr = S.sb([128, 2, 64], F32, "bbr")
        bbi = S.sb([128, 2, 64], F32, "bbi")
        S.tt("dve", bbr[:], qr[:], btr[:], ALU.mult)
        S.tt("dve", t4[:], qi[:], bti[:], ALU.mult)
        S.tt("dve", bbr[:], bbr[:], t4[:], ALU.subtract)
        S.tt("dve", bbi[:], qr[:], bti[:], ALU.mult)
        S.tt("dve", t4[:], qi[:], btr[:], ALU.mult)
        S.tt("dve", bbi[:], bbi[:], t4[:], ALU.add)
        nbbr = S.sb([128, 2, 64], F32, "nbbr")
        S.ts("dve", nbbr[:], bbr[:], -1.0, None, ALU.mult)
        lrc = S.sb([128, 16], F32, "lrc")
        lic = S.sb([128, 16], F32, "lic")
        dtc = S.sb([128, 16], F32, "dtc")
        S.dma("sp", lrc[:], I["lamre_col"][:])
        S.dma("act", lic[:], I["lamim_col"][:])
        S.dma("sp", dtc[:], I["logdt_col"][:])
        S.act(dtc[:], dtc[:], AF.Exp)
        thc = S.sb([128, 16], F32, "thc")
        rc = S.sb([128, 16], F32, "rc")
        S.tt("dve", thc[:], lic[:], dtc[:], ALU.mult)
        S.tt("dve", rc[:], lrc[:], dtc[:], ALU.mult)
        S.act(rc[:], rc[:], AF.Exp)
        c1 = S.sb([128, 16], F32, "c1")
        s1 = S.sb([128, 16], F32, "s1")
        t32 = S.sb([128, 16], F32, "t32")
        t32b = S.sb([128, 16], F32, "t32b")
        t32i = S.sb([128, 16], I32, "t32i")
        sincos(S, s1[:], c1[:], thc[:], (t32[:], t32b[:], t32i[:]))
        ctc = S.sb([128, 16, 16], F32, "ctc")
        cti = S.sb([128, 16, 16], F32, "cti")
        S.dma("sp", ctc[:], I["cT_re_col"][:])
        S.dma("act", cti[:], I["cT_im_col"][:])
        S.ts("dve", cti[:], cti[:], -1.0, None, ALU.mult)
        Ct = S.sb([128, 8, 512], F32, "Ct")
        St = S.sb([128, 8, 512], F32, "St")
        Cm = S.sb([128, 8, 512], F32, "Cm")
        Sm = S.sb([128, 8, 512], F32, "Sm")
        Rt = S.sb([128, 8, 512], F32, "Rt")
        tA = S.sb([128, 8, 256], F32, "tA")
        LA = S.sb([128, 8, 128], BF16, "LA")
        LB = S.sb([128, 8, 128], BF16, "LB")
        LC1 = S.sb([128, 8, 128], BF16, "LC1")
        LC2 = S.sb([128, 8, 128], BF16, "LC2")
        W = S.sb([128, 8, 512], F32, "W")
        w0 = S.sb([128, 8], F32, "w0")
        wl = S.sb([128, 8], F32, "wl")
        tw = S.sb([128, 8], F32, "tw")
        u = [S.sb([128, 512], BF16, "u%d" % i) for i in range(2)]
        t1 = [S.sb([128, 512], F32, "t1_%d" % i) for i in range(2)]
        t2 = [S.sb([128, 512], F32, "t2_%d" % i) for i in range(2)]
        R1 = [S.sb([128, 512], BF16, "R1_%d" % i) for i in range(2)]
        R2 = [S.sb([128, 512], BF16, "R2_%d" % i) for i in range(2)]
        yv = S.sb([128, 512], F32, "yv")
        y2 = S.sb([128, 512], F32, "y2")
        yo = [S.sb([128, 512], BF16, "yo%d" % i) for i in range(2)]
        it = 0
        for cc in range(2):
            g0 = cc * 8
            S.copy("dve", Ct[:, :, 0:1], c1.r(c1.h[:, g0:g0 + 8].unsqueeze(2)))
            S.copy("dve", St[:, :, 0:1], s1.r(s1.h[:, g0:g0 + 8].unsqueeze(2)))
            n = 1
            while n < 512:
                cn = Ct.r(Ct.h[:, :, n - 1:n].to_broadcast([128, 8, n]))
                sn_ = St.r(St.h[:, :, n - 1:n].to_broadcast([128, 8, n]))
                ta = tA.r(tA.h[:, :, 0:n])
                S.tt("dve", ta, St[:, :, 0:n], sn_, ALU.mult)
                S.tt("dve", Ct[:, :, n:2 * n], Ct[:, :, 0:n], cn, ALU.mult)
                S.tt("dve", Ct[:, :, n:2 * n], Ct[:, :, n:2 * n], ta, ALU.subtract)
                S.tt("dve", ta, St[:, :, 0:n], cn, ALU.mult)
                S.tt("dve", St[:, :, n:2 * n], Ct[:, :, 0:n], sn_, ALU.mult)
                S.tt("dve", St[:, :, n:2 * n], St[:, :, n:2 * n], ta, ALU.add)
                n *= 2
            S.copy("pool", Cm[0:64, :, :], Ct[0:64, :, :])
            S.ts("pool", Cm[64:128, :, :], St[64:128, :, :], -1.0, None, ALU.mult)
            S.copy("pool", Sm[0:64, :, :], St[0:64, :, :])
            S.copy("pool", Sm[64:128, :, :], Ct[64:128, :, :])
            S.memset("pool", Rt[:], 1.0)
            S.tt("pool", Rt[:], Rt[:], rc.r(rc.h[:, g0:g0 + 8].unsqueeze(2).to_broadcast([128, 8, 512])), ALU.mult)
            S.memset("pool", LC1[:], 0.0)
            S.memset("pool", LC2[:], 0.0)
            for gi in range(8):
                S.ts("dve", LA[:, gi, 0:64], bbr[:, cc, :], rowmask[:, gi:gi + 1], None, ALU.mult)
                S.ts("dve", LA[:, gi, 64:128], bbi[:, cc, :], rowmask[:, gi:gi + 1], None, ALU.mult)
                S.ts("dve", LB[:, gi, 0:64], bbi[:, cc, :], rowmask[:, gi:gi + 1], None, ALU.mult)
                S.ts("dve", LB[:, gi, 64:128], nbbr[:, cc, :], rowmask[:, gi:gi + 1], None, ALU.mult)
                S.copy("pool", LC1[:, gi, gi * 16:(gi + 1) * 16], ctc[:, g0 + gi, :])
                S.copy("pool", LC2[:, gi, gi * 16:(gi + 1) * 16], cti[:, g0 + gi, :])
            S.memset("pool", w0[:], 0.0)
            for tb in range(NB):
                ut = u[tb % 2]
                S.dma("sp", ut[:], uT.r(uT.h[cc * 128:(cc + 1) * 128, tb * 512:(tb + 1) * 512]))
                for gi in range(8):
                    k = it % 2
                    it += 1
                    if gi == 0:
                        S.mm(psA[k][:, :], LA[:, gi, :], ut[:])
                        S.mm(psB[k][:, :], LB[:, gi, :], ut[:])
                    if gi + 1 < 8:
                        S.mm(psA[1 - k][:, :], LA[:, gi + 1, :], ut[:])
                        S.mm(psB[1 - k][:, :], LB[:, gi + 1, :], ut[:])
                    S.tt("dve", t1[k][:], psA[k][:, :], Ct[:, gi, :], ALU.mult)
                    S.tt("dve", t2[k][:], psB[k][:, :], St[:, gi, :], ALU.mult)
                    S.tt("pool", t1[k][:], t1[k][:], t2[k][:], ALU.add)
                    S.scan("dve", W[:, gi, :], Rt[:, gi, :], t1[k][:], w0[:, gi:gi + 1])
                    S.tt("pool", R1[k][:], W[:, gi, :], Cm[:, gi, :], ALU.mult)
                    S.tt("pool", R2[k][:], W[:, gi, :], Sm[:, gi, :], ALU.mult)
                    S.mm(psY[:, :], LC1[:, gi, :], R1[k][:], start=(gi == 0), stop=False)
                    S.mm(psY[:, :], LC2[:, gi, :], R2[k][:], start=False, stop=(gi == 7))
                S.copy("dve", wl[:], W.r(W.h[:, :, 511]))
                S.mm(psW[:, :], swap[:], wl[:])
                S.tt("dve", tw[:], psW[:, :], St.r(St.h[:, :, 511]), ALU.mult)
                S.tt("dve", w0[:], wl[:], Ct.r(Ct.h[:, :, 511]), ALU.mult)
                S.tt("dve", w0[:], w0[:], tw[:], ALU.subtract)
                S.stt("dve", yv[:], ut[:], dcol[:, cc:cc + 1], psY[:, :], ALU.mult, ALU.add)
                S.act(y2[:], yv[:], AF.Square)
                S.ts("dve", y2[:], y2[:], 0.044715, 1.0, ALU.mult, ALU.add)
                S.tt("pool", y2[:], y2[:], yv[:], ALU.mult)
                S.act(y2[:], y2[:], AF.Sigmoid, scale=1.5957691216057308)
                o = yo[tb % 2]
                S.tt("dve", o[:], yv[:], y2[:], ALU.mult)
                S.dma("pool", ygT.r(ygT.h[cc * 128:(cc + 1) * 128, tb * 512:(tb + 1) * 512]), o[:])


def phase_glu(S, C, I, T, ygF, ygL, moT):
    wv = I["ev_s5_w_glu_h"].h.rearrange("(kc p) n -> p kc n", p=128)
    with Phase(S):
        ps = [S.ps([128, 512], F32, "ps%d" % i) for i in range(2)]
        wst = S.sb([128, 4, 256], F32, "wst")
        S.dma("sp", wst[:], Ref(I["_w"], wv, None))
        w = S.sb([128, 4, 256], BF16, "w")
        S.copy("pool", w[:], wst[:])
        yg = [S.sb([128, 4, 512], BF16, "yg%d" % i) for i in range(2)]
        yl = [S.sb([128, 2, 512], BF16, "yl%d" % i) for i in range(2)]
        sg = [S.sb([128, 512], F32, "sg%d" % i) for i in range(2)]
        ob = [S.sb([128, 512], BF16, "ob%d" % i) for i in range(2)]
        it = 0
        for tb in range(T // 512):
            y = yg[tb % 2]
            yy = yl[tb % 2]
            S.dma("sp", y[:], ygF.r(ygF.h[:, tb * 512:(tb + 1) * 512].rearrange("(kc p) t -> p kc t", p=128)))
            S.dma("act", yy[:], ygL.r(ygL.h[:, tb * 512:(tb + 1) * 512].rearrange("(kc p) t -> p kc t", p=128)))
            for n in range(2):
                p = ps[it % 2]
                for kc in range(4):
                    S.mm(p[:, :], w[:, kc, n * 128:(n + 1) * 128], y[:, kc, :], start=(kc == 0), stop=(kc == 3))
                S.act(sg[it % 2][:], p[:, :], AF.Sigmoid)
                S.tt("dve", ob[it % 2][:], yy[:, n, :], sg[it % 2][:], ALU.mult)
                S.dma("pool", moT.r(moT.h[n * 128:(n + 1) * 128, tb * 512:(tb + 1) * 512]), ob[it % 2][:])
                it += 1


def phase_B0(S, C, I, T, xin, R):
    TB = 256
    NCH = TB // 64
    wv_ = {n: I[n].h[0].rearrange("(kc p) n -> p kc n", p=128) for n in ("od_w1", "od_a1", "od_g1")}
    for n in ("od_w_r", "od_w_k", "od_w_v"):
        wv_[n] = I[n + "_h"].h.rearrange("(kc p) n -> p kc n", p=128)
    with Phase(S):
        C.ps_ss = S.ps([128, 512], F32, "ss")
        psA_ = [S.ps([128, 512], F32, "psA%d" % i) for i in range(4)]
        psT_ = [S.ps([128, 512], F32, "psT%d" % i) for i in range(2)]

        class _V:
            def __init__(self, t, w):
                self.t, self.w = t, w

            def __getitem__(self, idx):
                return Ref(self.t, self.t.h[:, 0:self.w][idx], None)
        psA = [_V(t, TB) for t in psA_]
        psT = [_V(t, 128) for t in psT_]
        wst = [S.sb([128, 8, 128], F32, "wst%d" % i) for i in range(2)]
        W3 = [S.sb([128, 8, 512], BF16, "W%d" % i) for i in range(3)]
        wi = 0
        for m, nm in enumerate(("od_w_r", "od_w_k", "od_w_v")):
            for n in range(4):
                S.dma("sp" if wi % 2 == 0 else "act", wst[wi % 2][:], Ref(I["_w"], wv_[nm][:, :, n * 128:(n + 1) * 128], None))
                S.copy("pool", W3[m][:, :, n * 128:(n + 1) * 128], wst[wi % 2][:])
                wi += 1
        L1 = S.sb([128, 8, 256], BF16, "L1")
        for nm, lo, wd in (("od_w1", 0, 64), ("od_a1", 64, 64), ("od_g1", 128, 128)):
            S.dma("sp", wst[wi % 2][:, :, 0:wd], Ref(I["_w"], wv_[nm], None))
            S.copy("pool", L1[:, :, lo:lo + wd], wst[wi % 2][:, :, 0:wd])
            wi += 1
        L2st = S.sb([128, 3, 512], F32, "L2st")
        S.memset("pool", L2st[:], 0.0)
        S.dma("sp", L2st[0:64, 0, :], I["od_w2_h"][:])
        S.dma("sp", L2st[0:64, 1, :], I["od_a2_h"][:])
        S.dma("sp", L2st[:, 2, :], I["od_g2_h"][:])
        L2 = S.sb([128, 3, 512], BF16, "L2")
        S.copy("pool", L2[:], L2st[:])
        pv = S.sb([128, 7, 4], F32, "pv")
        S.dma("sp", pv[:, 0:5, :], I["od_vecs_pk"][:])
        S.ts("dve", pv[:, 5, :], pv[:, 3, :], -1.0, 1.0, ALU.mult, ALU.add)
        S.ts("dve", pv[:, 6, :], pv[:, 0, :], -1.0, None, ALU.mult)
        mu = S.sb([128, 48], F32, "mu")
        S.dma("sp", mu[:], I["od_mu_pk"][:])
        ident = S.sb([128, 128], F32, "ident")
        S.dma("sp", ident[:], I["ident"][:])
        rmask = S.sb([128, TB], F32, "rmask")
        S.memset("pool", rmask[:], 1.0)
        for c in range(NCH):
            S.memset("pool", rmask[:, c * 64:c * 64 + 1], 0.0)
        x = S.sb([128, 8, TB], F32, "x")
        sq = S.sb([128, 8, TB], BF16, "sq")
        hs = S.sb([128, 8, TB + 1], F32, "hs")
        S.memset("pool", hs[:], 0.0)
        hh = S.sb([128, 8, TB], F32, "hh")
        dx = S.sb([128, 8, TB], F32, "dx")
        xm = [S.sb([128, 8, TB], BF16, "xm%d" % i) for i in range(6)]
        rstd = S.sb([128, TB], F32, "rstd")
        tmp = [S.sb([128, TB], F32, "tmp%d" % i) for i in range(2)]
        lo1 = S.sb([128, 3, TB], BF16, "lo1")
        S.memset("pool", lo1[:], 0.0)
        Es = [{k: S.sb([128, TB], F32, k + str(i)) for k in ("r", "k", "v", "lw", "cum", "ep", "em", "ex", "a", "kk", "t0", "t1", "kmod", "b", "o0", "o1", "o2")} for i in range(2)]
        tq = [S.sb([128, 128], F32, "tq%d" % i) for i in range(2)]
        gls = [S.sb([128, NCH], F32, "gl%d" % i) for i in range(2)]
        pi = 0
        ti = 0
        for t0 in range(0, T, TB):
            S.dma("sp", x[:], xin.r(xin.h[:, t0:t0 + TB].rearrange("(kc p) t -> p kc t", p=128)))
            for kc in range(8):
                S.act(sq[:, kc, :], x[:, kc, :], AF.Square)
            for kc in range(8):
                S.mm(C.ps_ss[:, 0:TB], C.ones_bf[:], sq[:, kc, :], start=(kc == 0), stop=(kc == 7))
            rsqrt_to(S, rstd[:], C.ps_ss[:, 0:TB], 1.0 / D, EPS)
            for kc in range(8):
                S.tt("dve", tmp[kc % 2][:], x[:, kc, :], rstd[:], ALU.mult)
                S.act(hh[:, kc, :], tmp[kc % 2][:], AF.Identity, bias=modcol(C, 1, 0, kc), scale=C.gsm[:, 8 + kc:9 + kc])
            S.copy("pool", hs[:, :, 1:TB + 1], hh[:])
            S.tt("dve", dx[:], hs[:, :, 0:TB], hh[:], ALU.subtract)
            S.copy("pool", hs[:, :, 0:1], hh[:, :, TB - 1:TB])
            for m in range(6):
                for kc in range(8):
                    S.stt("dve", xm[m][:, kc, :], dx[:, kc, :],
                          mu[:, m * 8 + kc:m * 8 + kc + 1], hh[:, kc, :], ALU.mult, ALU.add)
            for j, (mx, lo, wd, fn) in enumerate(((1, 0, 64, AF.Tanh), (4, 64, 64, AF.Identity), (5, 128, 128, AF.Sigmoid))):
                p = psA[pi % 4]
                pi += 1
                for kc in range(8):
                    S.mm(p[0:wd, :], L1[:, kc, lo:lo + wd], xm[mx][:, kc, :], start=(kc == 0), stop=(kc == 7))
                S.act(lo1[0:wd, j, :], p[0:wd, :], fn)
            for n in range(4):
                f0 = n * 128
                E = Es[n % 2]
                gl = gls[n % 2]
                pr, pk_, pv_, pw = [psA[(pi + i) % 4] for i in range(4)]
                for (p, m, mx) in ((pr, 0, 0), (pk_, 1, 2), (pv_, 2, 3)):
                    for kc in range(8):
                        S.mm(p[:, :], W3[m][:, kc, f0:f0 + 128], xm[mx][:, kc, :], start=(kc == 0), stop=(kc == 7))
                S.copy("act", E["r"][:], pr[:, :])
                S.copy("act", E["k"][:], pk_[:, :])
                S.copy("act", E["v"][:], pv_[:, :])
                S.mm(pw[:, :], L2[:, 0, f0:f0 + 128], lo1[:, 0, :])
                S.act(E["t0"][:], pw[:, :], AF.Exp, bias=pv[:, 6, n:n + 1], scale=-1.0)
                S.ts("dve", E["t0"][:], E["t0"][:], 1.0, None, ALU.add)
                S.act(E["t0"][:], E["t0"][:], AF.Ln)
                S.ts("dve", E["t0"][:], E["t0"][:], -1.0, -0.5, ALU.mult, ALU.add)
                S.act(E["t0"][:], E["t0"][:], AF.Exp)
                S.ts("dve", E["lw"][:], E["t0"][:], -1.0, None, ALU.mult)
                S.scan("dve", E["cum"][:], rmask[:], E["lw"][:], 0.0)
                S.act(E["ep"][:], E["cum"][:], AF.Exp)
                S.act(E["em"][:], E["cum"][:], AF.Exp, scale=-1.0)
                S.tt("pool", E["t1"][:], E["cum"][:], E["lw"][:], ALU.subtract)
                S.act(E["ex"][:], E["t1"][:], AF.Exp)
                S.mm(pw[:, :], L2[:, 1, f0:f0 + 128], lo1[:, 1, :])
                S.act(E["a"][:], pw[:, :], AF.Sigmoid, bias=pv[:, 1, n:n + 1])
                S.ts("dve", E["kk"][:], E["k"][:], pv[:, 2, n:n + 1], None, ALU.mult)
                S.tt("pool", E["t0"][:], E["kk"][:], E["kk"][:], ALU.mult)
                S.mm(C.ps_ss[:, 0:TB], C.blk_f[:], E["t0"][:])
                S.act(E["t0"][:], C.ps_ss[:, 0:TB], AF.Sqrt)
                S.ts("dve", E["t0"][:], E["t0"][:], 1e-12, None, ALU.max)
                S.op("dve", lambda g: g.reciprocal(E["t0"].h[:], E["t0"].h[:]), [E["t0"][:]], [E["t0"][:]])
                S.tt("dve", E["kk"][:], E["kk"][:], E["t0"][:], ALU.mult)
                S.ts("dve", E["t1"][:], E["a"][:], pv[:, 3, n:n + 1], pv[:, 5, n:n + 1], ALU.mult, ALU.add)
                S.tt("dve", E["kmod"][:], E["k"][:], E["t1"][:], ALU.mult)
                S.tt("pool", E["b"][:], E["kk"][:], E["a"][:], ALU.mult)
                S.mm(pw[:, :], L2[:, 2, f0:f0 + 128], lo1[:, 2, :])
                S.copy("act", E["o2"][:], pw[:, :])
                S.dma("pool", R["GG"].r(R["GG"].h[f0:f0 + 128, t0:t0 + TB]), E["o2"][:])
                pi += 4
                S.tt("dve", E["t0"][:], E["r"][:], E["kmod"][:], ALU.mult)
                S.ts("dve", E["t0"][:], E["t0"][:], pv[:, 4, n:n + 1], None, ALU.mult)
                S.mm(C.ps_ss[:, 0:TB], C.blk_f[:], E["t0"][:])
                S.tt("dve", E["o0"][:], C.ps_ss[:, 0:TB], E["v"][:], ALU.mult)
                S.dma("pool", R["BON"].r(R["BON"].h[f0:f0 + 128, t0:t0 + TB]), E["o0"][:])
                S.tt("dve", E["o1"][:], E["r"][:], E["ep"][:], ALU.mult)
                S.dma("pool", R["RR"].r(R["RR"].h[f0:f0 + 128, t0:t0 + TB]), E["o1"][:])
                S.stt("dve", E["o0"][:], E["kk"][:], -1.0, E["ex"][:], ALU.mult, ALU.mult)
                S.dma("pool", R["AR"].r(R["AR"].h[f0:f0 + 128, t0:t0 + TB]), E["o0"][:])
                S.tt("dve", E["kmod"][:], E["kmod"][:], E["em"][:], ALU.mult)
                S.dma("pool", R["KB"].r(R["KB"].h[f0:f0 + 128, t0:t0 + TB]), E["kmod"][:])
                S.tt("dve", E["b"][:], E["b"][:], E["em"][:], ALU.mult)
                S.dma("pool", R["BB"].r(R["BB"].h[f0:f0 + 128, t0:t0 + TB]), E["b"][:])
                S.copy("dve", gl[:], E["ep"].r(E["ep"].h[:, 63:TB:64]))
                S.dma("pool", R["GL"].r(R["GL"].h[f0:f0 + 128, t0 // 64:t0 // 64 + NCH]), gl[:])
                elb = gl.r(gl.h[:, :].unsqueeze(2).to_broadcast([128, NCH, 64]))
                S.tt("dve", E["kmod"].r(E["kmod"].h[:, :].rearrange("p (c t) -> p c t", t=64)),
                     E["kmod"].r(E["kmod"].h[:, :].rearrange("p (c t) -> p c t", t=64)), elb, ALU.mult)
                S.tt("dve", E["b"].r(E["b"].h[:, :].rearrange("p (c t) -> p c t", t=64)),
                     E["b"].r(E["b"].h[:, :].rearrange("p (c t) -> p c t", t=64)), elb, ALU.mult)
                for (src, dst) in ((E["kmod"], R["KT"]), (E["b"], R["BT"]), (E["v"], R["VT"])):
                    for s in range(TB // 128):
                        pt_ = psT[ti % 2]
                        q_ = tq[ti % 2]
                        ti += 1
                        S.transpose(pt_[:, :], src[:, s * 128:(s + 1) * 128], ident[:])
                        S.copy("act", q_[:], pt_[:, :])
                        S.dma("sp", dst.r(dst.h[t0 + s * 128:t0 + (s + 1) * 128, f0:f0 + 128]), q_[:])


def phase_B1(S, C, I, T, R, moT):
    SC = 256
    NCS = SC // 64
    NCH = T // 64
    GN_EPS = 64e-5
    F32R = mybir.dt.float32r
    with Phase(S):
        banks = [S.ps([64, 8, 64], F32, "bk%d" % i) for i in range(7)]
        psE = S.ps([64, 512], F32, "psE")
        bstate = [0]

        def mmr(out, lhsT, rhs, start=True, stop=True):
            return S.mm(out, lhsT, rhs, start=start, stop=stop)

        def bank():
            b = banks[bstate[0] % 7]
            bstate[0] += 1
            return b
        cm = S.sb([64, 4, 8, 64], F32, "cmask")
        S.dma("sp", cm[:], I["rw_masks"][:])
        MS, MI, MST, I8 = [cm.r(cm.h[:, i]) for i in range(4)]
        lng = S.sb([64, 2, 8], F32, "lng")
        S.dma("sp", lng[:], I["od_ln_pk64"][:])
        ones64 = C.ones_f[0:64, 0:64]
        fm = {k: [S.sb([64, 8, SC], F32R, "%s%d" % (k, i)) for i in range(2)] for k in ("AR", "RR", "KB", "BB")}
        tm = {k: [S.sb([64, NCS, 512], F32R, "%s%d" % (k, i)) for i in range(2)] for k in ("KT", "BT", "VT")}
        ep = {k: S.sb([64, 8, SC], F32, k) for k in ("BON", "GG")}
        GLt = S.sb([64, 8, NCH], F32, "GLt")
        ST = S.sb([64, 8, 64], F32R, "ST")
        Y = S.sb([64, 8, SC], F32, "Y")
        Yc = S.sb([64, 8, SC], F32, "Yc")
        Ysq = S.sb([64, 8, SC], F32, "Ysq")
        rstd = S.sb([64, 512], F32, "rstd")
        ob = S.sb([64, 8, SC], BF16, "ob")
        A = {k: S.sb([64, 8, 64], F32R, k) for k in ("N", "NT", "ak", "kr", "br", "Tm", "X", "UT")}
        Mp = [S.sb([64, 8, 64], F32R, "M%d" % i) for i in range(2)]
        MTp = [S.sb([64, 8, 64], F32R, "MT%d" % i) for i in range(2)]
        for g in range(1):
            rows = slice(g * 512, (g + 1) * 512)
            S.dma("sp", GLt[:], R["GL"].r(R["GL"].h[rows, :].rearrange("(h i) c -> i h c", i=64)))
            S.ts("dve", ST[:], I8, 0.0, None, ALU.mult)
            for si, t0 in enumerate(range(0, T, SC)):
                F = {}
                for k in fm:
                    F[k] = fm[k][si % 2]
                    S.dma("sp" if k in ("AR", "KB") else "act", F[k][:],
                          R[k].r(R[k].h[rows, t0:t0 + SC].rearrange("(h i) t -> i h t", i=64).bitcast(F32R)))
                for k in tm:
                    F[k] = tm[k][si % 2]
                    S.dma("sp", F[k][:], R[k].r(R[k].h[t0:t0 + SC, rows].rearrange("(c s) f -> s c f", s=64).bitcast(F32R)))
                for k in ep:
                    S.dma("act", ep[k][:], R[k].r(R[k].h[rows, t0:t0 + SC].rearrange("(h i) t -> i h t", i=64)))
                for c in range(NCS):
                    cs = slice(c * 64, (c + 1) * 64)
                    cg = t0 // 64 + c

                    def hv(name, h):
                        return F[name][:, c, h * 64:(h + 1) * 64]
                    for (dst, l, r_, msk) in ((A["N"], "BB", "AR", MS), (A["NT"], "AR", "BB", MST)):
                        p = bank()
                        for h in range(8):
                            mmr(p[:, h, :], F[l][:, h, cs], F[r_][:, h, cs])
                        S.tt("dve", dst[:], p[:], msk, ALU.mult)
                    late = [(A["ak"], "KB", "AR", MS), (A["kr"], "KB", "RR", MI), (A["br"], "BB", "RR", MI)]
                    S.tt("dve", A["Tm"][:], A["N"][:], I8, ALU.add)
                    M, MT = A["N"], A["NT"]
                    for lv in range(5):
                        p1, p2 = bank(), bank()
                        for h in range(8):
                            mmr(p1[:, h, :], MT[:, h, :], M[:, h, :])
                            mmr(p2[:, h, :], M[:, h, :], MT[:, h, :])
                        pl = None
                        if lv < len(late):
                            dstl, l_, r_, mskl = late[lv]
                            pl = bank()
                            for h in range(8):
                                mmr(pl[:, h, :], F[l_][:, h, cs], F[r_][:, h, cs])
                        M2, MT2 = Mp[lv % 2], MTp[lv % 2]
                        S.copy("act", M2[:], p1[:])
                        S.copy("dve", MT2[:], p2[:])
                        p3 = bank()
                        for h in range(8):
                            mmr(p3[:, h, :], MT2[:, h, :], A["Tm"][:, h, :])
                        S.tt("dve", A["Tm"][:], A["Tm"][:], p3[:], ALU.add)
                        if pl is not None:
                            S.tt("dve", dstl[:], pl[:], mskl, ALU.mult)
                        M, MT = M2, MT2
                    px = bank()
                    for h in range(8):
                        mmr(px[:, h, :], F["AR"][:, h, cs], ST[:, h, :], start=True, stop=False)
                        mmr(px[:, h, :], A["ak"][:, h, :], hv("VT", h), start=False, stop=True)
                    S.copy("act", A["X"][:], px[:])
                    pu = bank()
                    for h in range(8):
                        mmr(pu[:, h, :], A["Tm"][:, h, :], A["X"][:, h, :])
                    S.copy("act", A["UT"][:], pu[:])
                    py = bank()
                    for h in range(8):
                        mmr(py[:, h, :], ST[:, h, :], F["RR"][:, h, cs], start=True, stop=False)
                        mmr(py[:, h, :], hv("VT", h), A["kr"][:, h, :], start=False, stop=False)
                        mmr(py[:, h, :], A["UT"][:, h, :], A["br"][:, h, :], start=False, stop=True)
                    S.copy("act", Y[:, :, cs], py[:])
                    pst = bank()
                    for h in range(8):
                        mmr(pst[:, h, :], hv("KT", h), hv("VT", h), start=True, stop=False)
                        mmr(pst[:, h, :], hv("BT", h), A["UT"][:, h, :], start=False, stop=True)
                    S.tt("dve", ST[:], ST[:], GLt.r(GLt.h[:, :, cg:cg + 1].to_broadcast([64, 8, 64])), ALU.mult)
                    S.tt("dve", ST[:], ST[:], pst[:], ALU.add)
                for q4 in range(8 * SC // 512):
                    hs_ = slice(q4 * (512 // SC), (q4 + 1) * (512 // SC))
                    S.mm(psE[:, :], ones64, Y[:, hs_, :])
                    S.stt("dve", Yc[:, hs_, :], psE[:, :], -1.0 / 64, Y[:, hs_, :], ALU.mult, ALU.add)
                    S.act(Ysq[:, hs_, :], Yc[:, hs_, :], AF.Square)
                    S.mm(psE[:, :], ones64, Ysq[:, hs_, :])
                    rsqrt_to(S, rstd[:], psE[:, :], 1.0 / 64, GN_EPS)
                    S.tt("dve", Yc[:, hs_, :], Yc[:, hs_, :], rstd[:], ALU.mult)
                for h in range(8):
                    hg = g * 8 + h
                    S.act(Yc[:, h, :], Yc[:, h, :], AF.Identity, bias=lng[:, 1, hg:hg + 1], scale=lng[:, 0, hg:hg + 1])
                S.tt("pool", Yc[:], Yc[:], ep["BON"][:], ALU.add)
                S.tt("dve", ob[:], Yc[:], ep["GG"][:], ALU.mult)
                S.dma("pool", moT.r(moT.h[rows, t0:t0 + SC].rearrange("(h i) t -> i h t", i=64)), ob[:])


_NC_CACHE = {}


def kernel(**inputs):
    T = 4096
    if "nc" not in _NC_CACHE:
        _NC_CACHE["nc"] = build(T, "full")
    nc = _NC_CACHE["nc"]
    maps = [host_inputs(inputs, c % 4, c // 4, T) for c in range(8)]
    res = run_bass_kernel_spmd(nc, maps, core_ids=list(range(8)))
    out = np.stack([np.ascontiguousarray(np.asarray(res.results[b]["yT"]).T) for b in range(4)], axis=0)
    return out.astype(np.float32)
```

```python
import contextlib
import math
import numpy as np
import concourse.bass as bass
import concourse.mybir as mybir
from concourse.bass_utils import run_bass_kernel_spmd

F32 = mybir.dt.float32
BF16 = mybir.dt.bfloat16
I32 = mybir.dt.int32
AF = mybir.ActivationFunctionType
ALU = mybir.AluOpType

D = 1024
KC = 8
FFN = 2816
HC = 22
EPS = 1e-6


class _State:
    __slots__ = ("w", "r")

    def __init__(self):
        self.w = None
        self.r = {}


class Tile:
    def __init__(self, h, name):
        self.h = h
        self.name = name
        self.st = {None: _State()}

    def __getitem__(self, idx):
        return Ref(self, self.h[idx], None)

    def sub(self, key, ap):
        return Ref(self, ap, key)

    def r(self, ap):
        return Ref(self, ap, None)

    def states(self, key):
        if key is None:
            return list(self.st.values())
        if key not in self.st:
            self.st[key] = _State()
        return [self.st[key], self.st[None]]

    def wstate(self, key):
        if key not in self.st:
            self.st[key] = _State()
        return self.st[key]


class Ref:
    __slots__ = ("t", "ap", "key")

    def __init__(self, t, ap, key):
        self.t, self.ap, self.key = t, ap, key


class Sched:
    NDMA = 32

    def __init__(self, nc):
        self.nc = nc
        self.es = contextlib.ExitStack()
        self.eng = {"pe": nc.tensor, "act": nc.scalar, "dve": nc.vector,
                    "pool": nc.gpsimd, "sp": nc.sync}
        self.root_es = self.es
        self.semmap = {}
        self.ekey = {}
        self.gen = 0
        self.cnt = {}
        self.known = {}
        for e in self.eng:
            self.known[e] = {}
        self.new_engine_sems()
        self.lastclk = {e: {} for e in self.eng}
        self.dsem = [self.es.enter_context(nc.semaphore("d%d" % i)) for i in range(self.NDMA)]
        self.dtarget = [0] * self.NDMA
        self.dclock = [None] * self.NDMA
        self.dnext = 0
        self.ntile = 0
        self.nwait = 0
        self.ninstr = 0

    def new_engine_sems(self):
        self.gen += 1
        for e in self.eng:
            key = "%s#%d" % (e, self.gen)
            self.ekey[e] = key
            self.semmap[key] = self.root_es.enter_context(self.nc.semaphore("s_%s_%d" % (e, self.gen)))
            self.cnt[e] = 0

    def sb(self, shape, dt, name=None):
        self.ntile += 1
        name = "%s_%d" % (name or "t", self.ntile)
        h = self.es.enter_context(self.nc.sbuf_tensor(name, list(shape), dt))
        return Tile(h, name)

    def ps(self, shape, dt=F32, name=None):
        self.ntile += 1
        name = "%s_%d" % (name or "p", self.ntile)
        h = self.es.enter_context(self.nc.psum_tensor(name, list(shape), dt))
        return Tile(h, name)

    def dram(self, shape, dt, name, kind="Internal"):
        h = self.nc.dram_tensor(name, list(shape), dt, kind=kind).ap()
        return Tile(h, name)

    def _semobj(self, key):
        return self.semmap[key] if isinstance(key, str) else self.dsem[key]

    def _wait(self, e, ev):
        if ev is None:
            return
        key, val, clock = ev
        kn = self.known[e]
        if kn.get(key, 0) >= val:
            return
        self.eng[e].wait_ge(self._semobj(key), val)
        self.nwait += 1
        new = dict(kn)
        if clock:
            for k2, v2 in clock.items():
                if new.get(k2, 0) < v2:
                    new[k2] = v2
        new[key] = val
        self.known[e] = new

    def _deps(self, e, reads, writes, is_dma):
        for rf in reads:
            for st in rf.t.states(rf.key):
                if st.w is not None:
                    self._wait(e, st.w)
        for rf in writes:
            for st in rf.t.states(rf.key):
                if st.w is not None and (is_dma or st.w[0] != self.ekey[e]):
                    self._wait(e, st.w)
                for k, (v, c) in st.r.items():
                    if is_dma or k != self.ekey[e]:
                        self._wait(e, (k, v, c))

    def _record(self, ev, reads, writes):
        key, val, clock = ev
        for rf in reads:
            st = rf.t.wstate(rf.key)
            st.r[key] = (val, clock)
        for rf in writes:
            if rf.key is None:
                for k in list(rf.t.st.keys()):
                    if k is not None:
                        del rf.t.st[k]
            st = rf.t.wstate(rf.key)
            st.w = ev
            st.r = {}

    def op(self, e, fn, reads, writes):
        reads = [r for r in reads if isinstance(r, Ref)]
        self._deps(e, reads, writes, False)
        ins = fn(self.eng[e])
        self.cnt[e] += 1
        ins.then_inc(self.semmap[self.ekey[e]], 1)
        ev = (self.ekey[e], self.cnt[e], self.known[e])
        self.lastclk[e] = self.known[e]
        self._record(ev, reads, writes)
        self.ninstr += 1
        return ev

    def dma(self, q, out, in_, **kw):
        i = self.dnext
        self.dnext = (self.dnext + 1) % self.NDMA
        if self.dtarget[i] > 0:
            self._wait(q, (i, self.dtarget[i], self.dclock[i]))
        self._deps(q, [in_], [out], True)
        ins = self.eng[q].dma_start(out=out.ap, in_=in_.ap, **kw)
        ins.then_inc(self.dsem[i], 16)
        self.dtarget[i] += 16
        self.dclock[i] = self.known[q]
        ev = (i, self.dtarget[i], self.known[q])
        self._record(ev, [in_], [out])
        self.ninstr += 1
        return ev

    def collective(self, kind, alu, groups, in_, out):
        sem = self.root_es.enter_context(self.nc.semaphore("cc%d" % len(self.dsem)))
        self.dsem.append(sem)
        self.dtarget.append(0)
        self.dclock.append(None)
        i = len(self.dsem) - 1
        self._deps("pool", [in_], [out], True)
        ins = self.eng["pool"].collective_compute(kind, alu, replica_groups=groups,
                                                  ins=[in_.ap.opt()], outs=[out.ap.opt()])
        ins.then_inc(sem, 1)
        self.dtarget[i] = 1
        self.dclock[i] = self.known["pool"]
        ev = (i, 1, self.known["pool"])
        self._record(ev, [in_], [out])
        self.ninstr += 1
        return ev

    def wait_all(self, e, refs):
        for rf in refs:
            for st in rf.t.states(rf.key):
                self._wait(e, st.w)

    def close(self):
        self.es.close()
        if self.root_es is not self.es:
            self.root_es.close()

    def mm(self, out, lhsT, rhs, start=True, stop=True):
        return self.op("pe", lambda g: g.matmul(out.ap, lhsT.ap, rhs.ap, start=start, stop=stop),
                       [lhsT, rhs], [out])

    def transpose(self, out, in_, ident):
        return self.op("pe", lambda g: g.transpose(out.ap, in_.ap, ident.ap), [in_, ident], [out])

    def act(self, out, in_, func, bias=None, scale=None, e="act"):
        kw = {}
        if bias is not None:
            kw["bias"] = bias.ap if isinstance(bias, Ref) else bias
        if scale is not None:
            kw["scale"] = scale.ap if isinstance(scale, Ref) else scale
        return self.op(e, lambda g: g.activation(out.ap, in_.ap, func, **kw),
                       [in_, bias, scale], [out])

    def tt(self, e, out, in0, in1, op):
        return self.op(e, lambda g: g.tensor_tensor(out.ap, in0.ap, in1.ap, op), [in0, in1], [out])

    def ts(self, e, out, in0, s1, s2, op0, op1=None):
        a1 = s1.ap if isinstance(s1, Ref) else s1
        a2 = s2.ap if isinstance(s2, Ref) else s2
        if op1 is None:
            return self.op(e, lambda g: g.tensor_scalar(out.ap, in0.ap, a1, None, op0), [in0, s1], [out])
        return self.op(e, lambda g: g.tensor_scalar(out.ap, in0.ap, a1, a2, op0, op1), [in0, s1, s2], [out])

    def stt(self, e, out, in0, sc, in1, op0, op1):
        a = sc.ap if isinstance(sc, Ref) else sc
        return self.op(e, lambda g: g.scalar_tensor_tensor(out.ap, in0.ap, a, in1.ap, op0, op1),
                       [in0, sc, in1], [out])

    def scan(self, e, out, d0, d1, init, op0=ALU.mult, op1=ALU.add):
        a = init.ap if isinstance(init, Ref) else init
        return self.op(e, lambda g: g.tensor_tensor_scan(out.ap, d0.ap, d1.ap, a, op0, op1),
                       [d0, d1, init], [out])

    def copy(self, e, out, in_):
        if e == "act":
            return self.op(e, lambda g: g.copy(out.ap, in_.ap), [in_], [out])
        return self.op(e, lambda g: g.tensor_copy(out.ap, in_.ap), [in_], [out])

    def memset(self, e, out, val):
        return self.op(e, lambda g: g.memset(out.ap, val), [], [out])


def _barrier(S):
    for e in S.eng:
        for f in S.eng:
            if f != e and S.cnt[f] > 0:
                S._wait(e, (S.ekey[f], S.cnt[f], S.lastclk[f]))
        for i in range(len(S.dsem)):
            if S.dtarget[i] > 0:
                S._wait(e, (i, S.dtarget[i], S.dclock[i]))


class Phase:
    def __init__(self, S):
        self.S = S

    def __enter__(self):
        self.saved = self.S.es
        self.S.es = contextlib.ExitStack()
        return self

    def __exit__(self, *a):
        _barrier(self.S)
        self.S.es.close()
        self.S.es = self.saved
        if max(self.S.cnt.values()) > 16000:
            self.S.new_engine_sems()
        return False


def pk(v):
    v = np.asarray(v, dtype=np.float32)
    lead = v.shape[:-1]
    n = v.shape[-1] // 128
    v = v.reshape(lead + (n, 128))
    v = np.moveaxis(v, -1, 0)
    return np.ascontiguousarray(v.reshape(128, -1))


class Ctx:
    pass


def load_cast(S, C, dram_ref, shape, q="sp", ceng="pool", name="w"):
    st = S.sb(shape, F32, name + "_st")
    S.dma(q, st[:], dram_ref)
    wt = S.sb(shape, BF16, name + "_bf")
    S.copy(ceng, wt[:], st[:])
    return wt


def setup_consts(S, C):
    C.ones_bf = S.sb([128, 128], BF16, "ones")
    S.memset("pool", C.ones_bf[:], 1.0)
    C.ones_f = S.sb([128, 128], F32, "onesf")
    S.memset("pool", C.ones_f[:], 1.0)
    C.blk_f = S.sb([128, 128], F32, "blkf")
    S.memset("pool", C.blk_f[:], 0.0)
    S.memset("pool", C.blk_f[0:64, 0:64], 1.0)
    S.memset("pool", C.blk_f[64:128, 64:128], 1.0)
    C.blk_bf = S.sb([128, 128], BF16, "blkbf")
    S.copy("pool", C.blk_bf[:], C.blk_f[:])


def setup_adaln(S, C, I):
    C.mod = S.sb([128, 96], F32, "mod")
    C.gsm = S.sb([128, 16], F32, "gsm")
    C.gsf = S.sb([128, 16], F32, "gsf")
    with Phase(S):
        cact = S.sb([128, 8], F32, "cact")
        S.dma("sp", cact[:], I["c_pk"][:])
        S.act(cact[:], cact[:], AF.Silu)
        bada = S.sb([128, 96], F32, "bada")
        S.dma("sp", bada[:], I["b_ada_pk"][:])
        nm = S.sb([128, 16], F32, "nm")
        S.dma("sp", nm[:], I["norm_mix_pk"][:])
        nf = S.sb([128, 16], F32, "nf")
        S.dma("sp", nf[:], I["norm_ffn_pk"][:])
        pm = S.ps([128, 96], F32, "pmod")
        wt = [S.sb([128, 8, 512], F32, "wada%d" % i) for i in range(3)]
        it = 0
        for l in range(2):
            wl = I["w_ada"].h[l].rearrange("(kc p) n -> p kc n", p=128)
            for ng in range(12):
                w = wt[it % 3]
                it += 1
                S.dma(("sp", "act", "pool")[it % 3], w[:], I["w_ada"].r(wl[:, :, ng * 512:(ng + 1) * 512]))
                for j in range(4):
                    col = l * 48 + ng * 4 + j
                    for kc in range(8):
                        S.mm(pm[:, col:col + 1], w[:, kc, j * 128:(j + 1) * 128], cact[:, kc:kc + 1],
                             start=(kc == 0), stop=(kc == 7))
        S.tt("dve", C.mod[:], pm[:], bada[:], ALU.add)
        for l in range(2):
            S.stt("dve", C.gsm[:, l * 8:(l + 1) * 8], C.mod[:, l * 48 + 8:l * 48 + 16], 1.0,
                  nm[:, l * 8:(l + 1) * 8], ALU.add, ALU.mult)
            S.stt("dve", C.gsf[:, l * 8:(l + 1) * 8], C.mod[:, l * 48 + 32:l * 48 + 40], 1.0,
                  nf[:, l * 8:(l + 1) * 8], ALU.add, ALU.mult)


def rsqrt_to(S, out, in_, scale, eps):
    S.ts("dve", out, in_, scale, eps, ALU.mult, ALU.add)
    S.act(out, out, AF.Sqrt)
    S.op("dve", lambda g: g.reciprocal(out.ap, out.ap), [out], [out])


def modcol(C, l, part, kc):
    c = l * 48 + part * 8 + kc
    return C.mod[:, c:c + 1]


def rmsnorm_mod(S, C, x, h, gs, l, shift_part, W, tmp, sq, rstd):
    for kc in range(8):
        S.act(sq[:, kc, :], x[:, kc, :], AF.Square)
    for s0 in range(0, W, 512):
        ss = C.ps_ss
        for kc in range(8):
            S.mm(ss[:, :], C.ones_bf[:], sq[:, kc, s0:s0 + 512], start=(kc == 0), stop=(kc == 7))
        rsqrt_to(S, rstd[:, s0:s0 + 512], ss[:, :], 1.0 / D, EPS)
    for kc in range(8):
        e = "dve" if kc % 2 == 0 else "pool"
        S.tt(e, tmp[kc % 2][:, :], x[:, kc, :], rstd[:, :], ALU.mult)
        S.act(h[:, kc, :], tmp[kc % 2][:, :], AF.Identity, bias=modcol(C, l, shift_part, kc),
              scale=gs[:, l * 8 + kc:l * 8 + kc + 1])


def phase_F(S, C, I, l, T, mo, xin, xmid, xout, wo, PD, PDS, groups, WS, TBS=1024):
    HCL = HC // 2
    wg = I["ffn_w_gate_h"].h[l].rearrange("(kc p) n -> p kc n", p=128)
    wu = I["ffn_w_up_h"].h[l].rearrange("(kc p) n -> p kc n", p=128)
    wd = I["ffn_w_down_h"].h[l].rearrange("(j p) n -> p j n", p=128)
    wov = wo.rearrange("(kc p) n -> p kc n", p=128)
    NS = TBS // 512
    with Phase(S):
        C.ps_ss = S.ps([128, 512], F32, "ss")
        psA = [S.ps([128, 512], F32, "psA%d" % i) for i in range(3)]
        psB = [S.ps([128, 512], F32, "psB%d" % i) for i in range(3)]
        x = S.sb([128, 8, TBS], F32, "x")
        mot = S.sb([128, 8, TBS], BF16, "mo")
        sq = S.sb([128, 8, TBS], BF16, "sq")
        h = S.sb([128, 8, TBS], BF16, "h")
        a = S.sb([128, HCL, TBS], BF16, "a")
        rstd = S.sb([128, TBS], F32, "rstd")
        tmp = [S.sb([128, TBS], F32, "tmp%d" % i) for i in range(2)]
        sg = [S.sb([128, 512], F32, "sg%d" % i) for i in range(2)]
        pdt = [S.sb([128, 512], F32, "pdt%d" % i) for i in range(3)]
        wst = [S.sb([128, 8, 128], F32, "wst%d" % i) for i in range(4)]
        wbf = [S.sb([128, 8, 128], BF16, "wbf%d" % i) for i in range(4)]
        wdst = [S.sb([128, HCL, 128], F32, "wdst%d" % i) for i in range(2)]
        wdbf = [S.sb([128, HCL, 128], BF16, "wdbf%d" % i) for i in range(2)]
        wi = 0
        pi = 0
        di = 0

        def getw(kind, idx, src, st, bf, q, sb):
            view = WS[kind].h[idx].rearrange("p (a b) -> p a b", b=128)
            if sb == 0:
                S.dma(q, st[:], Ref(I["_w"], src, None))
                S.copy("pool", bf[:], st[:])
                S.dma("pool", WS[kind].sub(idx, view), bf[:])
            else:
                S.dma(q, bf[:], WS[kind].sub(idx, view))
        for t0 in range(0, T, TBS):
            sb = t0 // TBS
            S.dma("sp", x[:], xin.r(xin.h[:, t0:t0 + TBS].rearrange("(kc p) t -> p kc t", p=128)))
            S.dma("act", mot[:], mo.r(mo.h[:, t0:t0 + TBS].rearrange("(kc p) t -> p kc t", p=128)))
            for n in range(8):
                k = wi % 4
                wi += 1
                getw("o", n, wov[:, :, n * 128:(n + 1) * 128], wst[k], wbf[k], "sp", sb)
                for s in range(NS):
                    p = psA[pi % 3]
                    pi += 1
                    for kc in range(8):
                        S.mm(p[:, :], wbf[k][:, kc, :], mot[:, kc, s * 512:(s + 1) * 512],
                             start=(kc == 0), stop=(kc == 7))
                    S.stt("dve", x[:, n, s * 512:(s + 1) * 512], p[:, :], modcol(C, l, 2, n),
                          x[:, n, s * 512:(s + 1) * 512], ALU.mult, ALU.add)
            S.dma("act", xmid.r(xmid.h[:, t0:t0 + TBS].rearrange("(kc p) t -> p kc t", p=128)), x[:])
            rmsnorm_mod(S, C, x, h, C.gsf, l, 3, TBS, tmp, sq, rstd)
            for j in range(HCL):
                k0 = wi % 4
                k1 = (wi + 1) % 4
                wi += 2
                getw("g", j, wg[:, :, j * 128:(j + 1) * 128], wst[k0], wbf[k0], "sp", sb)
                getw("u", j, wu[:, :, j * 128:(j + 1) * 128], wst[k1], wbf[k1], "act", sb)
                for s in range(NS):
                    pg = psA[pi % 3]
                    pu = psB[pi % 3]
                    pi += 1
                    for kc in range(8):
                        S.mm(pg[:, :], wbf[k0][:, kc, :], h[:, kc, s * 512:(s + 1) * 512],
                             start=(kc == 0), stop=(kc == 7))
                    for kc in range(8):
                        S.mm(pu[:, :], wbf[k1][:, kc, :], h[:, kc, s * 512:(s + 1) * 512],
                             start=(kc == 0), stop=(kc == 7))
                    g = sg[pi % 2]
                    S.act(g[:, :], pg[:, :], AF.Silu)
                    S.tt("dve", a[:, j, s * 512:(s + 1) * 512], g[:, :], pu[:, :], ALU.mult)
            for n in range(8):
                k = n % 2
                getw("d", n, wd[:, :, n * 128:(n + 1) * 128], wdst[k], wdbf[k], "sp" if n % 2 == 0 else "act", sb)
                for s in range(NS):
                    p = psA[pi % 3]
                    pi += 1
                    for j in range(HCL):
                        S.mm(p[:, :], wdbf[k][:, j, :], a[:, j, s * 512:(s + 1) * 512],
                             start=(j == 0), stop=(j == HCL - 1))
                    o = pdt[di % 3]
                    di += 1
                    S.copy("act", o[:], p[:, :])
                    S.dma("pool", PD.sub(sb, PD.h[sb, n * 128:(n + 1) * 128, s * 512:(s + 1) * 512]), o[:])
            for hh_ in range(2):
                S.collective("AllReduce", ALU.add, groups, PD.sub(sb, PD.h[sb, hh_ * 512:(hh_ + 1) * 512, :]),
                             PDS.sub(sb, PDS.h[sb, hh_ * 512:(hh_ + 1) * 512, :]))
    with Phase(S):
        xs = [S.sb([128, 8, 512], F32, "xs%d" % i) for i in range(2)]
        ps_ = [S.sb([128, 8, 512], F32, "pds%d" % i) for i in range(2)]
        for bi, t0 in enumerate(range(0, T, 512)):
            xx, pp = xs[bi % 2], ps_[bi % 2]
            S.dma("sp", xx[:], xmid.r(xmid.h[:, t0:t0 + 512].rearrange("(kc p) t -> p kc t", p=128)))
            sb, so = t0 // TBS, t0 % TBS
            S.dma("act", pp[:], PDS.r(PDS.h[sb, :, so:so + 512].rearrange("(kc p) t -> p kc t", p=128)))
            for n in range(8):
                S.stt("dve", xx[:, n, :], pp[:, n, :], modcol(C, l, 5, n), xx[:, n, :], ALU.mult, ALU.add)
            S.dma("pool", xout.r(xout.h[:, t0:t0 + 512].rearrange("(kc p) t -> p kc t", p=128)), xx[:])


def declare_inputs(nc, specs):
    I = {}
    for name, (shape, dt) in specs.items():
        ap = nc.dram_tensor(name, list(shape), dt, kind="ExternalInput").ap()
        I[name] = Tile(ap, name)
    I["_w"] = Tile(None, "_w")
    return I


def input_specs(T):
    HH = FFN // 2
    sp = {
        "xT": ([D, T], F32), "c_pk": ([128, 8], F32),
        "w_ada": ([2, D, 6 * D], F32), "b_ada_pk": ([128, 96], F32),
        "norm_mix_pk": ([128, 16], F32), "norm_ffn_pk": ([128, 16], F32),
        "ffn_w_gate_h": ([2, D, HH], F32), "ffn_w_up_h": ([2, D, HH], F32),
        "ffn_w_down_h": ([2, HH, D], F32),
        "ev_w_out_p": ([D, D], F32), "od_w_o": ([D, D], F32),
        "ev_w_in_h": ([D, 1024], F32), "ev_s5_w_glu_h": ([512, 256], F32),
        "pos": ([T], I32), "rotm": ([128, 128], F32), "invf_col": ([128, 1], F32),
        "qk_gain_col": ([128, 2], F32), "masks": ([128, 4, 512], BF16),
        "lam_vecs": ([4, 64], F32), "subln_col": ([128, 1], F32),
        "swapm": ([128, 128], F32), "rowmask": ([128, 8], F32), "s5_d_pk": ([128, 2], F32),
        "lamre_row": ([128, 2, 64], F32), "lamim_row": ([128, 2, 64], F32), "logdt_row": ([128, 2], F32),
        "bT_re": ([128, 2, 64], F32), "bT_im": ([128, 2, 64], F32),
        "lamre_col": ([128, 16], F32), "lamim_col": ([128, 16], F32), "logdt_col": ([128, 16], F32),
        "cT_re_col": ([128, 16, 16], F32), "cT_im_col": ([128, 16, 16], F32),
        "od_w_r_h": ([D, 512], F32), "od_w_k_h": ([D, 512], F32), "od_w_v_h": ([D, 512], F32),
        "od_w1": ([1, D, 64], F32), "od_a1": ([1, D, 64], F32), "od_g1": ([1, D, 128], F32),
        "od_w2_h": ([64, 512], F32), "od_a2_h": ([64, 512], F32), "od_g2_h": ([128, 512], F32),
        "od_vecs_pk": ([128, 5, 4], F32), "od_mu_pk": ([128, 48], F32), "ident": ([128, 128], F32),
        "rw_masks": ([64, 4, 8, 64], F32), "od_ln_pk64": ([64, 2, 8], F32),
    }
    return sp


GROUPS = [[0, 4], [1, 5], [2, 6], [3, 7]]


def build(T=4096, mode="full"):
    nc = bass.Bass("TRN2", target_bir_lowering=False)
    I = declare_inputs(nc, input_specs(T))
    out = Tile(nc.dram_tensor("yT", [D, T], F32, kind="ExternalOutput").ap(), "yT")
    S = Sched(nc)
    C = Ctx()
    setup_consts(S, C)
    setup_adaln(S, C, I)
    dr = lambda shape, dt, name: S.dram(shape, dt, name, "Internal")
    uT = dr([256, T], BF16, "uT")
    qT = dr([256, T], BF16, "qT")
    kT = dr([256, T], BF16, "kT")
    vtm = dr([T, 256], BF16, "vtm")
    ygT = dr([256, T], BF16, "ygT")
    ygF = dr([512, T], BF16, "ygF")
    moL = dr([512, T], BF16, "moL")
    mo0 = dr([D, T], BF16, "mo0")
    phase_A0(S, C, I, T, I["xT"], uT, qT, kT, vtm)
    phase_S5(S, C, I, T, uT, ygT)
    S.collective("AllGather", ALU.bypass, GROUPS, ygT[:], ygF[:])
    phase_glu(S, C, I, T, ygF, ygT, moL)
    phase_attn(S, C, I, T, qT, kT, vtm, moL, 0.8 - 0.6 * math.exp(-0.3 * 0))
    for i in range(2):
        S.collective("AllGather", ALU.bypass, GROUPS, moL.r(moL.h[i * 256:(i + 1) * 256, :]),
                     mo0.r(mo0.h[i * 512:(i + 1) * 512, :]))
    xm0 = dr([D, T], F32, "xm0")
    x1 = dr([D, T], F32, "x1T")
    PD0 = dr([T // 1024, D, 1024], F32, "PD0")
    PS0 = dr([T // 1024, D, 1024], F32, "PS0")
    mkws = lambda l: {"o": dr([8, 128, 1024], BF16, "WSo%d" % l), "g": dr([HC // 2, 128, 1024], BF16, "WSg%d" % l),
                      "u": dr([HC // 2, 128, 1024], BF16, "WSu%d" % l), "d": dr([8, 128, (HC // 2) * 128], BF16, "WSd%d" % l)}
    phase_F(S, C, I, 0, T, mo0, I["xT"], xm0, x1, I["ev_w_out_p"].h, PD0, PS0, GROUPS, mkws(0))
    R = {k: dr([512, T], F32, k) for k in ("AR", "RR", "KB", "BB", "BON", "GG")}
    for k in ("KT", "BT", "VT"):
        R[k] = dr([T, 512], F32, k)
    R["GL"] = dr([512, T // 64], F32, "GL")
    mo1L = dr([512, T], BF16, "mo1L")
    mo1 = dr([D, T], BF16, "mo1")
    phase_B0(S, C, I, T, x1, R)
    phase_B1(S, C, I, T, R, mo1L)
    for i in range(2):
        S.collective("AllGather", ALU.bypass, GROUPS, mo1L.r(mo1L.h[i * 256:(i + 1) * 256, :]),
                     mo1.r(mo1.h[i * 512:(i + 1) * 512, :]))
    xm1 = dr([D, T], F32, "xm1")
    PD1 = dr([T // 1024, D, 1024], F32, "PD1")
    PS1 = dr([T // 1024, D, 1024], F32, "PS1")
    phase_F(S, C, I, 1, T, mo1, x1, xm1, out, I["od_w_o"].h, PD1, PS1, GROUPS, mkws(1))
    S.wait_all("sp", [out[:]])
    _barrier(S)
    print("instrs", S.ninstr, "waits", S.nwait)
    S.close()
    return nc


def host_inputs(inp, b, hf, T):
    f = lambda a: np.ascontiguousarray(np.asarray(a, dtype=np.float32))
    HH = FFN // 2
    hs = slice(hf * HH, (hf + 1) * HH)
    q4 = slice(hf * 256, (hf + 1) * 256)
    win = f(inp["ev_w_in"])[0]
    wout = f(inp["ev_w_out"])[0]
    perm = np.concatenate([np.arange(0, 256), np.arange(512, 768), np.arange(256, 512), np.arange(768, 1024)])
    m = {
        "xT": np.ascontiguousarray(f(inp["x"][b, :T]).T),
        "c_pk": pk(f(inp["c"])[b]),
        "w_ada": f(inp["w_ada"]),
        "b_ada_pk": pk(f(inp["b_ada"]).reshape(-1)),
        "norm_mix_pk": pk(f(inp["norm_mix"]).reshape(-1)),
        "norm_ffn_pk": pk(f(inp["norm_ffn"]).reshape(-1)),
        "ffn_w_gate_h": f(f(inp["ffn_w_gate"])[:, :, hs]), "ffn_w_up_h": f(f(inp["ffn_w_up"])[:, :, hs]),
        "ffn_w_down_h": f(f(inp["ffn_w_down"])[:, hs, :]),
        "ev_w_out_p": f(wout), "od_w_o": f(f(inp["od_w_o"])[0][perm, :]),
        "ev_w_in_h": f(np.concatenate([win[:, 0 + hf * 256:0 + (hf + 1) * 256], win[:, 512 + hf * 256:512 + (hf + 1) * 256],
                                       win[:, 1024 + hf * 256:1024 + (hf + 1) * 256], win[:, 1536 + hf * 256:1536 + (hf + 1) * 256]], axis=1)),
        "ev_s5_w_glu_h": f(f(inp["ev_s5_w_glu"])[0][:, q4]),
        "pos": np.ascontiguousarray(np.asarray(inp["positions"])[b, :T].astype(np.int32)),
    }
    m.update(const_inputs())
    qg = f(inp["ev_q_norm"])[0]; kg = f(inp["ev_k_norm"])[0]
    m["qk_gain_col"] = np.ascontiguousarray(np.stack([np.tile(qg, 2), np.tile(kg, 2)], axis=1))
    m["lam_vecs"] = np.ascontiguousarray(np.stack([f(inp["ev_lambda_q1"])[0], f(inp["ev_lambda_k1"])[0],
                                                   f(inp["ev_lambda_q2"])[0], f(inp["ev_lambda_k2"])[0]]))
    m["subln_col"] = np.ascontiguousarray(f(inp["ev_subln"])[0].reshape(128, 1))
    gs = slice(hf * 16, (hf + 1) * 16)
    m["s5_d_pk"] = pk(f(inp["ev_s5_d"])[0][gs].reshape(-1))
    lre = f(inp["ev_s5_lam_re"])[0][gs]; lim = f(inp["ev_s5_lam_im"])[0][gs]; ldt = f(inp["ev_s5_log_dt"])[0][gs]
    rowg = (np.arange(128) // 16)[:, None] + 8 * np.arange(2)[None, :]
    m["lamre_row"] = np.ascontiguousarray(lre[rowg]); m["lamim_row"] = np.ascontiguousarray(lim[rowg])
    m["logdt_row"] = np.ascontiguousarray(ldt[rowg])
    hh = (np.arange(128) % 16)
    bre = f(inp["ev_s5_b_re"])[0][gs]; bim = f(inp["ev_s5_b_im"])[0][gs]
    m["bT_re"] = np.ascontiguousarray(bre[rowg, :, hh[:, None]]); m["bT_im"] = np.ascontiguousarray(bim[rowg, :, hh[:, None]])
    m["lamre_col"] = np.ascontiguousarray(np.concatenate([lre.T, lre.T], 0)); m["lamim_col"] = np.ascontiguousarray(np.concatenate([lim.T, lim.T], 0))
    m["logdt_col"] = np.ascontiguousarray(np.broadcast_to(ldt[None, :], (128, 16)))
    cre = np.transpose(f(inp["ev_s5_c_re"])[0][gs], (2, 0, 1)); cim = np.transpose(f(inp["ev_s5_c_im"])[0][gs], (2, 0, 1))
    m["cT_re_col"] = np.ascontiguousarray(np.concatenate([cre, cre], 0)); m["cT_im_col"] = np.ascontiguousarray(np.concatenate([cim, cim], 0))
    fs = slice(hf * 512, (hf + 1) * 512)
    for nm in ("od_w_r", "od_w_k", "od_w_v"):
        m[nm + "_h"] = f(f(inp[nm])[0][:, fs])
    for nm in ("od_w1", "od_a1", "od_g1"):
        m[nm] = f(inp[nm])
    for nm in ("od_w2", "od_a2", "od_g2"):
        m[nm + "_h"] = f(f(inp[nm])[0][:, fs])
    vecs = np.stack([f(inp["od_w0"])[0], f(inp["od_a0"])[0], f(inp["od_k_k"])[0], f(inp["od_k_a"])[0],
                     f(inp["od_r_k"])[0].reshape(-1)])[:, fs]
    m["od_vecs_pk"] = np.ascontiguousarray(pk(vecs).reshape(128, 5, 4))
    m["od_mu_pk"] = pk(f(inp["od_mu"])[0])
    ln = np.stack([f(inp["od_ln_g"])[0], f(inp["od_ln_b"])[0]])[:, fs]
    m["od_ln_pk64"] = np.ascontiguousarray(np.transpose(ln.reshape(2, 8, 64), (2, 0, 1)))
    return m


def const_inputs():
    import ml_dtypes
    c = {}
    rotm = np.zeros((128, 128), np.float32)
    for base in (0, 64):
        for i in range(8):
            rotm[base + i + 8, base + i] = -1.0
            rotm[base + i, base + i + 8] = 1.0
    c["rotm"] = rotm
    invf = np.zeros((128, 1), np.float32)
    fr = (500000.0 ** (-np.arange(0, 16, 2, dtype=np.float32) / 16)).astype(np.float32)
    for base in (0, 64):
        invf[base:base + 8, 0] = fr
        invf[base + 8:base + 16, 0] = fr
    c["invf_col"] = invf
    kk = np.arange(128)[:, None]; qq = np.arange(512)[None, :]
    c["masks"] = np.stack([(qq >= 128 * j + kk) for j in range(4)], axis=1).astype(np.float32).astype(ml_dtypes.bfloat16)
    sw = np.zeros((128, 128), np.float32)
    for p in range(64):
        sw[64 + p, p] = 1.0
        sw[p, 64 + p] = -1.0
    c["swapm"] = sw
    c["ident"] = np.eye(128, dtype=np.float32)
    ss = np.arange(64)[:, None]; tt = np.arange(64)[None, :]
    m4 = np.stack([(ss < tt), (ss <= tt), (tt < ss), (ss == tt)]).astype(np.float32)
    c["rw_masks"] = np.ascontiguousarray(np.broadcast_to(np.transpose(m4, (1, 0, 2))[:, :, None, :], (64, 4, 8, 64)))
    c["rowmask"] = (np.arange(128)[:, None] // 16 == np.arange(8)[None, :]).astype(np.float32)
    return c


TWO_PI = 2.0 * math.pi


def sincos(S, sin_out, cos_out, ang, tmps):
    y, kf, ki = tmps
    for out, off in ((sin_out, 0.5), (cos_out, 0.75)):
        if out is None:
            continue
        S.ts("dve", y, ang, 1.0 / TWO_PI, off, ALU.mult, ALU.add)
        S.copy("dve", ki, y)
        S.copy("dve", kf, ki)
        S.tt("dve", y, y, kf, ALU.subtract)
        S.ts("dve", kf, y, 0.0, None, ALU.is_lt)
        S.tt("dve", y, y, kf, ALU.add)
        S.ts("dve", y, y, TWO_PI, -math.pi, ALU.mult, ALU.add)
        S.act(out, y, AF.Sin)


def phase_A0(S, C, I, T, xin, uT, qT, kT, vtm):
    win_v = I["ev_w_in_h"].h.rearrange("(kc p) n -> p kc n", p=128)
    with Phase(S):
        C.ps_ss = S.ps([128, 512], F32, "ss")
        psA = [S.ps([128, 512], F32, "psA%d" % i) for i in range(3)]
        psR = S.ps([128, 512], F32, "psR")
        win = S.sb([128, 8, 1024], BF16, "win")
        wst = [S.sb([128, 8, 256], F32, "wst%d" % i) for i in range(2)]
        for i in range(4):
            S.dma("sp" if i % 2 == 0 else "act", wst[i % 2][:], Ref(I["_w"], win_v[:, :, i * 256:(i + 1) * 256], None))
            S.copy("pool", win[:, :, i * 256:(i + 1) * 256], wst[i % 2][:])
        rotm = S.sb([128, 128], F32, "rotm")
        S.dma("sp", rotm[:], I["rotm"][:])
        invf = S.sb([128, 1], F32, "invf")
        S.dma("sp", invf[:], I["invf_col"][:])
        gq = S.sb([128, 2], F32, "gq")
        S.dma("sp", gq[:], I["qk_gain_col"][:])
        S.ts("dve", gq[:, 0:1], gq[:, 0:1], 0.125, None, ALU.mult)
        x = S.sb([128, 8, 512], F32, "x")
        sq = S.sb([128, 8, 512], BF16, "sq")
        h = S.sb([128, 8, 512], BF16, "h")
        rstd = S.sb([128, 512], F32, "rstd")
        tmp = [S.sb([128, 512], F32, "tmp%d" % i) for i in range(2)]
        posi = S.sb([128, 512], I32, "posi")
        ang = S.sb([128, 512], F32, "ang")
        cosT = S.sb([128, 512], F32, "cosT")
        sinT = S.sb([128, 512], F32, "sinT")
        sq2 = S.sb([128, 512], F32, "sq2")
        qn = S.sb([128, 512], F32, "qn")
        r2 = S.sb([128, 512], F32, "r2")
        ob = [S.sb([128, 512], BF16, "ob%d" % i) for i in range(3)]
        oi = 0
        pi = 0
        for t0 in range(0, T, 512):
            S.dma("sp", x[:], xin.r(xin.h[:, t0:t0 + 512].rearrange("(kc p) t -> p kc t", p=128)))
            S.dma("act", posi[:], I["pos"].r(I["pos"].h[t0:t0 + 512].partition_broadcast(128)))
            S.copy("dve", ang[:], posi[:])
            S.ts("dve", ang[:], ang[:], invf[:, 0:1], None, ALU.mult)
            sincos(S, sinT[:], cosT[:], ang[:], (tmp[0][:], tmp[1][:], posi[:]))
            rmsnorm_mod(S, C, x, h, C.gsm, 0, 0, 512, tmp, sq, rstd)
            for n in range(6):
                p = psA[pi % 3]
                pi += 1
                for kc in range(8):
                    S.mm(p[:, :], win[:, kc, n * 128:(n + 1) * 128], h[:, kc, :], start=(kc == 0), stop=(kc == 7))
                o = ob[oi % 3]
                oi += 1
                if n < 2:
                    S.copy("act", o[:], p[:, :])
                    S.dma("pool", uT.r(uT.h[n * 128:(n + 1) * 128, t0:t0 + 512]), o[:])
                    continue
                isq = n < 4
                S.act(sq2[:], p[:, :], AF.Square)
                S.mm(C.ps_ss[:, :], C.blk_f[:], sq2[:])
                rsqrt_to(S, rstd[:], C.ps_ss[:, :], 1.0 / 64, EPS)
                S.tt("dve", qn[:], p[:, :], rstd[:], ALU.mult)
                S.ts("dve", qn[:], qn[:], gq[:, 0:1] if isq else gq[:, 1:2], None, ALU.mult)
                S.mm(psR[:, :], rotm[:], qn[:])
                S.tt("dve", r2[:], psR[:, :], sinT[:], ALU.mult)
                S.tt("pool", qn[:], qn[:], cosT[:], ALU.add if False else ALU.mult)
                S.tt("pool", o[:], qn[:], r2[:], ALU.add)
                dst = qT if isq else kT
                hd = (n - 2) % 2
                S.dma("pool", dst.r(dst.h[hd * 128:(hd + 1) * 128, t0:t0 + 512]), o[:])
            for tt_ in range(4):
                p = psA[pi % 3]
                pi += 1
                for kc in range(8):
                    S.mm(p[:, 0:256], h[:, kc, tt_ * 128:(tt_ + 1) * 128], win[:, kc, 768:1024],
                         start=(kc == 0), stop=(kc == 7))
                o = ob[oi % 3]
                oi += 1
                S.copy("act", o[:, 0:256], p[:, 0:256])
                S.dma("pool", vtm.r(vtm.h[t0 + tt_ * 128:t0 + (tt_ + 1) * 128, :]), o[:, 0:256])


def phase_attn(S, C, I, T, qT, kT, vtm, moT, lam_init):
    NQ = T // 512
    NK = T // 128
    with Phase(S):
        C.ps_ss = S.ps([128, 512], F32, "ss")
        psS = [S.ps([128, 512], F32, "psS%d" % i) for i in range(3)]
        psO = [S.ps([128, 512], F32, "psO%d" % i) for i in range(2)]
        psZ = [S.ps([128, 512], F32, "psZ%d" % i) for i in range(2)]
        masks = S.sb([128, 4, 512], BF16, "masks")
        S.dma("sp", masks[:], I["masks"][:])
        lv = S.sb([128, 4, 64], F32, "lv")
        S.dma("sp", lv[:], I["lam_vecs"].r(I["lam_vecs"].h[:, :].partition_broadcast(128)))
        lp = S.sb([128, 2, 64], F32, "lp")
        S.tt("dve", lp[:, 0, :], lv[:, 0, :], lv[:, 1, :], ALU.mult)
        S.tt("dve", lp[:, 1, :], lv[:, 2, :], lv[:, 3, :], ALU.mult)
        ls = S.sb([128, 2], F32, "ls")
        S.op("dve", lambda g: g.reduce_sum(ls.h[:, :], lp.h[:, :, :], axis=mybir.AxisListType.X), [lp[:]], [ls[:]])
        S.act(ls[:], ls[:], AF.Exp)
        nlam = S.sb([128, 1], F32, "nlam")
        S.tt("dve", nlam[:], ls[:, 1:2], ls[:, 0:1], ALU.subtract)
        S.ts("dve", nlam[:], nlam[:], -lam_init, None, ALU.add)
        sg = S.sb([128, 1], F32, "sg")
        S.dma("sp", sg[:], I["subln_col"][:])
        S.ts("dve", sg[:], sg[:], 1.0 - lam_init, None, ALU.mult)
        q = S.sb([128, T], BF16, "q")
        k = S.sb([128, T], BF16, "k")
        v = S.sb([128, NK, 128], BF16, "v")
        pt = [S.sb([128, 512], BF16, "pt%d" % i) for i in range(8)]
        rs = [S.sb([128, 512], F32, "rs%d" % i) for i in range(2)]
        o0 = S.sb([128, 512], F32, "o0")
        o1 = S.sb([128, 512], F32, "o1")
        sq2 = S.sb([128, 512], F32, "sq2")
        rstd = S.sb([128, 512], F32, "rstd")
        ob = [S.sb([128, 512], BF16, "ob%d" % i) for i in range(2)]
        it = 0
        for hd in range(2):
            S.dma("sp", q[:], qT.r(qT.h[hd * 128:(hd + 1) * 128, :]))
            S.dma("act", k[:], kT.r(kT.h[hd * 128:(hd + 1) * 128, :]))
            S.dma("sp", v[:], vtm.r(vtm.h[:, hd * 128:(hd + 1) * 128].rearrange("(kb p) e -> p kb e", p=128)))
            for qb in range(NQ):
                nkb = 4 * (qb + 1)
                tiles = [(kb, c) for kb in range(nkb) for c in range(2)]
                LA = 2
                pts = {}
                for idx in range(len(tiles) + LA):
                    if idx < len(tiles):
                        kb, c = tiles[idx]
                        ps = psS[it % 3]
                        p = pt[it % 8]
                        it += 1
                        pts[idx] = p
                        S.mm(ps[:, :], k[c * 64:(c + 1) * 64, kb * 128:(kb + 1) * 128],
                             q[c * 64:(c + 1) * 64, qb * 512:(qb + 1) * 512])
                        S.act(p[:], ps[:, :], AF.Exp)
                        j = kb - 4 * qb
                        if j >= 0:
                            S.tt("pool", p[:], p[:], masks[:, j, :], ALU.mult)
                    if idx - LA >= 0:
                        kb, c = tiles[idx - LA]
                        p = pts.pop(idx - LA)
                        S.mm(psO[c][:, :], v[:, kb, :], p[:], start=(kb == 0), stop=(kb == nkb - 1))
                        S.mm(psZ[c][:, :], C.ones_bf[:], p[:], start=(kb == 0), stop=(kb == nkb - 1))
                for c in range(2):
                    S.op("dve", lambda g, c=c: g.reciprocal(rs[c].h[:, :], psZ[c].h[:, :]), [psZ[c][:]], [rs[c][:]])
                S.tt("dve", o0[:], psO[0][:, :], rs[0][:], ALU.mult)
                S.tt("dve", o1[:], psO[1][:, :], rs[1][:], ALU.mult)
                S.stt("dve", o0[:], o1[:], nlam[:, 0:1], o0[:], ALU.mult, ALU.add)
                S.act(sq2[:], o0[:], AF.Square)
                S.mm(C.ps_ss[:, :], C.ones_f[:], sq2[:])
                rsqrt_to(S, rstd[:], C.ps_ss[:, :], 1.0 / 128, EPS)
                S.tt("pool", o0[:], o0[:], rstd[:], ALU.mult)
                o = ob[qb % 2]
                S.ts("dve", o[:], o0[:], sg[:, 0:1], None, ALU.mult)
                S.dma("pool", moT.r(moT.h[256 + hd * 128:256 + (hd + 1) * 128, qb * 512:(qb + 1) * 512]), o[:])


def phase_S5(S, C, I, T, uT, ygT):
    NB = T // 512
    with Phase(S):
        psA = [S.ps([128, 512], F32, "psA%d" % i) for i in range(2)]
        psB = [S.ps([128, 512], F32, "psB%d" % i) for i in range(2)]
        psY = S.ps([128, 512], F32, "psY")
        psW = S.ps([128, 8], F32, "psW")
        swap = S.sb([128, 128], F32, "swap")
        S.dma("sp", swap[:], I["swapm"][:])
        rowmask = S.sb([128, 8], F32, "rowmask")
        S.dma("sp", rowmask[:], I["rowmask"][:])
        dcol = S.sb([128, 2], F32, "dcol")
        S.dma("sp", dcol[:], I["s5_d_pk"][:])
        lr = S.sb([128, 2, 64], F32, "lr")
        li = S.sb([128, 2, 64], F32, "li")
        ldt = S.sb([128, 2], F32, "ldt")
        S.dma("sp", lr[:], I["lamre_row"][:])
        S.dma("act", li[:], I["lamim_row"][:])
        S.dma("sp", ldt[:], I["logdt_row"][:])
        S.act(ldt[:], ldt[:], AF.Exp)
        dtb = ldt.h[:, :].unsqueeze(2).to_broadcast([128, 2, 64])
        th = S.sb([128, 2, 64], F32, "th")
        mag = S.sb([128, 2, 64], F32, "mag")
        S.tt("dve", th[:], li[:], ldt.r(dtb), ALU.mult)
        S.tt("dve", mag[:], lr[:], ldt.r(dtb), ALU.mult)
        S.act(mag[:], mag[:], AF.Exp)
        cs = S.sb([128, 2, 64], F32, "cs")
        sn = S.sb([128, 2, 64], F32, "sn")
        t4 = S.sb([128, 2, 64], F32, "t4")
        t4b = S.sb([128, 2, 64], F32, "t4b")
        t4i = S.sb([128, 2, 64], I32, "t4i")
        sincos(S, sn[:], cs[:], th[:], (t4[:], t4b[:], t4i[:]))
        ar = S.sb([128, 2, 64], F32, "ar")
        ai = S.sb([128, 2, 64], F32, "ai")
        S.tt("dve", ar[:], mag[:], cs[:], ALU.mult)
        S.tt("dve", ai[:], mag[:], sn[:], ALU.mult)
        S.ts("dve", ar[:], ar[:], -1.0, None, ALU.add)
        den = S.sb([128, 2, 64], F32, "den")
        S.tt("dve", den[:], lr[:], lr[:], ALU.mult)
        S.tt("dve", t4[:], li[:], li[:], ALU.mult)
        S.tt("dve", den[:], den[:], t4[:], ALU.add)
        S.op("dve", lambda g: g.reciprocal(den.h[:], den.h[:]), [den[:]], [den[:]])
        qr = S.sb([128, 2, 64], F32, "qr")
        qi = S.sb([128, 2, 64], F32, "qi")
        S.tt("dve", qr[:], ar[:], lr[:], ALU.mult)
        S.tt("dve", t4[:], ai[:], li[:], ALU.mult)
        S.tt("dve", qr[:], qr[:], t4[:], ALU.add)
        S.tt("dve", qr[:], qr[:], den[:], ALU.mult)
        S.tt("dve", qi[:], ai[:], lr[:], ALU.mult)
        S.tt("dve", t4[:], ar[:], li[:], ALU.mult)
        S.tt("dve", qi[:], qi[:], t4[:], ALU.subtract)
        S.tt("dve", qi[:], qi[:], den[:], ALU.mult)
        btr = S.sb([128, 2, 64], F32, "btr")
        bti = S.sb([128, 2, 64], F32, "bti")
        S.dma("sp", btr[:], I["bT_re"][:])
        S.dma("act", bti[:], I["bT_im"][:])
        bbr = S.sb([128, 2, 64], F32, "bbr")
        bbi = S.sb([128, 2, 64], F32, "bbi")
        S.tt("dve", bbr[:], qr[:], btr[:], ALU.mult)
        S.tt("dve", t4[:], qi[:], bti[:], ALU.mult)
        S.tt("dve", bbr[:], bbr[:], t4[:], ALU.subtract)
        S.tt("dve", bbi[:], qr[:], bti[:], ALU.mult)
        S.tt("dve", t4[:], qi[:], btr[:], ALU.mult)
        S.tt("dve", bbi[:], bbi[:], t4[:], ALU.add)
        nbbr = S.sb([128, 2, 64], F32, "nbbr")
        S.ts("dve", nbbr[:], bbr[:], -1.0, None, ALU.mult)
        lrc = S.sb([128, 16], F32, "lrc")
        lic = S.sb([128, 16], F32, "lic")
        dtc = S.sb([128, 16], F32, "dtc")
        S.dma("sp", lrc[:], I["lamre_col"][:])
        S.dma("act", lic[:], I["lamim_col"][:])
        S.dma("sp", dtc[:], I["logdt_col"][:])
        S.act(dtc[:], dtc[:], AF.Exp)
        thc = S.sb([128, 16], F32, "thc")
        rc = S.sb([128, 16], F32, "rc")
        S.tt("dve", thc[:], lic[:], dtc[:], ALU.mult)
        S.tt("dve", rc[:], lrc[:], dtc[:], ALU.mult)
        S.act(rc[:], rc[:], AF.Exp)
        c1 = S.sb([128, 16], F32, "c1")
        s1 = S.sb([128, 16], F32, "s1")
        t32 = S.sb([128, 16], F32, "t32")
        t32b = S.sb([128, 16], F32, "t32b")
        t32i = S.sb([128, 16], I32, "t32i")
        sincos(S, s1[:], c1[:], thc[:], (t32[:], t32b[:], t32i[:]))
        ctc = S.sb([128, 16, 16], F32, "ctc")
        cti = S.sb([128, 16, 16], F32, "cti")
        S.dma("sp", ctc[:], I["cT_re_col"][:])
        S.dma("act", cti[:], I["cT_im_col"][:])
        S.ts("dve", cti[:], cti[:], -1.0, None, ALU.mult)
        Ct = S.sb([128, 8, 512], F32, "Ct")
        St = S.sb([128, 8, 512], F32, "St")
        Cm = S.sb([128, 8, 512], F32, "Cm")
        Sm = S.sb([128, 8, 512], F32, "Sm")
        Rt = S.sb([128, 8, 512], F32, "Rt")
        tA = S.sb([128, 8, 256], F32, "tA")
        LA = S.sb([128, 8, 128], BF16, "LA")
        LB = S.sb([128, 8, 128], BF16, "LB")
        LC1 = S.sb([128, 8, 128], BF16, "LC1")
        LC2 = S.sb([128, 8, 128], BF16, "LC2")
        W = S.sb([128, 8, 512], F32, "W")
        w0 = S.sb([128, 8], F32, "w0")
        wl = S.sb([128, 8], F32, "wl")
        tw = S.sb([128, 8], F32, "tw")
        u = [S.sb([128, 512], BF16, "u%d" % i) for i in range(2)]
        t1 = [S.sb([128, 512], F32, "t1_%d" % i) for i in range(2)]
        t2 = [S.sb([128, 512], F32, "t2_%d" % i) for i in range(2)]
        R1 = [S.sb([128, 512], BF16, "R1_%d" % i) for i in range(2)]
        R2 = [S.sb([128, 512], BF16, "R2_%d" % i) for i in range(2)]
        yv = S.sb([128, 512], F32, "yv")
        y2 = S.sb([128, 512], F32, "y2")
        yo = [S.sb([128, 512], BF16, "yo%d" % i) for i in range(2)]
        it = 0
        for cc in range(2):
            g0 = cc * 8
            S.copy("dve", Ct[:, :, 0:1], c1.r(c1.h[:, g0:g0 + 8].unsqueeze(2)))
            S.copy("dve", St[:, :, 0:1], s1.r(s1.h[:, g0:g0 + 8].unsqueeze(2)))
            n = 1
            while n < 512:
                cn = Ct.r(Ct.h[:, :, n - 1:n].to_broadcast([128, 8, n]))
                sn_ = St.r(St.h[:, :, n - 1:n].to_broadcast([128, 8, n]))
                ta = tA.r(tA.h[:, :, 0:n])
                S.tt("dve", ta, St[:, :, 0:n], sn_, ALU.mult)
                S.tt("dve", Ct[:, :, n:2 * n], Ct[:, :, 0:n], cn, ALU.mult)
                S.tt("dve", Ct[:, :, n:2 * n], Ct[:, :, n:2 * n], ta, ALU.subtract)
                S.tt("dve", ta, St[:, :, 0:n], cn, ALU.mult)
                S.tt("dve", St[:, :, n:2 * n], Ct[:, :, 0:n], sn_, ALU.mult)
                S.tt("dve", St[:, :, n:2 * n], St[:, :, n:2 * n], ta, ALU.add)
                n *= 2
            S.copy("pool", Cm[0:64, :, :], Ct[0:64, :, :])
            S.ts("pool", Cm[64:128, :, :], St[64:128, :, :], -1.0, None, ALU.mult)
            S.copy("pool", Sm[0:64, :, :], St[0:64, :, :])
            S.copy("pool", Sm[64:128, :, :], Ct[64:128, :, :])
            S.memset("pool", Rt[:], 1.0)
            S.tt("pool", Rt[:], Rt[:], rc.r(rc.h[:, g0:g0 + 8].unsqueeze(2).to_broadcast([128, 8, 512])), ALU.mult)
            S.memset("pool", LC1[:], 0.0)
            S.memset("pool", LC2[:], 0.0)
            for gi in range(8):
                S.ts("dve", LA[:, gi, 0:64], bbr[:, cc, :], rowmask[:, gi:gi + 1], None, ALU.mult)
                S.ts("dve", LA[:, gi, 64:128], bbi[:, cc, :], rowmask[:, gi:gi + 1], None, ALU.mult)
                S.ts("dve", LB[:, gi, 0:64], bbi[:, cc, :], rowmask[:, gi:gi + 1], None, ALU.mult)
                S.ts("dve", LB[:, gi, 64:128], nbbr[:, cc, :], rowmask[:, gi:gi + 1], None, ALU.mult)
                S.copy("pool", LC1[:, gi, gi * 16:(gi + 1) * 16], ctc[:, g0 + gi, :])
                S.copy("pool", LC2[:, gi, gi * 16:(gi + 1) * 16], cti[:, g0 + gi, :])
            S.memset("pool", w0[:], 0.0)
            for tb in range(NB):
                ut = u[tb % 2]
                S.dma("sp", ut[:], uT.r(uT.h[cc * 128:(cc + 1) * 128, tb * 512:(tb + 1) * 512]))
                for gi in range(8):
                    k = it % 2
                    it += 1
                    if gi == 0:
                        S.mm(psA[k][:, :], LA[:, gi, :], ut[:])
                        S.mm(psB[k][:, :], LB[:, gi, :], ut[:])
                    if gi + 1 < 8:
                        S.mm(psA[1 - k][:, :], LA[:, gi + 1, :], ut[:])
                        S.mm(psB[1 - k][:, :], LB[:, gi + 1, :], ut[:])
                    S.tt("dve", t1[k][:], psA[k][:, :], Ct[:, gi, :], ALU.mult)
                    S.tt("dve", t2[k][:], psB[k][:, :], St[:, gi, :], ALU.mult)
                    S.tt("pool", t1[k][:], t1[k][:], t2[k][:], ALU.add)
                    S.scan("dve", W[:, gi, :], Rt[:, gi, :], t1[k][:], w0[:, gi:gi + 1])
                    S.tt("pool", R1[k][:], W[:, gi, :], Cm[:, gi, :], ALU.mult)
                    S.tt("pool", R2[k][:], W[:, gi, :], Sm[:, gi, :], ALU.mult)
                    S.mm(psY[:, :], LC1[:, gi, :], R1[k][:], start=(gi == 0), stop=False)
                    S.mm(psY[:, :], LC2[:, gi, :], R2[k][:], start=False, stop=(gi == 7))
                S.copy("dve", wl[:], W.r(W.h[:, :, 511]))
                S.mm(psW[:, :], swap[:], wl[:])
                S.tt("dve", tw[:], psW[:, :], St.r(St.h[:, :, 511]), ALU.mult)
                S.tt("dve", w0[:], wl[:], Ct.r(Ct.h[:, :, 511]), ALU.mult)
                S.tt("dve", w0[:], w0[:], tw[:], ALU.subtract)
                S.stt("dve", yv[:], ut[:], dcol[:, cc:cc + 1], psY[:, :], ALU.mult, ALU.add)
                S.act(y2[:], yv[:], AF.Square)
                S.ts("dve", y2[:], y2[:], 0.044715, 1.0, ALU.mult, ALU.add)
                S.tt("pool", y2[:], y2[:], yv[:], ALU.mult)
                S.act(y2[:], y2[:], AF.Sigmoid, scale=1.5957691216057308)
                o = yo[tb % 2]
                S.tt("dve", o[:], yv[:], y2[:], ALU.mult)
                S.dma("pool", ygT.r(ygT.h[cc * 128:(cc + 1) * 128, tb * 512:(tb + 1) * 512]), o[:])


def phase_glu(S, C, I, T, ygF, ygL, moT):
    wv = I["ev_s5_w_glu_h"].h.rearrange("(kc p) n -> p kc n", p=128)
    with Phase(S):
        ps = [S.ps([128, 512], F32, "ps%d" % i) for i in range(2)]
        wst = S.sb([128, 4, 256], F32, "wst")
        S.dma("sp", wst[:], Ref(I["_w"], wv, None))
        w = S.sb([128, 4, 256], BF16, "w")
        S.copy("pool", w[:], wst[:])
        yg = [S.sb([128, 4, 512], BF16, "yg%d" % i) for i in range(2)]
        yl = [S.sb([128, 2, 512], BF16, "yl%d" % i) for i in range(2)]
        sg = [S.sb([128, 512], F32, "sg%d" % i) for i in range(2)]
        ob = [S.sb([128, 512], BF16, "ob%d" % i) for i in range(2)]
        it = 0
        for tb in range(T // 512):
            y = yg[tb % 2]
            yy = yl[tb % 2]
            S.dma("sp", y[:], ygF.r(ygF.h[:, tb * 512:(tb + 1) * 512].rearrange("(kc p) t -> p kc t", p=128)))
            S.dma("act", yy[:], ygL.r(ygL.h[:, tb * 512:(tb + 1) * 512].rearrange("(kc p) t -> p kc t", p=128)))
            for n in range(2):
                p = ps[it % 2]
                for kc in range(4):
                    S.mm(p[:, :], w[:, kc, n * 128:(n + 1) * 128], y[:, kc, :], start=(kc == 0), stop=(kc == 3))
                S.act(sg[it % 2][:], p[:, :], AF.Sigmoid)
                S.tt("dve", ob[it % 2][:], yy[:, n, :], sg[it % 2][:], ALU.mult)
                S.dma("pool", moT.r(moT.h[n * 128:(n + 1) * 128, tb * 512:(tb + 1) * 512]), ob[it % 2][:])
                it += 1


def phase_B0(S, C, I, T, xin, R):
    TB = 256
    NCH = TB // 64
    wv_ = {n: I[n].h[0].rearrange("(kc p) n -> p kc n", p=128) for n in ("od_w1", "od_a1", "od_g1")}
    for n in ("od_w_r", "od_w_k", "od_w_v"):
        wv_[n] = I[n + "_h"].h.rearrange("(kc p) n -> p kc n", p=128)
    with Phase(S):
        C.ps_ss = S.ps([128, 512], F32, "ss")
        psA_ = [S.ps([128, 512], F32, "psA%d" % i) for i in range(4)]
        psT_ = [S.ps([128, 512], F32, "psT%d" % i) for i in range(2)]

        class _V:
            def __init__(self, t, w):
                self.t, self.w = t, w

            def __getitem__(self, idx):
                return Ref(self.t, self.t.h[:, 0:self.w][idx], None)
        psA = [_V(t, TB) for t in psA_]
        psT = [_V(t, 128) for t in psT_]
        wst = [S.sb([128, 8, 128], F32, "wst%d" % i) for i in range(2)]
        W3 = [S.sb([128, 8, 512], BF16, "W%d" % i) for i in range(3)]
        wi = 0
        for m, nm in enumerate(("od_w_r", "od_w_k", "od_w_v")):
            for n in range(4):
                S.dma("sp" if wi % 2 == 0 else "act", wst[wi % 2][:], Ref(I["_w"], wv_[nm][:, :, n * 128:(n + 1) * 128], None))
                S.copy("pool", W3[m][:, :, n * 128:(n + 1) * 128], wst[wi % 2][:])
                wi += 1
        L1 = S.sb([128, 8, 256], BF16, "L1")
        for nm, lo, wd in (("od_w1", 0, 64), ("od_a1", 64, 64), ("od_g1", 128, 128)):
            S.dma("sp", wst[wi % 2][:, :, 0:wd], Ref(I["_w"], wv_[nm], None))
            S.copy("pool", L1[:, :, lo:lo + wd], wst[wi % 2][:, :, 0:wd])
            wi += 1
        L2st = S.sb([128, 3, 512], F32, "L2st")
        S.memset("pool", L2st[:], 0.0)
        S.dma("sp", L2st[0:64, 0, :], I["od_w2_h"][:])
        S.dma("sp", L2st[0:64, 1, :], I["od_a2_h"][:])
        S.dma("sp", L2st[:, 2, :], I["od_g2_h"][:])
        L2 = S.sb([128, 3, 512], BF16, "L2")
        S.copy("pool", L2[:], L2st[:])
        pv = S.sb([128, 7, 4], F32, "pv")
        S.dma("sp", pv[:, 0:5, :], I["od_vecs_pk"][:])
        S.ts("dve", pv[:, 5, :], pv[:, 3, :], -1.0, 1.0, ALU.mult, ALU.add)
        S.ts("dve", pv[:, 6, :], pv[:, 0, :], -1.0, None, ALU.mult)
        mu = S.sb([128, 48], F32, "mu")
        S.dma("sp", mu[:], I["od_mu_pk"][:])
        ident = S.sb([128, 128], F32, "ident")
        S.dma("sp", ident[:], I["ident"][:])
        rmask = S.sb([128, TB], F32, "rmask")
        S.memset("pool", rmask[:], 1.0)
        for c in range(NCH):
            S.memset("pool", rmask[:, c * 64:c * 64 + 1], 0.0)
        x = S.sb([128, 8, TB], F32, "x")
        sq = S.sb([128, 8, TB], BF16, "sq")
        hs = S.sb([128, 8, TB + 1], F32, "hs")
        S.memset("pool", hs[:], 0.0)
        hh = S.sb([128, 8, TB], F32, "hh")
        dx = S.sb([128, 8, TB], F32, "dx")
        xm = [S.sb([128, 8, TB], BF16, "xm%d" % i) for i in range(6)]
        rstd = S.sb([128, TB], F32, "rstd")
        tmp = [S.sb([128, TB], F32, "tmp%d" % i) for i in range(2)]
        lo1 = S.sb([128, 3, TB], BF16, "lo1")
        S.memset("pool", lo1[:], 0.0)
        Es = [{k: S.sb([128, TB], F32, k + str(i)) for k in ("r", "k", "v", "lw", "cum", "ep", "em", "ex", "a", "kk", "t0", "t1", "kmod", "b", "o0", "o1", "o2")} for i in range(2)]
        tq = [S.sb([128, 128], F32, "tq%d" % i) for i in range(2)]
        gls = [S.sb([128, NCH], F32, "gl%d" % i) for i in range(2)]
        pi = 0
        ti = 0
        for t0 in range(0, T, TB):
            S.dma("sp", x[:], xin.r(xin.h[:, t0:t0 + TB].rearrange("(kc p) t -> p kc t", p=128)))
            for kc in range(8):
                S.act(sq[:, kc, :], x[:, kc, :], AF.Square)
            for kc in range(8):
                S.mm(C.ps_ss[:, 0:TB], C.ones_bf[:], sq[:, kc, :], start=(kc == 0), stop=(kc == 7))
            rsqrt_to(S, rstd[:], C.ps_ss[:, 0:TB], 1.0 / D, EPS)
            for kc in range(8):
                S.tt("dve", tmp[kc % 2][:], x[:, kc, :], rstd[:], ALU.mult)
                S.act(hh[:, kc, :], tmp[kc % 2][:], AF.Identity, bias=modcol(C, 1, 0, kc), scale=C.gsm[:, 8 + kc:9 + kc])
            S.copy("pool", hs[:, :, 1:TB + 1], hh[:])
            S.tt("dve", dx[:], hs[:, :, 0:TB], hh[:], ALU.subtract)
            S.copy("pool", hs[:, :, 0:1], hh[:, :, TB - 1:TB])
            for m in range(6):
                for kc in range(8):
                    S.stt("dve", xm[m][:, kc, :], dx[:, kc, :],
                          mu[:, m * 8 + kc:m * 8 + kc + 1], hh[:, kc, :], ALU.mult, ALU.add)
            for j, (mx, lo, wd, fn) in enumerate(((1, 0, 64, AF.Tanh), (4, 64, 64, AF.Identity), (5, 128, 128, AF.Sigmoid))):
                p = psA[pi % 4]
                pi += 1
                for kc in range(8):
                    S.mm(p[0:wd, :], L1[:, kc, lo:lo + wd], xm[mx][:, kc, :], start=(kc == 0), stop=(kc == 7))
                S.act(lo1[0:wd, j, :], p[0:wd, :], fn)
            for n in range(4):
                f0 = n * 128
                E = Es[n % 2]
                gl = gls[n % 2]
                pr, pk_, pv_, pw = [psA[(pi + i) % 4] for i in range(4)]
                for (p, m, mx) in ((pr, 0, 0), (pk_, 1, 2), (pv_, 2, 3)):
                    for kc in range(8):
                        S.mm(p[:, :], W3[m][:, kc, f0:f0 + 128], xm[mx][:, kc, :], start=(kc == 0), stop=(kc == 7))
                S.copy("act", E["r"][:], pr[:, :])
                S.copy("act", E["k"][:], pk_[:, :])
                S.copy("act", E["v"][:], pv_[:, :])
                S.mm(pw[:, :], L2[:, 0, f0:f0 + 128], lo1[:, 0, :])
                S.act(E["t0"][:], pw[:, :], AF.Exp, bias=pv[:, 6, n:n + 1], scale=-1.0)
                S.ts("dve", E["t0"][:], E["t0"][:], 1.0, None, ALU.add)
                S.act(E["t0"][:], E["t0"][:], AF.Ln)
                S.ts("dve", E["t0"][:], E["t0"][:], -1.0, -0.5, ALU.mult, ALU.add)
                S.act(E["t0"][:], E["t0"][:], AF.Exp)
                S.ts("dve", E["lw"][:], E["t0"][:], -1.0, None, ALU.mult)
                S.scan("dve", E["cum"][:], rmask[:], E["lw"][:], 0.0)
                S.act(E["ep"][:], E["cum"][:], AF.Exp)
                S.act(E["em"][:], E["cum"][:], AF.Exp, scale=-1.0)
                S.tt("pool", E["t1"][:], E["cum"][:], E["lw"][:], ALU.subtract)
                S.act(E["ex"][:], E["t1"][:], AF.Exp)
                S.mm(pw[:, :], L2[:, 1, f0:f0 + 128], lo1[:, 1, :])
                S.act(E["a"][:], pw[:, :], AF.Sigmoid, bias=pv[:, 1, n:n + 1])
                S.ts("dve", E["kk"][:], E["k"][:], pv[:, 2, n:n + 1], None, ALU.mult)
                S.tt("pool", E["t0"][:], E["kk"][:], E["kk"][:], ALU.mult)
                S.mm(C.ps_ss[:, 0:TB], C.blk_f[:], E["t0"][:])
                S.act(E["t0"][:], C.ps_ss[:, 0:TB], AF.Sqrt)
                S.ts("dve", E["t0"][:], E["t0"][:], 1e-12, None, ALU.max)
                S.op("dve", lambda g: g.reciprocal(E["t0"].h[:], E["t0"].h[:]), [E["t0"][:]], [E["t0"][:]])
                S.tt("dve", E["kk"][:], E["kk"][:], E["t0"][:], ALU.mult)
                S.ts("dve", E["t1"][:], E["a"][:], pv[:, 3, n:n + 1], pv[:, 5, n:n + 1], ALU.mult, ALU.add)
                S.tt("dve", E["kmod"][:], E["k"][:], E["t1"][:], ALU.mult)
                S.tt("pool", E["b"][:], E["kk"][:], E["a"][:], ALU.mult)
                S.mm(pw[:, :], L2[:, 2, f0:f0 + 128], lo1[:, 2, :])
                S.copy("act", E["o2"][:], pw[:, :])
                S.dma("pool", R["GG"].r(R["GG"].h[f0:f0 + 128, t0:t0 + TB]), E["o2"][:])
                pi += 4
                S.tt("dve", E["t0"][:], E["r"][:], E["kmod"][:], ALU.mult)
                S.ts("dve", E["t0"][:], E["t0"][:], pv[:, 4, n:n + 1], None, ALU.mult)
                S.mm(C.ps_ss[:, 0:TB], C.blk_f[:], E["t0"][:])
                S.tt("dve", E["o0"][:], C.ps_ss[:, 0:TB], E["v"][:], ALU.mult)
                S.dma("pool", R["BON"].r(R["BON"].h[f0:f0 + 128, t0:t0 + TB]), E["o0"][:])
                S.tt("dve", E["o1"][:], E["r"][:], E["ep"][:], ALU.mult)
                S.dma("pool", R["RR"].r(R["RR"].h[f0:f0 + 128, t0:t0 + TB]), E["o1"][:])
                S.stt("dve", E["o0"][:], E["kk"][:], -1.0, E["ex"][:], ALU.mult, ALU.mult)
                S.dma("pool", R["AR"].r(R["AR"].h[f0:f0 + 128, t0:t0 + TB]), E["o0"][:])
                S.tt("dve", E["kmod"][:], E["kmod"][:], E["em"][:], ALU.mult)
                S.dma("pool", R["KB"].r(R["KB"].h[f0:f0 + 128, t0:t0 + TB]), E["kmod"][:])
                S.tt("dve", E["b"][:], E["b"][:], E["em"][:], ALU.mult)
                S.dma("pool", R["BB"].r(R["BB"].h[f0:f0 + 128, t0:t0 + TB]), E["b"][:])
                S.copy("dve", gl[:], E["ep"].r(E["ep"].h[:, 63:TB:64]))
                S.dma("pool", R["GL"].r(R["GL"].h[f0:f0 + 128, t0 // 64:t0 // 64 + NCH]), gl[:])
                elb = gl.r(gl.h[:, :].unsqueeze(2).to_broadcast([128, NCH, 64]))
                S.tt("dve", E["kmod"].r(E["kmod"].h[:, :].rearrange("p (c t) -> p c t", t=64)),
                     E["kmod"].r(E["kmod"].h[:, :].rearrange("p (c t) -> p c t", t=64)), elb, ALU.mult)
                S.tt("dve", E["b"].r(E["b"].h[:, :].rearrange("p (c t) -> p c t", t=64)),
                     E["b"].r(E["b"].h[:, :].rearrange("p (c t) -> p c t", t=64)), elb, ALU.mult)
                for (src, dst) in ((E["kmod"], R["KT"]), (E["b"], R["BT"]), (E["v"], R["VT"])):
                    for s in range(TB // 128):
                        pt_ = psT[ti % 2]
                        q_ = tq[ti % 2]
                        ti += 1
                        S.transpose(pt_[:, :], src[:, s * 128:(s + 1) * 128], ident[:])
                        S.copy("act", q_[:], pt_[:, :])
                        S.dma("sp", dst.r(dst.h[t0 + s * 128:t0 + (s + 1) * 128, f0:f0 + 128]), q_[:])


def phase_B1(S, C, I, T, R, moT):
    SC = 256
    NCS = SC // 64
    NCH = T // 64
    GN_EPS = 64e-5
    F32R = mybir.dt.float32r
    with Phase(S):
        banks = [S.ps([64, 8, 64], F32, "bk%d" % i) for i in range(7)]
        psE = S.ps([64, 512], F32, "psE")
        bstate = [0]

        def mmr(out, lhsT, rhs, start=True, stop=True):
            return S.mm(out, lhsT, rhs, start=start, stop=stop)

        def bank():
            b = banks[bstate[0] % 7]
            bstate[0] += 1
            return b
        cm = S.sb([64, 4, 8, 64], F32, "cmask")
        S.dma("sp", cm[:], I["rw_masks"][:])
        MS, MI, MST, I8 = [cm.r(cm.h[:, i]) for i in range(4)]
        lng = S.sb([64, 2, 8], F32, "lng")
        S.dma("sp", lng[:], I["od_ln_pk64"][:])
        ones64 = C.ones_f[0:64, 0:64]
        fm = {k: [S.sb([64, 8, SC], F32R, "%s%d" % (k, i)) for i in range(2)] for k in ("AR", "RR", "KB", "BB")}
        tm = {k: [S.sb([64, NCS, 512], F32R, "%s%d" % (k, i)) for i in range(2)] for k in ("KT", "BT", "VT")}
        ep = {k: S.sb([64, 8, SC], F32, k) for k in ("BON", "GG")}
        GLt = S.sb([64, 8, NCH], F32, "GLt")
        ST = S.sb([64, 8, 64], F32R, "ST")
        Y = S.sb([64, 8, SC], F32, "Y")
        Yc = S.sb([64, 8, SC], F32, "Yc")
        Ysq = S.sb([64, 8, SC], F32, "Ysq")
        rstd = S.sb([64, 512], F32, "rstd")
        ob = S.sb([64, 8, SC], BF16, "ob")
        A = {k: S.sb([64, 8, 64], F32R, k) for k in ("N", "NT", "ak", "kr", "br", "Tm", "X", "UT")}
        Mp = [S.sb([64, 8, 64], F32R, "M%d" % i) for i in range(2)]
        MTp = [S.sb([64, 8, 64], F32R, "MT%d" % i) for i in range(2)]
        for g in range(1):
            rows = slice(g * 512, (g + 1) * 512)
            S.dma("sp", GLt[:], R["GL"].r(R["GL"].h[rows, :].rearrange("(h i) c -> i h c", i=64)))
            S.ts("dve", ST[:], I8, 0.0, None, ALU.mult)
            for si, t0 in enumerate(range(0, T, SC)):
                F = {}
                for k in fm:
                    F[k] = fm[k][si % 2]
                    S.dma("sp" if k in ("AR", "KB") else "act", F[k][:],
                          R[k].r(R[k].h[rows, t0:t0 + SC].rearrange("(h i) t -> i h t", i=64).bitcast(F32R)))
                for k in tm:
                    F[k] = tm[k][si % 2]
                    S.dma("sp", F[k][:], R[k].r(R[k].h[t0:t0 + SC, rows].rearrange("(c s) f -> s c f", s=64).bitcast(F32R)))
                for k in ep:
                    S.dma("act", ep[k][:], R[k].r(R[k].h[rows, t0:t0 + SC].rearrange("(h i) t -> i h t", i=64)))
                for c in range(NCS):
                    cs = slice(c * 64, (c + 1) * 64)
                    cg = t0 // 64 + c

                    def hv(name, h):
                        return F[name][:, c, h * 64:(h + 1) * 64]
                    for (dst, l, r_, msk) in ((A["N"], "BB", "AR", MS), (A["NT"], "AR", "BB", MST)):
                        p = bank()
                        for h in range(8):
                            mmr(p[:, h, :], F[l][:, h, cs], F[r_][:, h, cs])
                        S.tt("dve", dst[:], p[:], msk, ALU.mult)
                    late = [(A["ak"], "KB", "AR", MS), (A["kr"], "KB", "RR", MI), (A["br"], "BB", "RR", MI)]
                    S.tt("dve", A["Tm"][:], A["N"][:], I8, ALU.add)
                    M, MT = A["N"], A["NT"]
                    for lv in range(5):
                        p1, p2 = bank(), bank()
                        for h in range(8):
                            mmr(p1[:, h, :], MT[:, h, :], M[:, h, :])
                            mmr(p2[:, h, :], M[:, h, :], MT[:, h, :])
                        pl = None
                        if lv < len(late):
                            dstl, l_, r_, mskl = late[lv]
                            pl = bank()
                            for h in range(8):
                                mmr(pl[:, h, :], F[l_][:, h, cs], F[r_][:, h, cs])
                        M2, MT2 = Mp[lv % 2], MTp[lv % 2]
                        S.copy("act", M2[:], p1[:])
                        S.copy("dve", MT2[:], p2[:])
                        p3 = bank()
                        for h in range(8):
                            mmr(p3[:, h, :], MT2[:, h, :], A["Tm"][:, h, :])
                        S.tt("dve", A["Tm"][:], A["Tm"][:], p3[:], ALU.add)
                        if pl is not None:
                            S.tt("dve", dstl[:], pl[:], mskl, ALU.mult)
                        M, MT = M2, MT2
                    px = bank()
                    for h in range(8):
                        mmr(px[:, h, :], F["AR"][:, h, cs], ST[:, h, :], start=True, stop=False)
                        mmr(px[:, h, :], A["ak"][:, h, :], hv("VT", h), start=False, stop=True)
                    S.copy("act", A["X"][:], px[:])
                    pu = bank()
                    for h in range(8):
                        mmr(pu[:, h, :], A["Tm"][:, h, :], A["X"][:, h, :])
                    S.copy("act", A["UT"][:], pu[:])
                    py = bank()
                    for h in range(8):
                        mmr(py[:, h, :], ST[:, h, :], F["RR"][:, h, cs], start=True, stop=False)
                        mmr(py[:, h, :], hv("VT", h), A["kr"][:, h, :], start=False, stop=False)
                        mmr(py[:, h, :], A["UT"][:, h, :], A["br"][:, h, :], start=False, stop=True)
                    S.copy("act", Y[:, :, cs], py[:])
                    pst = bank()
                    for h in range(8):
                        mmr(pst[:, h, :], hv("KT", h), hv("VT", h), start=True, stop=False)
                        mmr(pst[:, h, :], hv("BT", h), A["UT"][:, h, :], start=False, stop=True)
                    S.tt("dve", ST[:], ST[:], GLt.r(GLt.h[:, :, cg:cg + 1].to_broadcast([64, 8, 64])), ALU.mult)
                    S.tt("dve", ST[:], ST[:], pst[:], ALU.add)
                for q4 in range(8 * SC // 512):
                    hs_ = slice(q4 * (512 // SC), (q4 + 1) * (512 // SC))
                    S.mm(psE[:, :], ones64, Y[:, hs_, :])
                    S.stt("dve", Yc[:, hs_, :], psE[:, :], -1.0 / 64, Y[:, hs_, :], ALU.mult, ALU.add)
                    S.act(Ysq[:, hs_, :], Yc[:, hs_, :], AF.Square)
                    S.mm(psE[:, :], ones64, Ysq[:, hs_, :])
                    rsqrt_to(S, rstd[:], psE[:, :], 1.0 / 64, GN_EPS)
                    S.tt("dve", Yc[:, hs_, :], Yc[:, hs_, :], rstd[:], ALU.mult)
                for h in range(8):
                    hg = g * 8 + h
                    S.act(Yc[:, h, :], Yc[:, h, :], AF.Identity, bias=lng[:, 1, hg:hg + 1], scale=lng[:, 0, hg:hg + 1])
                S.tt("pool", Yc[:], Yc[:], ep["BON"][:], ALU.add)
                S.tt("dve", ob[:], Yc[:], ep["GG"][:], ALU.mult)
                S.dma("pool", moT.r(moT.h[rows, t0:t0 + SC].rearrange("(h i) t -> i h t", i=64)), ob[:])


_NC_CACHE = {}


def kernel(**inputs):
    T = 4096
    if "nc" not in _NC_CACHE:
        _NC_CACHE["nc"] = build(T, "full")
    nc = _NC_CACHE["nc"]
    maps = [host_inputs(inputs, c % 4, c // 4, T) for c in range(8)]
    res = run_bass_kernel_spmd(nc, maps, core_ids=list(range(8)))
    out = np.stack([np.ascontiguousarray(np.asarray(res.results[b]["yT"]).T) for b in range(4)], axis=0)
    return out.astype(np.float32)
```
